# Optimizing a Trainium2 kernel written in Bass

```python
import math
import jax, jax.numpy as jnp
from jax import lax
import numpy as np

D_MODEL = 1024
BATCH = 1
SEQ = 16384
DEPTH = 2

N_MIXERS = 2
HEAD_DIM = 64
N_HEADS = D_MODEL // HEAD_DIM
N_RWKV = (DEPTH + 1) // 2
N_MOBA = DEPTH // 2
DECAY_LORA = max(32, int(round(1.8 * D_MODEL ** 0.5 / 32)) * 32)
AAA_LORA = max(32, int(round(1.8 * D_MODEL ** 0.5 / 32)) * 32)
GATE_LORA = max(32, int(round(0.6 * D_MODEL ** 0.8 / 32)) * 32)
GN_EPS = HEAD_DIM * 1e-5
MOBA_BLOCK = 256
MOBA_TOPK = 3
Q_CHUNK = 128
ROPE_THETA = 10000.0
D_FF = -(-8 * D_MODEL // (3 * 256)) * 256
ALPHA = (2 * DEPTH) ** 0.25
BETA = (8 * DEPTH) ** -0.25
LN_EPS = 1e-5

kernel_name = 'rwkv7_moba_deepnorm_hybrid'


def layer_norm(x, g, b):
    xf = x.astype(jnp.float32)
    mu = xf.mean(-1, keepdims=True)
    var = jnp.square(xf - mu).mean(-1, keepdims=True)
    return ((xf - mu) * lax.rsqrt(var + LN_EPS) * g + b).astype(x.dtype)


def swiglu_ffn(x, w_in, w_down):
    gate, up = jnp.split(x @ w_in, 2, axis=-1)
    return (jax.nn.silu(gate) * up) @ w_down


def time_shift(x):
    return jnp.pad(x, ((0, 0), (1, 0), (0, 0)))[:, :-1, :]


def rwkv7_time_mix(x, mu, w_rkv, w0, w1, w2, a0, a1, a2, g1, g2, k_k, k_a, r_k, gn_g, gn_b, w_o):
    B, T, C = x.shape
    H, N = N_HEADS, HEAD_DIM
    f32 = jnp.float32
    xx = time_shift(x) - x
    xr, xw, xk, xv, xa, xg = [x + xx * mu[n] for n in range(6)]
    r, k, v = jnp.einsum('nbtc,ncd->nbtd', jnp.stack([xr, xk, xv]), w_rkv)
    w = -jax.nn.softplus(-(w0 + jnp.tanh(xw @ w1) @ w2)) - 0.5
    a = jax.nn.sigmoid(a0 + (xa @ a1) @ a2)
    g = jax.nn.sigmoid(xg @ g1) @ g2
    heads = lambda t: t.reshape(B, T, H, N).astype(f32)
    kk = heads(k * k_k)
    kk = kk / jnp.maximum(jnp.sqrt(jnp.sum(kk * kk, -1, keepdims=True)), 1e-12)
    k = k * (1 + (a - 1) * k_a)
    rh, kh, vh, ah = heads(r), heads(k), heads(v), heads(a)
    decay = jnp.exp(-jnp.exp(heads(w)))

    def step(S, inp):
        r_t, w_t, k_t, v_t, kk_t, b_t = inp
        sa = jnp.einsum('bhvk,bhk->bhv', S, -kk_t)
        S = S * w_t[:, :, None, :] + sa[..., None] * b_t[:, :, None, :] + v_t[..., None] * k_t[:, :, None, :]
        return S, jnp.einsum('bhvk,bhk->bhv', S, r_t)

    to_time = lambda t: jnp.swapaxes(t, 0, 1)
    S0 = jnp.zeros((B, H, N, N), f32)
    _, y = lax.scan(step, S0, tuple(to_time(t) for t in (rh, decay, kh, vh, kk, kk * ah)))
    y = to_time(y)
    mean = y.mean(-1, keepdims=True)
    var = jnp.square(y - mean).mean(-1, keepdims=True)
    y = ((y - mean) * lax.rsqrt(var + GN_EPS)).reshape(B, T, C) * gn_g + gn_b
    bonus = jnp.sum(rh * kh * r_k, -1, keepdims=True) * vh
    y = (y + bonus.reshape(B, T, C)).astype(x.dtype)
    return (y * g) @ w_o


def rope_tables(T):
    inv = ROPE_THETA ** (-jnp.arange(0, HEAD_DIM, 2, dtype=jnp.float32) / HEAD_DIM)
    ang = jnp.arange(T, dtype=jnp.float32)[:, None] * inv[None, :]
    return jnp.cos(ang), jnp.sin(ang)


def apply_rope(t, cos, sin):
    t1, t2 = jnp.split(t, 2, axis=-1)
    return jnp.concatenate([t1 * cos - t2 * sin, t2 * cos + t1 * sin], axis=-1)


def moba_attention(x, w_qkv, w_o):
    B, T, C = x.shape
    H, Dh = N_HEADS, HEAD_DIM
    f32 = jnp.float32
    qkv = (x @ w_qkv).reshape(B, T, 3, H, Dh)
    q, k, v = [jnp.transpose(qkv[:, :, n], (0, 2, 1, 3)).astype(f32) for n in range(3)]
    cos, sin = rope_tables(T)
    q = apply_rope(q, cos, sin) * (Dh ** -0.5)
    k = apply_rope(k, cos, sin)
    NB = -(-T // MOBA_BLOCK)
    K = min(MOBA_TOPK, NB)
    pad = NB * MOBA_BLOCK - T
    kb = jnp.pad(k, ((0, 0), (0, 0), (0, pad), (0, 0))).reshape(B, H, NB, MOBA_BLOCK, Dh)
    vb = jnp.pad(v, ((0, 0), (0, 0), (0, pad), (0, 0))).reshape(B, H, NB, MOBA_BLOCK, Dh)
    k_mean = kb.mean(axis=3)
    bi = jnp.arange(B)[:, None, None, None]
    hi = jnp.arange(H)[None, :, None, None]
    blk_ids = jnp.arange(NB)
    key_off = jnp.arange(MOBA_BLOCK)
    q_off = jnp.arange(Q_CHUNK)

    def attend_chunk(c):
        q0 = c * Q_CHUNK
        own = q0 // MOBA_BLOCK
        q_c = lax.dynamic_slice_in_dim(q, q0, Q_CHUNK, axis=2)
        gate = jnp.einsum('bhqd,bhnd->bhqn', q_c, k_mean)
        gate = jnp.where(blk_ids < own, gate, -jnp.inf)
        _, sel = lax.top_k(gate, K)
        valid = sel < own
        k_sel = kb[bi, hi, sel]
        v_sel = vb[bi, hi, sel]
        s_sel = jnp.einsum('bhqd,bhqksd->bhqks', q_c, k_sel)
        s_sel = jnp.where(valid[..., None], s_sel, -jnp.inf).reshape(B, H, Q_CHUNK, K * MOBA_BLOCK)
        k_own = lax.dynamic_index_in_dim(kb, own, axis=2, keepdims=False)
        v_own = lax.dynamic_index_in_dim(vb, own, axis=2, keepdims=False)
        s_own = jnp.einsum('bhqd,bhsd->bhqs', q_c, k_own)
        causal = (own * MOBA_BLOCK + key_off)[None, :] <= (q0 + q_off)[:, None]
        s_own = jnp.where(causal, s_own, -jnp.inf)
        p = jax.nn.softmax(jnp.concatenate([s_sel, s_own], axis=-1), axis=-1)
        p_sel = p[..., :K * MOBA_BLOCK].reshape(B, H, Q_CHUNK, K, MOBA_BLOCK)
        p_own = p[..., K * MOBA_BLOCK:]
        return (jnp.einsum('bhqks,bhqksd->bhqd', p_sel, v_sel)
                + jnp.einsum('bhqs,bhsd->bhqd', p_own, v_own))

    out = lax.map(attend_chunk, jnp.arange(T // Q_CHUNK))
    out = jnp.transpose(out, (1, 0, 3, 2, 4)).reshape(B, T, C)
    return out.astype(x.dtype) @ w_o


def setup_inputs(seed: int = 0) -> dict:
    key = jax.random.key(seed)
    ks = iter(jax.random.split(key, 32))
    C, H, N, F = D_MODEL, N_HEADS, HEAD_DIM, D_FF
    nrm = lambda shape, s: jax.random.normal(next(ks), shape, jnp.float32) * s
    return {
        'x': nrm((BATCH, SEQ, C), 1.0),
        'rwkv_mu': jax.random.uniform(next(ks), (N_RWKV, 6, C), jnp.float32),
        'rwkv_w_rkv': nrm((N_RWKV, 3, C, C), C ** -0.5),
        'rwkv_w0': jax.random.uniform(next(ks), (N_RWKV, C), jnp.float32, -5.0, 0.5),
        'rwkv_w1': nrm((N_RWKV, C, DECAY_LORA), C ** -0.5),
        'rwkv_w2': nrm((N_RWKV, DECAY_LORA, C), 0.5 * DECAY_LORA ** -0.5),
        'rwkv_a0': nrm((N_RWKV, C), 0.1),
        'rwkv_a1': nrm((N_RWKV, C, AAA_LORA), C ** -0.5),
        'rwkv_a2': nrm((N_RWKV, AAA_LORA, C), 0.5 * AAA_LORA ** -0.5),
        'rwkv_g1': nrm((N_RWKV, C, GATE_LORA), C ** -0.5),
        'rwkv_g2': nrm((N_RWKV, GATE_LORA, C), GATE_LORA ** -0.5),
        'rwkv_k_k': 0.85 + nrm((N_RWKV, C), 0.05),
        'rwkv_k_a': 1.0 + nrm((N_RWKV, C), 0.05),
        'rwkv_r_k': nrm((N_RWKV, H, N), 0.1),
        'rwkv_gn_g': 1.0 + nrm((N_RWKV, C), 0.05),
        'rwkv_gn_b': nrm((N_RWKV, C), 0.02),
        'rwkv_w_o': nrm((N_RWKV, C, C), BETA * C ** -0.5),
        'moba_w_qkv': nrm((N_MOBA, C, 3 * C), C ** -0.5),
        'moba_w_o': nrm((N_MOBA, C, C), BETA * C ** -0.5),
        'ffn_w_in': nrm((DEPTH, C, 2 * F), C ** -0.5),
        'ffn_w_down': nrm((DEPTH, F, C), BETA * F ** -0.5),
        'ln_mix_g': 1.0 + nrm((DEPTH, C), 0.05),
        'ln_mix_b': nrm((DEPTH, C), 0.02),
        'ln_ffn_g': 1.0 + nrm((DEPTH, C), 0.05),
        'ln_ffn_b': nrm((DEPTH, C), 0.02),
    }


def reference(x, rwkv_mu, rwkv_w_rkv, rwkv_w0, rwkv_w1, rwkv_w2, rwkv_a0, rwkv_a1, rwkv_a2,
              rwkv_g1, rwkv_g2, rwkv_k_k, rwkv_k_a, rwkv_r_k, rwkv_gn_g, rwkv_gn_b, rwkv_w_o,
              moba_w_qkv, moba_w_o, ffn_w_in, ffn_w_down, ln_mix_g, ln_mix_b, ln_ffn_g, ln_ffn_b):
    for i in range(DEPTH):
        j = i // N_MIXERS
        if i % N_MIXERS == 0:
            h = rwkv7_time_mix(x, rwkv_mu[j], rwkv_w_rkv[j], rwkv_w0[j], rwkv_w1[j], rwkv_w2[j],
                               rwkv_a0[j], rwkv_a1[j], rwkv_a2[j], rwkv_g1[j], rwkv_g2[j],
                               rwkv_k_k[j], rwkv_k_a[j], rwkv_r_k[j], rwkv_gn_g[j], rwkv_gn_b[j],
                               rwkv_w_o[j])
        else:
            h = moba_attention(x, moba_w_qkv[j], moba_w_o[j])
        x = layer_norm(ALPHA * x + h, ln_mix_g[i], ln_mix_b[i])
        x = layer_norm(ALPHA * x + swiglu_ffn(x, ffn_w_in[i], ffn_w_down[i]), ln_ffn_g[i], ln_ffn_b[i])
    return x
```

```python
import math
from contextlib import ExitStack

import numpy as np
import ml_dtypes

import concourse.bass as bass
import concourse.mybir as mybir
from concourse.bass_utils import run_bass_kernel_spmd

F32 = mybir.dt.float32
BF16 = mybir.dt.bfloat16
ALU = mybir.AluOpType
AF = mybir.ActivationFunctionType
AX = mybir.AxisListType

NCORES = 8
T = 16384
C = 1024
H = 16
DH = 64
DFF = 2816
DEPTH = 2
ALPHA = (2 * DEPTH) ** 0.25
LN_EPS = 1e-5
GN_EPS = 64 * 1e-5
NEG = -30000.0


class _Op:
    __slots__ = ("eng", "fn", "deps", "is_dma", "semkey", "sig", "need_sig", "idx")


class Prog:
    ENGS = ("pe", "act", "dve", "pool", "sp")

    def __init__(self, nc):
        self.nc = nc
        self.ops = []
        self.last_w = {}
        self.readers = {}
        self.store_keys = []

    def _deps(self, r, w):
        deps = set()
        for k in list(r) + list(w):
            lw = self.last_w.get(k)
            if lw is not None:
                deps.add(lw)
        for k in w:
            for rd in self.readers.get(k, ()):
                deps.add(rd)
        return deps

    def _commit(self, idx, r, w):
        for k in r:
            self.readers.setdefault(k, []).append(idx)
        for k in w:
            self.last_w[k] = idx
            self.readers[k] = []

    def op(self, eng, fn, r=(), w=()):
        o = _Op()
        o.eng, o.fn, o.is_dma, o.semkey, o.sig, o.need_sig = eng, fn, False, None, None, False
        o.deps = self._deps(r, w)
        o.idx = len(self.ops)
        self.ops.append(o)
        self._commit(o.idx, r, w)
        return o.idx

    def dma(self, q, out, in_, r=(), w=(), semkey=None, store=False):
        o = _Op()
        o.eng, o.is_dma, o.sig, o.need_sig = q, True, None, True
        o.fn = lambda e, out=out, in_=in_: e.dma_start(out=out, in_=in_)
        o.semkey = semkey if semkey is not None else (list(w)[0] if w else list(r)[0])
        o.deps = self._deps(r, w)
        o.idx = len(self.ops)
        self.ops.append(o)
        self._commit(o.idx, r, w)
        if store:
            self.store_keys.append(o.semkey)
        return o.idx

    def emit(self):
        nc = self.nc
        ops = self.ops
        for o in ops:
            for d in o.deps:
                od = ops[d]
                if od.is_dma:
                    continue
                if od.eng == o.eng and not o.is_dma and o.eng == "pe":
                    continue
                od.need_sig = True
        cnt = {e: 0 for e in self.ENGS}
        dcnt = {}
        for o in ops:
            if o.is_dma:
                dcnt[o.semkey] = dcnt.get(o.semkey, 0) + 16
                o.sig = ("d:" + o.semkey, dcnt[o.semkey])
            elif o.need_sig:
                cnt[o.eng] += 1
                o.sig = ("e:" + o.eng, cnt[o.eng])
        semnames = ["e:" + e for e in self.ENGS] + ["d:" + k for k in dcnt]
        with ExitStack() as es:
            sems = {}
            for i, n in enumerate(semnames):
                sems[n] = es.enter_context(nc.semaphore("s%d" % i))
            block = es.enter_context(nc.Block())
            per_eng = {e: [o for o in ops if o.eng == e] for e in self.ENGS}
            final_waits = [("d:" + k, dcnt[k]) for k in dict.fromkeys(self.store_keys)]

            def run(eng_name, e):
                waited = {}
                for o in per_eng[eng_name]:
                    need = {}
                    for d in o.deps:
                        od = ops[d]
                        if od.sig is None:
                            continue
                        if (not od.is_dma) and od.eng == eng_name and eng_name == "pe" and not o.is_dma:
                            continue
                        s, v = od.sig
                        if need.get(s, 0) < v:
                            need[s] = v
                    for s, v in need.items():
                        if waited.get(s, 0) < v:
                            e.wait_ge(sems[s], v)
                            waited[s] = v
                    ins = o.fn(e)
                    if o.sig is not None:
                        s, v = o.sig
                        ins.then_inc(sems[s], 16 if o.is_dma else 1)
                if eng_name == "sp":
                    for s, v in final_waits:
                        e.wait_ge(sems[s], v)

            @block.tensor
            def _(e):
                run("pe", e)

            @block.scalar
            def _(e):
                run("act", e)

            @block.vector
            def _(e):
                run("dve", e)

            @block.gpsimd
            def _(e):
                run("pool", e)

            @block.sync
            def _(e):
                run("sp", e)


def _bcast_rows(ap_1d, nparts):
    return ap_1d.partition_broadcast(nparts)


TOK = T // NCORES
GRP = 512
NGRP = TOK // GRP
NFB = DFF // 128


def build_post(with_qkv):
    nc = bass.Bass("TRN2", target_bir_lowering=False)
    aT = nc.dram_tensor("aT", [C, TOK], BF16, kind="ExternalInput").ap()
    xres = nc.dram_tensor("xres", [TOK, C], F32, kind="ExternalInput").ap()
    w_o = nc.dram_tensor("w_o", [C, C], F32, kind="ExternalInput").ap()
    w_in = nc.dram_tensor("w_in", [C, 2 * DFF], F32, kind="ExternalInput").ap()
    w_dn = nc.dram_tensor("w_dn", [DFF, C], F32, kind="ExternalInput").ap()
    lnp = nc.dram_tensor("lnp", [4, C], F32, kind="ExternalInput").ap()
    xout = nc.dram_tensor("xout", [TOK, C], F32, kind="ExternalOutput").ap()
    if with_qkv:
        w_qkv = nc.dram_tensor("w_qkv", [C, 3 * C], F32, kind="ExternalInput").ap()
        rope = nc.dram_tensor("rope", [TOK, 4, 32], F32, kind="ExternalInput").ap()
        qkv_out = nc.dram_tensor("qkv", [3, TOK, C], F32, kind="ExternalOutput").ap()

    es = ExitStack()
    sb = lambda name, shape, dt: es.enter_context(nc.sbuf_tensor(name, shape, dt))
    ps = lambda name, shape, dt: es.enter_context(nc.psum_tensor(name, shape, dt))
    with es:
        ident = sb("ident", [128, 128], BF16)
        lnb = sb("lnb", [128, 4, C], F32)
        wo_sb = sb("wo_sb", [128, 8, C], BF16)
        wdn_sb = sb("wdn_sb", [128, NFB, C], BF16)
        win_sb = [sb("win%d" % i, [128, 8, 512], BF16) for i in range(2)]
        aT_sb = [sb("aT%d" % i, [128, 8, GRP], BF16) for i in range(2)]
        xr_sb = [sb("xr%d" % i, [128, C], F32) for i in range(2)]
        xg = sb("xg", [128, 4, C], F32)
        xb = sb("xb", [128, C], BF16)
        xT = sb("xT", [128, 8, GRP], BF16)
        actT = sb("actT", [128, NFB, GRP], BF16)
        sg = [sb("sg%d" % i, [128, GRP], F32) for i in range(2)]
        junk = sb("junk", [128, C], F32)
        st = sb("st", [128, 8], F32)
        if with_qkv:
            wq_sb = [sb("wq%d" % i, [128, 8, 512], BF16) for i in range(2)]
            rp_sb = sb("rp", [128, 4, 4, 32], F32)
            qo = [sb("qo%d" % i, [128, 512], F32) for i in range(2)]
            tmp = [sb("tmp%d" % i, [128, 8, 32], F32) for i in range(4)]
        acc = ps("acc", [128, C], F32)
        trp = ps("trp", [128, C], BF16)
        gu = [ps("gu%d" % i, [128, 2, GRP], F32) for i in range(2)]

        p = Prog(nc)
        p.op("pool", lambda e: e.memset(ident[:], 0.0), w=["ident"])
        p.op("pool", lambda e: e.affine_select(out=ident[:], in_=ident[:], pattern=[[-1, 128]],
                                               compare_op=ALU.not_equal, fill=1.0, base=0,
                                               channel_multiplier=1), r=["ident"], w=["ident"])
        p.dma("sp", lnb[:], lnp.partition_broadcast(128), w=["lnb"])
        for kc in range(8):
            p.dma("pool", wo_sb[:, kc, :], w_o[kc * 128:(kc + 1) * 128, :], w=["wo"], semkey="wo")
        for fb in range(NFB):
            p.dma("pool", wdn_sb[:, fb, :], w_dn[fb * 128:(fb + 1) * 128, :], w=["wdn"], semkey="wdn")

        def layer_norm(tile_ap, gi, key_in, key_out, out_ap=None):
            p.op("act", lambda e: e.activation(out=junk[:], in_=tile_ap, func=AF.Copy, accum_out=st[:, 0:1]),
                 r=[key_in], w=["junk", "st"])
            p.op("act", lambda e: e.activation(out=junk[:], in_=tile_ap, func=AF.Square, accum_out=st[:, 1:2]),
                 r=[key_in], w=["junk", "st"])
            p.op("dve", lambda e: e.tensor_scalar(out=st[:, 2:3], in0=st[:, 0:1], scalar1=1.0 / C, scalar2=None, op0=ALU.mult),
                 r=["st"], w=["st2"])
            p.op("dve", lambda e: e.tensor_tensor(out=st[:, 3:4], in0=st[:, 2:3], in1=st[:, 2:3], op=ALU.mult),
                 r=["st2"], w=["st3"])
            p.op("dve", lambda e: e.scalar_tensor_tensor(out=st[:, 4:5], in0=st[:, 1:2], scalar=1.0 / C, in1=st[:, 3:4],
                                                         op0=ALU.mult, op1=ALU.subtract), r=["st", "st3"], w=["st4"])
            p.op("dve", lambda e: e.tensor_scalar(out=st[:, 4:5], in0=st[:, 4:5], scalar1=LN_EPS, scalar2=None, op0=ALU.add),
                 r=["st4"], w=["st4"])
            p.op("act", lambda e: e.activation(out=st[:, 5:6], in_=st[:, 4:5], func=AF.Sqrt), r=["st4"], w=["st5"])
            p.op("dve", lambda e: e.reciprocal(out=st[:, 6:7], in_=st[:, 5:6]), r=["st5"], w=["st6"])
            dst = tile_ap if out_ap is None else out_ap
            p.op("dve", lambda e: e.tensor_scalar(out=dst, in0=tile_ap, scalar1=st[:, 2:3], scalar2=st[:, 6:7],
                                                  op0=ALU.subtract, op1=ALU.mult), r=[key_in, "st2", "st6"], w=[key_out])
            p.op("pool", lambda e: e.tensor_tensor(out=dst, in0=dst, in1=lnb[:, gi, :], op=ALU.mult),
                 r=[key_out, "lnb"], w=[key_out])
            p.op("pool", lambda e: e.tensor_tensor(out=dst, in0=dst, in1=lnb[:, gi + 1, :], op=ALU.add),
                 r=[key_out, "lnb"], w=[key_out])

        def to_channel_major(tile_ap, key_in, ti):
            p.op("act", lambda e: e.activation(out=xb[:], in_=tile_ap, func=AF.Copy), r=[key_in], w=["xb"])
            for kc in range(8):
                p.op("pe", lambda e, kc=kc: e.transpose(trp[:, kc * 128:(kc + 1) * 128], xb[:, kc * 128:(kc + 1) * 128], ident[:]),
                     r=["xb", "ident"], w=["trp"])
            p.op("dve", lambda e: e.tensor_copy(out=xT[:, :, ti * 128:(ti + 1) * 128],
                                                in_=trp[:].rearrange("p (k t) -> p k t", k=8)),
                 r=["trp"], w=["xT"])

        for g in range(NGRP):
            t0 = g * GRP
            a_sb = aT_sb[g % 2]
            ak = "aT%d" % (g % 2)
            for kc in range(8):
                p.dma("sp", a_sb[:, kc, :], aT[kc * 128:(kc + 1) * 128, t0:t0 + GRP], w=[ak], semkey=ak)
            if with_qkv:
                p.dma("sp", rp_sb[:], rope[t0:t0 + GRP].rearrange("(i p) f c -> p i f c", p=128), w=["rp"])
            for ti in range(4):
                r0 = t0 + ti * 128
                xr = xr_sb[(g * 4 + ti) % 2]
                xk = "xr%d" % ((g * 4 + ti) % 2)
                p.dma("sp", xr[:], xres[r0:r0 + 128, :], w=[xk], semkey=xk)
                for hf in range(2):
                    for kc in range(8):
                        p.op("pe", lambda e, kc=kc, hf=hf, ti=ti, a_sb=a_sb: e.matmul(
                            acc[:, hf * 512:(hf + 1) * 512], lhsT=a_sb[:, kc, ti * 128:(ti + 1) * 128],
                            rhs=wo_sb[:, kc, hf * 512:(hf + 1) * 512], start=(kc == 0), stop=(kc == 7)),
                            r=[ak, "wo"], w=["acc"])
                xk_out = "xg%d" % ti
                p.op("dve", lambda e, xr=xr, ti=ti: e.scalar_tensor_tensor(out=xg[:, ti, :], in0=xr[:], scalar=ALPHA, in1=acc[:],
                                                                           op0=ALU.mult, op1=ALU.add),
                     r=[xk, "acc"], w=[xk_out])
                layer_norm(xg[:, ti, :], 0, xk_out, xk_out)
                to_channel_major(xg[:, ti, :], xk_out, ti)
            NU = NFB // 2
            for u in range(NU):
                wsb = win_sb[(g * NU + u) % 2]
                wk = "win%d" % ((g * NU + u) % 2)
                for kc in range(8):
                    p.dma("pool", wsb[:, kc, 0:256], w_in[kc * 128:(kc + 1) * 128, u * 256:(u + 1) * 256], w=[wk], semkey=wk)
                    p.dma("pool", wsb[:, kc, 256:512], w_in[kc * 128:(kc + 1) * 128, DFF + u * 256:DFF + (u + 1) * 256], w=[wk], semkey=wk)
                for j in range(2):
                    fb = u * 2 + j
                    gps = gu[fb % 2]
                    gk = "gu%d" % (fb % 2)
                    for which in range(2):
                        for kc in range(8):
                            p.op("pe", lambda e, kc=kc, which=which, j=j, wsb=wsb, gps=gps: e.matmul(
                                gps[:, which, :], lhsT=wsb[:, kc, which * 256 + j * 128: which * 256 + (j + 1) * 128],
                                rhs=xT[:, kc, :], start=(kc == 0), stop=(kc == 7)),
                                r=[wk, "xT"], w=[gk])
                    s_sb = sg[fb % 2]
                    sk = "sg%d" % (fb % 2)
                    p.op("act", lambda e, gps=gps, s_sb=s_sb: e.activation(out=s_sb[:], in_=gps[:, 0, :], func=AF.Silu),
                         r=[gk], w=[sk])
                    p.op("dve", lambda e, gps=gps, s_sb=s_sb, fb=fb: e.tensor_tensor(out=actT[:, fb, :], in0=s_sb[:], in1=gps[:, 1, :], op=ALU.mult),
                         r=[gk, sk], w=["actT"])
            for ti in range(4):
                r0 = t0 + ti * 128
                for hf in range(2):
                    for fb in range(NFB):
                        p.op("pe", lambda e, fb=fb, hf=hf, ti=ti: e.matmul(
                            acc[:, hf * 512:(hf + 1) * 512], lhsT=actT[:, fb, ti * 128:(ti + 1) * 128],
                            rhs=wdn_sb[:, fb, hf * 512:(hf + 1) * 512], start=(fb == 0), stop=(fb == NFB - 1)),
                            r=["actT", "wdn"], w=["acc"])
                xk_out = "xg%d" % ti
                p.op("dve", lambda e, ti=ti: e.scalar_tensor_tensor(out=xg[:, ti, :], in0=xg[:, ti, :], scalar=ALPHA, in1=acc[:],
                                                                    op0=ALU.mult, op1=ALU.add),
                     r=[xk_out, "acc"], w=[xk_out])
                layer_norm(xg[:, ti, :], 2, xk_out, xk_out)
                p.dma("sp", xout[r0:r0 + 128, :], xg[:, ti, :], r=[xk_out], semkey="o_" + xk_out, store=True)
                if with_qkv:
                    to_channel_major(xg[:, ti, :], xk_out, ti)
            if with_qkv:
                for cb in range(6):
                    wsb = wq_sb[(g * 6 + cb) % 2]
                    wk = "wq%d" % ((g * 6 + cb) % 2)
                    for kc in range(8):
                        p.dma("pool", wsb[:, kc, :], w_qkv[kc * 128:(kc + 1) * 128, cb * 512:(cb + 1) * 512], w=[wk], semkey=wk)
                    for ti in range(4):
                        r0 = t0 + ti * 128
                        gps = gu[(cb * 4 + ti) % 2]
                        gk = "gu%d" % ((cb * 4 + ti) % 2)
                        for kc in range(8):
                            p.op("pe", lambda e, kc=kc, ti=ti, wsb=wsb, gps=gps: e.matmul(
                                gps[:, 0, :], lhsT=xT[:, kc, ti * 128:(ti + 1) * 128], rhs=wsb[:, kc, :],
                                start=(kc == 0), stop=(kc == 7)), r=[wk, "xT"], w=[gk])
                        o_sb = qo[(cb * 4 + ti) % 2]
                        ok = "qo%d" % ((cb * 4 + ti) % 2)
                        which = cb // 2
                        if which == 2:
                            p.op("act", lambda e, gps=gps, o_sb=o_sb: e.activation(out=o_sb[:], in_=gps[:, 0, :], func=AF.Copy),
                                 r=[gk], w=[ok])
                        else:
                            src = gps[:, 0, :].rearrange("p (h d) -> p h d", h=8)
                            dst = o_sb[:].rearrange("p (h d) -> p h d", h=8)
                            cos = rp_sb[:, ti, 2 * which, :].unsqueeze(1).to_broadcast([128, 8, 32])
                            sin = rp_sb[:, ti, 2 * which + 1, :].unsqueeze(1).to_broadcast([128, 8, 32])
                            tk = ["tmp%d" % i for i in range(4)]
                            p.op("dve", lambda e, src=src, cos=cos: e.tensor_tensor(out=tmp[0][:], in0=src[:, :, 0:32], in1=cos, op=ALU.mult),
                                 r=[gk, "rp"], w=[tk[0]])
                            p.op("dve", lambda e, src=src, sin=sin: e.tensor_tensor(out=tmp[1][:], in0=src[:, :, 32:64], in1=sin, op=ALU.mult),
                                 r=[gk, "rp"], w=[tk[1]])
                            p.op("dve", lambda e, src=src, cos=cos: e.tensor_tensor(out=tmp[2][:], in0=src[:, :, 32:64], in1=cos, op=ALU.mult),
                                 r=[gk, "rp"], w=[tk[2]])
                            p.op("dve", lambda e, src=src, sin=sin: e.tensor_tensor(out=tmp[3][:], in0=src[:, :, 0:32], in1=sin, op=ALU.mult),
                                 r=[gk, "rp"], w=[tk[3]])
                            p.op("pool", lambda e, dst=dst: e.tensor_tensor(out=dst[:, :, 0:32], in0=tmp[0][:], in1=tmp[1][:], op=ALU.subtract),
                                 r=[tk[0], tk[1]], w=[ok])
                            p.op("pool", lambda e, dst=dst: e.tensor_tensor(out=dst[:, :, 32:64], in0=tmp[2][:], in1=tmp[3][:], op=ALU.add),
                                 r=[tk[2], tk[3]], w=[ok])
                        p.dma("sp", qkv_out[which, r0:r0 + 128, (cb % 2) * 512:(cb % 2 + 1) * 512], o_sb[:], r=[ok],
                              semkey="o_" + ok, store=True)
        p.emit()
    return nc


_NC_CACHE = {}


def _get(name, builder):
    if name not in _NC_CACHE:
        import time as _t
        t0 = _t.time()
        _NC_CACHE[name] = builder()
        print("[kernel] built", name, "in %.1fs" % (_t.time() - t0), flush=True)
    return _NC_CACHE[name]


def _run(nc, in_maps):
    return run_bass_kernel_spmd(nc, in_maps, core_ids=list(range(NCORES)))


def run_post(aT_full, xres_full, w_o, w_in, w_dn, lnp, w_qkv=None, rope=None):
    with_qkv = w_qkv is not None
    nc = _get("post_qkv" if with_qkv else "post", lambda: build_post(with_qkv))
    in_maps = []
    for c in range(NCORES):
        m = {
            "aT": np.ascontiguousarray(aT_full[:, c * TOK:(c + 1) * TOK]),
            "xres": np.ascontiguousarray(xres_full[c * TOK:(c + 1) * TOK]),
            "w_o": w_o, "w_in": w_in, "w_dn": w_dn, "lnp": lnp,
        }
        if with_qkv:
            m["w_qkv"] = w_qkv
            m["rope"] = np.ascontiguousarray(rope[c * TOK:(c + 1) * TOK])
        in_maps.append(m)
    res = _run(nc, in_maps)
    xo = np.concatenate([r["xout"] for r in res.results], axis=0)
    if with_qkv:
        qkv = np.concatenate([r["qkv"] for r in res.results], axis=1)
        return xo, qkv
    return xo, None


def rope_tables():
    inv = (10000.0 ** (-np.arange(0, DH, 2, dtype=np.float32) / DH)).astype(np.float32)
    ang = (np.arange(T, dtype=np.float32)[:, None] * inv[None, :]).astype(np.float32)
    cos, sin = np.cos(ang).astype(np.float32), np.sin(ang).astype(np.float32)
    s = np.float32(DH ** -0.5)
    return np.ascontiguousarray(np.stack([cos * s, sin * s, cos, sin], axis=1))


NBLK = T // 256
QG = 512
NQG = T // QG


def moba_consts():
    blk1h = np.zeros((64, T), np.float32)
    for b in range(NBLK):
        blk1h[b, b * 256:(b + 1) * 256] = 1.0
    n = np.arange(64)[:, None]
    b = np.arange(64)[None, :]
    p01 = (b < n).astype(np.float32)
    o01 = (b == n).astype(np.float32)
    pbias = np.where(b < n, 0.0, -1e9).astype(np.float32)
    tabs = np.stack([pbias, p01, o01], axis=0)
    dm = np.zeros((4, 128, 4, 128), np.float32)
    key = np.arange(128)[:, None]
    q = np.arange(128)[None, :]
    tri = np.where(key <= q, 0.0, NEG)
    for j in range(4):
        for g in range(4):
            if j > g:
                dm[j, :, g, :] = NEG
            elif j == g:
                dm[j, :, g, :] = tri
    return (blk1h.astype(ml_dtypes.bfloat16), tabs, dm.reshape(4, 128, 512).astype(ml_dtypes.bfloat16))


def build_attn():
    nc = bass.Bass("TRN2", target_bir_lowering=False)
    qT = nc.dram_tensor("qT", [128, T], F32, kind="ExternalInput").ap()
    kT = nc.dram_tensor("kT", [128, T], F32, kind="ExternalInput").ap()
    v = nc.dram_tensor("v", [T, 128], F32, kind="ExternalInput").ap()
    blk1h = nc.dram_tensor("blk1h", [64, T], BF16, kind="ExternalInput").ap()
    tabs = nc.dram_tensor("tabs", [3, 64 * 64], F32, kind="ExternalInput").ap()
    dmask = nc.dram_tensor("dmask", [4, 128, 512], BF16, kind="ExternalInput").ap()
    oT = nc.dram_tensor("oT", [128, T], BF16, kind="ExternalOutput").ap()

    es = ExitStack()
    sb = lambda name, shape, dt: es.enter_context(nc.sbuf_tensor(name, shape, dt))
    ps = lambda name, shape, dt: es.enter_context(nc.psum_tensor(name, shape, dt))
    with es:
        ident = sb("ident", [128, 128], BF16)
        ones_f = sb("ones_f", [128, 64], F32)
        kaug = sb("kaug", [128, T], BF16)
        qaug = sb("qaug", [128, T], BF16)
        vsb = sb("vsb", [128, 128, 2, 65], BF16)
        tb = sb("tb", [128, 3, 64 * 64], F32)
        dm = sb("dm", [128, 4, 512], BF16)
        kmean = sb("kmean", [64, 64], F32)
        kmean_b = sb("kmean_b", [64, 64], BF16)
        gm = [sb("gm%d" % i, [128, 8, 64], F32) for i in range(2)]
        top8 = [sb("top8%d" % i, [128, 8, 8], F32) for i in range(2)]
        selt = [sb("selt%d" % i, [128, 8, 64], F32) for i in range(2)]
        negm = [sb("negm%d" % i, [128, 8, 64], BF16) for i in range(2)]
        pT = [sb("pT%d" % i, [128, QG], BF16) for i in range(3)]
        osb = [sb("osb%d" % i, [65, QG], F32) for i in range(2)]
        obf = [sb("obf%d" % i, [64, QG], BF16) for i in range(2)]
        s_ps = [ps("s_ps%d" % i, [128, QG], F32) for i in range(3)]
        o_ps = [ps("o_ps%d" % i, [65, QG], F32) for i in range(2)]
        g_ps = ps("g_ps", [128, 8, 64], F32)
        m_ps = ps("m_ps", [128, 8 * 128], BF16)
        bc_ps = m_ps[:].bitcast(F32)

        p = Prog(nc)
        p.op("pool", lambda e: e.memset(ident[:], 0.0), w=["ident"])
        p.op("pool", lambda e: e.affine_select(out=ident[:], in_=ident[:], pattern=[[-1, 128]],
                                               compare_op=ALU.not_equal, fill=1.0, base=0,
                                               channel_multiplier=1), r=["ident"], w=["ident"])
        p.op("pool", lambda e: e.memset(ones_f[:], 1.0), w=["ones_f"])
        p.op("pool", lambda e: e.memset(vsb[:, :, :, 64:65], 1.0), w=["vsb1"])
        p.dma("sp", tb[:], tabs.partition_broadcast(128), w=["tb"])
        p.dma("sp", dm[:], dmask.rearrange("j k q -> k j q"), w=["dm"])
        p.dma("sp", kaug[64:128, :], blk1h, w=["kaug_hi"])
        for hh in range(2):
            for c4 in range(4):
                p.dma("pool", vsb[:, c4 * 32:(c4 + 1) * 32, hh, 0:64],
                      v[c4 * 4096:(c4 + 1) * 4096, hh * 64:(hh + 1) * 64].rearrange("(kb p) d -> p kb d", p=128),
                      w=["vsb"], semkey="vsb")

        gcount = 0
        for hh in range(2):
            for c4 in range(4):
                sl = slice(c4 * 4096, (c4 + 1) * 4096)
                p.dma("pool", kaug[0:64, sl], kT[hh * 64:(hh + 1) * 64, sl], w=["kaug_lo"], semkey="kaug_lo")
                p.dma("pool", qaug[0:64, sl], qT[hh * 64:(hh + 1) * 64, sl], w=["qaug_lo"], semkey="qaug_lo")
            p.op("dve", lambda e: e.tensor_reduce(out=kmean[:], in_=kaug[0:64, :].rearrange("p (b s) -> p b s", s=256),
                                                  axis=AX.X, op=ALU.add), r=["kaug_lo"], w=["kmean"])
            p.op("dve", lambda e: e.tensor_scalar(out=kmean_b[:], in0=kmean[:], scalar1=1.0 / 256, scalar2=None, op0=ALU.mult),
                 r=["kmean"], w=["kmean_b"])
            for G8 in range(T // 1024):
                i2 = gcount % 2
                gcount += 1
                n0 = 4 * G8
                for c in range(8):
                    q0 = G8 * 1024 + c * 128
                    p.op("pe", lambda e, c=c, q0=q0: e.matmul(g_ps[:, c, :], lhsT=qaug[0:64, q0:q0 + 128], rhs=kmean_b[:],
                                                            start=True, stop=True), r=["qaug_lo", "kmean_b"], w=["g_ps"])
                gmk, t8k, slk, ngk = "gm%d" % i2, "top8%d" % i2, "selt%d" % i2, "negm%d" % i2
                tbv = tb[:].rearrange("p t (n b) -> p t n b", b=64)
                b0, b1, b2 = [tbv[:, t, n0:n0 + 4, :].unsqueeze(2).to_broadcast([128, 4, 2, 64]) for t in range(3)]
                g4 = lambda tl: tl[:].rearrange("p (a c) b -> p a c b", c=2)
                gm4, sl4, gp4 = g4(gm[i2]), g4(selt[i2]), g_ps[:].rearrange("p (a c) b -> p a c b", c=2)
                p.op("dve", lambda e, gm4=gm4, gp4=gp4, b0=b0: e.tensor_tensor(out=gm4, in0=gp4, in1=b0, op=ALU.add),
                     r=["g_ps", "tb"], w=[gmk])
                for c in range(8):
                    p.op("dve", lambda e, c=c, i2=i2: e.max(out=top8[i2][:, c, :], in_=gm[i2][:, c, :]), r=[gmk], w=[t8k])
                p.op("dve", lambda e, i2=i2: e.tensor_tensor(out=selt[i2][:], in0=gm[i2][:],
                                                            in1=top8[i2][:, :, 2:3].to_broadcast([128, 8, 64]), op=ALU.is_ge),
                     r=[gmk, t8k], w=[slk])
                p.op("pool", lambda e, sl4=sl4, b1=b1: e.tensor_tensor(out=sl4, in0=sl4, in1=b1, op=ALU.mult),
                     r=[slk, "tb"], w=[slk])
                p.op("pool", lambda e, sl4=sl4, b2=b2: e.tensor_tensor(out=sl4, in0=sl4, in1=b2, op=ALU.add),
                     r=[slk, "tb"], w=[slk])
                p.op("pool", lambda e, i2=i2: e.tensor_scalar(out=negm[i2][:], in0=selt[i2][:], scalar1=-1.0, scalar2=-NEG,
                                                             op0=ALU.add, op1=ALU.mult), r=[slk], w=[ngk])
                for c in range(8):
                    p.op("pe", lambda e, c=c, i2=i2: e.transpose(m_ps[64:128, c * 128:(c + 1) * 128], negm[i2][:, c, :], ident[:],
                                                                tile_position=(0, 64)), r=[ngk, "ident"], w=["m_ps"])
                p.op("act", lambda e, G8=G8: e.activation(out=qaug[64:128, G8 * 1024:(G8 + 1) * 1024], in_=m_ps[64:128, :], func=AF.Copy),
                     r=["m_ps"], w=["qaug_hi"])
            for G in range(NQG):
                nkb = 4 * (G + 1)
                op_i = G % 2
                opk = "o_ps%d" % op_i
                qsl = slice(G * QG, (G + 1) * QG)

                def qk(kb, G=G, qsl=qsl):
                    si = kb % 3
                    diag = kb >= 4 * G
                    p.op("pe", lambda e: e.matmul(s_ps[si][:], lhsT=kaug[:, kb * 128:(kb + 1) * 128], rhs=qaug[:, qsl],
                                                  start=True, stop=not diag),
                         r=["kaug_lo", "kaug_hi", "qaug_lo", "qaug_hi"], w=["s_ps%d" % si])
                    if diag:
                        j = kb - 4 * G
                        p.op("pe", lambda e: e.matmul(s_ps[si][:], lhsT=ident[:], rhs=dm[:, j, :], start=False, stop=True),
                             r=["ident", "dm"], w=["s_ps%d" % si])

                def ex_pv(kb, G=G, nkb=nkb, op_i=op_i, opk=opk, hh=hh):
                    si = kb % 3
                    p.op("act", lambda e: e.activation(out=pT[si][:], in_=s_ps[si][:], func=AF.Exp),
                         r=["s_ps%d" % si], w=["pT%d" % si])
                    p.op("pe", lambda e: e.matmul(o_ps[op_i][:], lhsT=vsb[:, kb, hh, :], rhs=pT[si][:],
                                                  start=(kb == 0), stop=(kb == nkb - 1)),
                         r=["vsb", "vsb1", "pT%d" % si], w=[opk])

                LOOK = 2
                for kb in range(min(LOOK, nkb)):
                    qk(kb)
                for kb in range(nkb):
                    if kb + LOOK < nkb:
                        qk(kb + LOOK)
                    ex_pv(kb)
                ob_i = G % 2
                p.op("dve", lambda e, op_i=op_i, ob_i=ob_i: e.tensor_copy(out=osb[ob_i][:], in_=o_ps[op_i][:]), r=[opk], w=["osb%d" % ob_i])
                p.op("dve", lambda e, ob_i=ob_i: e.reciprocal(out=osb[ob_i][64:65, :], in_=osb[ob_i][64:65, :]),
                     r=["osb%d" % ob_i], w=["osb%d" % ob_i])
                p.op("pe", lambda e, ob_i=ob_i: e.matmul(bc_ps[0:64, :], lhsT=ones_f[64:65, :], rhs=osb[ob_i][64:65, :], start=True, stop=True),
                     r=["ones_f", "osb%d" % ob_i], w=["m_ps"])
                p.op("dve", lambda e, ob_i=ob_i: e.tensor_tensor(out=obf[ob_i][:], in0=osb[ob_i][0:64, :], in1=bc_ps[0:64, :], op=ALU.mult),
                     r=["osb%d" % ob_i, "m_ps"], w=["obf%d" % ob_i])
                p.dma("sp", oT[hh * 64:(hh + 1) * 64, qsl], obf[ob_i][:], r=["obf%d" % ob_i], semkey="o_obf%d" % ob_i, store=True)
        p.emit()
    return nc


def run_attn(qkv):
    nc = _get("attn", build_attn)
    blk1h, tabs, dm = moba_consts()
    tabs = np.ascontiguousarray(tabs.reshape(3, 64 * 64))
    in_maps = []
    for c in range(NCORES):
        cs = slice(c * 128, (c + 1) * 128)
        in_maps.append({
            "qT": np.ascontiguousarray(qkv[0][:, cs].T),
            "kT": np.ascontiguousarray(qkv[1][:, cs].T),
            "v": np.ascontiguousarray(qkv[2][:, cs]),
            "blk1h": blk1h, "tabs": tabs, "dmask": dm,
        })
    res = _run(nc, in_maps)
    return np.concatenate([r["oT"] for r in res.results], axis=0)


LCH = 64
TT = 512
NCH = TT // LCH
WCOLS = 672
DECAY_C = -math.exp(-0.5)


def rwkv_consts():
    j = np.arange(64)[:, None]
    t = np.arange(64)[None, :]
    incl = (j <= t).astype(np.float32)
    strict = (j < t).astype(np.float32)
    rev = (j > t).astype(np.float32)
    tri3 = (DECAY_C * np.concatenate([incl, strict, rev], axis=1)).astype(np.float32)
    tri3 = np.concatenate([tri3, tri3], axis=0)
    up_s = (j < t).astype(np.float32)
    up_i = (j <= t).astype(np.float32)
    lo_s = (t < j).astype(np.float32)
    msk = np.concatenate([up_s, up_i, up_s, up_i, lo_s], axis=1)
    msk = np.concatenate([msk, msk], axis=0)
    i2 = np.concatenate([np.eye(64, dtype=np.float32)] * 2, axis=0)
    onesbd = np.zeros((128, 128), np.float32)
    onesbd[:64, :64] = 1.0
    onesbd[64:, 64:] = 1.0
    return tri3, msk, i2, onesbd


def build_rwkv(ntiles, debug=False):
    TL = ntiles * TT
    nc = bass.Bass("TRN2", target_bir_lowering=False)
    xT = nc.dram_tensor("xT", [C, TL + 1], F32, kind="ExternalInput").ap()
    wbig = nc.dram_tensor("wbig", [C, WCOLS], F32, kind="ExternalInput").ap()
    mu6 = nc.dram_tensor("mu6", [C, 6], F32, kind="ExternalInput").ap()
    w2a = nc.dram_tensor("w2a", [65, 128], F32, kind="ExternalInput").ap()
    a2p = nc.dram_tensor("a2p", [128, 128], F32, kind="ExternalInput").ap()
    g2p = nc.dram_tensor("g2p", [160, 128], F32, kind="ExternalInput").ap()
    vecs = nc.dram_tensor("vecs", [128, 8], F32, kind="ExternalInput").ap()
    tri3_d = nc.dram_tensor("tri3", [128, 192], F32, kind="ExternalInput").ap()
    msk_d = nc.dram_tensor("msk", [128, 320], F32, kind="ExternalInput").ap()
    i2_d = nc.dram_tensor("i2", [128, 64], F32, kind="ExternalInput").ap()
    obd_d = nc.dram_tensor("onesbd", [128, 128], F32, kind="ExternalInput").ap()
    ygT = nc.dram_tensor("ygT", [128, TL], BF16, kind="ExternalOutput").ap()
    if debug:
        dbg = nc.dram_tensor("dbg", [24, 128, 512], F32, kind="ExternalOutput").ap()

    es = ExitStack()
    sb = lambda name, shape, dt: es.enter_context(nc.sbuf_tensor(name, shape, dt))
    with es:
        ident_b = sb("ident_b", [128, 128], BF16)
        ident_f = sb("ident_f", [128, 128], F32)
        wf = sb("wf", [128, 8, WCOLS], F32)
        wc = sb("wc", [128, 8, WCOLS], BF16)
        wp = sb("wp", [128, 8, WCOLS], BF16)
        mu_sb = sb("mu_sb", [128, 8, 6], F32)
        w2a_b = sb("w2a_b", [65, 128], BF16)
        a2_b = sb("a2_b", [128, 128], BF16)
        g2a_b = sb("g2a_b", [128, 128], BF16)
        g2b_b = sb("g2b_b", [32, 128], BF16)
        vec = sb("vec", [128, 8], F32)
        tri3 = sb("tri3_s", [128, 192], F32)
        msk = sb("msk_s", [128, 320], F32)
        i2 = sb("i2_s", [128, 64], F32)
        obd_f = sb("obd_f", [128, 128], F32)
        obd_b = sb("obd_b", [128, 128], BF16)
        xb = [sb("xb%d" % i, [128, 8, TT + 1], BF16) for i in range(2)]
        r_f = sb("r_f", [128, TT], F32)
        k_f = sb("k_f", [128, TT], F32)
        v_f = sb("v_f", [128, TT], F32)
        v_b = sb("v_b", [128, TT], BF16)
        twa = sb("twa", [65, TT], BF16)
        a1o = sb("a1o", [128, TT], BF16)
        sg_a = sb("sg_a", [128, TT], BF16)
        sg_b = sb("sg_b", [32, TT], BF16)
        g_f = sb("g_f", [128, TT], F32)
        al_f = sb("al_f", [128, TT], F32)
        kkr = sb("kkr", [128, TT], F32)
        sq_b = sb("sq_b", [128, TT], BF16)
        rn = sb("rn", [128, TT], F32)
        kk_f = sb("kk_f", [128, TT], F32)
        kt_f = sb("kt_f", [128, TT], F32)
        b_f = sb("b_f", [128, TT], F32)
        tmp1 = sb("tmp1", [128, TT], F32)
        rk_b = sb("rk_b", [128, TT], BF16)
        bon_f = sb("bon_f", [128, TT], F32)
        sgw = sb("sgw", [128, NCH, 128], F32)
        e_pos = sb("e_pos", [128, NCH, 64], F32)
        e_neg = sb("e_neg", [128, NCH, 64], F32)
        e_ex = sb("e_ex", [128, NCH, 64], F32)
        e_rem = sb("e_rem", [128, NCH, 64], F32)
        AR = sb("AR", [128, NCH, 2, 64], BF16)
        Rh_f = sb("Rh_f", [128, TT], F32)
        BhT = sb("BhT", [128, TT], BF16)
        KhT = sb("KhT", [128, TT], BF16)
        BbT = sb("BbT", [128, TT], BF16)
        KbT = sb("KbT", [128, TT], BF16)
        TM = sb("TM", [128, NCH, 4, 64], BF16)
        GM = sb("GM", [128, NCH, 320], BF16)
        Xf = sb("Xf", [128, NCH, 128], F32)
        Xb = sb("Xb", [128, NCH, 128], BF16)
        PP = [sb("PP%d" % i, [128, NCH, 128], BF16) for i in range(2)]
        RtT = sb("RtT", [128, TT], BF16)
        Y0 = sb("Y0", [128, NCH, 64], F32)
        DG = sb("DG", [128, NCH, 64], F32)
        PT_f = sb("PT_f", [128, NCH, 64], F32)
        Q_f = sb("Q_f", [128, NCH, 64], F32)
        S_f = [sb("S_f%d" % i, [128, 64], F32) for i in range(2)]
        S_b = [sb("S_b%d" % i, [128, 64], BF16) for i in range(2)]
        y_tm = sb("y_tm", [128, NCH, 64], F32)
        yT_f = sb("yT_f", [128, TT], F32)
        d_f = sb("d_f", [128, TT], F32)
        sq_f = sb("sq_f", [128, TT], F32)
        rstd = sb("rstd", [128, TT], F32)
        yo = [sb("yo%d" % i, [128, TT], BF16) for i in range(2)]
        bank = [es.enter_context(nc.psum_tensor("bank%d" % i, [128, 512], F32)) for i in range(8)]
        bk = lambda i: "bank%d" % i

        p = Prog(nc)
        for idt, nm in ((ident_b, "ident_b"), (ident_f, "ident_f")):
            p.op("pool", lambda e, idt=idt: e.memset(idt[:], 0.0), w=[nm])
            p.op("pool", lambda e, idt=idt: e.affine_select(out=idt[:], in_=idt[:], pattern=[[-1, 128]],
                                                         compare_op=ALU.not_equal, fill=1.0, base=0,
                                                         channel_multiplier=1), r=[nm], w=[nm])
        p.op("pool", lambda e: e.memset(twa[64:65, :], 1.0), w=["twa1"])
        p.op("pool", lambda e: e.memset(S_f[0][:], 0.0), w=["S_f0"])
        p.op("pool", lambda e: e.memset(S_b[0][:], 0.0), w=["S_b0"])
        for kc in range(8):
            p.dma("sp", wf[:, kc, :], wbig[kc * 128:(kc + 1) * 128, :], w=["wf"], semkey="wf")
        p.dma("sp", mu_sb[:], mu6.rearrange("(k p) n -> p k n", p=128), w=["mu"])
        p.dma("pool", w2a_b[:], w2a, w=["w2a_b"])
        p.dma("pool", a2_b[:], a2p, w=["a2_b"])
        p.dma("pool", g2a_b[:], g2p[0:128, :], w=["g2a_b"])
        p.dma("pool", g2b_b[:], g2p[128:160, :], w=["g2b_b"])
        p.dma("pool", obd_b[:], obd_d, w=["obd_b"])
        p.dma("sp", obd_f[:], obd_d, w=["obd_f"])
        p.dma("sp", vec[:], vecs, w=["vec"])
        p.dma("sp", tri3[:], tri3_d, w=["tri3"])
        p.dma("sp", msk[:], msk_d, w=["msk"])
        p.dma("sp", i2[:], i2_d, w=["i2"])
        groups = [(0, 128, 0), (128, 256, 2), (256, 384, 3), (384, 448, 1), (448, 512, 4), (512, 672, 5)]
        for kc in range(8):
            for (c0, c1, n) in groups:
                eng = "dve" if (kc % 2 == 0) else "pool"
                p.op(eng, lambda e, kc=kc, c0=c0, c1=c1, n=n: e.tensor_scalar(
                    out=wp[:, kc, c0:c1], in0=wf[:, kc, c0:c1], scalar1=mu_sb[:, kc, n:n + 1], scalar2=None, op0=ALU.mult),
                    r=["wf", "mu"], w=["wp"])
            p.op("dve" if (kc % 2 == 0) else "pool",
                 lambda e, kc=kc: e.tensor_tensor(out=wc[:, kc, :], in0=wf[:, kc, :], in1=wp[:, kc, :], op=ALU.subtract),
                 r=["wf", "wp"], w=["wc"])

        V_KK, V_KA, V_1KA, V_RK, V_GG, V_GB, V_A0 = range(7)
        vcol = lambda i: vec[:, i:i + 1]
        tp = lambda h: (64 * h, 64 * h)
        hs = lambda h: slice(64 * h, 64 * h + 64)
        s_cur = 0

        for ti in range(ntiles):
            t0 = ti * TT
            x_sb = xb[ti % 2]
            xk = "xb%d" % (ti % 2)
            for kc in range(8):
                p.dma("pool", x_sb[:, kc, :], xT[kc * 128:(kc + 1) * 128, t0:t0 + TT + 1], w=[xk], semkey=xk)

            def proj(c0, c1, bi):
                m = c1 - c0
                for kc in range(8):
                    p.op("pe", lambda e, kc=kc, x_sb=x_sb: e.matmul(bank[bi][0:m, :], lhsT=wc[:, kc, c0:c1], rhs=x_sb[:, kc, 1:TT + 1],
                                                        start=(kc == 0), stop=False), r=[xk, "wc"], w=[bk(bi)])
                for kc in range(8):
                    p.op("pe", lambda e, kc=kc, x_sb=x_sb: e.matmul(bank[bi][0:m, :], lhsT=wp[:, kc, c0:c1], rhs=x_sb[:, kc, 0:TT],
                                                        start=False, stop=(kc == 7)), r=[xk, "wp"], w=[bk(bi)])

            proj(0, 128, 0)
            p.op("act", lambda e: e.activation(out=r_f[:], in_=bank[0][:], func=AF.Copy), r=[bk(0)], w=["r_f"])
            proj(128, 256, 1)
            p.op("act", lambda e: e.activation(out=k_f[:], in_=bank[1][:], func=AF.Copy), r=[bk(1)], w=["k_f"])
            proj(256, 384, 2)
            p.op("act", lambda e: e.activation(out=v_f[:], in_=bank[2][:], func=AF.Copy), r=[bk(2)], w=["v_f"])
            p.op("pool", lambda e: e.tensor_copy(out=v_b[:], in_=v_f[:]), r=["v_f"], w=["v_b"])
            proj(384, 512, 3)
            p.op("act", lambda e: e.activation(out=twa[0:64, :], in_=bank[3][0:64, :], func=AF.Tanh), r=[bk(3)], w=["twa"])
            p.op("dve", lambda e: e.tensor_copy(out=a1o[64:128, :], in_=bank[3][64:128, :]), r=[bk(3)], w=["a1o"])
            proj(512, 640, 4)
            p.op("act", lambda e: e.activation(out=sg_a[:], in_=bank[4][:], func=AF.Sigmoid), r=[bk(4)], w=["sg_a"])
            proj(640, 672, 5)
            p.op("act", lambda e: e.activation(out=sg_b[:], in_=bank[5][0:32, :], func=AF.Sigmoid), r=[bk(5)], w=["sg_b"])
            p.op("pe", lambda e: e.matmul(bank[6][:], lhsT=g2a_b[:], rhs=sg_a[:], start=True, stop=False), r=["g2a_b", "sg_a"], w=[bk(6)])
            p.op("pe", lambda e: e.matmul(bank[6][:], lhsT=g2b_b[:], rhs=sg_b[:], start=False, stop=True), r=["g2b_b", "sg_b"], w=[bk(6)])
            p.op("act", lambda e: e.activation(out=g_f[:], in_=bank[6][:], func=AF.Copy), r=[bk(6)], w=["g_f"])
            p.op("pe", lambda e: e.matmul(bank[7][:], lhsT=a2_b[64:128, :], rhs=a1o[64:128, :], start=True, stop=True),
                 r=["a2_b", "a1o"], w=[bk(7)])
            p.op("act", lambda e: e.activation(out=al_f[:], in_=bank[7][:], func=AF.Sigmoid, bias=vcol(V_A0)), r=[bk(7), "vec"], w=["al_f"])
            p.op("dve", lambda e: e.tensor_scalar(out=kkr[:], in0=k_f[:], scalar1=vcol(V_KK), scalar2=None, op0=ALU.mult), r=["k_f", "vec"], w=["kkr"])
            p.op("pool", lambda e: e.tensor_tensor(out=sq_b[:], in0=kkr[:], in1=kkr[:], op=ALU.mult), r=["kkr"], w=["sq_b"])
            p.op("pe", lambda e: e.matmul(bank[0][:], lhsT=obd_b[:], rhs=sq_b[:], start=True, stop=True), r=["obd_b", "sq_b"], w=[bk(0)])
            p.op("dve", lambda e: e.tensor_scalar(out=rn[:], in0=bank[0][:], scalar1=1e-24, scalar2=None, op0=ALU.add), r=[bk(0)], w=["rn"])
            p.op("act", lambda e: e.activation(out=rn[:], in_=rn[:], func=AF.Sqrt), r=["rn"], w=["rn"])
            p.op("dve", lambda e: e.reciprocal(out=rn[:], in_=rn[:]), r=["rn"], w=["rn"])
            p.op("dve", lambda e: e.tensor_tensor(out=kk_f[:], in0=kkr[:], in1=rn[:], op=ALU.mult), r=["kkr", "rn"], w=["kk_f"])
            p.op("pool", lambda e: e.tensor_scalar(out=tmp1[:], in0=al_f[:], scalar1=-1.0, scalar2=vcol(V_KA), op0=ALU.add, op1=ALU.mult),
                 r=["al_f", "vec"], w=["tmp1"])
            p.op("dve", lambda e: e.scalar_tensor_tensor(out=kt_f[:], in0=tmp1[:], scalar=1.0, in1=k_f[:], op0=ALU.add, op1=ALU.mult),
                 r=["k_f", "tmp1"], w=["kt_f"])
            p.op("dve", lambda e: e.tensor_tensor(out=b_f[:], in0=kk_f[:], in1=al_f[:], op=ALU.mult), r=["kk_f", "al_f"], w=["b_f"])
            p.op("dve", lambda e: e.scalar_tensor_tensor(out=rk_b[:], in0=r_f[:], scalar=vcol(V_RK), in1=kt_f[:], op0=ALU.mult, op1=ALU.mult),
                 r=["r_f", "kt_f", "vec"], w=["rk_b"])
            p.op("pe", lambda e: e.matmul(bank[1][:], lhsT=obd_b[:], rhs=rk_b[:], start=True, stop=True), r=["obd_b", "rk_b"], w=[bk(1)])
            p.op("dve", lambda e: e.tensor_tensor(out=bon_f[:], in0=v_f[:], in1=bank[1][:], op=ALU.mult), r=["v_f", bk(1)], w=["bon_f"])

            for c in range(NCH):
                p.op("pe", lambda e, c=c: e.matmul(bank[2 + c // 4][0:64, (c % 4) * 128:(c % 4 + 1) * 128],
                                                   lhsT=twa[:, c * 64:(c + 1) * 64], rhs=w2a_b[:], start=True, stop=True),
                     r=["twa", "twa1", "w2a_b"], w=[bk(2 + c // 4)])
            for hf in range(2):
                p.op("act", lambda e, hf=hf: e.activation(out=sgw[0:64, hf * 4:(hf + 1) * 4, :],
                                                         in_=bank[2 + hf][0:64, :].rearrange("p (c d) -> p c d", c=4), func=AF.Sigmoid),
                     r=[bk(2 + hf)], w=["sgw"])
            for c in range(NCH):
                for kind in range(3):
                    p.op("pe", lambda e, c=c, kind=kind: e.matmul(bank[4 + kind][:, c * 64:(c + 1) * 64], lhsT=sgw[0:64, c, :],
                                                                 rhs=tri3[0:64, kind * 64:(kind + 1) * 64], start=True, stop=True),
                         r=["sgw", "tri3"], w=[bk(4 + kind)])
            v3 = lambda tl: tl[:].rearrange("p c t -> p (c t)")
            p.op("act", lambda e: e.activation(out=v3(e_pos), in_=bank[4][:], func=AF.Exp), r=[bk(4)], w=["e_pos"])
            p.op("act", lambda e: e.activation(out=v3(e_neg), in_=bank[4][:], func=AF.Exp, scale=-1.0), r=[bk(4)], w=["e_neg"])
            p.op("act", lambda e: e.activation(out=v3(e_ex), in_=bank[5][:], func=AF.Exp), r=[bk(5)], w=["e_ex"])
            p.op("act", lambda e: e.activation(out=v3(e_rem), in_=bank[6][:], func=AF.Exp), r=[bk(6)], w=["e_rem"])
            c3 = lambda ap2: ap2.rearrange("p (c t) -> p c t", c=NCH)
            p.op("dve", lambda e: e.scalar_tensor_tensor(out=AR[:, :, 0, :], in0=c3(kk_f[:]), scalar=-1.0, in1=e_ex[:], op0=ALU.mult, op1=ALU.mult),
                 r=["kk_f", "e_ex"], w=["AR"])
            p.op("dve", lambda e: e.tensor_tensor(out=Rh_f[:], in0=r_f[:], in1=v3(e_pos), op=ALU.mult), r=["r_f", "e_pos"], w=["Rh_f"])
            p.op("pool", lambda e: e.tensor_copy(out=AR[:, :, 1, :], in_=c3(Rh_f[:])), r=["Rh_f"], w=["AR"])
            p.op("dve", lambda e: e.tensor_tensor(out=BhT[:], in0=b_f[:], in1=v3(e_neg), op=ALU.mult), r=["b_f", "e_neg"], w=["BhT"])
            p.op("pool", lambda e: e.tensor_tensor(out=KhT[:], in0=kt_f[:], in1=v3(e_neg), op=ALU.mult), r=["kt_f", "e_neg"], w=["KhT"])
            p.op("dve", lambda e: e.tensor_tensor(out=BbT[:], in0=b_f[:], in1=v3(e_rem), op=ALU.mult), r=["b_f", "e_rem"], w=["BbT"])
            p.op("pool", lambda e: e.tensor_tensor(out=KbT[:], in0=kt_f[:], in1=v3(e_rem), op=ALU.mult), r=["kt_f", "e_rem"], w=["KbT"])

            tmb = [bank[0][:].bitcast(BF16), bank[1][:].bitcast(BF16)]
            srcs = [(lambda c: AR[:, c, 0, :], "AR"), (lambda c: v_b[:, c * 64:(c + 1) * 64], "v_b"),
                    (lambda c: BbT[:, c * 64:(c + 1) * 64], "BbT"), (lambda c: KbT[:, c * 64:(c + 1) * 64], "KbT")]
            for c in range(NCH):
                for si, (sf, sk) in enumerate(srcs):
                    for h in range(2):
                        col = (c % 4) * 256 + si * 64
                        p.op("pe", lambda e, c=c, sf=sf, h=h, col=col: e.transpose(
                            tmb[c // 4][hs(h), col:col + 64], sf(c)[hs(h), :], ident_b[hs(h), hs(h)], tile_position=tp(h)),
                            r=[sk, "ident_b"], w=[bk(c // 4)])
            for hf in range(2):
                p.op("act" if hf == 0 else "dve",
                     (lambda e, hf=hf: e.activation(out=TM[:, hf * 4:(hf + 1) * 4, :, :].rearrange("p c s t -> p (c s t)"), in_=tmb[hf], func=AF.Copy))
                     if hf == 0 else
                     (lambda e, hf=hf: e.tensor_copy(out=TM[:, hf * 4:(hf + 1) * 4, :, :].rearrange("p c s t -> p (c s t)"), in_=tmb[hf])),
                     r=[bk(hf)], w=["TM"])

            for c in range(NCH):
                bi = 2 + (c % 2)
                cs_ = slice(c * 64, (c + 1) * 64)
                for h in range(2):
                    arh = AR[hs(h), c, :, :].rearrange("p s t -> p (s t)")
                    p.op("pe", lambda e, h=h, arh=arh, cs_=cs_, bi=bi: e.matmul(bank[bi][hs(h), 0:128], lhsT=BhT[hs(h), cs_], rhs=arh,
                                                                           start=True, stop=True, tile_position=tp(h)),
                         r=["BhT", "AR"], w=[bk(bi)])
                    p.op("pe", lambda e, h=h, arh=arh, cs_=cs_, bi=bi: e.matmul(bank[bi][hs(h), 128:256], lhsT=KhT[hs(h), cs_], rhs=arh,
                                                                           start=True, stop=True, tile_position=tp(h)),
                         r=["KhT", "AR"], w=[bk(bi)])
                    p.op("pe", lambda e, h=h, c=c, cs_=cs_, bi=bi: e.matmul(bank[bi][hs(h), 256:320], lhsT=AR[hs(h), c, 0, :], rhs=BhT[hs(h), cs_],
                                                                       start=True, stop=True, tile_position=tp(h)),
                         r=["BhT", "AR"], w=[bk(bi)])
                p.op("dve", lambda e, c=c, bi=bi: e.tensor_tensor(out=GM[:, c, :], in0=bank[bi][:, 0:320], in1=msk[:], op=ALU.mult),
                     r=[bk(bi), "msk"], w=["GM"])

            for c in range(NCH):
                for h in range(2):
                    p.op("pe", lambda e, c=c, h=h: e.matmul(bank[4][hs(h), c * 64:(c + 1) * 64], lhsT=GM[hs(h), c, 128:192], rhs=TM[hs(h), c, 1, :],
                                                           start=True, stop=True, tile_position=tp(h)), r=["GM", "TM"], w=[bk(4)])
            p.op("pool", lambda e: e.tensor_copy(out=Xf[:, :, 0:64], in_=TM[:, :, 0, :]), r=["TM"], w=["Xf"])
            p.op("dve", lambda e: e.tensor_copy(out=Xf[:, :, 64:128], in_=bank[4][:].rearrange("p (c t) -> p c t", c=NCH)), r=[bk(4)], w=["Xf"])
            p.op("pool", lambda e: e.tensor_copy(out=Xb[:], in_=Xf[:]), r=["Xf"], w=["Xb"])

            NLEV = 6
            for lev in range(NLEV):
                if lev == 0:
                    P_of = lambda c: GM[:, c, 256:320]
                    PT_of = lambda c: GM[:, c, 0:64]
                    pk = "GM"
                else:
                    ppt = PP[(lev - 1) % 2]
                    P_of = lambda c, ppt=ppt: ppt[:, c, 0:64]
                    PT_of = lambda c, ppt=ppt: ppt[:, c, 64:128]
                    pk = "PP%d" % ((lev - 1) % 2)
                for c in range(NCH):
                    bi = 0 + c // 4
                    for h in range(2):
                        p.op("pe", lambda e, c=c, h=h, bi=bi, PT_of=PT_of: e.matmul(
                            bank[bi][hs(h), (c % 4) * 128:(c % 4 + 1) * 128], lhsT=PT_of(c)[hs(h), :], rhs=Xb[hs(h), c, :],
                            start=True, stop=True, tile_position=tp(h)), r=[pk, "Xb"], w=[bk(bi)])
                if lev < NLEV - 1:
                    for c in range(NCH):
                        bi = 2 + c // 4
                        for h in range(2):
                            p.op("pe", lambda e, c=c, h=h, bi=bi, P_of=P_of, PT_of=PT_of: e.matmul(
                                bank[bi][hs(h), (c % 4) * 128:(c % 4) * 128 + 64], lhsT=PT_of(c)[hs(h), :], rhs=P_of(c)[hs(h), :],
                                start=True, stop=True, tile_position=tp(h)), r=[pk], w=[bk(bi)])
                            p.op("pe", lambda e, c=c, h=h, bi=bi, P_of=P_of, PT_of=PT_of: e.matmul(
                                bank[bi][hs(h), (c % 4) * 128 + 64:(c % 4 + 1) * 128], lhsT=P_of(c)[hs(h), :], rhs=PT_of(c)[hs(h), :],
                                start=True, stop=True, tile_position=tp(h)), r=[pk], w=[bk(bi)])
                for hf in range(2):
                    xs = Xf[:, hf * 4:(hf + 1) * 4, :].rearrange("p c t -> p (c t)")
                    p.op("dve", lambda e, hf=hf, xs=xs: e.tensor_tensor(out=xs, in0=xs, in1=bank[hf][:], op=ALU.add), r=["Xf", bk(hf)], w=["Xf"])
                p.op("pool", lambda e: e.tensor_copy(out=Xb[:], in_=Xf[:]), r=["Xf"], w=["Xb"])
                if lev < NLEV - 1:
                    ppn = PP[lev % 2]
                    for hf in range(2):
                        p.op("act", lambda e, hf=hf, ppn=ppn: e.activation(out=ppn[:, hf * 4:(hf + 1) * 4, :].rearrange("p c t -> p (c t)"),
                                                                          in_=bank[2 + hf][:], func=AF.Copy),
                             r=[bk(2 + hf)], w=["PP%d" % (lev % 2)])

            for c in range(NCH):
                for h in range(2):
                    p.op("pe", lambda e, c=c, h=h: e.matmul(bank[4][hs(h), c * 64:(c + 1) * 64], lhsT=Xb[hs(h), c, 0:64], rhs=GM[hs(h), c, 64:128],
                                                           start=True, stop=True, tile_position=tp(h)), r=["Xb", "GM"], w=[bk(4)])
                    p.op("pe", lambda e, c=c, h=h: e.matmul(bank[5][hs(h), c * 64:(c + 1) * 64], lhsT=GM[hs(h), c, 64:128], rhs=Xb[hs(h), c, 64:128],
                                                           start=True, stop=False, tile_position=tp(h)), r=["Xb", "GM"], w=[bk(5)])
                    p.op("pe", lambda e, c=c, h=h: e.matmul(bank[5][hs(h), c * 64:(c + 1) * 64], lhsT=GM[hs(h), c, 192:256], rhs=TM[hs(h), c, 1, :],
                                                           start=False, stop=True, tile_position=tp(h)), r=["TM", "GM"], w=[bk(5)])
                    p.op("pe", lambda e, c=c, h=h: e.matmul(bank[6][hs(h), c * 64:(c + 1) * 64], lhsT=Xb[hs(h), c, 0:64], rhs=TM[hs(h), c, 2, :],
                                                           start=True, stop=True, tile_position=tp(h)), r=["Xb", "TM"], w=[bk(6)])
                    p.op("pe", lambda e, c=c, h=h: e.matmul(bank[7][hs(h), c * 64:(c + 1) * 64], lhsT=TM[hs(h), c, 2, :], rhs=Xb[hs(h), c, 64:128],
                                                           start=True, stop=False, tile_position=tp(h)), r=["Xb", "TM"], w=[bk(7)])
                    p.op("pe", lambda e, c=c, h=h: e.matmul(bank[7][hs(h), c * 64:(c + 1) * 64], lhsT=TM[hs(h), c, 3, :], rhs=TM[hs(h), c, 1, :],
                                                           start=False, stop=True, tile_position=tp(h)), r=["TM"], w=[bk(7)])
            p.op("dve", lambda e: e.tensor_tensor(out=RtT[:], in0=bank[4][:], in1=Rh_f[:], op=ALU.add), r=[bk(4), "Rh_f"], w=["RtT"])
            p.op("act", lambda e: e.activation(out=v3(Y0), in_=bank[5][:], func=AF.Copy), r=[bk(5)], w=["Y0"])
            p.op("pool", lambda e: e.tensor_tensor(out=DG[:], in0=i2[:].unsqueeze(1).to_broadcast([128, NCH, 64]),
                                                  in1=e_pos[:, :, 63:64].to_broadcast([128, NCH, 64]), op=ALU.mult), r=["i2", "e_pos"], w=["DG"])
            p.op("dve", lambda e: e.tensor_tensor(out=v3(PT_f), in0=bank[6][:], in1=v3(DG), op=ALU.add), r=[bk(6), "DG"], w=["PT_f"])
            p.op("act", lambda e: e.activation(out=v3(Q_f), in_=bank[7][:], func=AF.Copy), r=[bk(7)], w=["Q_f"])

            for c in range(NCH):
                sn = 1 - s_cur
                for h in range(2):
                    p.op("pe", lambda e, c=c, h=h, s_cur=s_cur: e.matmul(bank[0][hs(h), c * 64:(c + 1) * 64], lhsT=RtT[hs(h), c * 64:(c + 1) * 64],
                                                                        rhs=S_b[s_cur][hs(h), :], start=True, stop=True, tile_position=tp(h)),
                         r=["RtT", "S_b%d" % s_cur], w=[bk(0)])
                    p.op("pe", lambda e, c=c, h=h, s_cur=s_cur: e.matmul(bank[1][hs(h), (c % 2) * 64:(c % 2 + 1) * 64], lhsT=PT_f[hs(h), c, :],
                                                                        rhs=S_f[s_cur][hs(h), :], start=True, stop=True, tile_position=tp(h)),
                         r=["PT_f", "S_f%d" % s_cur], w=[bk(1)])
                p.op("dve", lambda e, c=c, sn=sn: e.tensor_tensor(out=S_f[sn][:], in0=bank[1][:, (c % 2) * 64:(c % 2 + 1) * 64], in1=Q_f[:, c, :], op=ALU.add),
                     r=[bk(1), "Q_f"], w=["S_f%d" % sn])
                p.op("act", lambda e, sn=sn: e.activation(out=S_b[sn][:], in_=S_f[sn][:], func=AF.Copy), r=["S_f%d" % sn], w=["S_b%d" % sn])
                s_cur = sn
            p.op("dve", lambda e: e.tensor_tensor(out=v3(y_tm), in0=bank[0][:], in1=v3(Y0), op=ALU.add), r=[bk(0), "Y0"], w=["y_tm"])

            for c in range(NCH):
                for h in range(2):
                    p.op("pe", lambda e, c=c, h=h: e.matmul(bank[2][hs(h), c * 64:(c + 1) * 64], lhsT=y_tm[hs(h), c, :], rhs=ident_f[hs(h), hs(h)],
                                                           start=True, stop=True, tile_position=tp(h)), r=["y_tm", "ident_f"], w=[bk(2)])
            p.op("act", lambda e: e.activation(out=yT_f[:], in_=bank[2][:], func=AF.Copy), r=[bk(2)], w=["yT_f"])
            p.op("pe", lambda e: e.matmul(bank[3][:], lhsT=obd_f[:], rhs=yT_f[:], start=True, stop=True), r=["obd_f", "yT_f"], w=[bk(3)])
            p.op("dve", lambda e: e.scalar_tensor_tensor(out=d_f[:], in0=bank[3][:], scalar=-1.0 / 64, in1=yT_f[:], op0=ALU.mult, op1=ALU.add),
                 r=[bk(3), "yT_f"], w=["d_f"])
            p.op("pool", lambda e: e.tensor_tensor(out=sq_f[:], in0=d_f[:], in1=d_f[:], op=ALU.mult), r=["d_f"], w=["sq_f"])
            p.op("pe", lambda e: e.matmul(bank[4][:], lhsT=obd_f[:], rhs=sq_f[:], start=True, stop=True), r=["obd_f", "sq_f"], w=[bk(4)])
            p.op("dve", lambda e: e.tensor_scalar(out=rstd[:], in0=bank[4][:], scalar1=1.0 / 64, scalar2=GN_EPS, op0=ALU.mult, op1=ALU.add),
                 r=[bk(4)], w=["rstd"])
            p.op("act", lambda e: e.activation(out=rstd[:], in_=rstd[:], func=AF.Sqrt), r=["rstd"], w=["rstd"])
            p.op("dve", lambda e: e.reciprocal(out=rstd[:], in_=rstd[:]), r=["rstd"], w=["rstd"])
            p.op("dve", lambda e: e.tensor_tensor(out=d_f[:], in0=d_f[:], in1=rstd[:], op=ALU.mult), r=["d_f", "rstd"], w=["d_f"])
            p.op("pool", lambda e: e.tensor_scalar(out=d_f[:], in0=d_f[:], scalar1=vcol(V_GG), scalar2=vcol(V_GB), op0=ALU.mult, op1=ALU.add),
                 r=["d_f", "vec"], w=["d_f"])
            p.op("pool", lambda e: e.tensor_tensor(out=d_f[:], in0=d_f[:], in1=bon_f[:], op=ALU.add), r=["d_f", "bon_f"], w=["d_f"])
            y_o = yo[ti % 2]
            yk = "yo%d" % (ti % 2)
            p.op("dve", lambda e, y_o=y_o: e.tensor_tensor(out=y_o[:], in0=d_f[:], in1=g_f[:], op=ALU.mult), r=["d_f", "g_f"], w=[yk])
            p.dma("sp", ygT[:, t0:t0 + TT], y_o[:], r=[yk], semkey="o_" + yk, store=True)
            if debug and ti == 0:
                f2 = lambda tl: tl[:].rearrange("p c t -> p (c t)")
                dl = [(r_f[:], "r_f"), (k_f[:], "k_f"), (v_f[:], "v_f"), (al_f[:], "al_f"), (g_f[:], "g_f"), (kk_f[:], "kk_f"),
                      (bon_f[:], "bon_f"), (f2(e_pos), "e_pos"), (f2(e_ex), "e_ex"), (f2(e_rem), "e_rem"), (Rh_f[:], "Rh_f"),
                      (yT_f[:], "yT_f"), (rstd[:], "rstd"), (d_f[:], "d_f"), (f2(Y0), "Y0"), (f2(PT_f), "PT_f"), (f2(Q_f), "Q_f"),
                      (f2(y_tm), "y_tm"), (Xf[:, 0:4, :].rearrange("p c t -> p (c t)"), "Xf"), (kt_f[:], "kt_f"), (b_f[:], "b_f"),
                      (f2(e_neg), "e_neg")]
                for i, (ap_, key) in enumerate(dl):
                    p.dma("sp", dbg[i], ap_, r=[key], semkey="dbgo", store=True)
        p.emit()
    return nc


def rwkv_inputs(inp, ntiles=T // TT, cores=range(NCORES)):
    TL = ntiles * TT
    x = inp["x"][0]
    xT = np.zeros((C, TL + 1), np.float32)
    xT[:, 1:] = x[:TL].T
    tri3, msk, i2, onesbd = rwkv_consts()
    mu6 = np.ascontiguousarray(inp["rwkv_mu"][0].T)
    maps = []
    for c in cores:
        cs = slice(c * 128, (c + 1) * 128)
        wrkv = inp["rwkv_w_rkv"][0]
        wbig = np.concatenate([wrkv[0][:, cs], wrkv[1][:, cs], wrkv[2][:, cs], inp["rwkv_w1"][0], inp["rwkv_a1"][0], inp["rwkv_g1"][0]], axis=1)
        w2a = np.concatenate([inp["rwkv_w2"][0][:, cs], inp["rwkv_w0"][0][None, cs]], axis=0)
        a2p = np.zeros((128, 128), np.float32)
        a2p[64:128] = inp["rwkv_a2"][0][:, cs]
        ka = inp["rwkv_k_a"][0][cs]
        one = np.ones_like(ka)
        vecs = np.stack([inp["rwkv_k_k"][0][cs], ka, one, inp["rwkv_r_k"][0].reshape(-1)[cs], inp["rwkv_gn_g"][0][cs],
                         inp["rwkv_gn_b"][0][cs], inp["rwkv_a0"][0][cs], one], axis=1).astype(np.float32)
        maps.append({
            "xT": xT, "wbig": np.ascontiguousarray(wbig), "mu6": mu6, "w2a": np.ascontiguousarray(w2a), "a2p": a2p,
            "g2p": np.ascontiguousarray(inp["rwkv_g2"][0][:, cs]), "vecs": np.ascontiguousarray(vecs),
            "tri3": tri3, "msk": msk, "i2": i2, "onesbd": onesbd,
        })
    return maps


def run_rwkv(inp):
    nc = _get("rwkv", lambda: build_rwkv(T // TT))
    maps = rwkv_inputs(inp)
    res = _run(nc, maps)
    return np.concatenate([r["ygT"] for r in res.results], axis=0)


def kernel(**inputs):
    inp = {k: np.asarray(v) for k, v in inputs.items()}
    x0 = np.ascontiguousarray(inp["x"][0], dtype=np.float32)
    ygT = run_rwkv(inp)
    lnp0 = np.stack([inp["ln_mix_g"][0], inp["ln_mix_b"][0], inp["ln_ffn_g"][0], inp["ln_ffn_b"][0]]).astype(np.float32)
    x1, qkv = run_post(ygT, x0, inp["rwkv_w_o"][0], inp["ffn_w_in"][0], inp["ffn_w_down"][0], lnp0,
                       inp["moba_w_qkv"][0], rope_tables())
    oT = run_attn(qkv)
    lnp1 = np.stack([inp["ln_mix_g"][1], inp["ln_mix_b"][1], inp["ln_ffn_g"][1], inp["ln_ffn_b"][1]]).astype(np.float32)
    out, _ = run_post(oT, x1, inp["moba_w_o"][0], inp["ffn_w_in"][1], inp["ffn_w_down"][1], lnp1)
    return out.reshape(1, T, C).astype(np.float32)
```

```python
import math
from contextlib import ExitStack

import numpy as np
import ml_dtypes

import concourse.bass as bass
import concourse.mybir as mybir
from concourse.bass_utils import run_bass_kernel_spmd

F32 = mybir.dt.float32
BF16 = mybir.dt.bfloat16
ALU = mybir.AluOpType
AF = mybir.ActivationFunctionType
AX = mybir.AxisListType

NCORES = 8
T = 16384
C = 1024
H = 16
DH = 64
DFF = 2816
DEPTH = 2
ALPHA = (2 * DEPTH) ** 0.25
LN_EPS = 1e-5
GN_EPS = 64 * 1e-5
NEG = -30000.0


class _Op:
    __slots__ = ("eng", "fn", "deps", "is_dma", "semkey", "sig", "need_sig", "idx")


class Prog:
    ENGS = ("pe", "act", "dve", "pool", "sp")

    def __init__(self, nc):
        self.nc = nc
        self.ops = []
        self.last_w = {}
        self.readers = {}
        self.store_keys = []

    def _deps(self, r, w):
        deps = set()
        for k in list(r) + list(w):
            lw = self.last_w.get(k)
            if lw is not None:
                deps.add(lw)
        for k in w:
            for rd in self.readers.get(k, ()):
                deps.add(rd)
        return deps

    def _commit(self, idx, r, w):
        for k in r:
            self.readers.setdefault(k, []).append(idx)
        for k in w:
            self.last_w[k] = idx
            self.readers[k] = []

    def op(self, eng, fn, r=(), w=()):
        o = _Op()
        o.eng, o.fn, o.is_dma, o.semkey, o.sig, o.need_sig = eng, fn, False, None, None, False
        o.deps = self._deps(r, w)
        o.idx = len(self.ops)
        self.ops.append(o)
        self._commit(o.idx, r, w)
        return o.idx

    def dma(self, q, out, in_, r=(), w=(), semkey=None, store=False):
        o = _Op()
        o.eng, o.is_dma, o.sig, o.need_sig = q, True, None, True
        o.fn = lambda e, out=out, in_=in_: e.dma_start(out=out, in_=in_)
        o.semkey = semkey if semkey is not None else (list(w)[0] if w else list(r)[0])
        o.deps = self._deps(r, w)
        o.idx = len(self.ops)
        self.ops.append(o)
        self._commit(o.idx, r, w)
        if store:
            self.store_keys.append(o.semkey)
        return o.idx

    def emit(self):
        nc = self.nc
        ops = self.ops
        for o in ops:
            for d in o.deps:
                od = ops[d]
                if od.is_dma:
                    continue
                if od.eng == o.eng and not o.is_dma and o.eng == "pe":
                    continue
                od.need_sig = True
        cnt = {e: 0 for e in self.ENGS}
        dcnt = {}
        for o in ops:
            if o.is_dma:
                dcnt[o.semkey] = dcnt.get(o.semkey, 0) + 16
                o.sig = ("d:" + o.semkey, dcnt[o.semkey])
            elif o.need_sig:
                cnt[o.eng] += 1
                o.sig = ("e:" + o.eng, cnt[o.eng])
        semnames = ["e:" + e for e in self.ENGS] + ["d:" + k for k in dcnt]
        with ExitStack() as es:
            sems = {}
            for i, n in enumerate(semnames):
                sems[n] = es.enter_context(nc.semaphore("s%d" % i))
            block = es.enter_context(nc.Block())
            per_eng = {e: [o for o in ops if o.eng == e] for e in self.ENGS}
            final_waits = [("d:" + k, dcnt[k]) for k in dict.fromkeys(self.store_keys)]

            def run(eng_name, e):
                waited = {}
                for o in per_eng[eng_name]:
                    need = {}
                    for d in o.deps:
                        od = ops[d]
                        if od.sig is None:
                            continue
                        if (not od.is_dma) and od.eng == eng_name and eng_name == "pe" and not o.is_dma:
                            continue
                        s, v = od.sig
                        if need.get(s, 0) < v:
                            need[s] = v
                    for s, v in need.items():
                        if waited.get(s, 0) < v:
                            e.wait_ge(sems[s], v)
                            waited[s] = v
                    ins = o.fn(e)
                    if o.sig is not None:
                        s, v = o.sig
                        ins.then_inc(sems[s], 16 if o.is_dma else 1)
                if eng_name == "sp":
                    for s, v in final_waits:
                        e.wait_ge(sems[s], v)

            @block.tensor
            def _(e):
                run("pe", e)

            @block.scalar
            def _(e):
                run("act", e)

            @block.vector
            def _(e):
                run("dve", e)

            @block.gpsimd
            def _(e):
                run("pool", e)

            @block.sync
            def _(e):
                run("sp", e)


def _bcast_rows(ap_1d, nparts):
    return ap_1d.partition_broadcast(nparts)


TOK = T // NCORES
GRP = 512
NGRP = TOK // GRP
NFB = DFF // 128


def build_post(with_qkv):
    nc = bass.Bass("TRN2", target_bir_lowering=False)
    aT = nc.dram_tensor("aT", [C, TOK], BF16, kind="ExternalInput").ap()
    xres = nc.dram_tensor("xres", [TOK, C], F32, kind="ExternalInput").ap()
    w_o = nc.dram_tensor("w_o", [C, C], F32, kind="ExternalInput").ap()
    w_in = nc.dram_tensor("w_in", [C, 2 * DFF], F32, kind="ExternalInput").ap()
    w_dn = nc.dram_tensor("w_dn", [DFF, C], F32, kind="ExternalInput").ap()
    lnp = nc.dram_tensor("lnp", [4, C], F32, kind="ExternalInput").ap()
    xout = nc.dram_tensor("xout", [TOK, C], F32, kind="ExternalOutput").ap()
    if with_qkv:
        w_qkv = nc.dram_tensor("w_qkv", [C, 3 * C], F32, kind="ExternalInput").ap()
        rope = nc.dram_tensor("rope", [TOK, 4, 32], F32, kind="ExternalInput").ap()
        qkv_out = nc.dram_tensor("qkv", [3, TOK, C], F32, kind="ExternalOutput").ap()

    es = ExitStack()
    sb = lambda name, shape, dt: es.enter_context(nc.sbuf_tensor(name, shape, dt))
    ps = lambda name, shape, dt: es.enter_context(nc.psum_tensor(name, shape, dt))
    with es:
        ident = sb("ident", [128, 128], BF16)
        lnb = sb("lnb", [128, 4, C], F32)
        wdn_sb = sb("wdn_sb", [128, NFB, C], BF16)
        stg = [sb("stg%d" % i, [128, 4096], F32) for i in range(2)]
        win_sb = [sb("win%d" % i, [128, 8, 512], BF16) for i in range(2)]
        aT_sb = [sb("aT%d" % i, [128, 8, GRP], BF16) for i in range(2)]
        xr_sb = [sb("xr%d" % i, [128, C], F32) for i in range(2)]
        xg = sb("xg", [128, 4, C], F32)
        xb = sb("xb", [128, C], BF16)
        xT = sb("xT", [128, 8, GRP], BF16)
        actT = sb("actT", [128, NFB, GRP], BF16)
        sg = [sb("sg%d" % i, [128, GRP], F32) for i in range(2)]
        junk = sb("junk", [128, C], BF16)
        st = sb("st", [128, 8], F32)
        if with_qkv:
            rp_sb = sb("rp", [128, 4, 4, 32], F32)
            qo = [sb("qo%d" % i, [128, 512], F32) for i in range(2)]
            tmp = [sb("tmp%d" % i, [128, 8, 32], F32) for i in range(4)]
        acc = ps("acc", [128, C], F32)
        trp = ps("trp", [128, C], BF16)
        gu = [ps("gu%d" % i, [128, 2, GRP], F32) for i in range(2)]

        p = Prog(nc)
        p.op("pool", lambda e: e.memset(ident[:], 0.0), w=["ident"])
        p.op("pool", lambda e: e.affine_select(out=ident[:], in_=ident[:], pattern=[[-1, 128]],
                                               compare_op=ALU.not_equal, fill=1.0, base=0,
                                               channel_multiplier=1), r=["ident"], w=["ident"])
        p.dma("sp", lnb[:], lnp.partition_broadcast(128), w=["lnb"])

        stg_n = [0]

        def load_cast(parts, dst_ap, dst_keys):
            i = stg_n[0] % 2
            stg_n[0] += 1
            sk = "stg%d" % i
            for view_fn, src in parts:
                p.dma("sp", view_fn(stg[i]), src, w=[sk], semkey=sk)
            return i, sk

        def load_w8(dst, dkey, src_cols_list):
            i = stg_n[0] % 2
            stg_n[0] += 1
            sk = "stg%d" % i
            sv = stg[i][:].rearrange("p (k c) -> p k c", k=8)
            for src, c0 in src_cols_list:
                n = src.shape[1]
                p.dma("sp", sv[:, :, c0:c0 + n], src.rearrange("(k p) c -> p k c", p=128), w=[sk], semkey=sk)
            p.op("act", lambda e, dst=dst, sv=sv: e.activation(out=dst[:, 0:4, :], in_=sv[:, 0:4, :], func=AF.Copy), r=[sk], w=[dkey])
            p.op("dve", lambda e, dst=dst, sv=sv: e.tensor_copy(out=dst[:, 4:8, :], in_=sv[:, 4:8, :]), r=[sk], w=[dkey])

        def load_rows4(dst_ap, dkey, src_rows):
            i = stg_n[0] % 2
            stg_n[0] += 1
            sk = "stg%d" % i
            nblk = src_rows.shape[0] // 128
            sv = stg[i][:, 0:nblk * 1024].rearrange("p (k c) -> p k c", k=nblk)
            p.dma("sp", sv, src_rows.rearrange("(k p) c -> p k c", p=128), w=[sk], semkey=sk)
            h2 = max(1, nblk // 2)
            p.op("act", lambda e, dst_ap=dst_ap, sv=sv, h2=h2: e.activation(out=dst_ap[:, 0:h2, :], in_=sv[:, 0:h2, :], func=AF.Copy), r=[sk], w=[dkey])
            if nblk > h2:
                p.op("dve", lambda e, dst_ap=dst_ap, sv=sv, h2=h2: e.tensor_copy(out=dst_ap[:, h2:, :], in_=sv[:, h2:, :]), r=[sk], w=[dkey])

        for f4 in range(0, NFB, 4):
            n = min(4, NFB - f4)
            load_rows4(wdn_sb[:, f4:f4 + n, :], "wdn", w_dn[f4 * 128:(f4 + n) * 128, :])

        def layer_norm(tile_ap, gi, key):
            p.op("act", lambda e: e.activation(out=junk[:], in_=tile_ap, func=AF.Copy, accum_out=st[:, 0:1]),
                 r=[key], w=["junk", "st"])
            p.op("act", lambda e: e.activation(out=junk[:], in_=tile_ap, func=AF.Square, accum_out=st[:, 1:2]),
                 r=[key], w=["junk", "st"])
            p.op("dve", lambda e: e.tensor_scalar(out=st[:, 2:3], in0=st[:, 0:1], scalar1=1.0 / C, scalar2=None, op0=ALU.mult),
                 r=["st"], w=["st2"])
            p.op("dve", lambda e: e.tensor_tensor(out=st[:, 3:4], in0=st[:, 2:3], in1=st[:, 2:3], op=ALU.mult),
                 r=["st2"], w=["st3"])
            p.op("dve", lambda e: e.scalar_tensor_tensor(out=st[:, 4:5], in0=st[:, 1:2], scalar=1.0 / C, in1=st[:, 3:4],
                                                         op0=ALU.mult, op1=ALU.subtract), r=["st", "st3"], w=["st4"])
            p.op("dve", lambda e: e.tensor_scalar(out=st[:, 4:5], in0=st[:, 4:5], scalar1=LN_EPS, scalar2=None, op0=ALU.add),
                 r=["st4"], w=["st4"])
            p.op("act", lambda e: e.activation(out=st[:, 5:6], in_=st[:, 4:5], func=AF.Sqrt), r=["st4"], w=["st5"])
            p.op("dve", lambda e: e.reciprocal(out=st[:, 6:7], in_=st[:, 5:6]), r=["st5"], w=["st6"])
            p.op("dve", lambda e: e.tensor_scalar(out=tile_ap, in0=tile_ap, scalar1=st[:, 2:3], scalar2=st[:, 6:7],
                                                  op0=ALU.subtract, op1=ALU.mult), r=[key, "st2", "st6"], w=[key])
            p.op("dve", lambda e: e.tensor_tensor(out=tile_ap, in0=tile_ap, in1=lnb[:, gi, :], op=ALU.mult),
                 r=[key, "lnb"], w=[key])
            p.op("dve", lambda e: e.tensor_tensor(out=tile_ap, in0=tile_ap, in1=lnb[:, gi + 1, :], op=ALU.add),
                 r=[key, "lnb"], w=[key])

        def to_channel_major(tile_ap, key_in, ti):
            p.op("act", lambda e: e.activation(out=xb[:], in_=tile_ap, func=AF.Copy), r=[key_in], w=["xb"])
            for kc in range(8):
                p.op("pe", lambda e, kc=kc: e.transpose(trp[:, kc * 128:(kc + 1) * 128], xb[:, kc * 128:(kc + 1) * 128], ident[:]),
                     r=["xb", "ident"], w=["trp"])
            p.op("dve", lambda e: e.tensor_copy(out=xT[:, :, ti * 128:(ti + 1) * 128],
                                                in_=trp[:].rearrange("p (k t) -> p k t", k=8)),
                 r=["trp"], w=["xT"])

        def load_acts(g):
            t0 = g * GRP
            a_sb = aT_sb[g % 2]
            ak = "aT%d" % (g % 2)
            p.dma("sp", a_sb[:], aT[:, t0:t0 + GRP].rearrange("(k p) t -> p k t", p=128), w=[ak], semkey=ak)

        load_acts(0)
        for g in range(NGRP):
            t0 = g * GRP
            a_sb = aT_sb[g % 2]
            ak = "aT%d" % (g % 2)
            if with_qkv:
                p.dma("sp", rp_sb[:], rope[t0:t0 + GRP].rearrange("(i p) f c -> p i f c", p=128), w=["rp"])
            wo_v = [win_sb[h][:].rearrange("p k c -> p (k c)").rearrange("p (k c) -> p k c", k=4) for h in range(2)]
            for h in range(2):
                load_rows4(wo_v[h], "win%d" % h, w_o[h * 512:(h + 1) * 512, :])
            def phaseA_tail(ti):
                layer_norm(xg[:, ti, :], 0, "xg%d" % ti)
                to_channel_major(xg[:, ti, :], "xg%d" % ti, ti)

            for ti in range(4):
                r0 = t0 + ti * 128
                xr = xr_sb[(g * 4 + ti) % 2]
                xk = "xr%d" % ((g * 4 + ti) % 2)
                p.dma("sp", xr[:], xres[r0:r0 + 128, :], w=[xk], semkey=xk)
                for hf in range(2):
                    for kc in range(8):
                        p.op("pe", lambda e, kc=kc, hf=hf, ti=ti, a_sb=a_sb, wv=wo_v[kc // 4]: e.matmul(
                            acc[:, hf * 512:(hf + 1) * 512], lhsT=a_sb[:, kc, ti * 128:(ti + 1) * 128],
                            rhs=wv[:, kc % 4, hf * 512:(hf + 1) * 512], start=(kc == 0), stop=(kc == 7)),
                            r=[ak, "win%d" % (kc // 4)], w=["acc"])
                xk_out = "xg%d" % ti
                p.op("dve", lambda e, xr=xr, ti=ti: e.scalar_tensor_tensor(out=xg[:, ti, :], in0=xr[:], scalar=ALPHA, in1=acc[:],
                                                                           op0=ALU.mult, op1=ALU.add),
                     r=[xk, "acc"], w=[xk_out])
                if ti > 0:
                    phaseA_tail(ti - 1)
            phaseA_tail(3)
            if g + 1 < NGRP:
                load_acts(g + 1)
            NU = NFB // 2
            for u in range(NU):
                wsb = win_sb[u % 2]
                wk = "win%d" % (u % 2)
                load_w8(wsb, wk, [(w_in[:, u * 256:(u + 1) * 256], 0), (w_in[:, DFF + u * 256:DFF + (u + 1) * 256], 256)])
                for j in range(2):
                    fb = u * 2 + j
                    gps = gu[fb % 2]
                    gk = "gu%d" % (fb % 2)
                    for which in range(2):
                        for kc in range(8):
                            p.op("pe", lambda e, kc=kc, which=which, j=j, wsb=wsb, gps=gps: e.matmul(
                                gps[:, which, :], lhsT=wsb[:, kc, which * 256 + j * 128: which * 256 + (j + 1) * 128],
                                rhs=xT[:, kc, :], start=(kc == 0), stop=(kc == 7)),
                                r=[wk, "xT"], w=[gk])
                    s_sb = sg[fb % 2]
                    sk = "sg%d" % (fb % 2)
                    p.op("act", lambda e, gps=gps, s_sb=s_sb: e.activation(out=s_sb[:], in_=gps[:, 0, :], func=AF.Silu),
                         r=[gk], w=[sk])
                    p.op("dve", lambda e, gps=gps, s_sb=s_sb, fb=fb: e.tensor_tensor(out=actT[:, fb, :], in0=s_sb[:], in1=gps[:, 1, :], op=ALU.mult),
                         r=[gk, sk], w=["actT"])
            def down_tail(ti):
                r0 = t0 + ti * 128
                xk_out = "xg%d" % ti
                layer_norm(xg[:, ti, :], 2, xk_out)
                p.dma("act", xout[r0:r0 + 128, :], xg[:, ti, :], r=[xk_out], semkey="o_" + xk_out, store=True)
                if with_qkv:
                    to_channel_major(xg[:, ti, :], xk_out, ti)

            for ti in range(4):
                for hf in range(2):
                    for fb in range(NFB):
                        p.op("pe", lambda e, fb=fb, hf=hf, ti=ti: e.matmul(
                            acc[:, hf * 512:(hf + 1) * 512], lhsT=actT[:, fb, ti * 128:(ti + 1) * 128],
                            rhs=wdn_sb[:, fb, hf * 512:(hf + 1) * 512], start=(fb == 0), stop=(fb == NFB - 1)),
                            r=["actT", "wdn"], w=["acc"])
                xk_out = "xg%d" % ti
                p.op("dve", lambda e, ti=ti: e.scalar_tensor_tensor(out=xg[:, ti, :], in0=xg[:, ti, :], scalar=ALPHA, in1=acc[:],
                                                                    op0=ALU.mult, op1=ALU.add),
                     r=[xk_out, "acc"], w=[xk_out])
                if ti > 0:
                    down_tail(ti - 1)
            down_tail(3)
            if with_qkv:
                for cb in range(6):
                    wsb = win_sb[cb % 2]
                    wk = "win%d" % (cb % 2)
                    load_w8(wsb, wk, [(w_qkv[:, cb * 512:(cb + 1) * 512], 0)])
                    for ti in range(4):
                        r0 = t0 + ti * 128
                        gps = gu[(cb * 4 + ti) % 2]
                        gk = "gu%d" % ((cb * 4 + ti) % 2)
                        for kc in range(8):
                            p.op("pe", lambda e, kc=kc, ti=ti, wsb=wsb, gps=gps: e.matmul(
                                gps[:, 0, :], lhsT=xT[:, kc, ti * 128:(ti + 1) * 128], rhs=wsb[:, kc, :],
                                start=(kc == 0), stop=(kc == 7)), r=[wk, "xT"], w=[gk])
                        o_sb = qo[(cb * 4 + ti) % 2]
                        ok = "qo%d" % ((cb * 4 + ti) % 2)
                        which = cb // 2
                        if which == 2:
                            p.op("act", lambda e, gps=gps, o_sb=o_sb: e.activation(out=o_sb[:], in_=gps[:, 0, :], func=AF.Copy),
                                 r=[gk], w=[ok])
                        else:
                            src = gps[:, 0, :].rearrange("p (h d) -> p h d", h=8)
                            dst = o_sb[:].rearrange("p (h d) -> p h d", h=8)
                            cos = rp_sb[:, ti, 2 * which, :].unsqueeze(1).to_broadcast([128, 8, 32])
                            sin = rp_sb[:, ti, 2 * which + 1, :].unsqueeze(1).to_broadcast([128, 8, 32])
                            tk = ["tmp%d" % i for i in range(4)]
                            p.op("dve", lambda e, src=src, cos=cos: e.tensor_tensor(out=tmp[0][:], in0=src[:, :, 0:32], in1=cos, op=ALU.mult),
                                 r=[gk, "rp"], w=[tk[0]])
                            p.op("dve", lambda e, src=src, sin=sin: e.tensor_tensor(out=tmp[1][:], in0=src[:, :, 32:64], in1=sin, op=ALU.mult),
                                 r=[gk, "rp"], w=[tk[1]])
                            p.op("dve", lambda e, src=src, cos=cos: e.tensor_tensor(out=tmp[2][:], in0=src[:, :, 32:64], in1=cos, op=ALU.mult),
                                 r=[gk, "rp"], w=[tk[2]])
                            p.op("dve", lambda e, src=src, sin=sin: e.tensor_tensor(out=tmp[3][:], in0=src[:, :, 0:32], in1=sin, op=ALU.mult),
                                 r=[gk, "rp"], w=[tk[3]])
                            p.op("pool", lambda e, dst=dst: e.tensor_tensor(out=dst[:, :, 0:32], in0=tmp[0][:], in1=tmp[1][:], op=ALU.subtract),
                                 r=[tk[0], tk[1]], w=[ok])
                            p.op("pool", lambda e, dst=dst: e.tensor_tensor(out=dst[:, :, 32:64], in0=tmp[2][:], in1=tmp[3][:], op=ALU.add),
                                 r=[tk[2], tk[3]], w=[ok])
                        p.dma("act", qkv_out[which, r0:r0 + 128, (cb % 2) * 512:(cb % 2 + 1) * 512], o_sb[:], r=[ok],
                              semkey="o_" + ok, store=True)
        p.emit()
    return nc


_NC_CACHE = {}


def _get(name, builder):
    if name not in _NC_CACHE:
        import time as _t
        t0 = _t.time()
        _NC_CACHE[name] = builder()
        print("[kernel] built", name, "in %.1fs" % (_t.time() - t0), flush=True)
    return _NC_CACHE[name]


def _run(nc, in_maps):
    return run_bass_kernel_spmd(nc, in_maps, core_ids=list(range(NCORES)))


def run_post(aT_full, xres_full, w_o, w_in, w_dn, lnp, w_qkv=None, rope=None):
    with_qkv = w_qkv is not None
    nc = _get("post_qkv" if with_qkv else "post", lambda: build_post(with_qkv))
    in_maps = []
    for c in range(NCORES):
        m = {
            "aT": np.ascontiguousarray(aT_full[:, c * TOK:(c + 1) * TOK]),
            "xres": np.ascontiguousarray(xres_full[c * TOK:(c + 1) * TOK]),
            "w_o": w_o, "w_in": w_in, "w_dn": w_dn, "lnp": lnp,
        }
        if with_qkv:
            m["w_qkv"] = w_qkv
            m["rope"] = np.ascontiguousarray(rope[c * TOK:(c + 1) * TOK])
        in_maps.append(m)
    res = _run(nc, in_maps)
    xo = np.concatenate([r["xout"] for r in res.results], axis=0)
    if with_qkv:
        qkv = np.concatenate([r["qkv"] for r in res.results], axis=1)
        return xo, qkv
    return xo, None


def rope_tables():
    inv = (10000.0 ** (-np.arange(0, DH, 2, dtype=np.float32) / DH)).astype(np.float32)
    ang = (np.arange(T, dtype=np.float32)[:, None] * inv[None, :]).astype(np.float32)
    cos, sin = np.cos(ang).astype(np.float32), np.sin(ang).astype(np.float32)
    s = np.float32(DH ** -0.5)
    return np.ascontiguousarray(np.stack([cos * s, sin * s, cos, sin], axis=1))


NBLK = T // 256
QG = 512
NQG = T // QG


def moba_consts():
    blk1h = np.zeros((64, T), np.float32)
    for b in range(NBLK):
        blk1h[b, b * 256:(b + 1) * 256] = 1.0
    n = np.arange(64)[:, None]
    b = np.arange(64)[None, :]
    p01 = (b < n).astype(np.float32)
    o01 = (b == n).astype(np.float32)
    pbias = np.where(b < n, 0.0, -1e9).astype(np.float32)
    tabs = np.stack([pbias, p01, o01], axis=0)
    dm = np.zeros((4, 128, 4, 128), np.float32)
    key = np.arange(128)[:, None]
    q = np.arange(128)[None, :]
    tri = np.where(key <= q, 0.0, NEG)
    for j in range(4):
        for g in range(4):
            if j > g:
                dm[j, :, g, :] = NEG
            elif j == g:
                dm[j, :, g, :] = tri
    return (blk1h.astype(ml_dtypes.bfloat16), tabs, dm.reshape(4, 128, 512).astype(ml_dtypes.bfloat16))


def build_attn():
    nc = bass.Bass("TRN2", target_bir_lowering=False)
    qT = nc.dram_tensor("qT", [128, T], F32, kind="ExternalInput").ap()
    kT = nc.dram_tensor("kT", [128, T], F32, kind="ExternalInput").ap()
    v = nc.dram_tensor("v", [T, 128], F32, kind="ExternalInput").ap()
    blk1h = nc.dram_tensor("blk1h", [64, T], BF16, kind="ExternalInput").ap()
    tabs = nc.dram_tensor("tabs", [3, 64 * 64], F32, kind="ExternalInput").ap()
    dmask = nc.dram_tensor("dmask", [4, 128, 512], BF16, kind="ExternalInput").ap()
    oT = nc.dram_tensor("oT", [128, T], BF16, kind="ExternalOutput").ap()

    es = ExitStack()
    sb = lambda name, shape, dt: es.enter_context(nc.sbuf_tensor(name, shape, dt))
    ps = lambda name, shape, dt: es.enter_context(nc.psum_tensor(name, shape, dt))
    with es:
        ident = sb("ident", [128, 128], BF16)
        ones_f = sb("ones_f", [128, 64], F32)
        kaug = sb("kaug", [128, T], BF16)
        qaug = sb("qaug", [128, T], BF16)
        vsb = sb("vsb", [128, 128, 2, 65], BF16)
        tb = sb("tb", [128, 3, 64 * 64], F32)
        dm = sb("dm", [128, 4, 512], BF16)
        kmean = sb("kmean", [64, 64], F32)
        kmean_b = sb("kmean_b", [64, 64], BF16)
        gm = [sb("gm%d" % i, [128, 8, 64], F32) for i in range(2)]
        top8 = [sb("top8%d" % i, [128, 8, 8], F32) for i in range(2)]
        selt = [sb("selt%d" % i, [128, 8, 64], F32) for i in range(2)]
        negm = [sb("negm%d" % i, [128, 8, 64], BF16) for i in range(2)]
        pT = [sb("pT%d" % i, [128, QG], BF16) for i in range(3)]
        osb = [sb("osb%d" % i, [65, QG], F32) for i in range(2)]
        obf = [sb("obf%d" % i, [64, QG], BF16) for i in range(2)]
        s_ps = [ps("s_ps%d" % i, [128, QG], F32) for i in range(3)]
        o_ps = [ps("o_ps%d" % i, [65, QG], F32) for i in range(2)]
        g_ps = ps("g_ps", [128, 8, 64], F32)
        m_ps = ps("m_ps", [128, 8 * 128], BF16)
        bc_ps = m_ps[:].bitcast(F32)

        p = Prog(nc)
        p.op("pool", lambda e: e.memset(ident[:], 0.0), w=["ident"])
        p.op("pool", lambda e: e.affine_select(out=ident[:], in_=ident[:], pattern=[[-1, 128]],
                                               compare_op=ALU.not_equal, fill=1.0, base=0,
                                               channel_multiplier=1), r=["ident"], w=["ident"])
        p.op("pool", lambda e: e.memset(ones_f[:], 1.0), w=["ones_f"])
        p.op("pool", lambda e: e.memset(vsb[:, :, :, 64:65], 1.0), w=["vsb1"])
        p.dma("sp", tb[:], tabs.partition_broadcast(128), w=["tb"])
        p.dma("sp", dm[:], dmask.rearrange("j k q -> k j q"), w=["dm"])
        p.dma("sp", kaug[64:128, :], blk1h, w=["kaug_hi"])
        for hh in range(2):
            for c4 in range(4):
                p.dma("pool", vsb[:, c4 * 32:(c4 + 1) * 32, hh, 0:64],
                      v[c4 * 4096:(c4 + 1) * 4096, hh * 64:(hh + 1) * 64].rearrange("(kb p) d -> p kb d", p=128),
                      w=["vsb"], semkey="vsb")

        gcount = 0
        for hh in range(2):
            for c4 in range(4):
                sl = slice(c4 * 4096, (c4 + 1) * 4096)
                p.dma("pool", kaug[0:64, sl], kT[hh * 64:(hh + 1) * 64, sl], w=["kaug_lo"], semkey="kaug_lo")
                p.dma("pool", qaug[0:64, sl], qT[hh * 64:(hh + 1) * 64, sl], w=["qaug_lo"], semkey="qaug_lo")
            p.op("dve", lambda e: e.tensor_reduce(out=kmean[:], in_=kaug[0:64, :].rearrange("p (b s) -> p b s", s=256),
                                                  axis=AX.X, op=ALU.add), r=["kaug_lo"], w=["kmean"])
            p.op("dve", lambda e: e.tensor_scalar(out=kmean_b[:], in0=kmean[:], scalar1=1.0 / 256, scalar2=None, op0=ALU.mult),
                 r=["kmean"], w=["kmean_b"])
            for G8 in range(T // 1024):
                i2 = gcount % 2
                gcount += 1
                n0 = 4 * G8
                for c in range(8):
                    q0 = G8 * 1024 + c * 128
                    p.op("pe", lambda e, c=c, q0=q0: e.matmul(g_ps[:, c, :], lhsT=qaug[0:64, q0:q0 + 128], rhs=kmean_b[:],
                                                            start=True, stop=True), r=["qaug_lo", "kmean_b"], w=["g_ps"])
                gmk, t8k, slk, ngk = "gm%d" % i2, "top8%d" % i2, "selt%d" % i2, "negm%d" % i2
                tbv = tb[:].rearrange("p t (n b) -> p t n b", b=64)
                b0, b1, b2 = [tbv[:, t, n0:n0 + 4, :].unsqueeze(2).to_broadcast([128, 4, 2, 64]) for t in range(3)]
                g4 = lambda tl: tl[:].rearrange("p (a c) b -> p a c b", c=2)
                gm4, sl4, gp4 = g4(gm[i2]), g4(selt[i2]), g_ps[:].rearrange("p (a c) b -> p a c b", c=2)
                p.op("dve", lambda e, gm4=gm4, gp4=gp4, b0=b0: e.tensor_tensor(out=gm4, in0=gp4, in1=b0, op=ALU.add),
                     r=["g_ps", "tb"], w=[gmk])
                for c in range(8):
                    p.op("dve", lambda e, c=c, i2=i2: e.max(out=top8[i2][:, c, :], in_=gm[i2][:, c, :]), r=[gmk], w=[t8k])
                p.op("dve", lambda e, i2=i2: e.tensor_tensor(out=selt[i2][:], in0=gm[i2][:],
                                                            in1=top8[i2][:, :, 2:3].to_broadcast([128, 8, 64]), op=ALU.is_ge),
                     r=[gmk, t8k], w=[slk])
                p.op("pool", lambda e, sl4=sl4, b1=b1: e.tensor_tensor(out=sl4, in0=sl4, in1=b1, op=ALU.mult),
                     r=[slk, "tb"], w=[slk])
                p.op("pool", lambda e, sl4=sl4, b2=b2: e.tensor_tensor(out=sl4, in0=sl4, in1=b2, op=ALU.add),
                     r=[slk, "tb"], w=[slk])
                p.op("pool", lambda e, i2=i2: e.tensor_scalar(out=negm[i2][:], in0=selt[i2][:], scalar1=-1.0, scalar2=-NEG,
                                                             op0=ALU.add, op1=ALU.mult), r=[slk], w=[ngk])
                for c in range(8):
                    p.op("pe", lambda e, c=c, i2=i2: e.transpose(m_ps[64:128, c * 128:(c + 1) * 128], negm[i2][:, c, :], ident[:],
                                                                tile_position=(0, 64)), r=[ngk, "ident"], w=["m_ps"])
                p.op("act", lambda e, G8=G8: e.activation(out=qaug[64:128, G8 * 1024:(G8 + 1) * 1024], in_=m_ps[64:128, :], func=AF.Copy),
                     r=["m_ps"], w=["qaug_hi"])
            for G in range(NQG):
                nkb = 4 * (G + 1)
                op_i = G % 2
                opk = "o_ps%d" % op_i
                qsl = slice(G * QG, (G + 1) * QG)

                def qk(kb, G=G, qsl=qsl):
                    si = kb % 3
                    diag = kb >= 4 * G
                    p.op("pe", lambda e: e.matmul(s_ps[si][:], lhsT=kaug[:, kb * 128:(kb + 1) * 128], rhs=qaug[:, qsl],
                                                  start=True, stop=not diag),
                         r=["kaug_lo", "kaug_hi", "qaug_lo", "qaug_hi"], w=["s_ps%d" % si])
                    if diag:
                        j = kb - 4 * G
                        p.op("pe", lambda e: e.matmul(s_ps[si][:], lhsT=ident[:], rhs=dm[:, j, :], start=False, stop=True),
                             r=["ident", "dm"], w=["s_ps%d" % si])

                def ex_pv(kb, G=G, nkb=nkb, op_i=op_i, opk=opk, hh=hh):
                    si = kb % 3
                    p.op("act", lambda e: e.activation(out=pT[si][:], in_=s_ps[si][:], func=AF.Exp),
                         r=["s_ps%d" % si], w=["pT%d" % si])
                    p.op("pe", lambda e: e.matmul(o_ps[op_i][:], lhsT=vsb[:, kb, hh, :], rhs=pT[si][:],
                                                  start=(kb == 0), stop=(kb == nkb - 1)),
                         r=["vsb", "vsb1", "pT%d" % si], w=[opk])

                LOOK = 2
                for kb in range(min(LOOK, nkb)):
                    qk(kb)
                for kb in range(nkb):
                    if kb + LOOK < nkb:
                        qk(kb + LOOK)
                    ex_pv(kb)
                ob_i = G % 2
                p.op("dve", lambda e, op_i=op_i, ob_i=ob_i: e.tensor_copy(out=osb[ob_i][:], in_=o_ps[op_i][:]), r=[opk], w=["osb%d" % ob_i])
                p.op("dve", lambda e, ob_i=ob_i: e.reciprocal(out=osb[ob_i][64:65, :], in_=osb[ob_i][64:65, :]),
                     r=["osb%d" % ob_i], w=["osb%d" % ob_i])
                p.op("pe", lambda e, ob_i=ob_i: e.matmul(bc_ps[0:64, :], lhsT=ones_f[64:65, :], rhs=osb[ob_i][64:65, :], start=True, stop=True),
                     r=["ones_f", "osb%d" % ob_i], w=["m_ps"])
                p.op("dve", lambda e, ob_i=ob_i: e.tensor_tensor(out=obf[ob_i][:], in0=osb[ob_i][0:64, :], in1=bc_ps[0:64, :], op=ALU.mult),
                     r=["osb%d" % ob_i, "m_ps"], w=["obf%d" % ob_i])
                p.dma("sp", oT[hh * 64:(hh + 1) * 64, qsl], obf[ob_i][:], r=["obf%d" % ob_i], semkey="o_obf%d" % ob_i, store=True)
        p.emit()
    return nc


def run_attn(qkv):
    nc = _get("attn", build_attn)
    blk1h, tabs, dm = moba_consts()
    tabs = np.ascontiguousarray(tabs.reshape(3, 64 * 64))
    in_maps = []
    for c in range(NCORES):
        cs = slice(c * 128, (c + 1) * 128)
        in_maps.append({
            "qT": np.ascontiguousarray(qkv[0][:, cs].T),
            "kT": np.ascontiguousarray(qkv[1][:, cs].T),
            "v": np.ascontiguousarray(qkv[2][:, cs]),
            "blk1h": blk1h, "tabs": tabs, "dmask": dm,
        })
    res = _run(nc, in_maps)
    return np.concatenate([r["oT"] for r in res.results], axis=0)


LCH = 64
TT = 512
NCH = TT // LCH
WCOLS = 672
DECAY_C = -math.exp(-0.5)


def rwkv_consts():
    j = np.arange(64)[:, None]
    t = np.arange(64)[None, :]
    incl = (j <= t).astype(np.float32)
    strict = (j < t).astype(np.float32)
    rev = (j > t).astype(np.float32)
    tri3 = (DECAY_C * np.concatenate([incl, strict, rev], axis=1)).astype(np.float32)
    tri3 = np.concatenate([tri3, tri3], axis=0)
    up_s = (j < t).astype(np.float32)
    up_i = (j <= t).astype(np.float32)
    lo_s = (t < j).astype(np.float32)
    msk = np.concatenate([up_s, up_i, up_s, up_i, lo_s], axis=1)
    msk = np.concatenate([msk, msk], axis=0)
    i2 = np.concatenate([np.eye(64, dtype=np.float32)] * 2, axis=0)
    onesbd = np.zeros((128, 128), np.float32)
    onesbd[:64, :64] = 1.0
    onesbd[64:, 64:] = 1.0
    return tri3, msk, i2, onesbd


def build_rwkv(ntiles, debug=False):
    TL = ntiles * TT
    nc = bass.Bass("TRN2", target_bir_lowering=False)
    xT = nc.dram_tensor("xT", [C, TL + 1], F32, kind="ExternalInput").ap()
    wbig = nc.dram_tensor("wbig", [C, WCOLS], F32, kind="ExternalInput").ap()
    mu6 = nc.dram_tensor("mu6", [C, 6], F32, kind="ExternalInput").ap()
    w2a = nc.dram_tensor("w2a", [65, 128], F32, kind="ExternalInput").ap()
    a2p = nc.dram_tensor("a2p", [128, 128], F32, kind="ExternalInput").ap()
    g2p = nc.dram_tensor("g2p", [160, 128], F32, kind="ExternalInput").ap()
    vecs = nc.dram_tensor("vecs", [128, 8], F32, kind="ExternalInput").ap()
    tri3_d = nc.dram_tensor("tri3", [128, 192], F32, kind="ExternalInput").ap()
    msk_d = nc.dram_tensor("msk", [128, 320], F32, kind="ExternalInput").ap()
    i2_d = nc.dram_tensor("i2", [128, 64], F32, kind="ExternalInput").ap()
    obd_d = nc.dram_tensor("onesbd", [128, 128], F32, kind="ExternalInput").ap()
    ygT = nc.dram_tensor("ygT", [128, TL], BF16, kind="ExternalOutput").ap()
    if debug:
        dbg = nc.dram_tensor("dbg", [24, 128, 512], F32, kind="ExternalOutput").ap()

    es = ExitStack()
    sb = lambda name, shape, dt: es.enter_context(nc.sbuf_tensor(name, shape, dt))
    with es:
        ident_b = sb("ident_b", [128, 128], BF16)
        ident_f = sb("ident_f", [128, 128], F32)
        wf = sb("wf", [128, 8, WCOLS], F32)
        wc = sb("wc", [128, 8, WCOLS], BF16)
        wp = sb("wp", [128, 8, WCOLS], BF16)
        mu_sb = sb("mu_sb", [128, 8, 6], F32)
        w2a_b = sb("w2a_b", [65, 128], BF16)
        a2_b = sb("a2_b", [128, 128], BF16)
        g2a_b = sb("g2a_b", [128, 128], BF16)
        g2b_b = sb("g2b_b", [32, 128], BF16)
        vec = sb("vec", [128, 8], F32)
        tri3 = sb("tri3_s", [128, 192], F32)
        msk = sb("msk_s", [128, 320], F32)
        i2 = sb("i2_s", [128, 64], F32)
        obd_f = sb("obd_f", [128, 128], F32)
        obd_b = sb("obd_b", [128, 128], BF16)
        xb = [sb("xb%d" % i, [128, 8, TT + 1], BF16) for i in range(2)]
        r_f = sb("r_f", [128, TT], F32)
        k_f = sb("k_f", [128, TT], F32)
        v_f = sb("v_f", [128, TT], F32)
        v_b = sb("v_b", [128, TT], BF16)
        twa = sb("twa", [65, TT], BF16)
        a1o = sb("a1o", [128, TT], BF16)
        sg_a = sb("sg_a", [128, TT], BF16)
        sg_b = sb("sg_b", [32, TT], BF16)
        g_f = sb("g_f", [128, TT], F32)
        al_f = sb("al_f", [128, TT], F32)
        kkr = sb("kkr", [128, TT], F32)
        sq_b = sb("sq_b", [128, TT], BF16)
        rn = sb("rn", [128, TT], F32)
        kk_f = sb("kk_f", [128, TT], F32)
        kt_f = sb("kt_f", [128, TT], F32)
        b_f = sb("b_f", [128, TT], F32)
        tmp1 = sb("tmp1", [128, TT], F32)
        rk_b = sb("rk_b", [128, TT], BF16)
        bon_f = sb("bon_f", [128, TT], F32)
        sgw = sb("sgw", [128, NCH, 128], F32)
        e_pos = sb("e_pos", [128, NCH, 64], F32)
        e_neg = sb("e_neg", [128, NCH, 64], F32)
        e_ex = sb("e_ex", [128, NCH, 64], F32)
        e_rem = sb("e_rem", [128, NCH, 64], F32)
        AR = sb("AR", [128, NCH, 2, 64], BF16)
        Rh_f = sb("Rh_f", [128, TT], F32)
        BhT = sb("BhT", [128, TT], BF16)
        KhT = sb("KhT", [128, TT], BF16)
        BbT = sb("BbT", [128, TT], BF16)
        KbT = sb("KbT", [128, TT], BF16)
        TM = sb("TM", [128, NCH, 4, 64], BF16)
        GM = sb("GM", [128, NCH, 320], BF16)
        Xf = sb("Xf", [128, NCH, 128], F32)
        Xb = sb("Xb", [128, NCH, 128], BF16)
        PP = [sb("PP%d" % i, [128, NCH, 128], BF16) for i in range(2)]
        RtT = sb("RtT", [128, TT], BF16)
        Y0 = sb("Y0", [128, NCH, 64], F32)
        DG = sb("DG", [128, NCH, 64], F32)
        PT_f = sb("PT_f", [128, NCH, 64], F32)
        Q_f = sb("Q_f", [128, NCH, 64], F32)
        S_f = [sb("S_f%d" % i, [128, 64], F32) for i in range(2)]
        S_b = [sb("S_b%d" % i, [128, 64], BF16) for i in range(2)]
        y_tm = sb("y_tm", [128, NCH, 64], F32)
        yT_f = sb("yT_f", [128, TT], F32)
        d_f = sb("d_f", [128, TT], F32)
        sq_f = sb("sq_f", [128, TT], F32)
        rstd = sb("rstd", [128, TT], F32)
        yo = [sb("yo%d" % i, [128, TT], BF16) for i in range(2)]
        bank = [es.enter_context(nc.psum_tensor("bank%d" % i, [128, 512], F32)) for i in range(8)]
        bk = lambda i: "bank%d" % i

        p = Prog(nc)
        for idt, nm in ((ident_b, "ident_b"), (ident_f, "ident_f")):
            p.op("pool", lambda e, idt=idt: e.memset(idt[:], 0.0), w=[nm])
            p.op("pool", lambda e, idt=idt: e.affine_select(out=idt[:], in_=idt[:], pattern=[[-1, 128]],
                                                         compare_op=ALU.not_equal, fill=1.0, base=0,
                                                         channel_multiplier=1), r=[nm], w=[nm])
        p.op("pool", lambda e: e.memset(twa[64:65, :], 1.0), w=["twa1"])
        p.op("pool", lambda e: e.memset(S_f[0][:], 0.0), w=["S_f0"])
        p.op("pool", lambda e: e.memset(S_b[0][:], 0.0), w=["S_b0"])
        for kc in range(8):
            p.dma("sp", wf[:, kc, :], wbig[kc * 128:(kc + 1) * 128, :], w=["wf"], semkey="wf")
        p.dma("sp", mu_sb[:], mu6.rearrange("(k p) n -> p k n", p=128), w=["mu"])
        p.dma("pool", w2a_b[:], w2a, w=["w2a_b"])
        p.dma("pool", a2_b[:], a2p, w=["a2_b"])
        p.dma("pool", g2a_b[:], g2p[0:128, :], w=["g2a_b"])
        p.dma("pool", g2b_b[:], g2p[128:160, :], w=["g2b_b"])
        p.dma("pool", obd_b[:], obd_d, w=["obd_b"])
        p.dma("sp", obd_f[:], obd_d, w=["obd_f"])
        p.dma("sp", vec[:], vecs, w=["vec"])
        p.dma("sp", tri3[:], tri3_d, w=["tri3"])
        p.dma("sp", msk[:], msk_d, w=["msk"])
        p.dma("sp", i2[:], i2_d, w=["i2"])
        groups = [(0, 128, 0), (128, 256, 2), (256, 384, 3), (384, 448, 1), (448, 512, 4), (512, 672, 5)]
        for kc in range(8):
            for (c0, c1, n) in groups:
                eng = "dve" if (kc % 2 == 0) else "pool"
                p.op(eng, lambda e, kc=kc, c0=c0, c1=c1, n=n: e.tensor_scalar(
                    out=wp[:, kc, c0:c1], in0=wf[:, kc, c0:c1], scalar1=mu_sb[:, kc, n:n + 1], scalar2=None, op0=ALU.mult),
                    r=["wf", "mu"], w=["wp"])
            p.op("dve" if (kc % 2 == 0) else "pool",
                 lambda e, kc=kc: e.tensor_tensor(out=wc[:, kc, :], in0=wf[:, kc, :], in1=wp[:, kc, :], op=ALU.subtract),
                 r=["wf", "wp"], w=["wc"])

        V_KK, V_KA, V_1KA, V_RK, V_GG, V_GB, V_A0 = range(7)
        vcol = lambda i: vec[:, i:i + 1]
        tp = lambda h: (64 * h, 64 * h)
        hs = lambda h: slice(64 * h, 64 * h + 64)
        s_cur = 0

        def load_x(tj):
            for kc in range(8):
                p.dma("pool", xb[tj % 2][:, kc, :], xT[kc * 128:(kc + 1) * 128, tj * TT:tj * TT + TT + 1],
                      w=["xb%d" % (tj % 2)], semkey="xb%d" % (tj % 2))

        load_x(0)
        for ti in range(ntiles):
            t0 = ti * TT
            x_sb = xb[ti % 2]
            xk = "xb%d" % (ti % 2)
            if ti + 1 < ntiles:
                load_x(ti + 1)

            def proj(c0, c1, bi):
                m = c1 - c0
                for kc in range(8):
                    p.op("pe", lambda e, kc=kc, x_sb=x_sb: e.matmul(bank[bi][0:m, :], lhsT=wc[:, kc, c0:c1], rhs=x_sb[:, kc, 1:TT + 1],
                                                        start=(kc == 0), stop=False), r=[xk, "wc"], w=[bk(bi)])
                for kc in range(8):
                    p.op("pe", lambda e, kc=kc, x_sb=x_sb: e.matmul(bank[bi][0:m, :], lhsT=wp[:, kc, c0:c1], rhs=x_sb[:, kc, 0:TT],
                                                        start=False, stop=(kc == 7)), r=[xk, "wp"], w=[bk(bi)])

            proj(0, 128, 0)
            p.op("act", lambda e: e.activation(out=r_f[:], in_=bank[0][:], func=AF.Copy), r=[bk(0)], w=["r_f"])
            proj(128, 256, 1)
            p.op("act", lambda e: e.activation(out=k_f[:], in_=bank[1][:], func=AF.Copy), r=[bk(1)], w=["k_f"])
            proj(256, 384, 2)
            p.op("act", lambda e: e.activation(out=v_f[:], in_=bank[2][:], func=AF.Copy), r=[bk(2)], w=["v_f"])
            p.op("act", lambda e: e.activation(out=v_b[:], in_=bank[2][:], func=AF.Copy), r=[bk(2)], w=["v_b"])
            proj(384, 512, 3)
            p.op("act", lambda e: e.activation(out=twa[0:64, :], in_=bank[3][0:64, :], func=AF.Tanh), r=[bk(3)], w=["twa"])
            p.op("dve", lambda e: e.tensor_copy(out=a1o[64:128, :], in_=bank[3][64:128, :]), r=[bk(3)], w=["a1o"])
            proj(512, 640, 4)
            p.op("act", lambda e: e.activation(out=sg_a[:], in_=bank[4][:], func=AF.Sigmoid), r=[bk(4)], w=["sg_a"])
            proj(640, 672, 5)
            p.op("act", lambda e: e.activation(out=sg_b[:], in_=bank[5][0:32, :], func=AF.Sigmoid), r=[bk(5)], w=["sg_b"])
            p.op("pe", lambda e: e.matmul(bank[6][:], lhsT=g2a_b[:], rhs=sg_a[:], start=True, stop=False), r=["g2a_b", "sg_a"], w=[bk(6)])
            p.op("pe", lambda e: e.matmul(bank[6][:], lhsT=g2b_b[:], rhs=sg_b[:], start=False, stop=True), r=["g2b_b", "sg_b"], w=[bk(6)])
            p.op("act", lambda e: e.activation(out=g_f[:], in_=bank[6][:], func=AF.Copy), r=[bk(6)], w=["g_f"])
            p.op("pe", lambda e: e.matmul(bank[7][:], lhsT=a2_b[64:128, :], rhs=a1o[64:128, :], start=True, stop=True),
                 r=["a2_b", "a1o"], w=[bk(7)])
            p.op("act", lambda e: e.activation(out=al_f[:], in_=bank[7][:], func=AF.Sigmoid, bias=vcol(V_A0)), r=[bk(7), "vec"], w=["al_f"])
            p.op("dve", lambda e: e.tensor_scalar(out=kkr[:], in0=k_f[:], scalar1=vcol(V_KK), scalar2=None, op0=ALU.mult), r=["k_f", "vec"], w=["kkr"])
            p.op("act", lambda e: e.activation(out=sq_b[:], in_=kkr[:], func=AF.Square), r=["kkr"], w=["sq_b"])
            p.op("pe", lambda e: e.matmul(bank[0][:], lhsT=obd_b[:], rhs=sq_b[:], start=True, stop=True), r=["obd_b", "sq_b"], w=[bk(0)])
            p.op("dve", lambda e: e.tensor_scalar(out=rn[:], in0=bank[0][:], scalar1=1e-24, scalar2=None, op0=ALU.add), r=[bk(0)], w=["rn"])
            p.op("act", lambda e: e.activation(out=rn[:], in_=rn[:], func=AF.Sqrt), r=["rn"], w=["rn"])
            p.op("dve", lambda e: e.reciprocal(out=rn[:], in_=rn[:]), r=["rn"], w=["rn"])
            p.op("dve", lambda e: e.tensor_tensor(out=kk_f[:], in0=kkr[:], in1=rn[:], op=ALU.mult), r=["kkr", "rn"], w=["kk_f"])
            p.op("pool", lambda e: e.tensor_scalar(out=tmp1[:], in0=al_f[:], scalar1=-1.0, scalar2=vcol(V_KA), op0=ALU.add, op1=ALU.mult),
                 r=["al_f", "vec"], w=["tmp1"])
            p.op("dve", lambda e: e.scalar_tensor_tensor(out=kt_f[:], in0=tmp1[:], scalar=1.0, in1=k_f[:], op0=ALU.add, op1=ALU.mult),
                 r=["k_f", "tmp1"], w=["kt_f"])
            p.op("dve", lambda e: e.tensor_tensor(out=b_f[:], in0=kk_f[:], in1=al_f[:], op=ALU.mult), r=["kk_f", "al_f"], w=["b_f"])
            p.op("dve", lambda e: e.scalar_tensor_tensor(out=rk_b[:], in0=r_f[:], scalar=vcol(V_RK), in1=kt_f[:], op0=ALU.mult, op1=ALU.mult),
                 r=["r_f", "kt_f", "vec"], w=["rk_b"])
            p.op("pe", lambda e: e.matmul(bank[1][:], lhsT=obd_b[:], rhs=rk_b[:], start=True, stop=True), r=["obd_b", "rk_b"], w=[bk(1)])
            p.op("dve", lambda e: e.tensor_tensor(out=bon_f[:], in0=v_f[:], in1=bank[1][:], op=ALU.mult), r=["v_f", bk(1)], w=["bon_f"])

            for c in range(NCH):
                p.op("pe", lambda e, c=c: e.matmul(bank[2 + c // 4][0:64, (c % 4) * 128:(c % 4 + 1) * 128],
                                                   lhsT=twa[:, c * 64:(c + 1) * 64], rhs=w2a_b[:], start=True, stop=True),
                     r=["twa", "twa1", "w2a_b"], w=[bk(2 + c // 4)])
            for hf in range(2):
                p.op("act", lambda e, hf=hf: e.activation(out=sgw[0:64, hf * 4:(hf + 1) * 4, :],
                                                         in_=bank[2 + hf][0:64, :].rearrange("p (c d) -> p c d", c=4), func=AF.Sigmoid),
                     r=[bk(2 + hf)], w=["sgw"])
            for c in range(NCH):
                for kind in range(3):
                    p.op("pe", lambda e, c=c, kind=kind: e.matmul(bank[4 + kind][:, c * 64:(c + 1) * 64], lhsT=sgw[0:64, c, :],
                                                                 rhs=tri3[0:64, kind * 64:(kind + 1) * 64], start=True, stop=True),
                         r=["sgw", "tri3"], w=[bk(4 + kind)])
            v3 = lambda tl: tl[:].rearrange("p c t -> p (c t)")
            p.op("act", lambda e: e.activation(out=v3(e_pos), in_=bank[4][:], func=AF.Exp), r=[bk(4)], w=["e_pos"])
            p.op("act", lambda e: e.activation(out=v3(e_neg), in_=bank[4][:], func=AF.Exp, scale=-1.0), r=[bk(4)], w=["e_neg"])
            p.op("act", lambda e: e.activation(out=v3(e_ex), in_=bank[5][:], func=AF.Exp), r=[bk(5)], w=["e_ex"])
            p.op("act", lambda e: e.activation(out=v3(e_rem), in_=bank[6][:], func=AF.Exp), r=[bk(6)], w=["e_rem"])
            c3 = lambda ap2: ap2.rearrange("p (c t) -> p c t", c=NCH)
            p.op("dve", lambda e: e.scalar_tensor_tensor(out=AR[:, :, 0, :], in0=c3(kk_f[:]), scalar=-1.0, in1=e_ex[:], op0=ALU.mult, op1=ALU.mult),
                 r=["kk_f", "e_ex"], w=["AR"])
            p.op("dve", lambda e: e.tensor_tensor(out=Rh_f[:], in0=r_f[:], in1=v3(e_pos), op=ALU.mult), r=["r_f", "e_pos"], w=["Rh_f"])
            p.op("act", lambda e: e.activation(out=AR[:, :, 1, :], in_=c3(Rh_f[:]), func=AF.Copy), r=["Rh_f"], w=["AR"])
            p.op("dve", lambda e: e.tensor_tensor(out=BhT[:], in0=b_f[:], in1=v3(e_neg), op=ALU.mult), r=["b_f", "e_neg"], w=["BhT"])
            p.op("pool", lambda e: e.tensor_tensor(out=KhT[:], in0=kt_f[:], in1=v3(e_neg), op=ALU.mult), r=["kt_f", "e_neg"], w=["KhT"])
            p.op("dve", lambda e: e.tensor_tensor(out=BbT[:], in0=b_f[:], in1=v3(e_rem), op=ALU.mult), r=["b_f", "e_rem"], w=["BbT"])
            p.op("pool", lambda e: e.tensor_tensor(out=KbT[:], in0=kt_f[:], in1=v3(e_rem), op=ALU.mult), r=["kt_f", "e_rem"], w=["KbT"])

            tmb = [bank[0][:].bitcast(BF16), bank[1][:].bitcast(BF16)]
            srcs = [(lambda c: AR[:, c, 0, :], "AR"), (lambda c: v_b[:, c * 64:(c + 1) * 64], "v_b"),
                    (lambda c: BbT[:, c * 64:(c + 1) * 64], "BbT"), (lambda c: KbT[:, c * 64:(c + 1) * 64], "KbT")]
            for c in range(NCH):
                for si, (sf, sk) in enumerate(srcs):
                    for h in range(2):
                        col = (c % 4) * 256 + si * 64
                        p.op("pe", lambda e, c=c, sf=sf, h=h, col=col: e.transpose(
                            tmb[c // 4][hs(h), col:col + 64], sf(c)[hs(h), :], ident_b[hs(h), hs(h)], tile_position=tp(h)),
                            r=[sk, "ident_b"], w=[bk(c // 4)])
            for hf in range(2):
                p.op("act" if hf == 0 else "dve",
                     (lambda e, hf=hf: e.activation(out=TM[:, hf * 4:(hf + 1) * 4, :, :].rearrange("p c s t -> p (c s t)"), in_=tmb[hf], func=AF.Copy))
                     if hf == 0 else
                     (lambda e, hf=hf: e.tensor_copy(out=TM[:, hf * 4:(hf + 1) * 4, :, :].rearrange("p c s t -> p (c s t)"), in_=tmb[hf])),
                     r=[bk(hf)], w=["TM"])

            for c in range(NCH):
                bi = 2 + (c % 2)
                cs_ = slice(c * 64, (c + 1) * 64)
                for h in range(2):
                    arh = AR[hs(h), c, :, :].rearrange("p s t -> p (s t)")
                    p.op("pe", lambda e, h=h, arh=arh, cs_=cs_, bi=bi: e.matmul(bank[bi][hs(h), 0:128], lhsT=BhT[hs(h), cs_], rhs=arh,
                                                                           start=True, stop=True, tile_position=tp(h)),
                         r=["BhT", "AR"], w=[bk(bi)])
                    p.op("pe", lambda e, h=h, arh=arh, cs_=cs_, bi=bi: e.matmul(bank[bi][hs(h), 128:256], lhsT=KhT[hs(h), cs_], rhs=arh,
                                                                           start=True, stop=True, tile_position=tp(h)),
                         r=["KhT", "AR"], w=[bk(bi)])
                    p.op("pe", lambda e, h=h, c=c, cs_=cs_, bi=bi: e.matmul(bank[bi][hs(h), 256:320], lhsT=AR[hs(h), c, 0, :], rhs=BhT[hs(h), cs_],
                                                                       start=True, stop=True, tile_position=tp(h)),
                         r=["BhT", "AR"], w=[bk(bi)])
                p.op("dve", lambda e, c=c, bi=bi: e.tensor_tensor(out=GM[:, c, :], in0=bank[bi][:, 0:320], in1=msk[:], op=ALU.mult),
                     r=[bk(bi), "msk"], w=["GM"])

            for c in range(NCH):
                for h in range(2):
                    p.op("pe", lambda e, c=c, h=h: e.matmul(bank[4][hs(h), c * 64:(c + 1) * 64], lhsT=GM[hs(h), c, 128:192], rhs=TM[hs(h), c, 1, :],
                                                           start=True, stop=True, tile_position=tp(h)), r=["GM", "TM"], w=[bk(4)])
            p.op("pool", lambda e: e.tensor_copy(out=Xf[:, :, 0:64], in_=TM[:, :, 0, :]), r=["TM"], w=["Xf"])
            p.op("dve", lambda e: e.tensor_copy(out=Xf[:, :, 64:128], in_=bank[4][:].rearrange("p (c t) -> p c t", c=NCH)), r=[bk(4)], w=["Xf"])
            p.op("act", lambda e: e.activation(out=Xb[:], in_=Xf[:], func=AF.Copy), r=["Xf"], w=["Xb"])

            NLEV = 6
            for lev in range(NLEV):
                if lev == 0:
                    P_of = lambda c: GM[:, c, 256:320]
                    PT_of = lambda c: GM[:, c, 0:64]
                    pk = "GM"
                else:
                    ppt = PP[(lev - 1) % 2]
                    P_of = lambda c, ppt=ppt: ppt[:, c, 0:64]
                    PT_of = lambda c, ppt=ppt: ppt[:, c, 64:128]
                    pk = "PP%d" % ((lev - 1) % 2)
                for c in range(NCH):
                    bi = 0 + c // 4
                    for h in range(2):
                        p.op("pe", lambda e, c=c, h=h, bi=bi, PT_of=PT_of: e.matmul(
                            bank[bi][hs(h), (c % 4) * 128:(c % 4 + 1) * 128], lhsT=PT_of(c)[hs(h), :], rhs=Xb[hs(h), c, :],
                            start=True, stop=True, tile_position=tp(h)), r=[pk, "Xb"], w=[bk(bi)])
                if lev < NLEV - 1:
                    for c in range(NCH):
                        bi = 2 + c // 4
                        for h in range(2):
                            p.op("pe", lambda e, c=c, h=h, bi=bi, P_of=P_of, PT_of=PT_of: e.matmul(
                                bank[bi][hs(h), (c % 4) * 128:(c % 4) * 128 + 64], lhsT=PT_of(c)[hs(h), :], rhs=P_of(c)[hs(h), :],
                                start=True, stop=True, tile_position=tp(h)), r=[pk], w=[bk(bi)])
                            p.op("pe", lambda e, c=c, h=h, bi=bi, P_of=P_of, PT_of=PT_of: e.matmul(
                                bank[bi][hs(h), (c % 4) * 128 + 64:(c % 4 + 1) * 128], lhsT=P_of(c)[hs(h), :], rhs=PT_of(c)[hs(h), :],
                                start=True, stop=True, tile_position=tp(h)), r=[pk], w=[bk(bi)])
                for hf in range(2):
                    xs = Xf[:, hf * 4:(hf + 1) * 4, :].rearrange("p c t -> p (c t)")
                    p.op("dve", lambda e, hf=hf, xs=xs: e.tensor_tensor(out=xs, in0=xs, in1=bank[hf][:], op=ALU.add), r=["Xf", bk(hf)], w=["Xf"])
                p.op("act", lambda e: e.activation(out=Xb[:], in_=Xf[:], func=AF.Copy), r=["Xf"], w=["Xb"])
                if lev < NLEV - 1:
                    ppn = PP[lev % 2]
                    for hf in range(2):
                        p.op("act", lambda e, hf=hf, ppn=ppn: e.activation(out=ppn[:, hf * 4:(hf + 1) * 4, :].rearrange("p c t -> p (c t)"),
                                                                          in_=bank[2 + hf][:], func=AF.Copy),
                             r=[bk(2 + hf)], w=["PP%d" % (lev % 2)])

            for c in range(NCH):
                for h in range(2):
                    p.op("pe", lambda e, c=c, h=h: e.matmul(bank[4][hs(h), c * 64:(c + 1) * 64], lhsT=Xb[hs(h), c, 0:64], rhs=GM[hs(h), c, 64:128],
                                                           start=True, stop=True, tile_position=tp(h)), r=["Xb", "GM"], w=[bk(4)])
                    p.op("pe", lambda e, c=c, h=h: e.matmul(bank[5][hs(h), c * 64:(c + 1) * 64], lhsT=GM[hs(h), c, 64:128], rhs=Xb[hs(h), c, 64:128],
                                                           start=True, stop=False, tile_position=tp(h)), r=["Xb", "GM"], w=[bk(5)])
                    p.op("pe", lambda e, c=c, h=h: e.matmul(bank[5][hs(h), c * 64:(c + 1) * 64], lhsT=GM[hs(h), c, 192:256], rhs=TM[hs(h), c, 1, :],
                                                           start=False, stop=True, tile_position=tp(h)), r=["TM", "GM"], w=[bk(5)])
                    p.op("pe", lambda e, c=c, h=h: e.matmul(bank[6][hs(h), c * 64:(c + 1) * 64], lhsT=Xb[hs(h), c, 0:64], rhs=TM[hs(h), c, 2, :],
                                                           start=True, stop=True, tile_position=tp(h)), r=["Xb", "TM"], w=[bk(6)])
                    p.op("pe", lambda e, c=c, h=h: e.matmul(bank[7][hs(h), c * 64:(c + 1) * 64], lhsT=TM[hs(h), c, 2, :], rhs=Xb[hs(h), c, 64:128],
                                                           start=True, stop=False, tile_position=tp(h)), r=["Xb", "TM"], w=[bk(7)])
                    p.op("pe", lambda e, c=c, h=h: e.matmul(bank[7][hs(h), c * 64:(c + 1) * 64], lhsT=TM[hs(h), c, 3, :], rhs=TM[hs(h), c, 1, :],
                                                           start=False, stop=True, tile_position=tp(h)), r=["TM"], w=[bk(7)])
            p.op("dve", lambda e: e.tensor_tensor(out=RtT[:], in0=bank[4][:], in1=Rh_f[:], op=ALU.add), r=[bk(4), "Rh_f"], w=["RtT"])
            p.op("act", lambda e: e.activation(out=v3(Y0), in_=bank[5][:], func=AF.Copy), r=[bk(5)], w=["Y0"])
            p.op("pool", lambda e: e.tensor_tensor(out=DG[:], in0=i2[:].unsqueeze(1).to_broadcast([128, NCH, 64]),
                                                  in1=e_pos[:, :, 63:64].to_broadcast([128, NCH, 64]), op=ALU.mult), r=["i2", "e_pos"], w=["DG"])
            p.op("dve", lambda e: e.tensor_tensor(out=v3(PT_f), in0=bank[6][:], in1=v3(DG), op=ALU.add), r=[bk(6), "DG"], w=["PT_f"])
            p.op("act", lambda e: e.activation(out=v3(Q_f), in_=bank[7][:], func=AF.Copy), r=[bk(7)], w=["Q_f"])

            for c in range(NCH):
                sn = 1 - s_cur
                for h in range(2):
                    p.op("pe", lambda e, c=c, h=h, s_cur=s_cur: e.matmul(bank[1][hs(h), (c % 2) * 64:(c % 2 + 1) * 64], lhsT=PT_f[hs(h), c, :],
                                                                        rhs=S_f[s_cur][hs(h), :], start=True, stop=True, tile_position=tp(h)),
                         r=["PT_f", "S_f%d" % s_cur, bk(1)], w=[bk(1) + ("a" if c % 2 else "b")])
                p.op("dve", lambda e, c=c, sn=sn: e.tensor_tensor(out=S_f[sn][:], in0=bank[1][:, (c % 2) * 64:(c % 2 + 1) * 64], in1=Q_f[:, c, :], op=ALU.add),
                     r=[bk(1) + ("a" if c % 2 else "b"), bk(1), "Q_f"], w=["S_f%d" % sn])
                for h in range(2):
                    p.op("pe", lambda e, c=c, h=h, s_cur=s_cur: e.matmul(bank[0][hs(h), c * 64:(c + 1) * 64], lhsT=RtT[hs(h), c * 64:(c + 1) * 64],
                                                                        rhs=S_b[s_cur][hs(h), :], start=True, stop=True, tile_position=tp(h)),
                         r=["RtT", "S_b%d" % s_cur], w=[bk(0)])
                p.op("act", lambda e, sn=sn: e.activation(out=S_b[sn][:], in_=S_f[sn][:], func=AF.Copy), r=["S_f%d" % sn], w=["S_b%d" % sn])
                s_cur = sn
            p.op("dve", lambda e: e.tensor_tensor(out=v3(y_tm), in0=bank[0][:], in1=v3(Y0), op=ALU.add), r=[bk(0), "Y0"], w=["y_tm"])

            for c in range(NCH):
                for h in range(2):
                    p.op("pe", lambda e, c=c, h=h: e.matmul(bank[2][hs(h), c * 64:(c + 1) * 64], lhsT=y_tm[hs(h), c, :], rhs=ident_f[hs(h), hs(h)],
                                                           start=True, stop=True, tile_position=tp(h)), r=["y_tm", "ident_f"], w=[bk(2)])
            p.op("act", lambda e: e.activation(out=yT_f[:], in_=bank[2][:], func=AF.Copy), r=[bk(2)], w=["yT_f"])
            p.op("pe", lambda e: e.matmul(bank[3][:], lhsT=obd_f[:], rhs=yT_f[:], start=True, stop=True), r=["obd_f", "yT_f"], w=[bk(3)])
            p.op("dve", lambda e: e.scalar_tensor_tensor(out=d_f[:], in0=bank[3][:], scalar=-1.0 / 64, in1=yT_f[:], op0=ALU.mult, op1=ALU.add),
                 r=[bk(3), "yT_f"], w=["d_f"])
            p.op("act", lambda e: e.activation(out=sq_f[:], in_=d_f[:], func=AF.Square), r=["d_f"], w=["sq_f"])
            p.op("pe", lambda e: e.matmul(bank[4][:], lhsT=obd_f[:], rhs=sq_f[:], start=True, stop=True), r=["obd_f", "sq_f"], w=[bk(4)])
            p.op("dve", lambda e: e.tensor_scalar(out=rstd[:], in0=bank[4][:], scalar1=1.0 / 64, scalar2=GN_EPS, op0=ALU.mult, op1=ALU.add),
                 r=[bk(4)], w=["rstd"])
            p.op("act", lambda e: e.activation(out=rstd[:], in_=rstd[:], func=AF.Sqrt), r=["rstd"], w=["rstd"])
            p.op("dve", lambda e: e.reciprocal(out=rstd[:], in_=rstd[:]), r=["rstd"], w=["rstd"])
            p.op("dve", lambda e: e.tensor_tensor(out=d_f[:], in0=d_f[:], in1=rstd[:], op=ALU.mult), r=["d_f", "rstd"], w=["d_f"])
            p.op("act", lambda e: e.activation(out=d_f[:], in_=d_f[:], func=AF.Identity, scale=vcol(V_GG), bias=vcol(V_GB)),
                 r=["d_f", "vec"], w=["d_f"])
            p.op("pool", lambda e: e.tensor_tensor(out=d_f[:], in0=d_f[:], in1=bon_f[:], op=ALU.add), r=["d_f", "bon_f"], w=["d_f"])
            y_o = yo[ti % 2]
            yk = "yo%d" % (ti % 2)
            p.op("dve", lambda e, y_o=y_o: e.tensor_tensor(out=y_o[:], in0=d_f[:], in1=g_f[:], op=ALU.mult), r=["d_f", "g_f"], w=[yk])
            p.dma("sp", ygT[:, t0:t0 + TT], y_o[:], r=[yk], semkey="o_" + yk, store=True)
            if debug and ti == 0:
                f2 = lambda tl: tl[:].rearrange("p c t -> p (c t)")
                dl = [(r_f[:], "r_f"), (k_f[:], "k_f"), (v_f[:], "v_f"), (al_f[:], "al_f"), (g_f[:], "g_f"), (kk_f[:], "kk_f"),
                      (bon_f[:], "bon_f"), (f2(e_pos), "e_pos"), (f2(e_ex), "e_ex"), (f2(e_rem), "e_rem"), (Rh_f[:], "Rh_f"),
                      (yT_f[:], "yT_f"), (rstd[:], "rstd"), (d_f[:], "d_f"), (f2(Y0), "Y0"), (f2(PT_f), "PT_f"), (f2(Q_f), "Q_f"),
                      (f2(y_tm), "y_tm"), (Xf[:, 0:4, :].rearrange("p c t -> p (c t)"), "Xf"), (kt_f[:], "kt_f"), (b_f[:], "b_f"),
                      (f2(e_neg), "e_neg")]
                for i, (ap_, key) in enumerate(dl):
                    p.dma("sp", dbg[i], ap_, r=[key], semkey="dbgo", store=True)
        p.emit()
    return nc


def rwkv_inputs(inp, ntiles=T // TT, cores=range(NCORES)):
    TL = ntiles * TT
    x = inp["x"][0]
    xT = np.zeros((C, TL + 1), np.float32)
    xT[:, 1:] = x[:TL].T
    tri3, msk, i2, onesbd = rwkv_consts()
    mu6 = np.ascontiguousarray(inp["rwkv_mu"][0].T)
    maps = []
    for c in cores:
        cs = slice(c * 128, (c + 1) * 128)
        wrkv = inp["rwkv_w_rkv"][0]
        wbig = np.concatenate([wrkv[0][:, cs], wrkv[1][:, cs], wrkv[2][:, cs], inp["rwkv_w1"][0], inp["rwkv_a1"][0], inp["rwkv_g1"][0]], axis=1)
        w2a = np.concatenate([inp["rwkv_w2"][0][:, cs], inp["rwkv_w0"][0][None, cs]], axis=0)
        a2p = np.zeros((128, 128), np.float32)
        a2p[64:128] = inp["rwkv_a2"][0][:, cs]
        ka = inp["rwkv_k_a"][0][cs]
        one = np.ones_like(ka)
        vecs = np.stack([inp["rwkv_k_k"][0][cs], ka, one, inp["rwkv_r_k"][0].reshape(-1)[cs], inp["rwkv_gn_g"][0][cs],
                         inp["rwkv_gn_b"][0][cs], inp["rwkv_a0"][0][cs], one], axis=1).astype(np.float32)
        maps.append({
            "xT": xT, "wbig": np.ascontiguousarray(wbig), "mu6": mu6, "w2a": np.ascontiguousarray(w2a), "a2p": a2p,
            "g2p": np.ascontiguousarray(inp["rwkv_g2"][0][:, cs]), "vecs": np.ascontiguousarray(vecs),
            "tri3": tri3, "msk": msk, "i2": i2, "onesbd": onesbd,
        })
    return maps


def run_rwkv(inp):
    nc = _get("rwkv", lambda: build_rwkv(T // TT))
    maps = rwkv_inputs(inp)
    res = _run(nc, maps)
    return np.concatenate([r["ygT"] for r in res.results], axis=0)


def kernel(**inputs):
    inp = {k: np.asarray(v) for k, v in inputs.items()}
    x0 = np.ascontiguousarray(inp["x"][0], dtype=np.float32)
    ygT = run_rwkv(inp)
    lnp0 = np.stack([inp["ln_mix_g"][0], inp["ln_mix_b"][0], inp["ln_ffn_g"][0], inp["ln_ffn_b"][0]]).astype(np.float32)
    x1, qkv = run_post(ygT, x0, inp["rwkv_w_o"][0], inp["ffn_w_in"][0], inp["ffn_w_down"][0], lnp0,
                       inp["moba_w_qkv"][0], rope_tables())
    oT = run_attn(qkv)
    lnp1 = np.stack([inp["ln_mix_g"][1], inp["ln_mix_b"][1], inp["ln_ffn_g"][1], inp["ln_ffn_b"][1]]).astype(np.float32)
    out, _ = run_post(oT, x1, inp["moba_w_o"][0], inp["ffn_w_in"][1], inp["ffn_w_down"][1], lnp1)
    return out.reshape(1, T, C).astype(np.float32)
```

```python
import math
from contextlib import ExitStack

import numpy as np
import ml_dtypes

import concourse.bass as bass
import concourse.mybir as mybir
from concourse.bass_utils import run_bass_kernel_spmd

F32 = mybir.dt.float32
BF16 = mybir.dt.bfloat16
ALU = mybir.AluOpType
AF = mybir.ActivationFunctionType
AX = mybir.AxisListType

NCORES = 8
T = 16384
C = 1024
H = 16
DH = 64
DFF = 2816
DEPTH = 2
ALPHA = (2 * DEPTH) ** 0.25
LN_EPS = 1e-5
GN_EPS = 64 * 1e-5
NEG = -30000.0


class _Op:
    __slots__ = ("eng", "fn", "deps", "is_dma", "semkey", "sig", "need_sig", "idx")


class Prog:
    ENGS = ("pe", "act", "dve", "pool", "sp")

    def __init__(self, nc):
        self.nc = nc
        self.ops = []
        self.last_w = {}
        self.readers = {}
        self.store_keys = []

    def _deps(self, r, w):
        deps = set()
        for k in list(r) + list(w):
            lw = self.last_w.get(k)
            if lw is not None:
                deps.add(lw)
        for k in w:
            for rd in self.readers.get(k, ()):
                deps.add(rd)
        return deps

    def _commit(self, idx, r, w):
        for k in r:
            self.readers.setdefault(k, []).append(idx)
        for k in w:
            self.last_w[k] = idx
            self.readers[k] = []

    def op(self, eng, fn, r=(), w=()):
        o = _Op()
        o.eng, o.fn, o.is_dma, o.semkey, o.sig, o.need_sig = eng, fn, False, None, None, False
        o.deps = self._deps(r, w)
        o.idx = len(self.ops)
        self.ops.append(o)
        self._commit(o.idx, r, w)
        return o.idx

    def dma(self, q, out, in_, r=(), w=(), semkey=None, store=False):
        o = _Op()
        o.eng, o.is_dma, o.sig, o.need_sig = q, True, None, True
        o.fn = lambda e, out=out, in_=in_: e.dma_start(out=out, in_=in_)
        o.semkey = semkey if semkey is not None else (list(w)[0] if w else list(r)[0])
        o.deps = self._deps(r, w)
        o.idx = len(self.ops)
        self.ops.append(o)
        self._commit(o.idx, r, w)
        if store:
            self.store_keys.append(o.semkey)
        return o.idx

    def emit(self):
        nc = self.nc
        ops = self.ops
        for o in ops:
            for d in o.deps:
                od = ops[d]
                if od.is_dma:
                    continue
                if od.eng == o.eng and not o.is_dma and o.eng == "pe":
                    continue
                od.need_sig = True
        cnt = {e: 0 for e in self.ENGS}
        dcnt = {}
        for o in ops:
            if o.is_dma:
                dcnt[o.semkey] = dcnt.get(o.semkey, 0) + 16
                o.sig = ("d:" + o.semkey, dcnt[o.semkey])
            elif o.need_sig:
                cnt[o.eng] += 1
                o.sig = ("e:" + o.eng, cnt[o.eng])
        semnames = ["e:" + e for e in self.ENGS] + ["d:" + k for k in dcnt]
        with ExitStack() as es:
            sems = {}
            for i, n in enumerate(semnames):
                sems[n] = es.enter_context(nc.semaphore("s%d" % i))
            block = es.enter_context(nc.Block())
            per_eng = {e: [o for o in ops if o.eng == e] for e in self.ENGS}
            final_waits = [("d:" + k, dcnt[k]) for k in dict.fromkeys(self.store_keys)]

            def run(eng_name, e):
                waited = {}
                for o in per_eng[eng_name]:
                    need = {}
                    for d in o.deps:
                        od = ops[d]
                        if od.sig is None:
                            continue
                        if (not od.is_dma) and od.eng == eng_name and eng_name == "pe" and not o.is_dma:
                            continue
                        s, v = od.sig
                        if need.get(s, 0) < v:
                            need[s] = v
                    for s, v in need.items():
                        if waited.get(s, 0) < v:
                            e.wait_ge(sems[s], v)
                            waited[s] = v
                    ins = o.fn(e)
                    if o.sig is not None:
                        s, v = o.sig
                        ins.then_inc(sems[s], 16 if o.is_dma else 1)
                if eng_name == "sp":
                    for s, v in final_waits:
                        e.wait_ge(sems[s], v)

            @block.tensor
            def _(e):
                run("pe", e)

            @block.scalar
            def _(e):
                run("act", e)

            @block.vector
            def _(e):
                run("dve", e)

            @block.gpsimd
            def _(e):
                run("pool", e)

            @block.sync
            def _(e):
                run("sp", e)


def _bcast_rows(ap_1d, nparts):
    return ap_1d.partition_broadcast(nparts)


TOK = T // NCORES
GRP = 512
NGRP = TOK // GRP
NFB = DFF // 128


def build_post(with_qkv):
    nc = bass.Bass("TRN2", target_bir_lowering=False)
    aT = nc.dram_tensor("aT", [C, TOK], BF16, kind="ExternalInput").ap()
    xres = nc.dram_tensor("xres", [TOK, C], F32, kind="ExternalInput").ap()
    w_o = nc.dram_tensor("w_o", [C, C], F32, kind="ExternalInput").ap()
    w_in = nc.dram_tensor("w_in", [C, 2 * DFF], F32, kind="ExternalInput").ap()
    w_dn = nc.dram_tensor("w_dn", [DFF, C], F32, kind="ExternalInput").ap()
    lnp = nc.dram_tensor("lnp", [4, C], F32, kind="ExternalInput").ap()
    xout = nc.dram_tensor("xout", [TOK, C], F32, kind="ExternalOutput").ap()
    if with_qkv:
        w_qkv = nc.dram_tensor("w_qkv", [C, 3 * C], F32, kind="ExternalInput").ap()
        rope = nc.dram_tensor("rope", [TOK, 4, 32], F32, kind="ExternalInput").ap()
        qkv_out = nc.dram_tensor("qkv", [3, TOK, C], F32, kind="ExternalOutput").ap()

    NUNITS = NFB // 2 + (6 if with_qkv else 0)
    wscr = nc.dram_tensor("wscr", [NUNITS, 128, 8 * 512], BF16).ap()

    es = ExitStack()
    sb = lambda name, shape, dt: es.enter_context(nc.sbuf_tensor(name, shape, dt))
    ps = lambda name, shape, dt: es.enter_context(nc.psum_tensor(name, shape, dt))
    with es:
        ident = sb("ident", [128, 128], BF16)
        lnb = sb("lnb", [128, 4, C], F32)
        wdn_sb = sb("wdn_sb", [128, NFB, C], BF16)
        stg = [sb("stg%d" % i, [128, 4096], F32) for i in range(2)]
        win_sb = [sb("win%d" % i, [128, 8, 512], BF16) for i in range(2)]
        aT_sb = [sb("aT%d" % i, [128, 8, GRP], BF16) for i in range(2)]
        xr_sb = [sb("xr%d" % i, [128, C], F32) for i in range(2)]
        xg = sb("xg", [128, 4, C], F32)
        xb = sb("xb", [128, C], BF16)
        xT = sb("xT", [128, 8, GRP], BF16)
        actT = sb("actT", [128, NFB, GRP], BF16)
        sg = [sb("sg%d" % i, [128, GRP], F32) for i in range(2)]
        junk = sb("junk", [128, C], BF16)
        st = sb("st", [128, 8], F32)
        if with_qkv:
            rp_sb = sb("rp", [128, 4, 4, 32], F32)
            qo = [sb("qo%d" % i, [128, 512], F32) for i in range(2)]
            tmp = [sb("tmp%d" % i, [128, 8, 32], F32) for i in range(4)]
        acc = ps("acc", [128, C], F32)
        trp = ps("trp", [128, C], BF16)
        gu = [ps("gu%d" % i, [128, 2, GRP], F32) for i in range(2)]

        p = Prog(nc)
        p.op("pool", lambda e: e.memset(ident[:], 0.0), w=["ident"])
        p.op("pool", lambda e: e.affine_select(out=ident[:], in_=ident[:], pattern=[[-1, 128]],
                                               compare_op=ALU.not_equal, fill=1.0, base=0,
                                               channel_multiplier=1), r=["ident"], w=["ident"])
        p.dma("sp", lnb[:], lnp.partition_broadcast(128), w=["lnb"])

        stg_n = [0]

        def load_cast(parts, dst_ap, dst_keys):
            i = stg_n[0] % 2
            stg_n[0] += 1
            sk = "stg%d" % i
            for view_fn, src in parts:
                p.dma("sp", view_fn(stg[i]), src, w=[sk], semkey=sk)
            return i, sk

        def load_w8(dst, dkey, src_cols_list):
            i = stg_n[0] % 2
            stg_n[0] += 1
            sk = "stg%d" % i
            sv = stg[i][:].rearrange("p (k c) -> p k c", k=8)
            for src, c0 in src_cols_list:
                n = src.shape[1]
                p.dma("sp", sv[:, :, c0:c0 + n], src.rearrange("(k p) c -> p k c", p=128), w=[sk], semkey=sk)
            p.op("act", lambda e, dst=dst, sv=sv: e.activation(out=dst[:, 0:4, :], in_=sv[:, 0:4, :], func=AF.Copy), r=[sk], w=[dkey])
            p.op("dve", lambda e, dst=dst, sv=sv: e.tensor_copy(out=dst[:, 4:8, :], in_=sv[:, 4:8, :]), r=[sk], w=[dkey])

        def load_rows4(dst_ap, dkey, src_rows):
            i = stg_n[0] % 2
            stg_n[0] += 1
            sk = "stg%d" % i
            nblk = src_rows.shape[0] // 128
            sv = stg[i][:, 0:nblk * 1024].rearrange("p (k c) -> p k c", k=nblk)
            p.dma("sp", sv, src_rows.rearrange("(k p) c -> p k c", p=128), w=[sk], semkey=sk)
            h2 = max(1, nblk // 2)
            p.op("act", lambda e, dst_ap=dst_ap, sv=sv, h2=h2: e.activation(out=dst_ap[:, 0:h2, :], in_=sv[:, 0:h2, :], func=AF.Copy), r=[sk], w=[dkey])
            if nblk > h2:
                p.op("dve", lambda e, dst_ap=dst_ap, sv=sv, h2=h2: e.tensor_copy(out=dst_ap[:, h2:, :], in_=sv[:, h2:, :]), r=[sk], w=[dkey])

        for f4 in range(0, NFB, 4):
            n = min(4, NFB - f4)
            load_rows4(wdn_sb[:, f4:f4 + n, :], "wdn", w_dn[f4 * 128:(f4 + n) * 128, :])

        def layer_norm(tile_ap, gi, key):
            p.op("act", lambda e: e.activation(out=junk[:], in_=tile_ap, func=AF.Copy, accum_out=st[:, 0:1]),
                 r=[key], w=["junk", "st"])
            p.op("act", lambda e: e.activation(out=junk[:], in_=tile_ap, func=AF.Square, accum_out=st[:, 1:2]),
                 r=[key], w=["junk", "st"])
            p.op("dve", lambda e: e.tensor_scalar(out=st[:, 2:3], in0=st[:, 0:1], scalar1=1.0 / C, scalar2=None, op0=ALU.mult),
                 r=["st"], w=["st2"])
            p.op("dve", lambda e: e.tensor_tensor(out=st[:, 3:4], in0=st[:, 2:3], in1=st[:, 2:3], op=ALU.mult),
                 r=["st2"], w=["st3"])
            p.op("dve", lambda e: e.scalar_tensor_tensor(out=st[:, 4:5], in0=st[:, 1:2], scalar=1.0 / C, in1=st[:, 3:4],
                                                         op0=ALU.mult, op1=ALU.subtract), r=["st", "st3"], w=["st4"])
            p.op("dve", lambda e: e.tensor_scalar(out=st[:, 4:5], in0=st[:, 4:5], scalar1=LN_EPS, scalar2=None, op0=ALU.add),
                 r=["st4"], w=["st4"])
            p.op("act", lambda e: e.activation(out=st[:, 5:6], in_=st[:, 4:5], func=AF.Sqrt), r=["st4"], w=["st5"])
            p.op("dve", lambda e: e.reciprocal(out=st[:, 6:7], in_=st[:, 5:6]), r=["st5"], w=["st6"])
            p.op("dve", lambda e: e.tensor_scalar(out=tile_ap, in0=tile_ap, scalar1=st[:, 2:3], scalar2=st[:, 6:7],
                                                  op0=ALU.subtract, op1=ALU.mult), r=[key, "st2", "st6"], w=[key])
            p.op("pool", lambda e: e.tensor_tensor(out=tile_ap, in0=tile_ap, in1=lnb[:, gi, :], op=ALU.mult),
                 r=[key, "lnb"], w=[key])
            p.op("pool", lambda e: e.tensor_tensor(out=tile_ap, in0=tile_ap, in1=lnb[:, gi + 1, :], op=ALU.add),
                 r=[key, "lnb"], w=[key])

        def to_channel_major(tile_ap, key_in, ti):
            p.op("act", lambda e: e.activation(out=xb[:], in_=tile_ap, func=AF.Copy), r=[key_in], w=["xb"])
            for kc in range(8):
                p.op("pe", lambda e, kc=kc: e.transpose(trp[:, kc * 128:(kc + 1) * 128], xb[:, kc * 128:(kc + 1) * 128], ident[:]),
                     r=["xb", "ident"], w=["trp"])
            p.op("dve", lambda e: e.tensor_copy(out=xT[:, :, ti * 128:(ti + 1) * 128],
                                                in_=trp[:].rearrange("p (k t) -> p k t", k=8)),
                 r=["trp"], w=["xT"])

        def load_acts(g):
            t0 = g * GRP
            a_sb = aT_sb[g % 2]
            ak = "aT%d" % (g % 2)
            p.dma("sp", a_sb[:], aT[:, t0:t0 + GRP].rearrange("(k p) t -> p k t", p=128), w=[ak], semkey=ak)

        load_acts(0)
        for g in range(NGRP):
            t0 = g * GRP
            a_sb = aT_sb[g % 2]
            ak = "aT%d" % (g % 2)
            if with_qkv:
                p.dma("sp", rp_sb[:], rope[t0:t0 + GRP].rearrange("(i p) f c -> p i f c", p=128), w=["rp"])
            wo_v = [win_sb[h][:].rearrange("p k c -> p (k c)").rearrange("p (k c) -> p k c", k=4) for h in range(2)]
            for h in range(2):
                load_rows4(wo_v[h], "win%d" % h, w_o[h * 512:(h + 1) * 512, :])
            def phaseA_tail(ti):
                layer_norm(xg[:, ti, :], 0, "xg%d" % ti)
                to_channel_major(xg[:, ti, :], "xg%d" % ti, ti)

            for ti in range(4):
                r0 = t0 + ti * 128
                xr = xr_sb[(g * 4 + ti) % 2]
                xk = "xr%d" % ((g * 4 + ti) % 2)
                p.dma("sp", xr[:], xres[r0:r0 + 128, :], w=[xk], semkey=xk)
                for hf in range(2):
                    for kc in range(8):
                        p.op("pe", lambda e, kc=kc, hf=hf, ti=ti, a_sb=a_sb, wv=wo_v[kc // 4]: e.matmul(
                            acc[:, hf * 512:(hf + 1) * 512], lhsT=a_sb[:, kc, ti * 128:(ti + 1) * 128],
                            rhs=wv[:, kc % 4, hf * 512:(hf + 1) * 512], start=(kc == 0), stop=(kc == 7)),
                            r=[ak, "win%d" % (kc // 4)], w=["acc"])
                xk_out = "xg%d" % ti
                p.op("dve", lambda e, xr=xr, ti=ti: e.scalar_tensor_tensor(out=xg[:, ti, :], in0=xr[:], scalar=ALPHA, in1=acc[:],
                                                                           op0=ALU.mult, op1=ALU.add),
                     r=[xk, "acc"], w=[xk_out])
                if ti > 0:
                    phaseA_tail(ti - 1)
            phaseA_tail(3)
            if g + 1 < NGRP:
                load_acts(g + 1)
            NU = NFB // 2
            for u in range(NU):
                wsb = win_sb[u % 2]
                wk = "win%d" % (u % 2)
                if g == 0:
                    load_w8(wsb, wk, [(w_in[:, u * 256:(u + 1) * 256], 0), (w_in[:, DFF + u * 256:DFF + (u + 1) * 256], 256)])
                    p.dma("act", wscr[u], wsb[:].rearrange("p k c -> p (k c)"), r=[wk], w=["wscr%d" % u], semkey="wscr%d" % u)
                else:
                    p.dma("sp", wsb[:].rearrange("p k c -> p (k c)"), wscr[u], r=["wscr%d" % u], w=[wk], semkey=wk)
                for j in range(2):
                    fb = u * 2 + j
                    gps = gu[fb % 2]
                    gk = "gu%d" % (fb % 2)
                    for which in range(2):
                        for kc in range(8):
                            p.op("pe", lambda e, kc=kc, which=which, j=j, wsb=wsb, gps=gps: e.matmul(
                                gps[:, which, :], lhsT=wsb[:, kc, which * 256 + j * 128: which * 256 + (j + 1) * 128],
                                rhs=xT[:, kc, :], start=(kc == 0), stop=(kc == 7)),
                                r=[wk, "xT"], w=[gk])
                    s_sb = sg[fb % 2]
                    sk = "sg%d" % (fb % 2)
                    p.op("act", lambda e, gps=gps, s_sb=s_sb: e.activation(out=s_sb[:], in_=gps[:, 0, :], func=AF.Silu),
                         r=[gk], w=[sk])
                    p.op("dve", lambda e, gps=gps, s_sb=s_sb, fb=fb: e.tensor_tensor(out=actT[:, fb, :], in0=s_sb[:], in1=gps[:, 1, :], op=ALU.mult),
                         r=[gk, sk], w=["actT"])
            def down_tail(ti):
                r0 = t0 + ti * 128
                xk_out = "xg%d" % ti
                layer_norm(xg[:, ti, :], 2, xk_out)
                p.dma("act", xout[r0:r0 + 128, :], xg[:, ti, :], r=[xk_out], semkey="o_" + xk_out, store=True)
                if with_qkv:
                    to_channel_major(xg[:, ti, :], xk_out, ti)

            for ti in range(4):
                for hf in range(2):
                    for fb in range(NFB):
                        p.op("pe", lambda e, fb=fb, hf=hf, ti=ti: e.matmul(
                            acc[:, hf * 512:(hf + 1) * 512], lhsT=actT[:, fb, ti * 128:(ti + 1) * 128],
                            rhs=wdn_sb[:, fb, hf * 512:(hf + 1) * 512], start=(fb == 0), stop=(fb == NFB - 1)),
                            r=["actT", "wdn"], w=["acc"])
                xk_out = "xg%d" % ti
                p.op("dve", lambda e, ti=ti: e.scalar_tensor_tensor(out=xg[:, ti, :], in0=xg[:, ti, :], scalar=ALPHA, in1=acc[:],
                                                                    op0=ALU.mult, op1=ALU.add),
                     r=[xk_out, "acc"], w=[xk_out])
                if ti > 0:
                    down_tail(ti - 1)
            down_tail(3)
            if with_qkv:
                for cb in range(6):
                    wsb = win_sb[cb % 2]
                    wk = "win%d" % (cb % 2)
                    uq = NFB // 2 + cb
                    if g == 0:
                        load_w8(wsb, wk, [(w_qkv[:, cb * 512:(cb + 1) * 512], 0)])
                        p.dma("act", wscr[uq], wsb[:].rearrange("p k c -> p (k c)"), r=[wk], w=["wscr%d" % uq], semkey="wscr%d" % uq)
                    else:
                        p.dma("sp", wsb[:].rearrange("p k c -> p (k c)"), wscr[uq], r=["wscr%d" % uq], w=[wk], semkey=wk)
                    for ti in range(4):
                        r0 = t0 + ti * 128
                        gps = gu[(cb * 4 + ti) % 2]
                        gk = "gu%d" % ((cb * 4 + ti) % 2)
                        for kc in range(8):
                            p.op("pe", lambda e, kc=kc, ti=ti, wsb=wsb, gps=gps: e.matmul(
                                gps[:, 0, :], lhsT=xT[:, kc, ti * 128:(ti + 1) * 128], rhs=wsb[:, kc, :],
                                start=(kc == 0), stop=(kc == 7)), r=[wk, "xT"], w=[gk])
                        o_sb = qo[(cb * 4 + ti) % 2]
                        ok = "qo%d" % ((cb * 4 + ti) % 2)
                        which = cb // 2
                        if which == 2:
                            p.op("act", lambda e, gps=gps, o_sb=o_sb: e.activation(out=o_sb[:], in_=gps[:, 0, :], func=AF.Copy),
                                 r=[gk], w=[ok])
                        else:
                            src = gps[:, 0, :].rearrange("p (h d) -> p h d", h=8)
                            dst = o_sb[:].rearrange("p (h d) -> p h d", h=8)
                            cos = rp_sb[:, ti, 2 * which, :].unsqueeze(1).to_broadcast([128, 8, 32])
                            sin = rp_sb[:, ti, 2 * which + 1, :].unsqueeze(1).to_broadcast([128, 8, 32])
                            tk = ["tmp%d" % i for i in range(4)]
                            p.op("dve", lambda e, src=src, cos=cos: e.tensor_tensor(out=tmp[0][:], in0=src[:, :, 0:32], in1=cos, op=ALU.mult),
                                 r=[gk, "rp"], w=[tk[0]])
                            p.op("dve", lambda e, src=src, sin=sin: e.tensor_tensor(out=tmp[1][:], in0=src[:, :, 32:64], in1=sin, op=ALU.mult),
                                 r=[gk, "rp"], w=[tk[1]])
                            p.op("dve", lambda e, src=src, cos=cos: e.tensor_tensor(out=tmp[2][:], in0=src[:, :, 32:64], in1=cos, op=ALU.mult),
                                 r=[gk, "rp"], w=[tk[2]])
                            p.op("dve", lambda e, src=src, sin=sin: e.tensor_tensor(out=tmp[3][:], in0=src[:, :, 0:32], in1=sin, op=ALU.mult),
                                 r=[gk, "rp"], w=[tk[3]])
                            p.op("pool", lambda e, dst=dst: e.tensor_tensor(out=dst[:, :, 0:32], in0=tmp[0][:], in1=tmp[1][:], op=ALU.subtract),
                                 r=[tk[0], tk[1]], w=[ok])
                            p.op("pool", lambda e, dst=dst: e.tensor_tensor(out=dst[:, :, 32:64], in0=tmp[2][:], in1=tmp[3][:], op=ALU.add),
                                 r=[tk[2], tk[3]], w=[ok])
                        p.dma("act", qkv_out[which, r0:r0 + 128, (cb % 2) * 512:(cb % 2 + 1) * 512], o_sb[:], r=[ok],
                              semkey="o_" + ok, store=True)
        p.emit()
    return nc


_NC_CACHE = {}


def _get(name, builder):
    if name not in _NC_CACHE:
        import time as _t
        t0 = _t.time()
        _NC_CACHE[name] = builder()
        print("[kernel] built", name, "in %.1fs" % (_t.time() - t0), flush=True)
    return _NC_CACHE[name]


def _run(nc, in_maps):
    return run_bass_kernel_spmd(nc, in_maps, core_ids=list(range(NCORES)))


def run_post(aT_full, xres_full, w_o, w_in, w_dn, lnp, w_qkv=None, rope=None):
    with_qkv = w_qkv is not None
    nc = _get("post_qkv" if with_qkv else "post", lambda: build_post(with_qkv))
    in_maps = []
    for c in range(NCORES):
        m = {
            "aT": np.ascontiguousarray(aT_full[:, c * TOK:(c + 1) * TOK]),
            "xres": np.ascontiguousarray(xres_full[c * TOK:(c + 1) * TOK]),
            "w_o": w_o, "w_in": w_in, "w_dn": w_dn, "lnp": lnp,
        }
        if with_qkv:
            m["w_qkv"] = w_qkv
            m["rope"] = np.ascontiguousarray(rope[c * TOK:(c + 1) * TOK])
        in_maps.append(m)
    res = _run(nc, in_maps)
    xo = np.concatenate([r["xout"] for r in res.results], axis=0)
    if with_qkv:
        qkv = np.concatenate([r["qkv"] for r in res.results], axis=1)
        return xo, qkv
    return xo, None


def rope_tables():
    inv = (10000.0 ** (-np.arange(0, DH, 2, dtype=np.float32) / DH)).astype(np.float32)
    ang = (np.arange(T, dtype=np.float32)[:, None] * inv[None, :]).astype(np.float32)
    cos, sin = np.cos(ang).astype(np.float32), np.sin(ang).astype(np.float32)
    s = np.float32(DH ** -0.5)
    return np.ascontiguousarray(np.stack([cos * s, sin * s, cos, sin], axis=1))


NBLK = T // 256
QG = 512
NQG = T // QG


def moba_consts():
    blk1h = np.zeros((64, T), np.float32)
    for b in range(NBLK):
        blk1h[b, b * 256:(b + 1) * 256] = 1.0
    n = np.arange(64)[:, None]
    b = np.arange(64)[None, :]
    p01 = (b < n).astype(np.float32)
    o01 = (b == n).astype(np.float32)
    pbias = np.where(b < n, 0.0, -1e9).astype(np.float32)
    tabs = np.stack([pbias, p01, o01], axis=0)
    dm = np.zeros((4, 128, 4, 128), np.float32)
    key = np.arange(128)[:, None]
    q = np.arange(128)[None, :]
    tri = np.where(key <= q, 0.0, NEG)
    for j in range(4):
        for g in range(4):
            if j > g:
                dm[j, :, g, :] = NEG
            elif j == g:
                dm[j, :, g, :] = tri
    return (blk1h.astype(ml_dtypes.bfloat16), tabs, dm.reshape(4, 128, 512).astype(ml_dtypes.bfloat16))


def build_attn():
    nc = bass.Bass("TRN2", target_bir_lowering=False)
    qT = nc.dram_tensor("qT", [128, T], F32, kind="ExternalInput").ap()
    kT = nc.dram_tensor("kT", [128, T], F32, kind="ExternalInput").ap()
    v = nc.dram_tensor("v", [T, 128], F32, kind="ExternalInput").ap()
    blk1h = nc.dram_tensor("blk1h", [64, T], BF16, kind="ExternalInput").ap()
    tabs = nc.dram_tensor("tabs", [3, 64 * 64], F32, kind="ExternalInput").ap()
    dmask = nc.dram_tensor("dmask", [4, 128, 512], BF16, kind="ExternalInput").ap()
    oT = nc.dram_tensor("oT", [128, T], BF16, kind="ExternalOutput").ap()

    es = ExitStack()
    sb = lambda name, shape, dt: es.enter_context(nc.sbuf_tensor(name, shape, dt))
    ps = lambda name, shape, dt: es.enter_context(nc.psum_tensor(name, shape, dt))
    with es:
        ident = sb("ident", [128, 128], BF16)
        ones_f = sb("ones_f", [128, 64], F32)
        kaug = sb("kaug", [128, T], BF16)
        qaug = sb("qaug", [128, T], BF16)
        vsb = sb("vsb", [128, 128, 2, 65], BF16)
        tb = sb("tb", [128, 3, 64 * 64], F32)
        stg = [sb("stg%d" % i, [128, 2048], F32) for i in range(2)]
        dm = sb("dm", [128, 4, 512], BF16)
        kmean = sb("kmean", [64, 64], F32)
        kmean_b = sb("kmean_b", [64, 64], BF16)
        gm = [sb("gm%d" % i, [128, 8, 64], F32) for i in range(2)]
        top8 = [sb("top8%d" % i, [128, 8, 8], F32) for i in range(2)]
        selt = [sb("selt%d" % i, [128, 8, 64], F32) for i in range(2)]
        negm = [sb("negm%d" % i, [128, 8, 64], BF16) for i in range(2)]
        pT = [sb("pT%d" % i, [128, QG], BF16) for i in range(3)]
        osb = [sb("osb%d" % i, [65, QG], F32) for i in range(2)]
        obf = [sb("obf%d" % i, [64, QG], BF16) for i in range(2)]
        s_ps = [ps("s_ps%d" % i, [128, QG], F32) for i in range(3)]
        o_ps = [ps("o_ps%d" % i, [65, QG], F32) for i in range(2)]
        g_ps = ps("g_ps", [128, 8, 64], F32)
        m_ps = ps("m_ps", [128, 8 * 128], BF16)
        bc_ps = m_ps[:].bitcast(F32)

        p = Prog(nc)
        p.op("pool", lambda e: e.memset(ident[:], 0.0), w=["ident"])
        p.op("pool", lambda e: e.affine_select(out=ident[:], in_=ident[:], pattern=[[-1, 128]],
                                               compare_op=ALU.not_equal, fill=1.0, base=0,
                                               channel_multiplier=1), r=["ident"], w=["ident"])
        p.op("pool", lambda e: e.memset(ones_f[:], 1.0), w=["ones_f"])
        p.op("pool", lambda e: e.memset(vsb[:, :, :, 64:65], 1.0), w=["vsb1"])
        p.dma("sp", tb[:], tabs.partition_broadcast(128), w=["tb"])
        p.dma("sp", dm[:], dmask.rearrange("j k q -> k j q"), w=["dm"])
        p.dma("sp", kaug[64:128, :], blk1h, w=["kaug_hi"])
        stg_n = [0]

        def stage(view_fn, src, cast_fn, wkeys):
            i = stg_n[0] % 2
            stg_n[0] += 1
            sk = "stg%d" % i
            sv = view_fn(stg[i])
            p.dma("sp", sv, src, w=[sk], semkey=sk)
            eng = "act" if (stg_n[0] % 2) else "dve"
            p.op(eng, lambda e, sv=sv, eng=eng: cast_fn(e, sv, eng == "act"), r=[sk], w=wkeys)

        gcount = 0
        for hh in range(2):
            for c8 in range(8):
                sl = slice(c8 * 2048, (c8 + 1) * 2048)
                for dst, src, key in ((kaug, kT, "kaug_lo"), (qaug, qT, "qaug_lo")):
                    stage(lambda t_: t_[0:64, :], src[hh * 64:(hh + 1) * 64, sl],
                          lambda e, sv, is_act, dst=dst, sl=sl: (e.activation(out=dst[0:64, sl], in_=sv, func=AF.Copy) if is_act
                                                        else e.tensor_copy(out=dst[0:64, sl], in_=sv)),
                          [key])
            if hh == 0:
                for c8 in range(8):
                    kb0 = c8 * 16
                    stage(lambda t_: t_[:].rearrange("p (kb c) -> p kb c", c=128),
                          v[kb0 * 128:(kb0 + 16) * 128, :].rearrange("(kb p) c -> p kb c", p=128),
                          lambda e, sv, is_act, kb0=kb0: (e.activation(out=vsb[:, kb0:kb0 + 16, :, 0:64], in_=sv.rearrange("p kb (h d) -> p kb h d", h=2), func=AF.Copy)
                                                  if is_act else
                                                  e.tensor_copy(out=vsb[:, kb0:kb0 + 16, :, 0:64], in_=sv.rearrange("p kb (h d) -> p kb h d", h=2))),
                          ["vsb"])

            p.op("dve", lambda e: e.tensor_reduce(out=kmean[:], in_=kaug[0:64, :].rearrange("p (b s) -> p b s", s=256),
                                                  axis=AX.X, op=ALU.add), r=["kaug_lo"], w=["kmean"])
            p.op("dve", lambda e: e.tensor_scalar(out=kmean_b[:], in0=kmean[:], scalar1=1.0 / 256, scalar2=None, op0=ALU.mult),
                 r=["kmean"], w=["kmean_b"])
            for G8 in range(T // 1024):
                i2 = gcount % 2
                gcount += 1
                n0 = 4 * G8
                for c in range(8):
                    q0 = G8 * 1024 + c * 128
                    p.op("pe", lambda e, c=c, q0=q0: e.matmul(g_ps[:, c, :], lhsT=qaug[0:64, q0:q0 + 128], rhs=kmean_b[:],
                                                            start=True, stop=True), r=["qaug_lo", "kmean_b"], w=["g_ps"])
                gmk, t8k, slk, ngk = "gm%d" % i2, "top8%d" % i2, "selt%d" % i2, "negm%d" % i2
                tbv = tb[:].rearrange("p t (n b) -> p t n b", b=64)
                b0, b1, b2 = [tbv[:, t, n0:n0 + 4, :].unsqueeze(2).to_broadcast([128, 4, 2, 64]) for t in range(3)]
                g4 = lambda tl: tl[:].rearrange("p (a c) b -> p a c b", c=2)
                gm4, sl4, gp4 = g4(gm[i2]), g4(selt[i2]), g_ps[:].rearrange("p (a c) b -> p a c b", c=2)
                p.op("dve", lambda e, gm4=gm4, gp4=gp4, b0=b0: e.tensor_tensor(out=gm4, in0=gp4, in1=b0, op=ALU.add),
                     r=["g_ps", "tb"], w=[gmk])
                for c in range(8):
                    p.op("dve", lambda e, c=c, i2=i2: e.max(out=top8[i2][:, c, :], in_=gm[i2][:, c, :]), r=[gmk], w=[t8k])
                p.op("dve", lambda e, i2=i2: e.tensor_tensor(out=selt[i2][:], in0=gm[i2][:],
                                                            in1=top8[i2][:, :, 2:3].to_broadcast([128, 8, 64]), op=ALU.is_ge),
                     r=[gmk, t8k], w=[slk])
                p.op("dve", lambda e, sl4=sl4, b1=b1: e.tensor_tensor(out=sl4, in0=sl4, in1=b1, op=ALU.mult),
                     r=[slk, "tb"], w=[slk])
                p.op("dve", lambda e, sl4=sl4, b2=b2: e.tensor_tensor(out=sl4, in0=sl4, in1=b2, op=ALU.add),
                     r=[slk, "tb"], w=[slk])
                p.op("dve", lambda e, i2=i2: e.tensor_scalar(out=negm[i2][:], in0=selt[i2][:], scalar1=-1.0, scalar2=-NEG,
                                                             op0=ALU.add, op1=ALU.mult), r=[slk], w=[ngk])
                for c in range(8):
                    p.op("pe", lambda e, c=c, i2=i2: e.transpose(m_ps[64:128, c * 128:(c + 1) * 128], negm[i2][:, c, :], ident[:],
                                                                tile_position=(0, 64)), r=[ngk, "ident"], w=["m_ps"])
                p.op("act", lambda e, G8=G8: e.activation(out=qaug[64:128, G8 * 1024:(G8 + 1) * 1024], in_=m_ps[64:128, :], func=AF.Copy),
                     r=["m_ps"], w=["qaug_hi"])
            for G in range(NQG):
                nkb = 4 * (G + 1)
                op_i = G % 2
                opk = "o_ps%d" % op_i
                qsl = slice(G * QG, (G + 1) * QG)

                def qk(kb, G=G, qsl=qsl):
                    si = kb % 3
                    diag = kb >= 4 * G
                    p.op("pe", lambda e: e.matmul(s_ps[si][:], lhsT=kaug[:, kb * 128:(kb + 1) * 128], rhs=qaug[:, qsl],
                                                  start=True, stop=not diag),
                         r=["kaug_lo", "kaug_hi", "qaug_lo", "qaug_hi"], w=["s_ps%d" % si])
                    if diag:
                        j = kb - 4 * G
                        p.op("pe", lambda e: e.matmul(s_ps[si][:], lhsT=ident[:], rhs=dm[:, j, :], start=False, stop=True),
                             r=["ident", "dm"], w=["s_ps%d" % si])

                def ex_pv(kb, G=G, nkb=nkb, op_i=op_i, opk=opk, hh=hh):
                    si = kb % 3
                    p.op("act", lambda e: e.activation(out=pT[si][:], in_=s_ps[si][:], func=AF.Exp),
                         r=["s_ps%d" % si], w=["pT%d" % si])
                    p.op("pe", lambda e: e.matmul(o_ps[op_i][:], lhsT=vsb[:, kb, hh, :], rhs=pT[si][:],
                                                  start=(kb == 0), stop=(kb == nkb - 1)),
                         r=["vsb", "vsb1", "pT%d" % si], w=[opk])

                LOOK = 2
                for kb in range(min(LOOK, nkb)):
                    qk(kb)
                for kb in range(nkb):
                    if kb + LOOK < nkb:
                        qk(kb + LOOK)
                    ex_pv(kb)
                ob_i = G % 2
                p.op("dve", lambda e, op_i=op_i, ob_i=ob_i: e.tensor_copy(out=osb[ob_i][:], in_=o_ps[op_i][:]), r=[opk], w=["osb%d" % ob_i])
                p.op("dve", lambda e, ob_i=ob_i: e.reciprocal(out=osb[ob_i][64:65, :], in_=osb[ob_i][64:65, :]),
                     r=["osb%d" % ob_i], w=["osb%d" % ob_i])
                p.op("pe", lambda e, ob_i=ob_i: e.matmul(bc_ps[0:64, :], lhsT=ones_f[64:65, :], rhs=osb[ob_i][64:65, :], start=True, stop=True),
                     r=["ones_f", "osb%d" % ob_i], w=["m_ps"])
                p.op("dve", lambda e, ob_i=ob_i: e.tensor_tensor(out=obf[ob_i][:], in0=osb[ob_i][0:64, :], in1=bc_ps[0:64, :], op=ALU.mult),
                     r=["osb%d" % ob_i, "m_ps"], w=["obf%d" % ob_i])
                p.dma("sp", oT[hh * 64:(hh + 1) * 64, qsl], obf[ob_i][:], r=["obf%d" % ob_i], semkey="o_obf%d" % ob_i, store=True)
        p.emit()
    return nc


def run_attn(qkv):
    nc = _get("attn", build_attn)
    blk1h, tabs, dm = moba_consts()
    tabs = np.ascontiguousarray(tabs.reshape(3, 64 * 64))
    in_maps = []
    for c in range(NCORES):
        cs = slice(c * 128, (c + 1) * 128)
        in_maps.append({
            "qT": np.ascontiguousarray(qkv[0][:, cs].T),
            "kT": np.ascontiguousarray(qkv[1][:, cs].T),
            "v": np.ascontiguousarray(qkv[2][:, cs]),
            "blk1h": blk1h, "tabs": tabs, "dmask": dm,
        })
    res = _run(nc, in_maps)
    return np.concatenate([r["oT"] for r in res.results], axis=0)


LCH = 64
TT = 512
NCH = TT // LCH
WCOLS = 672
DECAY_C = -math.exp(-0.5)


def rwkv_consts():
    j = np.arange(64)[:, None]
    t = np.arange(64)[None, :]
    incl = (j <= t).astype(np.float32)
    strict = (j < t).astype(np.float32)
    rev = (j > t).astype(np.float32)
    tri3 = (DECAY_C * np.concatenate([incl, strict, rev], axis=1)).astype(np.float32)
    tri3 = np.concatenate([tri3, tri3], axis=0)
    up_s = (j < t).astype(np.float32)
    up_i = (j <= t).astype(np.float32)
    lo_s = (t < j).astype(np.float32)
    msk = np.concatenate([up_s, up_i, up_s, up_i, lo_s], axis=1)
    msk = np.concatenate([msk, msk], axis=0)
    i2 = np.concatenate([np.eye(64, dtype=np.float32)] * 2, axis=0)
    onesbd = np.zeros((128, 128), np.float32)
    onesbd[:64, :64] = 1.0
    onesbd[64:, 64:] = 1.0
    return tri3, msk, i2, onesbd


def build_rwkv(ntiles, debug=False):
    TL = ntiles * TT
    nc = bass.Bass("TRN2", target_bir_lowering=False)
    xT = nc.dram_tensor("xT", [C, TL + 1], F32, kind="ExternalInput").ap()
    wbig = nc.dram_tensor("wbig", [C, WCOLS], F32, kind="ExternalInput").ap()
    mu6 = nc.dram_tensor("mu6", [C, 6], F32, kind="ExternalInput").ap()
    w2a = nc.dram_tensor("w2a", [65, 128], F32, kind="ExternalInput").ap()
    a2p = nc.dram_tensor("a2p", [128, 128], F32, kind="ExternalInput").ap()
    g2p = nc.dram_tensor("g2p", [160, 128], F32, kind="ExternalInput").ap()
    vecs = nc.dram_tensor("vecs", [128, 8], F32, kind="ExternalInput").ap()
    tri3_d = nc.dram_tensor("tri3", [128, 192], F32, kind="ExternalInput").ap()
    msk_d = nc.dram_tensor("msk", [128, 320], F32, kind="ExternalInput").ap()
    i2_d = nc.dram_tensor("i2", [128, 64], F32, kind="ExternalInput").ap()
    obd_d = nc.dram_tensor("onesbd", [128, 128], F32, kind="ExternalInput").ap()
    ygT = nc.dram_tensor("ygT", [128, TL], BF16, kind="ExternalOutput").ap()
    if debug:
        dbg = nc.dram_tensor("dbg", [24, 128, 512], F32, kind="ExternalOutput").ap()

    es = ExitStack()
    sb = lambda name, shape, dt: es.enter_context(nc.sbuf_tensor(name, shape, dt))
    with es:
        ident_b = sb("ident_b", [128, 128], BF16)
        ident_f = sb("ident_f", [128, 128], F32)
        wf = sb("wf", [128, 8, WCOLS], F32)
        wc = sb("wc", [128, 8, WCOLS], BF16)
        wp = sb("wp", [128, 8, WCOLS], BF16)
        mu_sb = sb("mu_sb", [128, 8, 6], F32)
        w2a_b = sb("w2a_b", [65, 128], BF16)
        a2_b = sb("a2_b", [128, 128], BF16)
        g2a_b = sb("g2a_b", [128, 128], BF16)
        g2b_b = sb("g2b_b", [32, 128], BF16)
        vec = sb("vec", [128, 8], F32)
        tri3 = sb("tri3_s", [128, 192], F32)
        msk = sb("msk_s", [128, 320], F32)
        i2 = sb("i2_s", [128, 64], F32)
        obd_f = sb("obd_f", [128, 128], F32)
        obd_b = sb("obd_b", [128, 128], BF16)
        xb = [sb("xb%d" % i, [128, 8, TT + 1], BF16) for i in range(2)]
        r_f = sb("r_f", [128, TT], F32)
        k_f = sb("k_f", [128, TT], F32)
        v_f = sb("v_f", [128, TT], F32)
        v_b = sb("v_b", [128, TT], BF16)
        twa = sb("twa", [65, TT], BF16)
        a1o = sb("a1o", [128, TT], BF16)
        sg_a = sb("sg_a", [128, TT], BF16)
        sg_b = sb("sg_b", [32, TT], BF16)
        g_f = sb("g_f", [128, TT], F32)
        al_f = sb("al_f", [128, TT], F32)
        kkr = sb("kkr", [128, TT], F32)
        sq_b = sb("sq_b", [128, TT], BF16)
        rn = sb("rn", [128, TT], F32)
        kk_f = sb("kk_f", [128, TT], F32)
        kt_f = sb("kt_f", [128, TT], F32)
        b_f = sb("b_f", [128, TT], F32)
        tmp1 = sb("tmp1", [128, TT], F32)
        rk_b = sb("rk_b", [128, TT], BF16)
        bon_f = sb("bon_f", [128, TT], F32)
        sgw = sb("sgw", [128, NCH, 128], F32)
        e_pos = sb("e_pos", [128, NCH, 64], F32)
        e_neg = sb("e_neg", [128, NCH, 64], F32)
        e_ex = sb("e_ex", [128, NCH, 64], F32)
        e_rem = sb("e_rem", [128, NCH, 64], F32)
        AR = sb("AR", [128, NCH, 2, 64], BF16)
        Rh_f = sb("Rh_f", [128, TT], F32)
        BhT = sb("BhT", [128, TT], BF16)
        KhT = sb("KhT", [128, TT], BF16)
        BbT = sb("BbT", [128, TT], BF16)
        KbT = sb("KbT", [128, TT], BF16)
        TM = sb("TM", [128, NCH, 4, 64], BF16)
        GM = sb("GM", [128, NCH, 320], BF16)
        Xf = sb("Xf", [128, NCH, 128], F32)
        Xb = sb("Xb", [128, NCH, 128], BF16)
        PP = [sb("PP%d" % i, [128, NCH, 128], BF16) for i in range(2)]
        RtT = sb("RtT", [128, TT], BF16)
        Y0 = sb("Y0", [128, NCH, 64], F32)
        DG = sb("DG", [128, NCH, 64], F32)
        PT_f = sb("PT_f", [128, NCH, 64], F32)
        Q_f = sb("Q_f", [128, NCH, 64], F32)
        S_f = [sb("S_f%d" % i, [128, 64], F32) for i in range(2)]
        S_b = [sb("S_b%d" % i, [128, 64], BF16) for i in range(2)]
        y_tm = sb("y_tm", [128, NCH, 64], F32)
        yT_f = sb("yT_f", [128, TT], F32)
        d_f = sb("d_f", [128, TT], F32)
        sq_f = sb("sq_f", [128, TT], F32)
        rstd = sb("rstd", [128, TT], F32)
        yo = [sb("yo%d" % i, [128, TT], BF16) for i in range(2)]
        bank = [es.enter_context(nc.psum_tensor("bank%d" % i, [128, 512], F32)) for i in range(8)]
        bk = lambda i: "bank%d" % i

        p = Prog(nc)
        for idt, nm in ((ident_b, "ident_b"), (ident_f, "ident_f")):
            p.op("pool", lambda e, idt=idt: e.memset(idt[:], 0.0), w=[nm])
            p.op("pool", lambda e, idt=idt: e.affine_select(out=idt[:], in_=idt[:], pattern=[[-1, 128]],
                                                         compare_op=ALU.not_equal, fill=1.0, base=0,
                                                         channel_multiplier=1), r=[nm], w=[nm])
        p.op("pool", lambda e: e.memset(twa[64:65, :], 1.0), w=["twa1"])
        p.op("pool", lambda e: e.memset(S_f[0][:], 0.0), w=["S_f0"])
        p.op("pool", lambda e: e.memset(S_b[0][:], 0.0), w=["S_b0"])
        for kc in range(8):
            p.dma("sp", wf[:, kc, :], wbig[kc * 128:(kc + 1) * 128, :], w=["wf"], semkey="wf")
        p.dma("sp", mu_sb[:], mu6.rearrange("(k p) n -> p k n", p=128), w=["mu"])
        p.dma("pool", w2a_b[:], w2a, w=["w2a_b"])
        p.dma("pool", a2_b[:], a2p, w=["a2_b"])
        p.dma("pool", g2a_b[:], g2p[0:128, :], w=["g2a_b"])
        p.dma("pool", g2b_b[:], g2p[128:160, :], w=["g2b_b"])
        p.dma("pool", obd_b[:], obd_d, w=["obd_b"])
        p.dma("sp", obd_f[:], obd_d, w=["obd_f"])
        p.dma("sp", vec[:], vecs, w=["vec"])
        p.dma("sp", tri3[:], tri3_d, w=["tri3"])
        p.dma("sp", msk[:], msk_d, w=["msk"])
        p.dma("sp", i2[:], i2_d, w=["i2"])
        groups = [(0, 128, 0), (128, 256, 2), (256, 384, 3), (384, 448, 1), (448, 512, 4), (512, 672, 5)]
        for kc in range(8):
            for (c0, c1, n) in groups:
                eng = "dve" if (kc % 2 == 0) else "pool"
                p.op(eng, lambda e, kc=kc, c0=c0, c1=c1, n=n: e.tensor_scalar(
                    out=wp[:, kc, c0:c1], in0=wf[:, kc, c0:c1], scalar1=mu_sb[:, kc, n:n + 1], scalar2=None, op0=ALU.mult),
                    r=["wf", "mu"], w=["wp"])
            p.op("dve" if (kc % 2 == 0) else "pool",
                 lambda e, kc=kc: e.tensor_tensor(out=wc[:, kc, :], in0=wf[:, kc, :], in1=wp[:, kc, :], op=ALU.subtract),
                 r=["wf", "wp"], w=["wc"])

        V_KK, V_KA, V_1KA, V_RK, V_GG, V_GB, V_A0 = range(7)
        vcol = lambda i: vec[:, i:i + 1]
        tp = lambda h: (64 * h, 64 * h)
        hs = lambda h: slice(64 * h, 64 * h + 64)
        s_cur = 0

        def load_x(tj):
            for kc in range(8):
                p.dma("pool", xb[tj % 2][:, kc, :], xT[kc * 128:(kc + 1) * 128, tj * TT:tj * TT + TT + 1],
                      w=["xb%d" % (tj % 2)], semkey="xb%d" % (tj % 2))

        load_x(0)
        for ti in range(ntiles):
            t0 = ti * TT
            x_sb = xb[ti % 2]
            xk = "xb%d" % (ti % 2)
            if ti + 1 < ntiles:
                load_x(ti + 1)

            def proj(c0, c1, bi):
                m = c1 - c0
                for kc in range(8):
                    p.op("pe", lambda e, kc=kc, x_sb=x_sb: e.matmul(bank[bi][0:m, :], lhsT=wc[:, kc, c0:c1], rhs=x_sb[:, kc, 1:TT + 1],
                                                        start=(kc == 0), stop=False), r=[xk, "wc"], w=[bk(bi)])
                for kc in range(8):
                    p.op("pe", lambda e, kc=kc, x_sb=x_sb: e.matmul(bank[bi][0:m, :], lhsT=wp[:, kc, c0:c1], rhs=x_sb[:, kc, 0:TT],
                                                        start=False, stop=(kc == 7)), r=[xk, "wp"], w=[bk(bi)])

            proj(0, 128, 0)
            p.op("act", lambda e: e.activation(out=r_f[:], in_=bank[0][:], func=AF.Copy), r=[bk(0)], w=["r_f"])
            proj(128, 256, 1)
            p.op("act", lambda e: e.activation(out=k_f[:], in_=bank[1][:], func=AF.Copy), r=[bk(1)], w=["k_f"])
            proj(256, 384, 2)
            p.op("act", lambda e: e.activation(out=v_f[:], in_=bank[2][:], func=AF.Copy), r=[bk(2)], w=["v_f"])
            p.op("act", lambda e: e.activation(out=v_b[:], in_=bank[2][:], func=AF.Copy), r=[bk(2)], w=["v_b"])
            proj(384, 512, 3)
            p.op("act", lambda e: e.activation(out=twa[0:64, :], in_=bank[3][0:64, :], func=AF.Tanh), r=[bk(3)], w=["twa"])
            p.op("dve", lambda e: e.tensor_copy(out=a1o[64:128, :], in_=bank[3][64:128, :]), r=[bk(3)], w=["a1o"])
            proj(512, 640, 4)
            p.op("act", lambda e: e.activation(out=sg_a[:], in_=bank[4][:], func=AF.Sigmoid), r=[bk(4)], w=["sg_a"])
            proj(640, 672, 5)
            p.op("act", lambda e: e.activation(out=sg_b[:], in_=bank[5][0:32, :], func=AF.Sigmoid), r=[bk(5)], w=["sg_b"])
            p.op("pe", lambda e: e.matmul(bank[6][:], lhsT=g2a_b[:], rhs=sg_a[:], start=True, stop=False), r=["g2a_b", "sg_a"], w=[bk(6)])
            p.op("pe", lambda e: e.matmul(bank[6][:], lhsT=g2b_b[:], rhs=sg_b[:], start=False, stop=True), r=["g2b_b", "sg_b"], w=[bk(6)])
            p.op("act", lambda e: e.activation(out=g_f[:], in_=bank[6][:], func=AF.Copy), r=[bk(6)], w=["g_f"])
            p.op("pe", lambda e: e.matmul(bank[7][:], lhsT=a2_b[64:128, :], rhs=a1o[64:128, :], start=True, stop=True),
                 r=["a2_b", "a1o"], w=[bk(7)])
            p.op("act", lambda e: e.activation(out=al_f[:], in_=bank[7][:], func=AF.Sigmoid, bias=vcol(V_A0)), r=[bk(7), "vec"], w=["al_f"])
            p.op("dve", lambda e: e.tensor_scalar(out=kkr[:], in0=k_f[:], scalar1=vcol(V_KK), scalar2=None, op0=ALU.mult), r=["k_f", "vec"], w=["kkr"])
            p.op("act", lambda e: e.activation(out=sq_b[:], in_=kkr[:], func=AF.Square), r=["kkr"], w=["sq_b"])
            p.op("pe", lambda e: e.matmul(bank[0][:], lhsT=obd_b[:], rhs=sq_b[:], start=True, stop=True), r=["obd_b", "sq_b"], w=[bk(0)])
            p.op("dve", lambda e: e.tensor_scalar(out=rn[:], in0=bank[0][:], scalar1=1e-24, scalar2=None, op0=ALU.add), r=[bk(0)], w=["rn"])
            p.op("act", lambda e: e.activation(out=rn[:], in_=rn[:], func=AF.Sqrt), r=["rn"], w=["rn"])
            p.op("dve", lambda e: e.reciprocal(out=rn[:], in_=rn[:]), r=["rn"], w=["rn"])
            p.op("dve", lambda e: e.tensor_tensor(out=kk_f[:], in0=kkr[:], in1=rn[:], op=ALU.mult), r=["kkr", "rn"], w=["kk_f"])
            p.op("pool", lambda e: e.tensor_scalar(out=tmp1[:], in0=al_f[:], scalar1=-1.0, scalar2=vcol(V_KA), op0=ALU.add, op1=ALU.mult),
                 r=["al_f", "vec"], w=["tmp1"])
            p.op("dve", lambda e: e.scalar_tensor_tensor(out=kt_f[:], in0=tmp1[:], scalar=1.0, in1=k_f[:], op0=ALU.add, op1=ALU.mult),
                 r=["k_f", "tmp1"], w=["kt_f"])
            p.op("dve", lambda e: e.tensor_tensor(out=b_f[:], in0=kk_f[:], in1=al_f[:], op=ALU.mult), r=["kk_f", "al_f"], w=["b_f"])
            p.op("dve", lambda e: e.scalar_tensor_tensor(out=rk_b[:], in0=r_f[:], scalar=vcol(V_RK), in1=kt_f[:], op0=ALU.mult, op1=ALU.mult),
                 r=["r_f", "kt_f", "vec"], w=["rk_b"])
            p.op("pe", lambda e: e.matmul(bank[1][:], lhsT=obd_b[:], rhs=rk_b[:], start=True, stop=True), r=["obd_b", "rk_b"], w=[bk(1)])
            p.op("dve", lambda e: e.tensor_tensor(out=bon_f[:], in0=v_f[:], in1=bank[1][:], op=ALU.mult), r=["v_f", bk(1)], w=["bon_f"])

            for c in range(NCH):
                p.op("pe", lambda e, c=c: e.matmul(bank[2 + c // 4][0:64, (c % 4) * 128:(c % 4 + 1) * 128],
                                                   lhsT=twa[:, c * 64:(c + 1) * 64], rhs=w2a_b[:], start=True, stop=True),
                     r=["twa", "twa1", "w2a_b"], w=[bk(2 + c // 4)])
            for hf in range(2):
                p.op("act", lambda e, hf=hf: e.activation(out=sgw[0:64, hf * 4:(hf + 1) * 4, :],
                                                         in_=bank[2 + hf][0:64, :].rearrange("p (c d) -> p c d", c=4), func=AF.Sigmoid),
                     r=[bk(2 + hf)], w=["sgw"])
            for c in range(NCH):
                for kind in range(3):
                    p.op("pe", lambda e, c=c, kind=kind: e.matmul(bank[4 + kind][:, c * 64:(c + 1) * 64], lhsT=sgw[0:64, c, :],
                                                                 rhs=tri3[0:64, kind * 64:(kind + 1) * 64], start=True, stop=True),
                         r=["sgw", "tri3"], w=[bk(4 + kind)])
            v3 = lambda tl: tl[:].rearrange("p c t -> p (c t)")
            p.op("act", lambda e: e.activation(out=v3(e_pos), in_=bank[4][:], func=AF.Exp), r=[bk(4)], w=["e_pos"])
            p.op("act", lambda e: e.activation(out=v3(e_neg), in_=bank[4][:], func=AF.Exp, scale=-1.0), r=[bk(4)], w=["e_neg"])
            p.op("act", lambda e: e.activation(out=v3(e_ex), in_=bank[5][:], func=AF.Exp), r=[bk(5)], w=["e_ex"])
            p.op("act", lambda e: e.activation(out=v3(e_rem), in_=bank[6][:], func=AF.Exp), r=[bk(6)], w=["e_rem"])
            c3 = lambda ap2: ap2.rearrange("p (c t) -> p c t", c=NCH)
            p.op("dve", lambda e: e.scalar_tensor_tensor(out=AR[:, :, 0, :], in0=c3(kk_f[:]), scalar=-1.0, in1=e_ex[:], op0=ALU.mult, op1=ALU.mult),
                 r=["kk_f", "e_ex"], w=["AR"])
            p.op("dve", lambda e: e.tensor_tensor(out=Rh_f[:], in0=r_f[:], in1=v3(e_pos), op=ALU.mult), r=["r_f", "e_pos"], w=["Rh_f"])
            p.op("act", lambda e: e.activation(out=AR[:, :, 1, :], in_=c3(Rh_f[:]), func=AF.Copy), r=["Rh_f"], w=["AR"])
            p.op("dve", lambda e: e.tensor_tensor(out=BhT[:], in0=b_f[:], in1=v3(e_neg), op=ALU.mult), r=["b_f", "e_neg"], w=["BhT"])
            p.op("pool", lambda e: e.tensor_tensor(out=KhT[:], in0=kt_f[:], in1=v3(e_neg), op=ALU.mult), r=["kt_f", "e_neg"], w=["KhT"])
            p.op("dve", lambda e: e.tensor_tensor(out=BbT[:], in0=b_f[:], in1=v3(e_rem), op=ALU.mult), r=["b_f", "e_rem"], w=["BbT"])
            p.op("pool", lambda e: e.tensor_tensor(out=KbT[:], in0=kt_f[:], in1=v3(e_rem), op=ALU.mult), r=["kt_f", "e_rem"], w=["KbT"])

            tmb = [bank[0][:].bitcast(BF16), bank[1][:].bitcast(BF16)]
            srcs = [(lambda c: AR[:, c, 0, :], "AR"), (lambda c: v_b[:, c * 64:(c + 1) * 64], "v_b"),
                    (lambda c: BbT[:, c * 64:(c + 1) * 64], "BbT"), (lambda c: KbT[:, c * 64:(c + 1) * 64], "KbT")]
            for c in range(NCH):
                for si, (sf, sk) in enumerate(srcs):
                    for h in range(2):
                        col = (c % 4) * 256 + si * 64
                        p.op("pe", lambda e, c=c, sf=sf, h=h, col=col: e.transpose(
                            tmb[c // 4][hs(h), col:col + 64], sf(c)[hs(h), :], ident_b[hs(h), hs(h)], tile_position=tp(h)),
                            r=[sk, "ident_b"], w=[bk(c // 4)])
            for hf in range(2):
                p.op("act" if hf == 0 else "dve",
                     (lambda e, hf=hf: e.activation(out=TM[:, hf * 4:(hf + 1) * 4, :, :].rearrange("p c s t -> p (c s t)"), in_=tmb[hf], func=AF.Copy))
                     if hf == 0 else
                     (lambda e, hf=hf: e.tensor_copy(out=TM[:, hf * 4:(hf + 1) * 4, :, :].rearrange("p c s t -> p (c s t)"), in_=tmb[hf])),
                     r=[bk(hf)], w=["TM"])

            for c in range(NCH):
                bi = 2 + (c % 2)
                cs_ = slice(c * 64, (c + 1) * 64)
                for h in range(2):
                    arh = AR[hs(h), c, :, :].rearrange("p s t -> p (s t)")
                    p.op("pe", lambda e, h=h, arh=arh, cs_=cs_, bi=bi: e.matmul(bank[bi][hs(h), 0:128], lhsT=BhT[hs(h), cs_], rhs=arh,
                                                                           start=True, stop=True, tile_position=tp(h)),
                         r=["BhT", "AR"], w=[bk(bi)])
                    p.op("pe", lambda e, h=h, arh=arh, cs_=cs_, bi=bi: e.matmul(bank[bi][hs(h), 128:256], lhsT=KhT[hs(h), cs_], rhs=arh,
                                                                           start=True, stop=True, tile_position=tp(h)),
                         r=["KhT", "AR"], w=[bk(bi)])
                    p.op("pe", lambda e, h=h, c=c, cs_=cs_, bi=bi: e.matmul(bank[bi][hs(h), 256:320], lhsT=AR[hs(h), c, 0, :], rhs=BhT[hs(h), cs_],
                                                                       start=True, stop=True, tile_position=tp(h)),
                         r=["BhT", "AR"], w=[bk(bi)])
                p.op("dve", lambda e, c=c, bi=bi: e.tensor_tensor(out=GM[:, c, :], in0=bank[bi][:, 0:320], in1=msk[:], op=ALU.mult),
                     r=[bk(bi), "msk"], w=["GM"])

            for c in range(NCH):
                for h in range(2):
                    p.op("pe", lambda e, c=c, h=h: e.matmul(bank[4][hs(h), c * 64:(c + 1) * 64], lhsT=GM[hs(h), c, 128:192], rhs=TM[hs(h), c, 1, :],
                                                           start=True, stop=True, tile_position=tp(h)), r=["GM", "TM"], w=[bk(4)])
            p.op("pool", lambda e: e.tensor_copy(out=Xf[:, :, 0:64], in_=TM[:, :, 0, :]), r=["TM"], w=["Xf"])
            p.op("dve", lambda e: e.tensor_copy(out=Xf[:, :, 64:128], in_=bank[4][:].rearrange("p (c t) -> p c t", c=NCH)), r=[bk(4)], w=["Xf"])
            p.op("act", lambda e: e.activation(out=Xb[:], in_=Xf[:], func=AF.Copy), r=["Xf"], w=["Xb"])

            NLEV = 6
            for lev in range(NLEV):
                if lev == 0:
                    P_of = lambda c: GM[:, c, 256:320]
                    PT_of = lambda c: GM[:, c, 0:64]
                    pk = "GM"
                else:
                    ppt = PP[(lev - 1) % 2]
                    P_of = lambda c, ppt=ppt: ppt[:, c, 0:64]
                    PT_of = lambda c, ppt=ppt: ppt[:, c, 64:128]
                    pk = "PP%d" % ((lev - 1) % 2)
                for c in range(NCH):
                    bi = 0 + c // 4
                    for h in range(2):
                        p.op("pe", lambda e, c=c, h=h, bi=bi, PT_of=PT_of: e.matmul(
                            bank[bi][hs(h), (c % 4) * 128:(c % 4 + 1) * 128], lhsT=PT_of(c)[hs(h), :], rhs=Xb[hs(h), c, :],
                            start=True, stop=True, tile_position=tp(h)), r=[pk, "Xb"], w=[bk(bi)])
                if lev < NLEV - 1:
                    for c in range(NCH):
                        bi = 2 + c // 4
                        for h in range(2):
                            p.op("pe", lambda e, c=c, h=h, bi=bi, P_of=P_of, PT_of=PT_of: e.matmul(
                                bank[bi][hs(h), (c % 4) * 128:(c % 4) * 128 + 64], lhsT=PT_of(c)[hs(h), :], rhs=P_of(c)[hs(h), :],
                                start=True, stop=True, tile_position=tp(h)), r=[pk], w=[bk(bi)])
                            p.op("pe", lambda e, c=c, h=h, bi=bi, P_of=P_of, PT_of=PT_of: e.matmul(
                                bank[bi][hs(h), (c % 4) * 128 + 64:(c % 4 + 1) * 128], lhsT=P_of(c)[hs(h), :], rhs=PT_of(c)[hs(h), :],
                                start=True, stop=True, tile_position=tp(h)), r=[pk], w=[bk(bi)])
                for hf in range(2):
                    xs = Xf[:, hf * 4:(hf + 1) * 4, :].rearrange("p c t -> p (c t)")
                    p.op("dve", lambda e, hf=hf, xs=xs: e.tensor_tensor(out=xs, in0=xs, in1=bank[hf][:], op=ALU.add), r=["Xf", bk(hf)], w=["Xf"])
                p.op("act", lambda e: e.activation(out=Xb[:], in_=Xf[:], func=AF.Copy), r=["Xf"], w=["Xb"])
                if lev < NLEV - 1:
                    ppn = PP[lev % 2]
                    for hf in range(2):
                        p.op("act", lambda e, hf=hf, ppn=ppn: e.activation(out=ppn[:, hf * 4:(hf + 1) * 4, :].rearrange("p c t -> p (c t)"),
                                                                          in_=bank[2 + hf][:], func=AF.Copy),
                             r=[bk(2 + hf)], w=["PP%d" % (lev % 2)])

            for c in range(NCH):
                for h in range(2):
                    p.op("pe", lambda e, c=c, h=h: e.matmul(bank[4][hs(h), c * 64:(c + 1) * 64], lhsT=Xb[hs(h), c, 0:64], rhs=GM[hs(h), c, 64:128],
                                                           start=True, stop=True, tile_position=tp(h)), r=["Xb", "GM"], w=[bk(4)])
                    p.op("pe", lambda e, c=c, h=h: e.matmul(bank[5][hs(h), c * 64:(c + 1) * 64], lhsT=GM[hs(h), c, 64:128], rhs=Xb[hs(h), c, 64:128],
                                                           start=True, stop=False, tile_position=tp(h)), r=["Xb", "GM"], w=[bk(5)])
                    p.op("pe", lambda e, c=c, h=h: e.matmul(bank[5][hs(h), c * 64:(c + 1) * 64], lhsT=GM[hs(h), c, 192:256], rhs=TM[hs(h), c, 1, :],
                                                           start=False, stop=True, tile_position=tp(h)), r=["TM", "GM"], w=[bk(5)])
                    p.op("pe", lambda e, c=c, h=h: e.matmul(bank[6][hs(h), c * 64:(c + 1) * 64], lhsT=Xb[hs(h), c, 0:64], rhs=TM[hs(h), c, 2, :],
                                                           start=True, stop=True, tile_position=tp(h)), r=["Xb", "TM"], w=[bk(6)])
                    p.op("pe", lambda e, c=c, h=h: e.matmul(bank[7][hs(h), c * 64:(c + 1) * 64], lhsT=TM[hs(h), c, 2, :], rhs=Xb[hs(h), c, 64:128],
                                                           start=True, stop=False, tile_position=tp(h)), r=["Xb", "TM"], w=[bk(7)])
                    p.op("pe", lambda e, c=c, h=h: e.matmul(bank[7][hs(h), c * 64:(c + 1) * 64], lhsT=TM[hs(h), c, 3, :], rhs=TM[hs(h), c, 1, :],
                                                           start=False, stop=True, tile_position=tp(h)), r=["TM"], w=[bk(7)])
            p.op("dve", lambda e: e.tensor_tensor(out=RtT[:], in0=bank[4][:], in1=Rh_f[:], op=ALU.add), r=[bk(4), "Rh_f"], w=["RtT"])
            p.op("act", lambda e: e.activation(out=v3(Y0), in_=bank[5][:], func=AF.Copy), r=[bk(5)], w=["Y0"])
            p.op("pool", lambda e: e.tensor_tensor(out=DG[:], in0=i2[:].unsqueeze(1).to_broadcast([128, NCH, 64]),
                                                  in1=e_pos[:, :, 63:64].to_broadcast([128, NCH, 64]), op=ALU.mult), r=["i2", "e_pos"], w=["DG"])
            p.op("dve", lambda e: e.tensor_tensor(out=v3(PT_f), in0=bank[6][:], in1=v3(DG), op=ALU.add), r=[bk(6), "DG"], w=["PT_f"])
            p.op("act", lambda e: e.activation(out=v3(Q_f), in_=bank[7][:], func=AF.Copy), r=[bk(7)], w=["Q_f"])

            for c in range(NCH):
                sn = 1 - s_cur
                for h in range(2):
                    p.op("pe", lambda e, c=c, h=h, s_cur=s_cur: e.matmul(bank[1][hs(h), (c % 2) * 64:(c % 2 + 1) * 64], lhsT=PT_f[hs(h), c, :],
                                                                        rhs=S_f[s_cur][hs(h), :], start=True, stop=True, tile_position=tp(h)),
                         r=["PT_f", "S_f%d" % s_cur, bk(1)], w=[bk(1) + ("a" if c % 2 else "b")])
                p.op("dve", lambda e, c=c, sn=sn: e.tensor_tensor(out=S_f[sn][:], in0=bank[1][:, (c % 2) * 64:(c % 2 + 1) * 64], in1=Q_f[:, c, :], op=ALU.add),
                     r=[bk(1) + ("a" if c % 2 else "b"), bk(1), "Q_f"], w=["S_f%d" % sn])
                for h in range(2):
                    p.op("pe", lambda e, c=c, h=h, s_cur=s_cur: e.matmul(bank[0][hs(h), c * 64:(c + 1) * 64], lhsT=RtT[hs(h), c * 64:(c + 1) * 64],
                                                                        rhs=S_b[s_cur][hs(h), :], start=True, stop=True, tile_position=tp(h)),
                         r=["RtT", "S_b%d" % s_cur], w=[bk(0)])
                p.op("act", lambda e, sn=sn: e.activation(out=S_b[sn][:], in_=S_f[sn][:], func=AF.Copy), r=["S_f%d" % sn], w=["S_b%d" % sn])
                s_cur = sn
            p.op("dve", lambda e: e.tensor_tensor(out=v3(y_tm), in0=bank[0][:], in1=v3(Y0), op=ALU.add), r=[bk(0), "Y0"], w=["y_tm"])

            for c in range(NCH):
                for h in range(2):
                    p.op("pe", lambda e, c=c, h=h: e.matmul(bank[2][hs(h), c * 64:(c + 1) * 64], lhsT=y_tm[hs(h), c, :], rhs=ident_f[hs(h), hs(h)],
                                                           start=True, stop=True, tile_position=tp(h)), r=["y_tm", "ident_f"], w=[bk(2)])
            p.op("act", lambda e: e.activation(out=yT_f[:], in_=bank[2][:], func=AF.Copy), r=[bk(2)], w=["yT_f"])
            p.op("pe", lambda e: e.matmul(bank[3][:], lhsT=obd_f[:], rhs=yT_f[:], start=True, stop=True), r=["obd_f", "yT_f"], w=[bk(3)])
            p.op("dve", lambda e: e.scalar_tensor_tensor(out=d_f[:], in0=bank[3][:], scalar=-1.0 / 64, in1=yT_f[:], op0=ALU.mult, op1=ALU.add),
                 r=[bk(3), "yT_f"], w=["d_f"])
            p.op("act", lambda e: e.activation(out=sq_f[:], in_=d_f[:], func=AF.Square), r=["d_f"], w=["sq_f"])
            p.op("pe", lambda e: e.matmul(bank[4][:], lhsT=obd_f[:], rhs=sq_f[:], start=True, stop=True), r=["obd_f", "sq_f"], w=[bk(4)])
            p.op("dve", lambda e: e.tensor_scalar(out=rstd[:], in0=bank[4][:], scalar1=1.0 / 64, scalar2=GN_EPS, op0=ALU.mult, op1=ALU.add),
                 r=[bk(4)], w=["rstd"])
            p.op("act", lambda e: e.activation(out=rstd[:], in_=rstd[:], func=AF.Sqrt), r=["rstd"], w=["rstd"])
            p.op("dve", lambda e: e.reciprocal(out=rstd[:], in_=rstd[:]), r=["rstd"], w=["rstd"])
            p.op("dve", lambda e: e.tensor_tensor(out=d_f[:], in0=d_f[:], in1=rstd[:], op=ALU.mult), r=["d_f", "rstd"], w=["d_f"])
            p.op("act", lambda e: e.activation(out=d_f[:], in_=d_f[:], func=AF.Identity, scale=vcol(V_GG), bias=vcol(V_GB)),
                 r=["d_f", "vec"], w=["d_f"])
            p.op("pool", lambda e: e.tensor_tensor(out=d_f[:], in0=d_f[:], in1=bon_f[:], op=ALU.add), r=["d_f", "bon_f"], w=["d_f"])
            y_o = yo[ti % 2]
            yk = "yo%d" % (ti % 2)
            p.op("dve", lambda e, y_o=y_o: e.tensor_tensor(out=y_o[:], in0=d_f[:], in1=g_f[:], op=ALU.mult), r=["d_f", "g_f"], w=[yk])
            p.dma("sp", ygT[:, t0:t0 + TT], y_o[:], r=[yk], semkey="o_" + yk, store=True)
            if debug and ti == 0:
                f2 = lambda tl: tl[:].rearrange("p c t -> p (c t)")
                dl = [(r_f[:], "r_f"), (k_f[:], "k_f"), (v_f[:], "v_f"), (al_f[:], "al_f"), (g_f[:], "g_f"), (kk_f[:], "kk_f"),
                      (bon_f[:], "bon_f"), (f2(e_pos), "e_pos"), (f2(e_ex), "e_ex"), (f2(e_rem), "e_rem"), (Rh_f[:], "Rh_f"),
                      (yT_f[:], "yT_f"), (rstd[:], "rstd"), (d_f[:], "d_f"), (f2(Y0), "Y0"), (f2(PT_f), "PT_f"), (f2(Q_f), "Q_f"),
                      (f2(y_tm), "y_tm"), (Xf[:, 0:4, :].rearrange("p c t -> p (c t)"), "Xf"), (kt_f[:], "kt_f"), (b_f[:], "b_f"),
                      (f2(e_neg), "e_neg")]
                for i, (ap_, key) in enumerate(dl):
                    p.dma("sp", dbg[i], ap_, r=[key], semkey="dbgo", store=True)
        p.emit()
    return nc


def rwkv_inputs(inp, ntiles=T // TT, cores=range(NCORES)):
    TL = ntiles * TT
    x = inp["x"][0]
    xT = np.zeros((C, TL + 1), np.float32)
    xT[:, 1:] = x[:TL].T
    tri3, msk, i2, onesbd = rwkv_consts()
    mu6 = np.ascontiguousarray(inp["rwkv_mu"][0].T)
    maps = []
    for c in cores:
        cs = slice(c * 128, (c + 1) * 128)
        wrkv = inp["rwkv_w_rkv"][0]
        wbig = np.concatenate([wrkv[0][:, cs], wrkv[1][:, cs], wrkv[2][:, cs], inp["rwkv_w1"][0], inp["rwkv_a1"][0], inp["rwkv_g1"][0]], axis=1)
        w2a = np.concatenate([inp["rwkv_w2"][0][:, cs], inp["rwkv_w0"][0][None, cs]], axis=0)
        a2p = np.zeros((128, 128), np.float32)
        a2p[64:128] = inp["rwkv_a2"][0][:, cs]
        ka = inp["rwkv_k_a"][0][cs]
        one = np.ones_like(ka)
        vecs = np.stack([inp["rwkv_k_k"][0][cs], ka, one, inp["rwkv_r_k"][0].reshape(-1)[cs], inp["rwkv_gn_g"][0][cs],
                         inp["rwkv_gn_b"][0][cs], inp["rwkv_a0"][0][cs], one], axis=1).astype(np.float32)
        maps.append({
            "xT": xT, "wbig": np.ascontiguousarray(wbig), "mu6": mu6, "w2a": np.ascontiguousarray(w2a), "a2p": a2p,
            "g2p": np.ascontiguousarray(inp["rwkv_g2"][0][:, cs]), "vecs": np.ascontiguousarray(vecs),
            "tri3": tri3, "msk": msk, "i2": i2, "onesbd": onesbd,
        })
    return maps


def run_rwkv(inp):
    nc = _get("rwkv", lambda: build_rwkv(T // TT))
    maps = rwkv_inputs(inp)
    res = _run(nc, maps)
    return np.concatenate([r["ygT"] for r in res.results], axis=0)


def kernel(**inputs):
    inp = {k: np.asarray(v) for k, v in inputs.items()}
    x0 = np.ascontiguousarray(inp["x"][0], dtype=np.float32)
    ygT = run_rwkv(inp)
    lnp0 = np.stack([inp["ln_mix_g"][0], inp["ln_mix_b"][0], inp["ln_ffn_g"][0], inp["ln_ffn_b"][0]]).astype(np.float32)
    x1, qkv = run_post(ygT, x0, inp["rwkv_w_o"][0], inp["ffn_w_in"][0], inp["ffn_w_down"][0], lnp0,
                       inp["moba_w_qkv"][0], rope_tables())
    oT = run_attn(qkv)
    lnp1 = np.stack([inp["ln_mix_g"][1], inp["ln_mix_b"][1], inp["ln_ffn_g"][1], inp["ln_ffn_b"][1]]).astype(np.float32)
    out, _ = run_post(oT, x1, inp["moba_w_o"][0], inp["ffn_w_in"][1], inp["ffn_w_down"][1], lnp1)
    return out.reshape(1, T, C).astype(np.float32)
```

```python
import math
from contextlib import ExitStack

import numpy as np
import ml_dtypes

import concourse.bass as bass
import concourse.mybir as mybir
from concourse.bass_utils import run_bass_kernel_spmd

F32 = mybir.dt.float32
BF16 = mybir.dt.bfloat16
ALU = mybir.AluOpType
AF = mybir.ActivationFunctionType
AX = mybir.AxisListType

NCORES = 8
T = 16384
C = 1024
H = 16
DH = 64
DFF = 2816
DEPTH = 2
ALPHA = (2 * DEPTH) ** 0.25
LN_EPS = 1e-5
GN_EPS = 64 * 1e-5
NEG = -30000.0


class _Op:
    __slots__ = ("eng", "fn", "deps", "is_dma", "semkey", "sig", "need_sig", "idx")


class Prog:
    ENGS = ("pe", "act", "dve", "pool", "sp")

    def __init__(self, nc):
        self.nc = nc
        self.ops = []
        self.last_w = {}
        self.readers = {}
        self.store_keys = []

    def _deps(self, r, w):
        deps = set()
        for k in list(r) + list(w):
            lw = self.last_w.get(k)
            if lw is not None:
                deps.add(lw)
        for k in w:
            for rd in self.readers.get(k, ()):
                deps.add(rd)
        return deps

    def _commit(self, idx, r, w):
        for k in r:
            self.readers.setdefault(k, []).append(idx)
        for k in w:
            self.last_w[k] = idx
            self.readers[k] = []

    def op(self, eng, fn, r=(), w=()):
        o = _Op()
        o.eng, o.fn, o.is_dma, o.semkey, o.sig, o.need_sig = eng, fn, False, None, None, False
        o.deps = self._deps(r, w)
        o.idx = len(self.ops)
        self.ops.append(o)
        self._commit(o.idx, r, w)
        return o.idx

    def dma(self, q, out, in_, r=(), w=(), semkey=None, store=False):
        o = _Op()
        o.eng, o.is_dma, o.sig, o.need_sig = q, True, None, True
        o.fn = lambda e, out=out, in_=in_: e.dma_start(out=out, in_=in_)
        o.semkey = semkey if semkey is not None else (list(w)[0] if w else list(r)[0])
        o.deps = self._deps(r, w)
        o.idx = len(self.ops)
        self.ops.append(o)
        self._commit(o.idx, r, w)
        if store:
            self.store_keys.append(o.semkey)
        return o.idx

    def emit(self):
        nc = self.nc
        ops = self.ops
        for o in ops:
            for d in o.deps:
                od = ops[d]
                if od.is_dma:
                    continue
                if od.eng == o.eng and not o.is_dma and o.eng == "pe":
                    continue
                od.need_sig = True
        cnt = {e: 0 for e in self.ENGS}
        dcnt = {}
        for o in ops:
            if o.is_dma:
                dcnt[o.semkey] = dcnt.get(o.semkey, 0) + 16
                o.sig = ("d:" + o.semkey, dcnt[o.semkey])
            elif o.need_sig:
                cnt[o.eng] += 1
                o.sig = ("e:" + o.eng, cnt[o.eng])
        semnames = ["e:" + e for e in self.ENGS] + ["d:" + k for k in dcnt]
        with ExitStack() as es:
            sems = {}
            for i, n in enumerate(semnames):
                sems[n] = es.enter_context(nc.semaphore("s%d" % i))
            block = es.enter_context(nc.Block())
            per_eng = {e: [o for o in ops if o.eng == e] for e in self.ENGS}
            final_waits = [("d:" + k, dcnt[k]) for k in dict.fromkeys(self.store_keys)]

            def run(eng_name, e):
                waited = {}
                for o in per_eng[eng_name]:
                    need = {}
                    for d in o.deps:
                        od = ops[d]
                        if od.sig is None:
                            continue
                        if (not od.is_dma) and od.eng == eng_name and eng_name == "pe" and not o.is_dma:
                            continue
                        s, v = od.sig
                        if need.get(s, 0) < v:
                            need[s] = v
                    for s, v in need.items():
                        if waited.get(s, 0) < v:
                            e.wait_ge(sems[s], v)
                            waited[s] = v
                    ins = o.fn(e)
                    if o.sig is not None:
                        s, v = o.sig
                        ins.then_inc(sems[s], 16 if o.is_dma else 1)
                if eng_name == "sp":
                    for s, v in final_waits:
                        e.wait_ge(sems[s], v)

            @block.tensor
            def _(e):
                run("pe", e)

            @block.scalar
            def _(e):
                run("act", e)

            @block.vector
            def _(e):
                run("dve", e)

            @block.gpsimd
            def _(e):
                run("pool", e)

            @block.sync
            def _(e):
                run("sp", e)


def _bcast_rows(ap_1d, nparts):
    return ap_1d.partition_broadcast(nparts)


TOK = T // NCORES
GRP = 512
NGRP = TOK // GRP
NFB = DFF // 128


def build_post(with_qkv):
    nc = bass.Bass("TRN2", target_bir_lowering=False)
    aT = nc.dram_tensor("aT", [C, TOK], BF16, kind="ExternalInput").ap()
    xres = nc.dram_tensor("xres", [TOK, C], F32, kind="ExternalInput").ap()
    w_o = nc.dram_tensor("w_o", [C, C], F32, kind="ExternalInput").ap()
    w_in = nc.dram_tensor("w_in", [C, 2 * DFF], F32, kind="ExternalInput").ap()
    w_dn = nc.dram_tensor("w_dn", [DFF, C], F32, kind="ExternalInput").ap()
    lnp = nc.dram_tensor("lnp", [4, C], F32, kind="ExternalInput").ap()
    xout = nc.dram_tensor("xout", [TOK, C], F32, kind="ExternalOutput").ap()
    if with_qkv:
        w_qkv = nc.dram_tensor("w_qkv", [C, 3 * C], F32, kind="ExternalInput").ap()
        rope = nc.dram_tensor("rope", [TOK, 4, 32], F32, kind="ExternalInput").ap()
        qkv_out = nc.dram_tensor("qkv", [3, TOK, C], F32, kind="ExternalOutput").ap()

    NUNITS = NFB // 2 + (6 if with_qkv else 0)
    wscr = nc.dram_tensor("wscr", [NUNITS, 128, 8 * 512], BF16).ap()

    es = ExitStack()
    sb = lambda name, shape, dt: es.enter_context(nc.sbuf_tensor(name, shape, dt))
    ps = lambda name, shape, dt: es.enter_context(nc.psum_tensor(name, shape, dt))
    with es:
        ident = sb("ident", [128, 128], BF16)
        lnb = sb("lnb", [128, 4, C], F32)
        wdn_sb = sb("wdn_sb", [128, NFB, C], BF16)
        stg = [sb("stg%d" % i, [128, 4096], F32) for i in range(2)]
        win_sb = [sb("win%d" % i, [128, 8, 512], BF16) for i in range(2)]
        aT_sb = [sb("aT%d" % i, [128, 8, GRP], BF16) for i in range(2)]
        xr_sb = [sb("xr%d" % i, [128, C], F32) for i in range(2)]
        xg = sb("xg", [128, 4, C], F32)
        xb = sb("xb", [128, C], BF16)
        xT = sb("xT", [128, 8, GRP], BF16)
        actT = sb("actT", [128, NFB, GRP], BF16)
        sg = [sb("sg%d" % i, [128, GRP], F32) for i in range(2)]
        junk = sb("junk", [128, C], BF16)
        st = sb("st", [128, 8], F32)
        if with_qkv:
            rp_sb = sb("rp", [128, 4, 4, 32], F32)
            qo = [sb("qo%d" % i, [128, 512], F32) for i in range(2)]
            tmp = [sb("tmp%d" % i, [128, 8, 32], F32) for i in range(4)]
        acc = ps("acc", [128, C], F32)
        trp = ps("trp", [128, C], BF16)
        gu = [ps("gu%d" % i, [128, 2, GRP], F32) for i in range(2)]

        p = Prog(nc)
        p.op("pool", lambda e: e.memset(ident[:], 0.0), w=["ident"])
        p.op("pool", lambda e: e.affine_select(out=ident[:], in_=ident[:], pattern=[[-1, 128]],
                                               compare_op=ALU.not_equal, fill=1.0, base=0,
                                               channel_multiplier=1), r=["ident"], w=["ident"])
        p.dma("sp", lnb[:], lnp.partition_broadcast(128), w=["lnb"])

        stg_n = [0]

        def load_cast(parts, dst_ap, dst_keys):
            i = stg_n[0] % 2
            stg_n[0] += 1
            sk = "stg%d" % i
            for view_fn, src in parts:
                p.dma("sp", view_fn(stg[i]), src, w=[sk], semkey=sk)
            return i, sk

        def load_w8(dst, dkey, src_cols_list):
            i = stg_n[0] % 2
            stg_n[0] += 1
            sk = "stg%d" % i
            sv = stg[i][:].rearrange("p (k c) -> p k c", k=8)
            for src, c0 in src_cols_list:
                n = src.shape[1]
                p.dma("sp", sv[:, :, c0:c0 + n], src.rearrange("(k p) c -> p k c", p=128), w=[sk], semkey=sk)
            p.op("act", lambda e, dst=dst, sv=sv: e.activation(out=dst[:, 0:4, :], in_=sv[:, 0:4, :], func=AF.Copy), r=[sk], w=[dkey])
            p.op("dve", lambda e, dst=dst, sv=sv: e.tensor_copy(out=dst[:, 4:8, :], in_=sv[:, 4:8, :]), r=[sk], w=[dkey])

        def load_rows4(dst_ap, dkey, src_rows):
            i = stg_n[0] % 2
            stg_n[0] += 1
            sk = "stg%d" % i
            nblk = src_rows.shape[0] // 128
            sv = stg[i][:, 0:nblk * 1024].rearrange("p (k c) -> p k c", k=nblk)
            p.dma("sp", sv, src_rows.rearrange("(k p) c -> p k c", p=128), w=[sk], semkey=sk)
            h2 = max(1, nblk // 2)
            p.op("act", lambda e, dst_ap=dst_ap, sv=sv, h2=h2: e.activation(out=dst_ap[:, 0:h2, :], in_=sv[:, 0:h2, :], func=AF.Copy), r=[sk], w=[dkey])
            if nblk > h2:
                p.op("dve", lambda e, dst_ap=dst_ap, sv=sv, h2=h2: e.tensor_copy(out=dst_ap[:, h2:, :], in_=sv[:, h2:, :]), r=[sk], w=[dkey])

        for f4 in range(0, NFB, 4):
            n = min(4, NFB - f4)
            load_rows4(wdn_sb[:, f4:f4 + n, :], "wdn", w_dn[f4 * 128:(f4 + n) * 128, :])

        def layer_norm(tile_ap, gi, key):
            p.op("act", lambda e: e.activation(out=junk[:], in_=tile_ap, func=AF.Copy, accum_out=st[:, 0:1]),
                 r=[key], w=["junk", "st"])
            p.op("act", lambda e: e.activation(out=junk[:], in_=tile_ap, func=AF.Square, accum_out=st[:, 1:2]),
                 r=[key], w=["junk", "st"])
            p.op("dve", lambda e: e.tensor_scalar(out=st[:, 2:3], in0=st[:, 0:1], scalar1=1.0 / C, scalar2=None, op0=ALU.mult),
                 r=["st"], w=["st2"])
            p.op("dve", lambda e: e.tensor_tensor(out=st[:, 3:4], in0=st[:, 2:3], in1=st[:, 2:3], op=ALU.mult),
                 r=["st2"], w=["st3"])
            p.op("dve", lambda e: e.scalar_tensor_tensor(out=st[:, 4:5], in0=st[:, 1:2], scalar=1.0 / C, in1=st[:, 3:4],
                                                         op0=ALU.mult, op1=ALU.subtract), r=["st", "st3"], w=["st4"])
            p.op("dve", lambda e: e.tensor_scalar(out=st[:, 4:5], in0=st[:, 4:5], scalar1=LN_EPS, scalar2=None, op0=ALU.add),
                 r=["st4"], w=["st4"])
            p.op("act", lambda e: e.activation(out=st[:, 5:6], in_=st[:, 4:5], func=AF.Sqrt), r=["st4"], w=["st5"])
            p.op("dve", lambda e: e.reciprocal(out=st[:, 6:7], in_=st[:, 5:6]), r=["st5"], w=["st6"])
            p.op("dve", lambda e: e.tensor_scalar(out=tile_ap, in0=tile_ap, scalar1=st[:, 2:3], scalar2=st[:, 6:7],
                                                  op0=ALU.subtract, op1=ALU.mult), r=[key, "st2", "st6"], w=[key])
            p.op("dve", lambda e: e.tensor_tensor(out=tile_ap, in0=tile_ap, in1=lnb[:, gi, :], op=ALU.mult),
                 r=[key, "lnb"], w=[key])
            p.op("dve", lambda e: e.tensor_tensor(out=tile_ap, in0=tile_ap, in1=lnb[:, gi + 1, :], op=ALU.add),
                 r=[key, "lnb"], w=[key])

        def to_channel_major(tile_ap, key_in, ti):
            p.op("act", lambda e: e.activation(out=xb[:], in_=tile_ap, func=AF.Copy), r=[key_in], w=["xb"])
            for kc in range(8):
                p.op("pe", lambda e, kc=kc: e.transpose(trp[:, kc * 128:(kc + 1) * 128], xb[:, kc * 128:(kc + 1) * 128], ident[:]),
                     r=["xb", "ident"], w=["trp"])
            p.op("dve", lambda e: e.tensor_copy(out=xT[:, :, ti * 128:(ti + 1) * 128],
                                                in_=trp[:].rearrange("p (k t) -> p k t", k=8)),
                 r=["trp"], w=["xT"])

        def load_acts(g):
            t0 = g * GRP
            a_sb = aT_sb[g % 2]
            ak = "aT%d" % (g % 2)
            p.dma("sp", a_sb[:], aT[:, t0:t0 + GRP].rearrange("(k p) t -> p k t", p=128), w=[ak], semkey=ak)

        load_acts(0)
        for g in range(NGRP):
            t0 = g * GRP
            a_sb = aT_sb[g % 2]
            ak = "aT%d" % (g % 2)
            if with_qkv:
                p.dma("sp", rp_sb[:], rope[t0:t0 + GRP].rearrange("(i p) f c -> p i f c", p=128), w=["rp"])
            wo_v = [win_sb[h][:].rearrange("p k c -> p (k c)").rearrange("p (k c) -> p k c", k=4) for h in range(2)]
            for h in range(2):
                load_rows4(wo_v[h], "win%d" % h, w_o[h * 512:(h + 1) * 512, :])
            def phaseA_tail(ti):
                layer_norm(xg[:, ti, :], 0, "xg%d" % ti)
                to_channel_major(xg[:, ti, :], "xg%d" % ti, ti)

            for ti in range(4):
                r0 = t0 + ti * 128
                xr = xr_sb[(g * 4 + ti) % 2]
                xk = "xr%d" % ((g * 4 + ti) % 2)
                p.dma("sp", xr[:], xres[r0:r0 + 128, :], w=[xk], semkey=xk)
                for hf in range(2):
                    for kc in range(8):
                        p.op("pe", lambda e, kc=kc, hf=hf, ti=ti, a_sb=a_sb, wv=wo_v[kc // 4]: e.matmul(
                            acc[:, hf * 512:(hf + 1) * 512], lhsT=a_sb[:, kc, ti * 128:(ti + 1) * 128],
                            rhs=wv[:, kc % 4, hf * 512:(hf + 1) * 512], start=(kc == 0), stop=(kc == 7)),
                            r=[ak, "win%d" % (kc // 4)], w=["acc"])
                xk_out = "xg%d" % ti
                p.op("dve", lambda e, xr=xr, ti=ti: e.scalar_tensor_tensor(out=xg[:, ti, :], in0=xr[:], scalar=ALPHA, in1=acc[:],
                                                                           op0=ALU.mult, op1=ALU.add),
                     r=[xk, "acc"], w=[xk_out])
                if ti > 0:
                    phaseA_tail(ti - 1)
            phaseA_tail(3)
            if g + 1 < NGRP:
                load_acts(g + 1)
            NU = NFB // 2
            for u in range(NU):
                wsb = win_sb[u % 2]
                wk = "win%d" % (u % 2)
                if g == 0:
                    load_w8(wsb, wk, [(w_in[:, u * 256:(u + 1) * 256], 0), (w_in[:, DFF + u * 256:DFF + (u + 1) * 256], 256)])
                    p.dma("act", wscr[u], wsb[:].rearrange("p k c -> p (k c)"), r=[wk], w=["wscr%d" % u], semkey="wscr%d" % u)
                else:
                    p.dma("sp", wsb[:].rearrange("p k c -> p (k c)"), wscr[u], r=["wscr%d" % u], w=[wk], semkey=wk)
                for j in range(2):
                    fb = u * 2 + j
                    gps = gu[fb % 2]
                    gk = "gu%d" % (fb % 2)
                    for which in range(2):
                        for kc in range(8):
                            p.op("pe", lambda e, kc=kc, which=which, j=j, wsb=wsb, gps=gps: e.matmul(
                                gps[:, which, :], lhsT=wsb[:, kc, which * 256 + j * 128: which * 256 + (j + 1) * 128],
                                rhs=xT[:, kc, :], start=(kc == 0), stop=(kc == 7)),
                                r=[wk, "xT"], w=[gk])
                    s_sb = sg[fb % 2]
                    sk = "sg%d" % (fb % 2)
                    p.op("act", lambda e, gps=gps, s_sb=s_sb: e.activation(out=s_sb[:], in_=gps[:, 0, :], func=AF.Silu),
                         r=[gk], w=[sk])
                    p.op("dve", lambda e, gps=gps, s_sb=s_sb, fb=fb: e.tensor_tensor(out=actT[:, fb, :], in0=s_sb[:], in1=gps[:, 1, :], op=ALU.mult),
                         r=[gk, sk], w=["actT"])
            def down_tail(ti):
                r0 = t0 + ti * 128
                xk_out = "xg%d" % ti
                layer_norm(xg[:, ti, :], 2, xk_out)
                p.dma("act", xout[r0:r0 + 128, :], xg[:, ti, :], r=[xk_out], semkey="o_" + xk_out, store=True)
                if with_qkv:
                    to_channel_major(xg[:, ti, :], xk_out, ti)

            for ti in range(4):
                for hf in range(2):
                    for fb in range(NFB):
                        p.op("pe", lambda e, fb=fb, hf=hf, ti=ti: e.matmul(
                            acc[:, hf * 512:(hf + 1) * 512], lhsT=actT[:, fb, ti * 128:(ti + 1) * 128],
                            rhs=wdn_sb[:, fb, hf * 512:(hf + 1) * 512], start=(fb == 0), stop=(fb == NFB - 1)),
                            r=["actT", "wdn"], w=["acc"])
                xk_out = "xg%d" % ti
                p.op("dve", lambda e, ti=ti: e.scalar_tensor_tensor(out=xg[:, ti, :], in0=xg[:, ti, :], scalar=ALPHA, in1=acc[:],
                                                                    op0=ALU.mult, op1=ALU.add),
                     r=[xk_out, "acc"], w=[xk_out])
                if ti > 0:
                    down_tail(ti - 1)
            down_tail(3)
            if with_qkv:
                for cb in range(6):
                    wsb = win_sb[cb % 2]
                    wk = "win%d" % (cb % 2)
                    uq = NFB // 2 + cb
                    if g == 0:
                        load_w8(wsb, wk, [(w_qkv[:, cb * 512:(cb + 1) * 512], 0)])
                        p.dma("act", wscr[uq], wsb[:].rearrange("p k c -> p (k c)"), r=[wk], w=["wscr%d" % uq], semkey="wscr%d" % uq)
                    else:
                        p.dma("sp", wsb[:].rearrange("p k c -> p (k c)"), wscr[uq], r=["wscr%d" % uq], w=[wk], semkey=wk)
                    for ti in range(4):
                        r0 = t0 + ti * 128
                        gps = gu[(cb * 4 + ti) % 2]
                        gk = "gu%d" % ((cb * 4 + ti) % 2)
                        for kc in range(8):
                            p.op("pe", lambda e, kc=kc, ti=ti, wsb=wsb, gps=gps: e.matmul(
                                gps[:, 0, :], lhsT=xT[:, kc, ti * 128:(ti + 1) * 128], rhs=wsb[:, kc, :],
                                start=(kc == 0), stop=(kc == 7)), r=[wk, "xT"], w=[gk])
                        o_sb = qo[(cb * 4 + ti) % 2]
                        ok = "qo%d" % ((cb * 4 + ti) % 2)
                        which = cb // 2
                        if which == 2:
                            p.op("act", lambda e, gps=gps, o_sb=o_sb: e.activation(out=o_sb[:], in_=gps[:, 0, :], func=AF.Copy),
                                 r=[gk], w=[ok])
                        else:
                            src = gps[:, 0, :].rearrange("p (h d) -> p h d", h=8)
                            dst = o_sb[:].rearrange("p (h d) -> p h d", h=8)
                            cos = rp_sb[:, ti, 2 * which, :].unsqueeze(1).to_broadcast([128, 8, 32])
                            sin = rp_sb[:, ti, 2 * which + 1, :].unsqueeze(1).to_broadcast([128, 8, 32])
                            tk = ["tmp%d" % i for i in range(4)]
                            p.op("dve", lambda e, src=src, cos=cos: e.tensor_tensor(out=tmp[0][:], in0=src[:, :, 0:32], in1=cos, op=ALU.mult),
                                 r=[gk, "rp"], w=[tk[0]])
                            p.op("dve", lambda e, src=src, sin=sin: e.tensor_tensor(out=tmp[1][:], in0=src[:, :, 32:64], in1=sin, op=ALU.mult),
                                 r=[gk, "rp"], w=[tk[1]])
                            p.op("dve", lambda e, src=src, cos=cos: e.tensor_tensor(out=tmp[2][:], in0=src[:, :, 32:64], in1=cos, op=ALU.mult),
                                 r=[gk, "rp"], w=[tk[2]])
                            p.op("dve", lambda e, src=src, sin=sin: e.tensor_tensor(out=tmp[3][:], in0=src[:, :, 0:32], in1=sin, op=ALU.mult),
                                 r=[gk, "rp"], w=[tk[3]])
                            p.op("pool", lambda e, dst=dst: e.tensor_tensor(out=dst[:, :, 0:32], in0=tmp[0][:], in1=tmp[1][:], op=ALU.subtract),
                                 r=[tk[0], tk[1]], w=[ok])
                            p.op("pool", lambda e, dst=dst: e.tensor_tensor(out=dst[:, :, 32:64], in0=tmp[2][:], in1=tmp[3][:], op=ALU.add),
                                 r=[tk[2], tk[3]], w=[ok])
                        p.dma("act", qkv_out[which, r0:r0 + 128, (cb % 2) * 512:(cb % 2 + 1) * 512], o_sb[:], r=[ok],
                              semkey="o_" + ok, store=True)
        p.emit()
    return nc


_NC_CACHE = {}


def _get(name, builder):
    if name not in _NC_CACHE:
        import time as _t
        t0 = _t.time()
        _NC_CACHE[name] = builder()
        print("[kernel] built", name, "in %.1fs" % (_t.time() - t0), flush=True)
    return _NC_CACHE[name]


def _run(nc, in_maps):
    return run_bass_kernel_spmd(nc, in_maps, core_ids=list(range(NCORES)))


def run_post(aT_full, xres_full, w_o, w_in, w_dn, lnp, w_qkv=None, rope=None):
    with_qkv = w_qkv is not None
    nc = _get("post_qkv" if with_qkv else "post", lambda: build_post(with_qkv))
    in_maps = []
    for c in range(NCORES):
        m = {
            "aT": np.ascontiguousarray(aT_full[:, c * TOK:(c + 1) * TOK]),
            "xres": np.ascontiguousarray(xres_full[c * TOK:(c + 1) * TOK]),
            "w_o": w_o, "w_in": w_in, "w_dn": w_dn, "lnp": lnp,
        }
        if with_qkv:
            m["w_qkv"] = w_qkv
            m["rope"] = np.ascontiguousarray(rope[c * TOK:(c + 1) * TOK])
        in_maps.append(m)
    res = _run(nc, in_maps)
    xo = np.concatenate([r["xout"] for r in res.results], axis=0)
    if with_qkv:
        qkv = np.concatenate([r["qkv"] for r in res.results], axis=1)
        return xo, qkv
    return xo, None


def rope_tables():
    inv = (10000.0 ** (-np.arange(0, DH, 2, dtype=np.float32) / DH)).astype(np.float32)
    ang = (np.arange(T, dtype=np.float32)[:, None] * inv[None, :]).astype(np.float32)
    cos, sin = np.cos(ang).astype(np.float32), np.sin(ang).astype(np.float32)
    s = np.float32(DH ** -0.5)
    return np.ascontiguousarray(np.stack([cos * s, sin * s, cos, sin], axis=1))


NBLK = T // 256
QG = 512
NQG = T // QG


def moba_consts():
    blk1h = np.zeros((64, T), np.float32)
    for b in range(NBLK):
        blk1h[b, b * 256:(b + 1) * 256] = 1.0
    n = np.arange(64)[:, None]
    b = np.arange(64)[None, :]
    p01 = (b < n).astype(np.float32)
    o01 = (b == n).astype(np.float32)
    pbias = np.where(b < n, 0.0, -1e9).astype(np.float32)
    tabs = np.stack([pbias, p01, o01], axis=0)
    dm = np.zeros((4, 128, 4, 128), np.float32)
    key = np.arange(128)[:, None]
    q = np.arange(128)[None, :]
    tri = np.where(key <= q, 0.0, NEG)
    for j in range(4):
        for g in range(4):
            if j > g:
                dm[j, :, g, :] = NEG
            elif j == g:
                dm[j, :, g, :] = tri
    return (blk1h.astype(ml_dtypes.bfloat16), tabs, dm.reshape(4, 128, 512).astype(ml_dtypes.bfloat16))


def build_attn():
    nc = bass.Bass("TRN2", target_bir_lowering=False)
    qT = nc.dram_tensor("qT", [128, T], F32, kind="ExternalInput").ap()
    kT = nc.dram_tensor("kT", [128, T], F32, kind="ExternalInput").ap()
    v = nc.dram_tensor("v", [T, 128], F32, kind="ExternalInput").ap()
    blk1h = nc.dram_tensor("blk1h", [64, T], BF16, kind="ExternalInput").ap()
    tabs = nc.dram_tensor("tabs", [3, 64 * 64], F32, kind="ExternalInput").ap()
    dmask = nc.dram_tensor("dmask", [4, 128, 512], BF16, kind="ExternalInput").ap()
    oT = nc.dram_tensor("oT", [128, T], BF16, kind="ExternalOutput").ap()

    es = ExitStack()
    sb = lambda name, shape, dt: es.enter_context(nc.sbuf_tensor(name, shape, dt))
    ps = lambda name, shape, dt: es.enter_context(nc.psum_tensor(name, shape, dt))
    with es:
        ident = sb("ident", [128, 128], BF16)
        ones_f = sb("ones_f", [128, 64], F32)
        kaug = sb("kaug", [128, T], BF16)
        qaug = sb("qaug", [128, T], BF16)
        vsb = sb("vsb", [128, 128, 2, 65], BF16)
        tb = sb("tb", [128, 3, 64 * 64], F32)
        stg = [sb("stg%d" % i, [128, 2048], F32) for i in range(2)]
        dm = sb("dm", [128, 4, 512], BF16)
        kmean = sb("kmean", [64, 64], F32)
        kmean_b = sb("kmean_b", [64, 64], BF16)
        gm = [sb("gm%d" % i, [128, 8, 64], F32) for i in range(2)]
        top8 = [sb("top8%d" % i, [128, 8, 8], F32) for i in range(2)]
        selt = [sb("selt%d" % i, [128, 8, 64], F32) for i in range(2)]
        negm = [sb("negm%d" % i, [128, 8, 64], BF16) for i in range(2)]
        pT = [sb("pT%d" % i, [128, QG], BF16) for i in range(3)]
        osb = [sb("osb%d" % i, [65, QG], F32) for i in range(2)]
        obf = [sb("obf%d" % i, [64, QG], BF16) for i in range(2)]
        s_ps = [ps("s_ps%d" % i, [128, QG], F32) for i in range(3)]
        o_ps = [ps("o_ps%d" % i, [65, QG], F32) for i in range(2)]
        g_ps = ps("g_ps", [128, 8, 64], F32)
        m_ps = ps("m_ps", [128, 8 * 128], BF16)
        bc_ps = m_ps[:].bitcast(F32)

        p = Prog(nc)
        p.op("pool", lambda e: e.memset(ident[:], 0.0), w=["ident"])
        p.op("pool", lambda e: e.affine_select(out=ident[:], in_=ident[:], pattern=[[-1, 128]],
                                               compare_op=ALU.not_equal, fill=1.0, base=0,
                                               channel_multiplier=1), r=["ident"], w=["ident"])
        p.op("pool", lambda e: e.memset(ones_f[:], 1.0), w=["ones_f"])
        p.op("pool", lambda e: e.memset(vsb[:, :, :, 64:65], 1.0), w=["vsb1"])
        p.dma("sp", tb[:], tabs.partition_broadcast(128), w=["tb"])
        p.dma("sp", dm[:], dmask.rearrange("j k q -> k j q"), w=["dm"])
        p.dma("sp", kaug[64:128, :], blk1h, w=["kaug_hi"])
        stg_n = [0]

        def stage(view_fn, src, cast_fn, wkeys):
            i = stg_n[0] % 2
            stg_n[0] += 1
            sk = "stg%d" % i
            sv = view_fn(stg[i])
            p.dma("sp", sv, src, w=[sk], semkey=sk)
            eng = "act" if (stg_n[0] % 2) else "dve"
            p.op(eng, lambda e, sv=sv, eng=eng: cast_fn(e, sv, eng == "act"), r=[sk], w=wkeys)

        gcount = 0
        for hh in range(2):
            for c8 in range(8):
                sl = slice(c8 * 2048, (c8 + 1) * 2048)
                for dst, src, key in ((kaug, kT, "kaug_lo"), (qaug, qT, "qaug_lo")):
                    stage(lambda t_: t_[0:64, :], src[hh * 64:(hh + 1) * 64, sl],
                          lambda e, sv, is_act, dst=dst, sl=sl: (e.activation(out=dst[0:64, sl], in_=sv, func=AF.Copy) if is_act
                                                        else e.tensor_copy(out=dst[0:64, sl], in_=sv)),
                          [key])
            if hh == 0:
                for c8 in range(8):
                    kb0 = c8 * 16
                    stage(lambda t_: t_[:].rearrange("p (kb c) -> p kb c", c=128),
                          v[kb0 * 128:(kb0 + 16) * 128, :].rearrange("(kb p) c -> p kb c", p=128),
                          lambda e, sv, is_act, kb0=kb0: (e.activation(out=vsb[:, kb0:kb0 + 16, :, 0:64], in_=sv.rearrange("p kb (h d) -> p kb h d", h=2), func=AF.Copy)
                                                  if is_act else
                                                  e.tensor_copy(out=vsb[:, kb0:kb0 + 16, :, 0:64], in_=sv.rearrange("p kb (h d) -> p kb h d", h=2))),
                          ["vsb"])

            p.op("dve", lambda e: e.tensor_reduce(out=kmean[:], in_=kaug[0:64, :].rearrange("p (b s) -> p b s", s=256),
                                                  axis=AX.X, op=ALU.add), r=["kaug_lo"], w=["kmean"])
            p.op("dve", lambda e: e.tensor_scalar(out=kmean_b[:], in0=kmean[:], scalar1=1.0 / 256, scalar2=None, op0=ALU.mult),
                 r=["kmean"], w=["kmean_b"])
            for G8 in range(T // 1024):
                i2 = gcount % 2
                gcount += 1
                n0 = 4 * G8
                for c in range(8):
                    q0 = G8 * 1024 + c * 128
                    p.op("pe", lambda e, c=c, q0=q0: e.matmul(g_ps[:, c, :], lhsT=qaug[0:64, q0:q0 + 128], rhs=kmean_b[:],
                                                            start=True, stop=True), r=["qaug_lo", "kmean_b"], w=["g_ps"])
                gmk, t8k, slk, ngk = "gm%d" % i2, "top8%d" % i2, "selt%d" % i2, "negm%d" % i2
                tbv = tb[:].rearrange("p t (n b) -> p t n b", b=64)
                b0, b1, b2 = [tbv[:, t, n0:n0 + 4, :].unsqueeze(2).to_broadcast([128, 4, 2, 64]) for t in range(3)]
                g4 = lambda tl: tl[:].rearrange("p (a c) b -> p a c b", c=2)
                gm4, sl4, gp4 = g4(gm[i2]), g4(selt[i2]), g_ps[:].rearrange("p (a c) b -> p a c b", c=2)
                p.op("dve", lambda e, gm4=gm4, gp4=gp4, b0=b0: e.tensor_tensor(out=gm4, in0=gp4, in1=b0, op=ALU.add),
                     r=["g_ps", "tb"], w=[gmk])
                for c in range(8):
                    p.op("dve", lambda e, c=c, i2=i2: e.max(out=top8[i2][:, c, :], in_=gm[i2][:, c, :]), r=[gmk], w=[t8k])
                p.op("dve", lambda e, i2=i2: e.tensor_tensor(out=selt[i2][:], in0=gm[i2][:],
                                                            in1=top8[i2][:, :, 2:3].to_broadcast([128, 8, 64]), op=ALU.is_ge),
                     r=[gmk, t8k], w=[slk])
                p.op("dve", lambda e, sl4=sl4, b1=b1: e.tensor_tensor(out=sl4, in0=sl4, in1=b1, op=ALU.mult),
                     r=[slk, "tb"], w=[slk])
                p.op("dve", lambda e, sl4=sl4, b2=b2: e.tensor_tensor(out=sl4, in0=sl4, in1=b2, op=ALU.add),
                     r=[slk, "tb"], w=[slk])
                p.op("dve", lambda e, i2=i2: e.tensor_scalar(out=negm[i2][:], in0=selt[i2][:], scalar1=-1.0, scalar2=-NEG,
                                                             op0=ALU.add, op1=ALU.mult), r=[slk], w=[ngk])
                for c in range(8):
                    p.op("pe", lambda e, c=c, i2=i2: e.transpose(m_ps[64:128, c * 128:(c + 1) * 128], negm[i2][:, c, :], ident[:],
                                                                tile_position=(0, 64)), r=[ngk, "ident"], w=["m_ps"])
                p.op("act", lambda e, G8=G8: e.activation(out=qaug[64:128, G8 * 1024:(G8 + 1) * 1024], in_=m_ps[64:128, :], func=AF.Copy),
                     r=["m_ps"], w=["qaug_hi"])
            for G in range(NQG):
                nkb = 4 * (G + 1)
                op_i = G % 2
                opk = "o_ps%d" % op_i
                qsl = slice(G * QG, (G + 1) * QG)

                def qk(kb, G=G, qsl=qsl):
                    si = kb % 3
                    diag = kb >= 4 * G
                    p.op("pe", lambda e: e.matmul(s_ps[si][:], lhsT=kaug[:, kb * 128:(kb + 1) * 128], rhs=qaug[:, qsl],
                                                  start=True, stop=not diag),
                         r=["kaug_lo", "kaug_hi", "qaug_lo", "qaug_hi"], w=["s_ps%d" % si])
                    if diag:
                        j = kb - 4 * G
                        p.op("pe", lambda e: e.matmul(s_ps[si][:], lhsT=ident[:], rhs=dm[:, j, :], start=False, stop=True),
                             r=["ident", "dm"], w=["s_ps%d" % si])

                def ex_pv(kb, G=G, nkb=nkb, op_i=op_i, opk=opk, hh=hh):
                    si = kb % 3
                    p.op("act", lambda e: e.activation(out=pT[si][:], in_=s_ps[si][:], func=AF.Exp),
                         r=["s_ps%d" % si], w=["pT%d" % si])
                    p.op("pe", lambda e: e.matmul(o_ps[op_i][:], lhsT=vsb[:, kb, hh, :], rhs=pT[si][:],
                                                  start=(kb == 0), stop=(kb == nkb - 1)),
                         r=["vsb", "vsb1", "pT%d" % si], w=[opk])

                LOOK = 2
                for kb in range(min(LOOK, nkb)):
                    qk(kb)
                for kb in range(nkb):
                    if kb + LOOK < nkb:
                        qk(kb + LOOK)
                    ex_pv(kb)
                ob_i = G % 2
                p.op("dve", lambda e, op_i=op_i, ob_i=ob_i: e.tensor_copy(out=osb[ob_i][:], in_=o_ps[op_i][:]), r=[opk], w=["osb%d" % ob_i])
                p.op("dve", lambda e, ob_i=ob_i: e.reciprocal(out=osb[ob_i][64:65, :], in_=osb[ob_i][64:65, :]),
                     r=["osb%d" % ob_i], w=["osb%d" % ob_i])
                p.op("pe", lambda e, ob_i=ob_i: e.matmul(bc_ps[0:64, :], lhsT=ones_f[64:65, :], rhs=osb[ob_i][64:65, :], start=True, stop=True),
                     r=["ones_f", "osb%d" % ob_i], w=["m_ps"])
                p.op("dve", lambda e, ob_i=ob_i: e.tensor_tensor(out=obf[ob_i][:], in0=osb[ob_i][0:64, :], in1=bc_ps[0:64, :], op=ALU.mult),
                     r=["osb%d" % ob_i, "m_ps"], w=["obf%d" % ob_i])
                p.dma("sp", oT[hh * 64:(hh + 1) * 64, qsl], obf[ob_i][:], r=["obf%d" % ob_i], semkey="o_obf%d" % ob_i, store=True)
        p.emit()
    return nc


def run_attn(qkv):
    nc = _get("attn", build_attn)
    blk1h, tabs, dm = moba_consts()
    tabs = np.ascontiguousarray(tabs.reshape(3, 64 * 64))
    in_maps = []
    for c in range(NCORES):
        cs = slice(c * 128, (c + 1) * 128)
        in_maps.append({
            "qT": np.ascontiguousarray(qkv[0][:, cs].T),
            "kT": np.ascontiguousarray(qkv[1][:, cs].T),
            "v": np.ascontiguousarray(qkv[2][:, cs]),
            "blk1h": blk1h, "tabs": tabs, "dmask": dm,
        })
    res = _run(nc, in_maps)
    return np.concatenate([r["oT"] for r in res.results], axis=0)


LCH = 64
TT = 512
NCH = TT // LCH
WCOLS = 672
DECAY_C = -math.exp(-0.5)


def rwkv_consts():
    j = np.arange(64)[:, None]
    t = np.arange(64)[None, :]
    incl = (j <= t).astype(np.float32)
    strict = (j < t).astype(np.float32)
    rev = (j > t).astype(np.float32)
    tri3 = (DECAY_C * np.concatenate([incl, strict, rev], axis=1)).astype(np.float32)
    tri3 = np.concatenate([tri3, tri3], axis=0)
    up_s = (j < t).astype(np.float32)
    up_i = (j <= t).astype(np.float32)
    lo_s = (t < j).astype(np.float32)
    msk = np.concatenate([up_s, up_i, up_s, up_i, lo_s], axis=1)
    msk = np.concatenate([msk, msk], axis=0)
    i2 = np.concatenate([np.eye(64, dtype=np.float32)] * 2, axis=0)
    onesbd = np.zeros((128, 128), np.float32)
    onesbd[:64, :64] = 1.0
    onesbd[64:, 64:] = 1.0
    return tri3, msk, i2, onesbd


def build_rwkv(ntiles, debug=False):
    TL = ntiles * TT
    nc = bass.Bass("TRN2", target_bir_lowering=False)
    xT = nc.dram_tensor("xT", [C, TL + 1], F32, kind="ExternalInput").ap()
    wbig = nc.dram_tensor("wbig", [C, WCOLS], F32, kind="ExternalInput").ap()
    mu6 = nc.dram_tensor("mu6", [C, 6], F32, kind="ExternalInput").ap()
    w2a = nc.dram_tensor("w2a", [65, 128], F32, kind="ExternalInput").ap()
    a2p = nc.dram_tensor("a2p", [128, 128], F32, kind="ExternalInput").ap()
    g2p = nc.dram_tensor("g2p", [160, 128], F32, kind="ExternalInput").ap()
    vecs = nc.dram_tensor("vecs", [128, 8], F32, kind="ExternalInput").ap()
    tri3_d = nc.dram_tensor("tri3", [128, 192], F32, kind="ExternalInput").ap()
    msk_d = nc.dram_tensor("msk", [128, 320], F32, kind="ExternalInput").ap()
    i2_d = nc.dram_tensor("i2", [128, 64], F32, kind="ExternalInput").ap()
    obd_d = nc.dram_tensor("onesbd", [128, 128], F32, kind="ExternalInput").ap()
    ygT = nc.dram_tensor("ygT", [128, TL], BF16, kind="ExternalOutput").ap()
    if debug:
        dbg = nc.dram_tensor("dbg", [24, 128, 512], F32, kind="ExternalOutput").ap()

    es = ExitStack()
    sb = lambda name, shape, dt: es.enter_context(nc.sbuf_tensor(name, shape, dt))
    with es:
        ident_b = sb("ident_b", [128, 128], BF16)
        ident_f = sb("ident_f", [128, 128], F32)
        wf = sb("wf", [128, 8, WCOLS], F32)
        wc = sb("wc", [128, 8, WCOLS], BF16)
        wp = sb("wp", [128, 8, WCOLS], BF16)
        mu_sb = sb("mu_sb", [128, 8, 6], F32)
        w2a_b = sb("w2a_b", [65, 128], BF16)
        a2_b = sb("a2_b", [128, 128], BF16)
        g2a_b = sb("g2a_b", [128, 128], BF16)
        g2b_b = sb("g2b_b", [32, 128], BF16)
        vec = sb("vec", [128, 8], F32)
        tri3 = sb("tri3_s", [128, 192], F32)
        msk = sb("msk_s", [128, 320], F32)
        i2 = sb("i2_s", [128, 64], F32)
        obd_f = sb("obd_f", [128, 128], F32)
        obd_b = sb("obd_b", [128, 128], BF16)
        xb = [sb("xb%d" % i, [128, 8, TT + 1], BF16) for i in range(2)]
        r_f = sb("r_f", [128, TT], F32)
        k_f = sb("k_f", [128, TT], F32)
        v_f = sb("v_f", [128, TT], F32)
        v_b = sb("v_b", [128, TT], BF16)
        twa = sb("twa", [65, TT], BF16)
        a1o = sb("a1o", [128, TT], BF16)
        sg_a = sb("sg_a", [128, TT], BF16)
        sg_b = sb("sg_b", [32, TT], BF16)
        g_f = sb("g_f", [128, TT], F32)
        al_f = sb("al_f", [128, TT], F32)
        kkr = sb("kkr", [128, TT], F32)
        sq_b = sb("sq_b", [128, TT], BF16)
        rn = sb("rn", [128, TT], F32)
        kk_f = sb("kk_f", [128, TT], F32)
        kt_f = sb("kt_f", [128, TT], F32)
        b_f = sb("b_f", [128, TT], F32)
        tmp1 = sb("tmp1", [128, TT], F32)
        rk_b = sb("rk_b", [128, TT], BF16)
        bon_f = sb("bon_f", [128, TT], F32)
        sgw = sb("sgw", [128, NCH, 128], F32)
        e_pos = sb("e_pos", [128, NCH, 64], F32)
        e_neg = sb("e_neg", [128, NCH, 64], F32)
        e_ex = sb("e_ex", [128, NCH, 64], F32)
        e_rem = sb("e_rem", [128, NCH, 64], F32)
        AR = sb("AR", [128, NCH, 2, 64], BF16)
        Rh_f = sb("Rh_f", [128, TT], F32)
        BhT = sb("BhT", [128, TT], BF16)
        KhT = sb("KhT", [128, TT], BF16)
        BbT = sb("BbT", [128, TT], BF16)
        KbT = sb("KbT", [128, TT], BF16)
        TM = sb("TM", [128, NCH, 4, 64], BF16)
        GM = sb("GM", [128, NCH, 320], BF16)
        Xf = sb("Xf", [128, NCH, 128], F32)
        Xb = sb("Xb", [128, NCH, 128], BF16)
        PP = [sb("PP%d" % i, [128, NCH, 128], BF16) for i in range(2)]
        RtT = sb("RtT", [128, TT], BF16)
        Y0 = sb("Y0", [128, NCH, 64], F32)
        DG = sb("DG", [128, NCH, 64], F32)
        PT_f = sb("PT_f", [128, NCH, 64], F32)
        Q_f = sb("Q_f", [128, NCH, 64], F32)
        S_f = [sb("S_f%d" % i, [128, 64], F32) for i in range(2)]
        S_b = [sb("S_b%d" % i, [128, 64], BF16) for i in range(2)]
        y_tm = sb("y_tm", [128, NCH, 64], F32)
        yT_f = sb("yT_f", [128, TT], F32)
        d_f = sb("d_f", [128, TT], F32)
        sq_f = sb("sq_f", [128, TT], F32)
        rstd = sb("rstd", [128, TT], F32)
        yo = [sb("yo%d" % i, [128, TT], BF16) for i in range(2)]
        bank = [es.enter_context(nc.psum_tensor("bank%d" % i, [128, 512], F32)) for i in range(8)]
        bk = lambda i: "bank%d" % i

        p = Prog(nc)
        for idt, nm in ((ident_b, "ident_b"), (ident_f, "ident_f")):
            p.op("pool", lambda e, idt=idt: e.memset(idt[:], 0.0), w=[nm])
            p.op("pool", lambda e, idt=idt: e.affine_select(out=idt[:], in_=idt[:], pattern=[[-1, 128]],
                                                         compare_op=ALU.not_equal, fill=1.0, base=0,
                                                         channel_multiplier=1), r=[nm], w=[nm])
        p.op("pool", lambda e: e.memset(twa[64:65, :], 1.0), w=["twa1"])
        p.op("pool", lambda e: e.memset(S_f[0][:], 0.0), w=["S_f0"])
        p.op("pool", lambda e: e.memset(S_b[0][:], 0.0), w=["S_b0"])
        for kc in range(8):
            p.dma("sp", wf[:, kc, :], wbig[kc * 128:(kc + 1) * 128, :], w=["wf"], semkey="wf")
        p.dma("sp", mu_sb[:], mu6.rearrange("(k p) n -> p k n", p=128), w=["mu"])
        p.dma("pool", w2a_b[:], w2a, w=["w2a_b"])
        p.dma("pool", a2_b[:], a2p, w=["a2_b"])
        p.dma("pool", g2a_b[:], g2p[0:128, :], w=["g2a_b"])
        p.dma("pool", g2b_b[:], g2p[128:160, :], w=["g2b_b"])
        p.dma("pool", obd_b[:], obd_d, w=["obd_b"])
        p.dma("sp", obd_f[:], obd_d, w=["obd_f"])
        p.dma("sp", vec[:], vecs, w=["vec"])
        p.dma("sp", tri3[:], tri3_d, w=["tri3"])
        p.dma("sp", msk[:], msk_d, w=["msk"])
        p.dma("sp", i2[:], i2_d, w=["i2"])
        groups = [(0, 128, 0), (128, 256, 2), (256, 384, 3), (384, 448, 1), (448, 512, 4), (512, 672, 5)]
        for kc in range(8):
            for (c0, c1, n) in groups:
                eng = "dve" if (kc % 2 == 0) else "pool"
                p.op(eng, lambda e, kc=kc, c0=c0, c1=c1, n=n: e.tensor_scalar(
                    out=wp[:, kc, c0:c1], in0=wf[:, kc, c0:c1], scalar1=mu_sb[:, kc, n:n + 1], scalar2=None, op0=ALU.mult),
                    r=["wf", "mu"], w=["wp"])
            p.op("dve" if (kc % 2 == 0) else "pool",
                 lambda e, kc=kc: e.tensor_tensor(out=wc[:, kc, :], in0=wf[:, kc, :], in1=wp[:, kc, :], op=ALU.subtract),
                 r=["wf", "wp"], w=["wc"])

        V_KK, V_KA, V_1KA, V_RK, V_GG, V_GB, V_A0 = range(7)
        vcol = lambda i: vec[:, i:i + 1]
        tp = lambda h: (64 * h, 64 * h)
        hs = lambda h: slice(64 * h, 64 * h + 64)
        s_cur = 0

        def load_x(tj):
            for kc in range(8):
                p.dma("pool", xb[tj % 2][:, kc, :], xT[kc * 128:(kc + 1) * 128, tj * TT:tj * TT + TT + 1],
                      w=["xb%d" % (tj % 2)], semkey="xb%d" % (tj % 2))

        load_x(0)
        for ti in range(ntiles):
            t0 = ti * TT
            x_sb = xb[ti % 2]
            xk = "xb%d" % (ti % 2)
            if ti + 1 < ntiles:
                load_x(ti + 1)

            def proj(c0, c1, bi):
                m = c1 - c0
                for kc in range(8):
                    p.op("pe", lambda e, kc=kc, x_sb=x_sb: e.matmul(bank[bi][0:m, :], lhsT=wc[:, kc, c0:c1], rhs=x_sb[:, kc, 1:TT + 1],
                                                        start=(kc == 0), stop=False), r=[xk, "wc"], w=[bk(bi)])
                for kc in range(8):
                    p.op("pe", lambda e, kc=kc, x_sb=x_sb: e.matmul(bank[bi][0:m, :], lhsT=wp[:, kc, c0:c1], rhs=x_sb[:, kc, 0:TT],
                                                        start=False, stop=(kc == 7)), r=[xk, "wp"], w=[bk(bi)])

            proj(0, 128, 0)
            p.op("act", lambda e: e.activation(out=r_f[:], in_=bank[0][:], func=AF.Copy), r=[bk(0)], w=["r_f"])
            proj(128, 256, 1)
            p.op("act", lambda e: e.activation(out=k_f[:], in_=bank[1][:], func=AF.Copy), r=[bk(1)], w=["k_f"])
            proj(256, 384, 2)
            p.op("act", lambda e: e.activation(out=v_f[:], in_=bank[2][:], func=AF.Copy), r=[bk(2)], w=["v_f"])
            p.op("act", lambda e: e.activation(out=v_b[:], in_=bank[2][:], func=AF.Copy), r=[bk(2)], w=["v_b"])
            proj(384, 512, 3)
            p.op("act", lambda e: e.activation(out=twa[0:64, :], in_=bank[3][0:64, :], func=AF.Tanh), r=[bk(3)], w=["twa"])
            p.op("dve", lambda e: e.tensor_copy(out=a1o[64:128, :], in_=bank[3][64:128, :]), r=[bk(3)], w=["a1o"])
            proj(512, 640, 4)
            p.op("act", lambda e: e.activation(out=sg_a[:], in_=bank[4][:], func=AF.Sigmoid), r=[bk(4)], w=["sg_a"])
            proj(640, 672, 5)
            p.op("act", lambda e: e.activation(out=sg_b[:], in_=bank[5][0:32, :], func=AF.Sigmoid), r=[bk(5)], w=["sg_b"])
            p.op("pe", lambda e: e.matmul(bank[6][:], lhsT=g2a_b[:], rhs=sg_a[:], start=True, stop=False), r=["g2a_b", "sg_a"], w=[bk(6)])
            p.op("pe", lambda e: e.matmul(bank[6][:], lhsT=g2b_b[:], rhs=sg_b[:], start=False, stop=True), r=["g2b_b", "sg_b"], w=[bk(6)])
            p.op("act", lambda e: e.activation(out=g_f[:], in_=bank[6][:], func=AF.Copy), r=[bk(6)], w=["g_f"])
            p.op("pe", lambda e: e.matmul(bank[7][:], lhsT=a2_b[64:128, :], rhs=a1o[64:128, :], start=True, stop=True),
                 r=["a2_b", "a1o"], w=[bk(7)])
            p.op("act", lambda e: e.activation(out=al_f[:], in_=bank[7][:], func=AF.Sigmoid, bias=vcol(V_A0)), r=[bk(7), "vec"], w=["al_f"])
            p.op("dve", lambda e: e.tensor_scalar(out=kkr[:], in0=k_f[:], scalar1=vcol(V_KK), scalar2=None, op0=ALU.mult), r=["k_f", "vec"], w=["kkr"])
            p.op("act", lambda e: e.activation(out=sq_b[:], in_=kkr[:], func=AF.Square), r=["kkr"], w=["sq_b"])
            p.op("pe", lambda e: e.matmul(bank[0][:], lhsT=obd_b[:], rhs=sq_b[:], start=True, stop=True), r=["obd_b", "sq_b"], w=[bk(0)])
            p.op("dve", lambda e: e.tensor_scalar(out=rn[:], in0=bank[0][:], scalar1=1e-24, scalar2=None, op0=ALU.add), r=[bk(0)], w=["rn"])
            p.op("act", lambda e: e.activation(out=rn[:], in_=rn[:], func=AF.Sqrt), r=["rn"], w=["rn"])
            p.op("dve", lambda e: e.reciprocal(out=rn[:], in_=rn[:]), r=["rn"], w=["rn"])
            p.op("dve", lambda e: e.tensor_tensor(out=kk_f[:], in0=kkr[:], in1=rn[:], op=ALU.mult), r=["kkr", "rn"], w=["kk_f"])
            p.op("pool", lambda e: e.tensor_scalar(out=tmp1[:], in0=al_f[:], scalar1=-1.0, scalar2=vcol(V_KA), op0=ALU.add, op1=ALU.mult),
                 r=["al_f", "vec"], w=["tmp1"])
            p.op("dve", lambda e: e.scalar_tensor_tensor(out=kt_f[:], in0=tmp1[:], scalar=1.0, in1=k_f[:], op0=ALU.add, op1=ALU.mult),
                 r=["k_f", "tmp1"], w=["kt_f"])
            p.op("dve", lambda e: e.tensor_tensor(out=b_f[:], in0=kk_f[:], in1=al_f[:], op=ALU.mult), r=["kk_f", "al_f"], w=["b_f"])
            p.op("dve", lambda e: e.scalar_tensor_tensor(out=rk_b[:], in0=r_f[:], scalar=vcol(V_RK), in1=kt_f[:], op0=ALU.mult, op1=ALU.mult),
                 r=["r_f", "kt_f", "vec"], w=["rk_b"])
            p.op("pe", lambda e: e.matmul(bank[1][:], lhsT=obd_b[:], rhs=rk_b[:], start=True, stop=True), r=["obd_b", "rk_b"], w=[bk(1)])
            p.op("dve", lambda e: e.tensor_tensor(out=bon_f[:], in0=v_f[:], in1=bank[1][:], op=ALU.mult), r=["v_f", bk(1)], w=["bon_f"])

            for c in range(NCH):
                p.op("pe", lambda e, c=c: e.matmul(bank[2 + c // 4][0:64, (c % 4) * 128:(c % 4 + 1) * 128],
                                                   lhsT=twa[:, c * 64:(c + 1) * 64], rhs=w2a_b[:], start=True, stop=True),
                     r=["twa", "twa1", "w2a_b"], w=[bk(2 + c // 4)])
            for hf in range(2):
                p.op("act", lambda e, hf=hf: e.activation(out=sgw[0:64, hf * 4:(hf + 1) * 4, :],
                                                         in_=bank[2 + hf][0:64, :].rearrange("p (c d) -> p c d", c=4), func=AF.Sigmoid),
                     r=[bk(2 + hf)], w=["sgw"])
            for c in range(NCH):
                for kind in range(3):
                    p.op("pe", lambda e, c=c, kind=kind: e.matmul(bank[4 + kind][:, c * 64:(c + 1) * 64], lhsT=sgw[0:64, c, :],
                                                                 rhs=tri3[0:64, kind * 64:(kind + 1) * 64], start=True, stop=True),
                         r=["sgw", "tri3"], w=[bk(4 + kind)])
            v3 = lambda tl: tl[:].rearrange("p c t -> p (c t)")
            p.op("act", lambda e: e.activation(out=v3(e_pos), in_=bank[4][:], func=AF.Exp), r=[bk(4)], w=["e_pos"])
            p.op("act", lambda e: e.activation(out=v3(e_neg), in_=bank[4][:], func=AF.Exp, scale=-1.0), r=[bk(4)], w=["e_neg"])
            p.op("act", lambda e: e.activation(out=v3(e_ex), in_=bank[5][:], func=AF.Exp), r=[bk(5)], w=["e_ex"])
            p.op("act", lambda e: e.activation(out=v3(e_rem), in_=bank[6][:], func=AF.Exp), r=[bk(6)], w=["e_rem"])
            c3 = lambda ap2: ap2.rearrange("p (c t) -> p c t", c=NCH)
            p.op("dve", lambda e: e.scalar_tensor_tensor(out=AR[:, :, 0, :], in0=c3(kk_f[:]), scalar=-1.0, in1=e_ex[:], op0=ALU.mult, op1=ALU.mult),
                 r=["kk_f", "e_ex"], w=["AR"])
            p.op("dve", lambda e: e.tensor_tensor(out=Rh_f[:], in0=r_f[:], in1=v3(e_pos), op=ALU.mult), r=["r_f", "e_pos"], w=["Rh_f"])
            p.op("act", lambda e: e.activation(out=AR[:, :, 1, :], in_=c3(Rh_f[:]), func=AF.Copy), r=["Rh_f"], w=["AR"])
            p.op("dve", lambda e: e.tensor_tensor(out=BhT[:], in0=b_f[:], in1=v3(e_neg), op=ALU.mult), r=["b_f", "e_neg"], w=["BhT"])
            p.op("pool", lambda e: e.tensor_tensor(out=KhT[:], in0=kt_f[:], in1=v3(e_neg), op=ALU.mult), r=["kt_f", "e_neg"], w=["KhT"])
            p.op("dve", lambda e: e.tensor_tensor(out=BbT[:], in0=b_f[:], in1=v3(e_rem), op=ALU.mult), r=["b_f", "e_rem"], w=["BbT"])
            p.op("pool", lambda e: e.tensor_tensor(out=KbT[:], in0=kt_f[:], in1=v3(e_rem), op=ALU.mult), r=["kt_f", "e_rem"], w=["KbT"])

            tmb = [bank[0][:].bitcast(BF16), bank[1][:].bitcast(BF16)]
            srcs = [(lambda c: AR[:, c, 0, :], "AR"), (lambda c: v_b[:, c * 64:(c + 1) * 64], "v_b"),
                    (lambda c: BbT[:, c * 64:(c + 1) * 64], "BbT"), (lambda c: KbT[:, c * 64:(c + 1) * 64], "KbT")]
            for c in range(NCH):
                for si, (sf, sk) in enumerate(srcs):
                    for h in range(2):
                        col = (c % 4) * 256 + si * 64
                        p.op("pe", lambda e, c=c, sf=sf, h=h, col=col: e.transpose(
                            tmb[c // 4][hs(h), col:col + 64], sf(c)[hs(h), :], ident_b[hs(h), hs(h)], tile_position=tp(h)),
                            r=[sk, "ident_b"], w=[bk(c // 4)])
            for hf in range(2):
                p.op("act" if hf == 0 else "dve",
                     (lambda e, hf=hf: e.activation(out=TM[:, hf * 4:(hf + 1) * 4, :, :].rearrange("p c s t -> p (c s t)"), in_=tmb[hf], func=AF.Copy))
                     if hf == 0 else
                     (lambda e, hf=hf: e.tensor_copy(out=TM[:, hf * 4:(hf + 1) * 4, :, :].rearrange("p c s t -> p (c s t)"), in_=tmb[hf])),
                     r=[bk(hf)], w=["TM"])

            for c in range(NCH):
                bi = 2 + (c % 2)
                cs_ = slice(c * 64, (c + 1) * 64)
                for h in range(2):
                    arh = AR[hs(h), c, :, :].rearrange("p s t -> p (s t)")
                    p.op("pe", lambda e, h=h, arh=arh, cs_=cs_, bi=bi: e.matmul(bank[bi][hs(h), 0:128], lhsT=BhT[hs(h), cs_], rhs=arh,
                                                                           start=True, stop=True, tile_position=tp(h)),
                         r=["BhT", "AR"], w=[bk(bi)])
                    p.op("pe", lambda e, h=h, arh=arh, cs_=cs_, bi=bi: e.matmul(bank[bi][hs(h), 128:256], lhsT=KhT[hs(h), cs_], rhs=arh,
                                                                           start=True, stop=True, tile_position=tp(h)),
                         r=["KhT", "AR"], w=[bk(bi)])
                    p.op("pe", lambda e, h=h, c=c, cs_=cs_, bi=bi: e.matmul(bank[bi][hs(h), 256:320], lhsT=AR[hs(h), c, 0, :], rhs=BhT[hs(h), cs_],
                                                                       start=True, stop=True, tile_position=tp(h)),
                         r=["BhT", "AR"], w=[bk(bi)])
                p.op("dve", lambda e, c=c, bi=bi: e.tensor_tensor(out=GM[:, c, :], in0=bank[bi][:, 0:320], in1=msk[:], op=ALU.mult),
                     r=[bk(bi), "msk"], w=["GM"])

            for c in range(NCH):
                for h in range(2):
                    p.op("pe", lambda e, c=c, h=h: e.matmul(bank[4][hs(h), c * 64:(c + 1) * 64], lhsT=GM[hs(h), c, 128:192], rhs=TM[hs(h), c, 1, :],
                                                           start=True, stop=True, tile_position=tp(h)), r=["GM", "TM"], w=[bk(4)])
            p.op("pool", lambda e: e.tensor_copy(out=Xf[:, :, 0:64], in_=TM[:, :, 0, :]), r=["TM"], w=["Xf"])
            p.op("dve", lambda e: e.tensor_copy(out=Xf[:, :, 64:128], in_=bank[4][:].rearrange("p (c t) -> p c t", c=NCH)), r=[bk(4)], w=["Xf"])
            p.op("act", lambda e: e.activation(out=Xb[:], in_=Xf[:], func=AF.Copy), r=["Xf"], w=["Xb"])

            NLEV = 6
            for lev in range(NLEV):
                if lev == 0:
                    P_of = lambda c: GM[:, c, 256:320]
                    PT_of = lambda c: GM[:, c, 0:64]
                    pk = "GM"
                else:
                    ppt = PP[(lev - 1) % 2]
                    P_of = lambda c, ppt=ppt: ppt[:, c, 0:64]
                    PT_of = lambda c, ppt=ppt: ppt[:, c, 64:128]
                    pk = "PP%d" % ((lev - 1) % 2)
                for c in range(NCH):
                    bi = 0 + c // 4
                    for h in range(2):
                        p.op("pe", lambda e, c=c, h=h, bi=bi, PT_of=PT_of: e.matmul(
                            bank[bi][hs(h), (c % 4) * 128:(c % 4 + 1) * 128], lhsT=PT_of(c)[hs(h), :], rhs=Xb[hs(h), c, :],
                            start=True, stop=True, tile_position=tp(h)), r=[pk, "Xb"], w=[bk(bi)])
                if lev < NLEV - 1:
                    for c in range(NCH):
                        bi = 2 + c // 4
                        for h in range(2):
                            p.op("pe", lambda e, c=c, h=h, bi=bi, P_of=P_of, PT_of=PT_of: e.matmul(
                                bank[bi][hs(h), (c % 4) * 128:(c % 4) * 128 + 64], lhsT=PT_of(c)[hs(h), :], rhs=P_of(c)[hs(h), :],
                                start=True, stop=True, tile_position=tp(h)), r=[pk], w=[bk(bi)])
                            p.op("pe", lambda e, c=c, h=h, bi=bi, P_of=P_of, PT_of=PT_of: e.matmul(
                                bank[bi][hs(h), (c % 4) * 128 + 64:(c % 4 + 1) * 128], lhsT=P_of(c)[hs(h), :], rhs=PT_of(c)[hs(h), :],
                                start=True, stop=True, tile_position=tp(h)), r=[pk], w=[bk(bi)])
                for hf in range(2):
                    xs = Xf[:, hf * 4:(hf + 1) * 4, :].rearrange("p c t -> p (c t)")
                    p.op("dve", lambda e, hf=hf, xs=xs: e.tensor_tensor(out=xs, in0=xs, in1=bank[hf][:], op=ALU.add), r=["Xf", bk(hf)], w=["Xf"])
                p.op("act", lambda e: e.activation(out=Xb[:], in_=Xf[:], func=AF.Copy), r=["Xf"], w=["Xb"])
                if lev < NLEV - 1:
                    ppn = PP[lev % 2]
                    for hf in range(2):
                        p.op("act", lambda e, hf=hf, ppn=ppn: e.activation(out=ppn[:, hf * 4:(hf + 1) * 4, :].rearrange("p c t -> p (c t)"),
                                                                          in_=bank[2 + hf][:], func=AF.Copy),
                             r=[bk(2 + hf)], w=["PP%d" % (lev % 2)])

            for c in range(NCH):
                for h in range(2):
                    p.op("pe", lambda e, c=c, h=h: e.matmul(bank[4][hs(h), c * 64:(c + 1) * 64], lhsT=Xb[hs(h), c, 0:64], rhs=GM[hs(h), c, 64:128],
                                                           start=True, stop=True, tile_position=tp(h)), r=["Xb", "GM"], w=[bk(4)])
                    p.op("pe", lambda e, c=c, h=h: e.matmul(bank[5][hs(h), c * 64:(c + 1) * 64], lhsT=GM[hs(h), c, 64:128], rhs=Xb[hs(h), c, 64:128],
                                                           start=True, stop=False, tile_position=tp(h)), r=["Xb", "GM"], w=[bk(5)])
                    p.op("pe", lambda e, c=c, h=h: e.matmul(bank[5][hs(h), c * 64:(c + 1) * 64], lhsT=GM[hs(h), c, 192:256], rhs=TM[hs(h), c, 1, :],
                                                           start=False, stop=True, tile_position=tp(h)), r=["TM", "GM"], w=[bk(5)])
                    p.op("pe", lambda e, c=c, h=h: e.matmul(bank[6][hs(h), c * 64:(c + 1) * 64], lhsT=Xb[hs(h), c, 0:64], rhs=TM[hs(h), c, 2, :],
                                                           start=True, stop=True, tile_position=tp(h)), r=["Xb", "TM"], w=[bk(6)])
                    p.op("pe", lambda e, c=c, h=h: e.matmul(bank[7][hs(h), c * 64:(c + 1) * 64], lhsT=TM[hs(h), c, 2, :], rhs=Xb[hs(h), c, 64:128],
                                                           start=True, stop=False, tile_position=tp(h)), r=["Xb", "TM"], w=[bk(7)])
                    p.op("pe", lambda e, c=c, h=h: e.matmul(bank[7][hs(h), c * 64:(c + 1) * 64], lhsT=TM[hs(h), c, 3, :], rhs=TM[hs(h), c, 1, :],
                                                           start=False, stop=True, tile_position=tp(h)), r=["TM"], w=[bk(7)])
            p.op("dve", lambda e: e.tensor_tensor(out=RtT[:], in0=bank[4][:], in1=Rh_f[:], op=ALU.add), r=[bk(4), "Rh_f"], w=["RtT"])
            p.op("act", lambda e: e.activation(out=v3(Y0), in_=bank[5][:], func=AF.Copy), r=[bk(5)], w=["Y0"])
            p.op("pool", lambda e: e.tensor_tensor(out=DG[:], in0=i2[:].unsqueeze(1).to_broadcast([128, NCH, 64]),
                                                  in1=e_pos[:, :, 63:64].to_broadcast([128, NCH, 64]), op=ALU.mult), r=["i2", "e_pos"], w=["DG"])
            p.op("dve", lambda e: e.tensor_tensor(out=v3(PT_f), in0=bank[6][:], in1=v3(DG), op=ALU.add), r=[bk(6), "DG"], w=["PT_f"])
            p.op("act", lambda e: e.activation(out=v3(Q_f), in_=bank[7][:], func=AF.Copy), r=[bk(7)], w=["Q_f"])

            for c in range(NCH):
                sn = 1 - s_cur
                for h in range(2):
                    p.op("pe", lambda e, c=c, h=h, s_cur=s_cur: e.matmul(bank[1][hs(h), (c % 2) * 64:(c % 2 + 1) * 64], lhsT=PT_f[hs(h), c, :],
                                                                        rhs=S_f[s_cur][hs(h), :], start=True, stop=True, tile_position=tp(h)),
                         r=["PT_f", "S_f%d" % s_cur, bk(1)], w=[bk(1) + ("a" if c % 2 else "b")])
                p.op("dve", lambda e, c=c, sn=sn: e.tensor_tensor(out=S_f[sn][:], in0=bank[1][:, (c % 2) * 64:(c % 2 + 1) * 64], in1=Q_f[:, c, :], op=ALU.add),
                     r=[bk(1) + ("a" if c % 2 else "b"), bk(1), "Q_f"], w=["S_f%d" % sn])
                for h in range(2):
                    p.op("pe", lambda e, c=c, h=h, s_cur=s_cur: e.matmul(bank[0][hs(h), c * 64:(c + 1) * 64], lhsT=RtT[hs(h), c * 64:(c + 1) * 64],
                                                                        rhs=S_b[s_cur][hs(h), :], start=True, stop=True, tile_position=tp(h)),
                         r=["RtT", "S_b%d" % s_cur], w=[bk(0)])
                p.op("act", lambda e, sn=sn: e.activation(out=S_b[sn][:], in_=S_f[sn][:], func=AF.Copy), r=["S_f%d" % sn], w=["S_b%d" % sn])
                s_cur = sn
            p.op("dve", lambda e: e.tensor_tensor(out=v3(y_tm), in0=bank[0][:], in1=v3(Y0), op=ALU.add), r=[bk(0), "Y0"], w=["y_tm"])

            for c in range(NCH):
                for h in range(2):
                    p.op("pe", lambda e, c=c, h=h: e.matmul(bank[2][hs(h), c * 64:(c + 1) * 64], lhsT=y_tm[hs(h), c, :], rhs=ident_f[hs(h), hs(h)],
                                                           start=True, stop=True, tile_position=tp(h)), r=["y_tm", "ident_f"], w=[bk(2)])
            p.op("act", lambda e: e.activation(out=yT_f[:], in_=bank[2][:], func=AF.Copy), r=[bk(2)], w=["yT_f"])
            p.op("pe", lambda e: e.matmul(bank[3][:], lhsT=obd_f[:], rhs=yT_f[:], start=True, stop=True), r=["obd_f", "yT_f"], w=[bk(3)])
            p.op("dve", lambda e: e.scalar_tensor_tensor(out=d_f[:], in0=bank[3][:], scalar=-1.0 / 64, in1=yT_f[:], op0=ALU.mult, op1=ALU.add),
                 r=[bk(3), "yT_f"], w=["d_f"])
            p.op("act", lambda e: e.activation(out=sq_f[:], in_=d_f[:], func=AF.Square), r=["d_f"], w=["sq_f"])
            p.op("pe", lambda e: e.matmul(bank[4][:], lhsT=obd_f[:], rhs=sq_f[:], start=True, stop=True), r=["obd_f", "sq_f"], w=[bk(4)])
            p.op("dve", lambda e: e.tensor_scalar(out=rstd[:], in0=bank[4][:], scalar1=1.0 / 64, scalar2=GN_EPS, op0=ALU.mult, op1=ALU.add),
                 r=[bk(4)], w=["rstd"])
            p.op("act", lambda e: e.activation(out=rstd[:], in_=rstd[:], func=AF.Sqrt), r=["rstd"], w=["rstd"])
            p.op("dve", lambda e: e.reciprocal(out=rstd[:], in_=rstd[:]), r=["rstd"], w=["rstd"])
            p.op("dve", lambda e: e.tensor_tensor(out=d_f[:], in0=d_f[:], in1=rstd[:], op=ALU.mult), r=["d_f", "rstd"], w=["d_f"])
            p.op("act", lambda e: e.activation(out=d_f[:], in_=d_f[:], func=AF.Identity, scale=vcol(V_GG), bias=vcol(V_GB)),
                 r=["d_f", "vec"], w=["d_f"])
            p.op("pool", lambda e: e.tensor_tensor(out=d_f[:], in0=d_f[:], in1=bon_f[:], op=ALU.add), r=["d_f", "bon_f"], w=["d_f"])
            y_o = yo[ti % 2]
            yk = "yo%d" % (ti % 2)
            p.op("dve", lambda e, y_o=y_o: e.tensor_tensor(out=y_o[:], in0=d_f[:], in1=g_f[:], op=ALU.mult), r=["d_f", "g_f"], w=[yk])
            p.dma("sp", ygT[:, t0:t0 + TT], y_o[:], r=[yk], semkey="o_" + yk, store=True)
            if debug and ti == 0:
                f2 = lambda tl: tl[:].rearrange("p c t -> p (c t)")
                dl = [(r_f[:], "r_f"), (k_f[:], "k_f"), (v_f[:], "v_f"), (al_f[:], "al_f"), (g_f[:], "g_f"), (kk_f[:], "kk_f"),
                      (bon_f[:], "bon_f"), (f2(e_pos), "e_pos"), (f2(e_ex), "e_ex"), (f2(e_rem), "e_rem"), (Rh_f[:], "Rh_f"),
                      (yT_f[:], "yT_f"), (rstd[:], "rstd"), (d_f[:], "d_f"), (f2(Y0), "Y0"), (f2(PT_f), "PT_f"), (f2(Q_f), "Q_f"),
                      (f2(y_tm), "y_tm"), (Xf[:, 0:4, :].rearrange("p c t -> p (c t)"), "Xf"), (kt_f[:], "kt_f"), (b_f[:], "b_f"),
                      (f2(e_neg), "e_neg")]
                for i, (ap_, key) in enumerate(dl):
                    p.dma("sp", dbg[i], ap_, r=[key], semkey="dbgo", store=True)
        p.emit()
    return nc


def rwkv_inputs(inp, ntiles=T // TT, cores=range(NCORES)):
    TL = ntiles * TT
    x = inp["x"][0]
    xT = np.zeros((C, TL + 1), np.float32)
    xT[:, 1:] = x[:TL].T
    tri3, msk, i2, onesbd = rwkv_consts()
    mu6 = np.ascontiguousarray(inp["rwkv_mu"][0].T)
    maps = []
    for c in cores:
        cs = slice(c * 128, (c + 1) * 128)
        wrkv = inp["rwkv_w_rkv"][0]
        wbig = np.concatenate([wrkv[0][:, cs], wrkv[1][:, cs], wrkv[2][:, cs], inp["rwkv_w1"][0], inp["rwkv_a1"][0], inp["rwkv_g1"][0]], axis=1)
        w2a = np.concatenate([inp["rwkv_w2"][0][:, cs], inp["rwkv_w0"][0][None, cs]], axis=0)
        a2p = np.zeros((128, 128), np.float32)
        a2p[64:128] = inp["rwkv_a2"][0][:, cs]
        ka = inp["rwkv_k_a"][0][cs]
        one = np.ones_like(ka)
        vecs = np.stack([inp["rwkv_k_k"][0][cs], ka, one, inp["rwkv_r_k"][0].reshape(-1)[cs], inp["rwkv_gn_g"][0][cs],
                         inp["rwkv_gn_b"][0][cs], inp["rwkv_a0"][0][cs], one], axis=1).astype(np.float32)
        maps.append({
            "xT": xT, "wbig": np.ascontiguousarray(wbig), "mu6": mu6, "w2a": np.ascontiguousarray(w2a), "a2p": a2p,
            "g2p": np.ascontiguousarray(inp["rwkv_g2"][0][:, cs]), "vecs": np.ascontiguousarray(vecs),
            "tri3": tri3, "msk": msk, "i2": i2, "onesbd": onesbd,
        })
    return maps


def run_rwkv(inp):
    nc = _get("rwkv", lambda: build_rwkv(T // TT))
    maps = rwkv_inputs(inp)
    res = _run(nc, maps)
    return np.concatenate([r["ygT"] for r in res.results], axis=0)


def kernel(**inputs):
    inp = {k: np.asarray(v) for k, v in inputs.items()}
    x0 = np.ascontiguousarray(inp["x"][0], dtype=np.float32)
    ygT = run_rwkv(inp)
    lnp0 = np.stack([inp["ln_mix_g"][0], inp["ln_mix_b"][0], inp["ln_ffn_g"][0], inp["ln_ffn_b"][0]]).astype(np.float32)
    x1, qkv = run_post(ygT, x0, inp["rwkv_w_o"][0], inp["ffn_w_in"][0], inp["ffn_w_down"][0], lnp0,
                       inp["moba_w_qkv"][0], rope_tables())
    oT = run_attn(qkv)
    lnp1 = np.stack([inp["ln_mix_g"][1], inp["ln_mix_b"][1], inp["ln_ffn_g"][1], inp["ln_ffn_b"][1]]).astype(np.float32)
    out, _ = run_post(oT, x1, inp["moba_w_o"][0], inp["ffn_w_in"][1], inp["ffn_w_down"][1], lnp1)
    return out.reshape(1, T, C).astype(np.float32)
```

```python
import math
from contextlib import ExitStack

import numpy as np
import ml_dtypes

import concourse.bass as bass
import concourse.mybir as mybir
from concourse.bass_utils import run_bass_kernel_spmd

F32 = mybir.dt.float32
BF16 = mybir.dt.bfloat16
ALU = mybir.AluOpType
AF = mybir.ActivationFunctionType
AX = mybir.AxisListType

NCORES = 8
T = 16384
C = 1024
H = 16
DH = 64
DFF = 2816
DEPTH = 2
ALPHA = (2 * DEPTH) ** 0.25
LN_EPS = 1e-5
GN_EPS = 64 * 1e-5
NEG = -30000.0


class _Op:
    __slots__ = ("eng", "fn", "deps", "is_dma", "semkey", "sig", "need_sig", "idx")


class Prog:
    ENGS = ("pe", "act", "dve", "pool", "sp")

    def __init__(self, nc):
        self.nc = nc
        self.ops = []
        self.last_w = {}
        self.readers = {}
        self.store_keys = []

    def _deps(self, r, w):
        deps = set()
        for k in list(r) + list(w):
            lw = self.last_w.get(k)
            if lw is not None:
                deps.add(lw)
        for k in w:
            for rd in self.readers.get(k, ()):
                deps.add(rd)
        return deps

    def _commit(self, idx, r, w):
        for k in r:
            self.readers.setdefault(k, []).append(idx)
        for k in w:
            self.last_w[k] = idx
            self.readers[k] = []

    def op(self, eng, fn, r=(), w=()):
        o = _Op()
        o.eng, o.fn, o.is_dma, o.semkey, o.sig, o.need_sig = eng, fn, False, None, None, False
        o.deps = self._deps(r, w)
        o.idx = len(self.ops)
        self.ops.append(o)
        self._commit(o.idx, r, w)
        return o.idx

    def dma(self, q, out, in_, r=(), w=(), semkey=None, store=False):
        o = _Op()
        o.eng, o.is_dma, o.sig, o.need_sig = q, True, None, True
        o.fn = lambda e, out=out, in_=in_: e.dma_start(out=out, in_=in_)
        o.semkey = semkey if semkey is not None else (list(w)[0] if w else list(r)[0])
        o.deps = self._deps(r, w)
        o.idx = len(self.ops)
        self.ops.append(o)
        self._commit(o.idx, r, w)
        if store:
            self.store_keys.append(o.semkey)
        return o.idx

    def emit(self):
        nc = self.nc
        ops = self.ops
        for o in ops:
            for d in o.deps:
                od = ops[d]
                if od.is_dma:
                    continue
                if od.eng == o.eng and not o.is_dma and o.eng == "pe":
                    continue
                od.need_sig = True
        cnt = {e: 0 for e in self.ENGS}
        dcnt = {}
        for o in ops:
            if o.is_dma:
                dcnt[o.semkey] = dcnt.get(o.semkey, 0) + 16
                o.sig = ("d:" + o.semkey, dcnt[o.semkey])
            elif o.need_sig:
                cnt[o.eng] += 1
                o.sig = ("e:" + o.eng, cnt[o.eng])
        semnames = ["e:" + e for e in self.ENGS] + ["d:" + k for k in dcnt]
        with ExitStack() as es:
            sems = {}
            for i, n in enumerate(semnames):
                sems[n] = es.enter_context(nc.semaphore("s%d" % i))
            block = es.enter_context(nc.Block())
            per_eng = {e: [o for o in ops if o.eng == e] for e in self.ENGS}
            final_waits = [("d:" + k, dcnt[k]) for k in dict.fromkeys(self.store_keys)]

            def run(eng_name, e):
                waited = {}
                for o in per_eng[eng_name]:
                    need = {}
                    for d in o.deps:
                        od = ops[d]
                        if od.sig is None:
                            continue
                        if (not od.is_dma) and od.eng == eng_name and eng_name == "pe" and not o.is_dma:
                            continue
                        s, v = od.sig
                        if need.get(s, 0) < v:
                            need[s] = v
                    for s, v in need.items():
                        if waited.get(s, 0) < v:
                            e.wait_ge(sems[s], v)
                            waited[s] = v
                    ins = o.fn(e)
                    if o.sig is not None:
                        s, v = o.sig
                        ins.then_inc(sems[s], 16 if o.is_dma else 1)
                if eng_name == "sp":
                    for s, v in final_waits:
                        e.wait_ge(sems[s], v)

            @block.tensor
            def _(e):
                run("pe", e)

            @block.scalar
            def _(e):
                run("act", e)

            @block.vector
            def _(e):
                run("dve", e)

            @block.gpsimd
            def _(e):
                run("pool", e)

            @block.sync
            def _(e):
                run("sp", e)


def _bcast_rows(ap_1d, nparts):
    return ap_1d.partition_broadcast(nparts)


TOK = T // NCORES
GRP = 512
NGRP = TOK // GRP
NFB = DFF // 128


def build_post(with_qkv):
    nc = bass.Bass("TRN2", target_bir_lowering=False)
    aT = nc.dram_tensor("aT", [C, TOK], BF16, kind="ExternalInput").ap()
    xres = nc.dram_tensor("xres", [TOK, C], F32, kind="ExternalInput").ap()
    w_o = nc.dram_tensor("w_o", [C, C], F32, kind="ExternalInput").ap()
    w_in = nc.dram_tensor("w_in", [C, 2 * DFF], F32, kind="ExternalInput").ap()
    w_dn = nc.dram_tensor("w_dn", [DFF, C], F32, kind="ExternalInput").ap()
    lnp = nc.dram_tensor("lnp", [4, C], F32, kind="ExternalInput").ap()
    xout = nc.dram_tensor("xout", [TOK, C], F32, kind="ExternalOutput").ap()
    if with_qkv:
        w_qkv = nc.dram_tensor("w_qkv", [C, 3 * C], F32, kind="ExternalInput").ap()
        rope = nc.dram_tensor("rope", [TOK, 4, 32], F32, kind="ExternalInput").ap()
        qkv_out = nc.dram_tensor("qkv", [3, TOK, C], F32, kind="ExternalOutput").ap()

    NUNITS = NFB // 2 + (6 if with_qkv else 0)
    wscr = nc.dram_tensor("wscr", [NUNITS, 128, 8 * 512], BF16).ap()

    es = ExitStack()
    sb = lambda name, shape, dt: es.enter_context(nc.sbuf_tensor(name, shape, dt))
    ps = lambda name, shape, dt: es.enter_context(nc.psum_tensor(name, shape, dt))
    with es:
        ident = sb("ident", [128, 128], BF16)
        lnb = sb("lnb", [128, 4, C], F32)
        wdn_sb = sb("wdn_sb", [128, NFB, C], BF16)
        stg = [sb("stg%d" % i, [128, 4096], F32) for i in range(2)]
        win_sb = [sb("win%d" % i, [128, 8, 512], BF16) for i in range(2)]
        aT_sb = [sb("aT%d" % i, [128, 8, GRP], BF16) for i in range(2)]
        xr_sb = [sb("xr%d" % i, [128, C], F32) for i in range(2)]
        xg = sb("xg", [128, 4, C], F32)
        xb = sb("xb", [128, C], BF16)
        xT = sb("xT", [128, 8, GRP], BF16)
        actT = sb("actT", [128, NFB, GRP], BF16)
        sg = [sb("sg%d" % i, [128, GRP], F32) for i in range(2)]
        junk = sb("junk", [128, C], BF16)
        st = sb("st", [128, 8], F32)
        if with_qkv:
            rp_sb = sb("rp", [128, 4, 4, 32], F32)
            qo = [sb("qo%d" % i, [128, 512], F32) for i in range(2)]
            tmp = [sb("tmp%d" % i, [128, 8, 32], F32) for i in range(4)]
        acc = ps("acc", [128, C], F32)
        trp = ps("trp", [128, C], BF16)
        gu = [ps("gu%d" % i, [128, 2, GRP], F32) for i in range(2)]

        p = Prog(nc)
        p.op("pool", lambda e: e.memset(ident[:], 0.0), w=["ident"])
        p.op("pool", lambda e: e.affine_select(out=ident[:], in_=ident[:], pattern=[[-1, 128]],
                                               compare_op=ALU.not_equal, fill=1.0, base=0,
                                               channel_multiplier=1), r=["ident"], w=["ident"])
        p.dma("sp", lnb[:], lnp.partition_broadcast(128), w=["lnb"])

        stg_n = [0]

        def load_cast(parts, dst_ap, dst_keys):
            i = stg_n[0] % 2
            stg_n[0] += 1
            sk = "stg%d" % i
            for view_fn, src in parts:
                p.dma("sp", view_fn(stg[i]), src, w=[sk], semkey=sk)
            return i, sk

        def load_w8(dst, dkey, src_cols_list):
            i = stg_n[0] % 2
            stg_n[0] += 1
            sk = "stg%d" % i
            sv = stg[i][:].rearrange("p (k c) -> p k c", k=8)
            for src, c0 in src_cols_list:
                n = src.shape[1]
                p.dma("sp", sv[:, :, c0:c0 + n], src.rearrange("(k p) c -> p k c", p=128), w=[sk], semkey=sk)
            p.op("act", lambda e, dst=dst, sv=sv: e.activation(out=dst[:, 0:4, :], in_=sv[:, 0:4, :], func=AF.Copy), r=[sk], w=[dkey])
            p.op("dve", lambda e, dst=dst, sv=sv: e.tensor_copy(out=dst[:, 4:8, :], in_=sv[:, 4:8, :]), r=[sk], w=[dkey])

        def load_rows4(dst_ap, dkey, src_rows):
            i = stg_n[0] % 2
            stg_n[0] += 1
            sk = "stg%d" % i
            nblk = src_rows.shape[0] // 128
            sv = stg[i][:, 0:nblk * 1024].rearrange("p (k c) -> p k c", k=nblk)
            p.dma("sp", sv, src_rows.rearrange("(k p) c -> p k c", p=128), w=[sk], semkey=sk)
            h2 = max(1, nblk // 2)
            p.op("act", lambda e, dst_ap=dst_ap, sv=sv, h2=h2: e.activation(out=dst_ap[:, 0:h2, :], in_=sv[:, 0:h2, :], func=AF.Copy), r=[sk], w=[dkey])
            if nblk > h2:
                p.op("dve", lambda e, dst_ap=dst_ap, sv=sv, h2=h2: e.tensor_copy(out=dst_ap[:, h2:, :], in_=sv[:, h2:, :]), r=[sk], w=[dkey])

        for f4 in range(0, NFB, 4):
            n = min(4, NFB - f4)
            load_rows4(wdn_sb[:, f4:f4 + n, :], "wdn", w_dn[f4 * 128:(f4 + n) * 128, :])

        def layer_norm(tile_ap, gi, key):
            p.op("act", lambda e: e.activation(out=junk[:], in_=tile_ap, func=AF.Copy, accum_out=st[:, 0:1]),
                 r=[key], w=["junk", "st"])
            p.op("act", lambda e: e.activation(out=junk[:], in_=tile_ap, func=AF.Square, accum_out=st[:, 1:2]),
                 r=[key], w=["junk", "st"])
            p.op("dve", lambda e: e.tensor_scalar(out=st[:, 2:3], in0=st[:, 0:1], scalar1=1.0 / C, scalar2=None, op0=ALU.mult),
                 r=["st"], w=["st2"])
            p.op("dve", lambda e: e.tensor_tensor(out=st[:, 3:4], in0=st[:, 2:3], in1=st[:, 2:3], op=ALU.mult),
                 r=["st2"], w=["st3"])
            p.op("dve", lambda e: e.scalar_tensor_tensor(out=st[:, 4:5], in0=st[:, 1:2], scalar=1.0 / C, in1=st[:, 3:4],
                                                         op0=ALU.mult, op1=ALU.subtract), r=["st", "st3"], w=["st4"])
            p.op("dve", lambda e: e.tensor_scalar(out=st[:, 4:5], in0=st[:, 4:5], scalar1=LN_EPS, scalar2=None, op0=ALU.add),
                 r=["st4"], w=["st4"])
            p.op("act", lambda e: e.activation(out=st[:, 5:6], in_=st[:, 4:5], func=AF.Sqrt), r=["st4"], w=["st5"])
            p.op("dve", lambda e: e.reciprocal(out=st[:, 6:7], in_=st[:, 5:6]), r=["st5"], w=["st6"])
            p.op("dve", lambda e: e.tensor_scalar(out=tile_ap, in0=tile_ap, scalar1=st[:, 2:3], scalar2=st[:, 6:7],
                                                  op0=ALU.subtract, op1=ALU.mult), r=[key, "st2", "st6"], w=[key])
            p.op("dve", lambda e: e.tensor_tensor(out=tile_ap, in0=tile_ap, in1=lnb[:, gi, :], op=ALU.mult),
                 r=[key, "lnb"], w=[key])
            p.op("dve", lambda e: e.tensor_tensor(out=tile_ap, in0=tile_ap, in1=lnb[:, gi + 1, :], op=ALU.add),
                 r=[key, "lnb"], w=[key])

        def to_channel_major(tile_ap, key_in, ti):
            p.op("act", lambda e: e.activation(out=xb[:], in_=tile_ap, func=AF.Copy), r=[key_in], w=["xb"])
            for kc in range(8):
                p.op("pe", lambda e, kc=kc: e.transpose(trp[:, kc * 128:(kc + 1) * 128], xb[:, kc * 128:(kc + 1) * 128], ident[:]),
                     r=["xb", "ident"], w=["trp"])
            p.op("dve", lambda e: e.tensor_copy(out=xT[:, :, ti * 128:(ti + 1) * 128],
                                                in_=trp[:].rearrange("p (k t) -> p k t", k=8)),
                 r=["trp"], w=["xT"])

        def load_acts(g):
            t0 = g * GRP
            a_sb = aT_sb[g % 2]
            ak = "aT%d" % (g % 2)
            p.dma("sp", a_sb[:], aT[:, t0:t0 + GRP].rearrange("(k p) t -> p k t", p=128), w=[ak], semkey=ak)

        load_acts(0)
        for g in range(NGRP):
            t0 = g * GRP
            a_sb = aT_sb[g % 2]
            ak = "aT%d" % (g % 2)
            if with_qkv:
                p.dma("sp", rp_sb[:], rope[t0:t0 + GRP].rearrange("(i p) f c -> p i f c", p=128), w=["rp"])
            wo_v = [win_sb[h][:].rearrange("p k c -> p (k c)").rearrange("p (k c) -> p k c", k=4) for h in range(2)]
            for h in range(2):
                load_rows4(wo_v[h], "win%d" % h, w_o[h * 512:(h + 1) * 512, :])
            def phaseA_tail(ti):
                layer_norm(xg[:, ti, :], 0, "xg%d" % ti)
                to_channel_major(xg[:, ti, :], "xg%d" % ti, ti)

            for ti in range(4):
                r0 = t0 + ti * 128
                xr = xr_sb[(g * 4 + ti) % 2]
                xk = "xr%d" % ((g * 4 + ti) % 2)
                p.dma("sp", xr[:], xres[r0:r0 + 128, :], w=[xk], semkey=xk)
                for hf in range(2):
                    for kc in range(8):
                        p.op("pe", lambda e, kc=kc, hf=hf, ti=ti, a_sb=a_sb, wv=wo_v[kc // 4]: e.matmul(
                            acc[:, hf * 512:(hf + 1) * 512], lhsT=a_sb[:, kc, ti * 128:(ti + 1) * 128],
                            rhs=wv[:, kc % 4, hf * 512:(hf + 1) * 512], start=(kc == 0), stop=(kc == 7)),
                            r=[ak, "win%d" % (kc // 4)], w=["acc"])
                xk_out = "xg%d" % ti
                p.op("dve", lambda e, xr=xr, ti=ti: e.scalar_tensor_tensor(out=xg[:, ti, :], in0=xr[:], scalar=ALPHA, in1=acc[:],
                                                                           op0=ALU.mult, op1=ALU.add),
                     r=[xk, "acc"], w=[xk_out])
                if ti > 0:
                    phaseA_tail(ti - 1)
            phaseA_tail(3)
            if g + 1 < NGRP:
                load_acts(g + 1)
            NU = NFB // 2
            for u in range(NU):
                wsb = win_sb[u % 2]
                wk = "win%d" % (u % 2)
                if g == 0:
                    load_w8(wsb, wk, [(w_in[:, u * 256:(u + 1) * 256], 0), (w_in[:, DFF + u * 256:DFF + (u + 1) * 256], 256)])
                    p.dma("act", wscr[u], wsb[:].rearrange("p k c -> p (k c)"), r=[wk], w=["wscr%d" % u], semkey="wscr%d" % u)
                else:
                    p.dma("sp", wsb[:].rearrange("p k c -> p (k c)"), wscr[u], r=["wscr%d" % u], w=[wk], semkey=wk)
                for j in range(2):
                    fb = u * 2 + j
                    gps = gu[fb % 2]
                    gk = "gu%d" % (fb % 2)
                    for which in range(2):
                        for kc in range(8):
                            p.op("pe", lambda e, kc=kc, which=which, j=j, wsb=wsb, gps=gps: e.matmul(
                                gps[:, which, :], lhsT=wsb[:, kc, which * 256 + j * 128: which * 256 + (j + 1) * 128],
                                rhs=xT[:, kc, :], start=(kc == 0), stop=(kc == 7)),
                                r=[wk, "xT"], w=[gk])
                    s_sb = sg[fb % 2]
                    sk = "sg%d" % (fb % 2)
                    p.op("act", lambda e, gps=gps, s_sb=s_sb: e.activation(out=s_sb[:], in_=gps[:, 0, :], func=AF.Silu),
                         r=[gk], w=[sk])
                    p.op("dve", lambda e, gps=gps, s_sb=s_sb, fb=fb: e.tensor_tensor(out=actT[:, fb, :], in0=s_sb[:], in1=gps[:, 1, :], op=ALU.mult),
                         r=[gk, sk], w=["actT"])
            def down_tail(ti):
                r0 = t0 + ti * 128
                xk_out = "xg%d" % ti
                layer_norm(xg[:, ti, :], 2, xk_out)
                p.dma("act", xout[r0:r0 + 128, :], xg[:, ti, :], r=[xk_out], semkey="o_" + xk_out, store=True)
                if with_qkv:
                    to_channel_major(xg[:, ti, :], xk_out, ti)

            for ti in range(4):
                for hf in range(2):
                    for fb in range(NFB):
                        p.op("pe", lambda e, fb=fb, hf=hf, ti=ti: e.matmul(
                            acc[:, hf * 512:(hf + 1) * 512], lhsT=actT[:, fb, ti * 128:(ti + 1) * 128],
                            rhs=wdn_sb[:, fb, hf * 512:(hf + 1) * 512], start=(fb == 0), stop=(fb == NFB - 1)),
                            r=["actT", "wdn"], w=["acc"])
                xk_out = "xg%d" % ti
                p.op("dve", lambda e, ti=ti: e.scalar_tensor_tensor(out=xg[:, ti, :], in0=xg[:, ti, :], scalar=ALPHA, in1=acc[:],
                                                                    op0=ALU.mult, op1=ALU.add),
                     r=[xk_out, "acc"], w=[xk_out])
                if ti > 0:
                    down_tail(ti - 1)
            down_tail(3)
            if with_qkv:
                for cb in range(6):
                    wsb = win_sb[cb % 2]
                    wk = "win%d" % (cb % 2)
                    uq = NFB // 2 + cb
                    if g == 0:
                        load_w8(wsb, wk, [(w_qkv[:, cb * 512:(cb + 1) * 512], 0)])
                        p.dma("act", wscr[uq], wsb[:].rearrange("p k c -> p (k c)"), r=[wk], w=["wscr%d" % uq], semkey="wscr%d" % uq)
                    else:
                        p.dma("sp", wsb[:].rearrange("p k c -> p (k c)"), wscr[uq], r=["wscr%d" % uq], w=[wk], semkey=wk)
                    for ti in range(4):
                        r0 = t0 + ti * 128
                        gps = gu[(cb * 4 + ti) % 2]
                        gk = "gu%d" % ((cb * 4 + ti) % 2)
                        for kc in range(8):
                            p.op("pe", lambda e, kc=kc, ti=ti, wsb=wsb, gps=gps: e.matmul(
                                gps[:, 0, :], lhsT=xT[:, kc, ti * 128:(ti + 1) * 128], rhs=wsb[:, kc, :],
                                start=(kc == 0), stop=(kc == 7)), r=[wk, "xT"], w=[gk])
                        o_sb = qo[(cb * 4 + ti) % 2]
                        ok = "qo%d" % ((cb * 4 + ti) % 2)
                        which = cb // 2
                        if which == 2:
                            p.op("act", lambda e, gps=gps, o_sb=o_sb: e.activation(out=o_sb[:], in_=gps[:, 0, :], func=AF.Copy),
                                 r=[gk], w=[ok])
                        else:
                            src = gps[:, 0, :].rearrange("p (h d) -> p h d", h=8)
                            dst = o_sb[:].rearrange("p (h d) -> p h d", h=8)
                            cos = rp_sb[:, ti, 2 * which, :].unsqueeze(1).to_broadcast([128, 8, 32])
                            sin = rp_sb[:, ti, 2 * which + 1, :].unsqueeze(1).to_broadcast([128, 8, 32])
                            tk = ["tmp%d" % i for i in range(4)]
                            p.op("dve", lambda e, src=src, cos=cos: e.tensor_tensor(out=tmp[0][:], in0=src[:, :, 0:32], in1=cos, op=ALU.mult),
                                 r=[gk, "rp"], w=[tk[0]])
                            p.op("dve", lambda e, src=src, sin=sin: e.tensor_tensor(out=tmp[1][:], in0=src[:, :, 32:64], in1=sin, op=ALU.mult),
                                 r=[gk, "rp"], w=[tk[1]])
                            p.op("dve", lambda e, src=src, cos=cos: e.tensor_tensor(out=tmp[2][:], in0=src[:, :, 32:64], in1=cos, op=ALU.mult),
                                 r=[gk, "rp"], w=[tk[2]])
                            p.op("dve", lambda e, src=src, sin=sin: e.tensor_tensor(out=tmp[3][:], in0=src[:, :, 0:32], in1=sin, op=ALU.mult),
                                 r=[gk, "rp"], w=[tk[3]])
                            p.op("pool", lambda e, dst=dst: e.tensor_tensor(out=dst[:, :, 0:32], in0=tmp[0][:], in1=tmp[1][:], op=ALU.subtract),
                                 r=[tk[0], tk[1]], w=[ok])
                            p.op("pool", lambda e, dst=dst: e.tensor_tensor(out=dst[:, :, 32:64], in0=tmp[2][:], in1=tmp[3][:], op=ALU.add),
                                 r=[tk[2], tk[3]], w=[ok])
                        p.dma("act", qkv_out[which, r0:r0 + 128, (cb % 2) * 512:(cb % 2 + 1) * 512], o_sb[:], r=[ok],
                              semkey="o_" + ok, store=True)
        p.emit()
    return nc


_NC_CACHE = {}


def _get(name, builder):
    if name not in _NC_CACHE:
        import time as _t
        t0 = _t.time()
        _NC_CACHE[name] = builder()
        print("[kernel] built", name, "in %.1fs" % (_t.time() - t0), flush=True)
    return _NC_CACHE[name]


def _run(nc, in_maps):
    return run_bass_kernel_spmd(nc, in_maps, core_ids=list(range(NCORES)))


def run_post(aT_full, xres_full, w_o, w_in, w_dn, lnp, w_qkv=None, rope=None):
    with_qkv = w_qkv is not None
    nc = _get("post_qkv" if with_qkv else "post", lambda: build_post(with_qkv))
    in_maps = []
    for c in range(NCORES):
        m = {
            "aT": np.ascontiguousarray(aT_full[:, c * TOK:(c + 1) * TOK]),
            "xres": np.ascontiguousarray(xres_full[c * TOK:(c + 1) * TOK]),
            "w_o": w_o, "w_in": w_in, "w_dn": w_dn, "lnp": lnp,
        }
        if with_qkv:
            m["w_qkv"] = w_qkv
            m["rope"] = np.ascontiguousarray(rope[c * TOK:(c + 1) * TOK])
        in_maps.append(m)
    res = _run(nc, in_maps)
    xo = np.concatenate([r["xout"] for r in res.results], axis=0)
    if with_qkv:
        qkv = np.concatenate([r["qkv"] for r in res.results], axis=1)
        return xo, qkv
    return xo, None


def rope_tables():
    inv = (10000.0 ** (-np.arange(0, DH, 2, dtype=np.float32) / DH)).astype(np.float32)
    ang = (np.arange(T, dtype=np.float32)[:, None] * inv[None, :]).astype(np.float32)
    cos, sin = np.cos(ang).astype(np.float32), np.sin(ang).astype(np.float32)
    s = np.float32(DH ** -0.5)
    return np.ascontiguousarray(np.stack([cos * s, sin * s, cos, sin], axis=1))


NBLK = T // 256
QG = 512
NQG = T // QG


def moba_consts():
    blk1h = np.zeros((64, T), np.float32)
    for b in range(NBLK):
        blk1h[b, b * 256:(b + 1) * 256] = 1.0
    n = np.arange(64)[:, None]
    b = np.arange(64)[None, :]
    p01 = (b < n).astype(np.float32)
    o01 = (b == n).astype(np.float32)
    pbias = np.where(b < n, 0.0, -1e9).astype(np.float32)
    tabs = np.stack([pbias, p01, o01], axis=0)
    dm = np.zeros((4, 128, 4, 128), np.float32)
    key = np.arange(128)[:, None]
    q = np.arange(128)[None, :]
    tri = np.where(key <= q, 0.0, NEG)
    for j in range(4):
        for g in range(4):
            if j > g:
                dm[j, :, g, :] = NEG
            elif j == g:
                dm[j, :, g, :] = tri
    return (blk1h.astype(ml_dtypes.bfloat16), tabs, dm.reshape(4, 128, 512).astype(ml_dtypes.bfloat16))


def build_attn():
    nc = bass.Bass("TRN2", target_bir_lowering=False)
    qT = nc.dram_tensor("qT", [128, T], F32, kind="ExternalInput").ap()
    kT = nc.dram_tensor("kT", [128, T], F32, kind="ExternalInput").ap()
    v = nc.dram_tensor("v", [T, 128], F32, kind="ExternalInput").ap()
    blk1h = nc.dram_tensor("blk1h", [64, T], BF16, kind="ExternalInput").ap()
    tabs = nc.dram_tensor("tabs", [3, 64 * 64], F32, kind="ExternalInput").ap()
    dmask = nc.dram_tensor("dmask", [4, 128, 512], BF16, kind="ExternalInput").ap()
    oT = nc.dram_tensor("oT", [128, T], BF16, kind="ExternalOutput").ap()

    es = ExitStack()
    sb = lambda name, shape, dt: es.enter_context(nc.sbuf_tensor(name, shape, dt))
    ps = lambda name, shape, dt: es.enter_context(nc.psum_tensor(name, shape, dt))
    with es:
        ident = sb("ident", [128, 128], BF16)
        ones_f = sb("ones_f", [128, 64], F32)
        kaug = sb("kaug", [128, T], BF16)
        qaug = sb("qaug", [128, T], BF16)
        vsb = sb("vsb", [128, 128, 2, 65], BF16)
        tb = sb("tb", [128, 3, 64 * 64], F32)
        stg = [sb("stg%d" % i, [128, 2048], F32) for i in range(2)]
        dm = sb("dm", [128, 4, 512], BF16)
        kmean = sb("kmean", [64, 64], F32)
        kmean_b = sb("kmean_b", [64, 64], BF16)
        gm = [sb("gm%d" % i, [128, 8, 64], F32) for i in range(2)]
        top8 = [sb("top8%d" % i, [128, 8, 8], F32) for i in range(2)]
        selt = [sb("selt%d" % i, [128, 8, 64], F32) for i in range(2)]
        negm = [sb("negm%d" % i, [128, 8, 64], BF16) for i in range(2)]
        pT = [sb("pT%d" % i, [128, QG], BF16) for i in range(3)]
        osb = [sb("osb%d" % i, [65, QG], F32) for i in range(2)]
        obf = [sb("obf%d" % i, [64, QG], BF16) for i in range(2)]
        s_ps = [ps("s_ps%d" % i, [128, QG], F32) for i in range(3)]
        o_ps = [ps("o_ps%d" % i, [65, QG], F32) for i in range(2)]
        g_ps = ps("g_ps", [128, 8, 64], F32)
        m_ps = ps("m_ps", [128, 8 * 128], BF16)
        bc_ps = m_ps[:].bitcast(F32)

        p = Prog(nc)
        p.op("pool", lambda e: e.memset(ident[:], 0.0), w=["ident"])
        p.op("pool", lambda e: e.affine_select(out=ident[:], in_=ident[:], pattern=[[-1, 128]],
                                               compare_op=ALU.not_equal, fill=1.0, base=0,
                                               channel_multiplier=1), r=["ident"], w=["ident"])
        p.op("pool", lambda e: e.memset(ones_f[:], 1.0), w=["ones_f"])
        p.op("pool", lambda e: e.memset(vsb[:, :, :, 64:65], 1.0), w=["vsb1"])
        p.dma("sp", tb[:], tabs.partition_broadcast(128), w=["tb"])
        p.dma("sp", dm[:], dmask.rearrange("j k q -> k j q"), w=["dm"])
        p.dma("sp", kaug[64:128, :], blk1h, w=["kaug_hi"])
        stg_n = [0]

        def stage(view_fn, src, cast_fn, wkeys):
            i = stg_n[0] % 2
            stg_n[0] += 1
            sk = "stg%d" % i
            sv = view_fn(stg[i])
            p.dma("sp", sv, src, w=[sk], semkey=sk)
            eng = "act" if (stg_n[0] % 2) else "dve"
            p.op(eng, lambda e, sv=sv, eng=eng: cast_fn(e, sv, eng == "act"), r=[sk], w=wkeys)

        gcount = 0
        for hh in range(2):
            for c8 in range(8):
                sl = slice(c8 * 2048, (c8 + 1) * 2048)
                for dst, src, key in ((kaug, kT, "kaug_lo"), (qaug, qT, "qaug_lo")):
                    stage(lambda t_: t_[0:64, :], src[hh * 64:(hh + 1) * 64, sl],
                          lambda e, sv, is_act, dst=dst, sl=sl: (e.activation(out=dst[0:64, sl], in_=sv, func=AF.Copy) if is_act
                                                        else e.tensor_copy(out=dst[0:64, sl], in_=sv)),
                          [key])
            if hh == 0:
                for c8 in range(8):
                    kb0 = c8 * 16
                    stage(lambda t_: t_[:].rearrange("p (kb c) -> p kb c", c=128),
                          v[kb0 * 128:(kb0 + 16) * 128, :].rearrange("(kb p) c -> p kb c", p=128),
                          lambda e, sv, is_act, kb0=kb0: (e.activation(out=vsb[:, kb0:kb0 + 16, :, 0:64], in_=sv.rearrange("p kb (h d) -> p kb h d", h=2), func=AF.Copy)
                                                  if is_act else
                                                  e.tensor_copy(out=vsb[:, kb0:kb0 + 16, :, 0:64], in_=sv.rearrange("p kb (h d) -> p kb h d", h=2))),
                          ["vsb"])

            p.op("dve", lambda e: e.tensor_reduce(out=kmean[:], in_=kaug[0:64, :].rearrange("p (b s) -> p b s", s=256),
                                                  axis=AX.X, op=ALU.add), r=["kaug_lo"], w=["kmean"])
            p.op("dve", lambda e: e.tensor_scalar(out=kmean_b[:], in0=kmean[:], scalar1=1.0 / 256, scalar2=None, op0=ALU.mult),
                 r=["kmean"], w=["kmean_b"])
            for G8 in range(T // 1024):
                i2 = gcount % 2
                gcount += 1
                n0 = 4 * G8
                for c in range(8):
                    q0 = G8 * 1024 + c * 128
                    p.op("pe", lambda e, c=c, q0=q0: e.matmul(g_ps[:, c, :], lhsT=qaug[0:64, q0:q0 + 128], rhs=kmean_b[:],
                                                            start=True, stop=True), r=["qaug_lo", "kmean_b"], w=["g_ps"])
                gmk, t8k, slk, ngk = "gm%d" % i2, "top8%d" % i2, "selt%d" % i2, "negm%d" % i2
                tbv = tb[:].rearrange("p t (n b) -> p t n b", b=64)
                b0, b1, b2 = [tbv[:, t, n0:n0 + 4, :].unsqueeze(2).to_broadcast([128, 4, 2, 64]) for t in range(3)]
                g4 = lambda tl: tl[:].rearrange("p (a c) b -> p a c b", c=2)
                gm4, sl4, gp4 = g4(gm[i2]), g4(selt[i2]), g_ps[:].rearrange("p (a c) b -> p a c b", c=2)
                p.op("dve", lambda e, gm4=gm4, gp4=gp4, b0=b0: e.tensor_tensor(out=gm4, in0=gp4, in1=b0, op=ALU.add),
                     r=["g_ps", "tb"], w=[gmk])
                for c in range(8):
                    p.op("dve", lambda e, c=c, i2=i2: e.max(out=top8[i2][:, c, :], in_=gm[i2][:, c, :]), r=[gmk], w=[t8k])
                p.op("dve", lambda e, i2=i2: e.tensor_tensor(out=selt[i2][:], in0=gm[i2][:],
                                                            in1=top8[i2][:, :, 2:3].to_broadcast([128, 8, 64]), op=ALU.is_ge),
                     r=[gmk, t8k], w=[slk])
                p.op("dve", lambda e, sl4=sl4, b1=b1: e.tensor_tensor(out=sl4, in0=sl4, in1=b1, op=ALU.mult),
                     r=[slk, "tb"], w=[slk])
                p.op("dve", lambda e, sl4=sl4, b2=b2: e.tensor_tensor(out=sl4, in0=sl4, in1=b2, op=ALU.add),
                     r=[slk, "tb"], w=[slk])
                p.op("dve", lambda e, i2=i2: e.tensor_scalar(out=negm[i2][:], in0=selt[i2][:], scalar1=-1.0, scalar2=-NEG,
                                                             op0=ALU.add, op1=ALU.mult), r=[slk], w=[ngk])
                for c in range(8):
                    p.op("pe", lambda e, c=c, i2=i2: e.transpose(m_ps[64:128, c * 128:(c + 1) * 128], negm[i2][:, c, :], ident[:],
                                                                tile_position=(0, 64)), r=[ngk, "ident"], w=["m_ps"])
                p.op("act", lambda e, G8=G8: e.activation(out=qaug[64:128, G8 * 1024:(G8 + 1) * 1024], in_=m_ps[64:128, :], func=AF.Copy),
                     r=["m_ps"], w=["qaug_hi"])
            for G in range(NQG):
                nkb = 4 * (G + 1)
                op_i = G % 2
                opk = "o_ps%d" % op_i
                qsl = slice(G * QG, (G + 1) * QG)

                def qk(kb, G=G, qsl=qsl):
                    si = kb % 3
                    diag = kb >= 4 * G
                    p.op("pe", lambda e: e.matmul(s_ps[si][:], lhsT=kaug[:, kb * 128:(kb + 1) * 128], rhs=qaug[:, qsl],
                                                  start=True, stop=not diag),
                         r=["kaug_lo", "kaug_hi", "qaug_lo", "qaug_hi"], w=["s_ps%d" % si])
                    if diag:
                        j = kb - 4 * G
                        p.op("pe", lambda e: e.matmul(s_ps[si][:], lhsT=ident[:], rhs=dm[:, j, :], start=False, stop=True),
                             r=["ident", "dm"], w=["s_ps%d" % si])

                def ex_pv(kb, G=G, nkb=nkb, op_i=op_i, opk=opk, hh=hh):
                    si = kb % 3
                    p.op("act", lambda e: e.activation(out=pT[si][:], in_=s_ps[si][:], func=AF.Exp),
                         r=["s_ps%d" % si], w=["pT%d" % si])
                    p.op("pe", lambda e: e.matmul(o_ps[op_i][:], lhsT=vsb[:, kb, hh, :], rhs=pT[si][:],
                                                  start=(kb == 0), stop=(kb == nkb - 1)),
                         r=["vsb", "vsb1", "pT%d" % si], w=[opk])

                LOOK = 2
                for kb in range(min(LOOK, nkb)):
                    qk(kb)
                for kb in range(nkb):
                    if kb + LOOK < nkb:
                        qk(kb + LOOK)
                    ex_pv(kb)
                ob_i = G % 2
                p.op("dve", lambda e, op_i=op_i, ob_i=ob_i: e.tensor_copy(out=osb[ob_i][:], in_=o_ps[op_i][:]), r=[opk], w=["osb%d" % ob_i])
                p.op("dve", lambda e, ob_i=ob_i: e.reciprocal(out=osb[ob_i][64:65, :], in_=osb[ob_i][64:65, :]),
                     r=["osb%d" % ob_i], w=["osb%d" % ob_i])
                p.op("pe", lambda e, ob_i=ob_i: e.matmul(bc_ps[0:64, :], lhsT=ones_f[64:65, :], rhs=osb[ob_i][64:65, :], start=True, stop=True),
                     r=["ones_f", "osb%d" % ob_i], w=["m_ps"])
                p.op("dve", lambda e, ob_i=ob_i: e.tensor_tensor(out=obf[ob_i][:], in0=osb[ob_i][0:64, :], in1=bc_ps[0:64, :], op=ALU.mult),
                     r=["osb%d" % ob_i, "m_ps"], w=["obf%d" % ob_i])
                p.dma("sp", oT[hh * 64:(hh + 1) * 64, qsl], obf[ob_i][:], r=["obf%d" % ob_i], semkey="o_obf%d" % ob_i, store=True)
        p.emit()
    return nc


def run_attn(qkv):
    nc = _get("attn", build_attn)
    blk1h, tabs, dm = moba_consts()
    tabs = np.ascontiguousarray(tabs.reshape(3, 64 * 64))
    in_maps = []
    for c in range(NCORES):
        cs = slice(c * 128, (c + 1) * 128)
        in_maps.append({
            "qT": np.ascontiguousarray(qkv[0][:, cs].T),
            "kT": np.ascontiguousarray(qkv[1][:, cs].T),
            "v": np.ascontiguousarray(qkv[2][:, cs]),
            "blk1h": blk1h, "tabs": tabs, "dmask": dm,
        })
    res = _run(nc, in_maps)
    return np.concatenate([r["oT"] for r in res.results], axis=0)


LCH = 64
TT = 512
NCH = TT // LCH
WCOLS = 672
DECAY_C = -math.exp(-0.5)


def rwkv_consts():
    j = np.arange(64)[:, None]
    t = np.arange(64)[None, :]
    incl = (j <= t).astype(np.float32)
    strict = (j < t).astype(np.float32)
    rev = (j > t).astype(np.float32)
    tri3 = (DECAY_C * np.concatenate([incl, strict, rev], axis=1)).astype(np.float32)
    tri3 = np.concatenate([tri3, tri3], axis=0)
    up_s = (j < t).astype(np.float32)
    up_i = (j <= t).astype(np.float32)
    lo_s = (t < j).astype(np.float32)
    msk = np.concatenate([up_s, up_i, up_s, up_i, lo_s], axis=1)
    msk = np.concatenate([msk, msk], axis=0)
    i2 = np.concatenate([np.eye(64, dtype=np.float32)] * 2, axis=0)
    onesbd = np.zeros((128, 128), np.float32)
    onesbd[:64, :64] = 1.0
    onesbd[64:, 64:] = 1.0
    return tri3, msk, i2, onesbd


def build_rwkv(ntiles, debug=False):
    TL = ntiles * TT
    nc = bass.Bass("TRN2", target_bir_lowering=False)
    xT = nc.dram_tensor("xT", [C, TL + 1], F32, kind="ExternalInput").ap()
    wbig = nc.dram_tensor("wbig", [C, WCOLS], F32, kind="ExternalInput").ap()
    mu6 = nc.dram_tensor("mu6", [C, 6], F32, kind="ExternalInput").ap()
    w2a = nc.dram_tensor("w2a", [65, 128], F32, kind="ExternalInput").ap()
    a2p = nc.dram_tensor("a2p", [128, 128], F32, kind="ExternalInput").ap()
    g2p = nc.dram_tensor("g2p", [160, 128], F32, kind="ExternalInput").ap()
    vecs = nc.dram_tensor("vecs", [128, 8], F32, kind="ExternalInput").ap()
    tri3_d = nc.dram_tensor("tri3", [128, 192], F32, kind="ExternalInput").ap()
    msk_d = nc.dram_tensor("msk", [128, 320], F32, kind="ExternalInput").ap()
    i2_d = nc.dram_tensor("i2", [128, 64], F32, kind="ExternalInput").ap()
    obd_d = nc.dram_tensor("onesbd", [128, 128], F32, kind="ExternalInput").ap()
    ygT = nc.dram_tensor("ygT", [128, TL], BF16, kind="ExternalOutput").ap()
    if debug:
        dbg = nc.dram_tensor("dbg", [24, 128, 512], F32, kind="ExternalOutput").ap()

    es = ExitStack()
    sb = lambda name, shape, dt: es.enter_context(nc.sbuf_tensor(name, shape, dt))
    with es:
        ident_b = sb("ident_b", [128, 128], BF16)
        ident_f = sb("ident_f", [128, 128], F32)
        wf = sb("wf", [128, 8, WCOLS], F32)
        wc = sb("wc", [128, 8, WCOLS], BF16)
        wp = sb("wp", [128, 8, WCOLS], BF16)
        mu_sb = sb("mu_sb", [128, 8, 6], F32)
        w2a_b = sb("w2a_b", [65, 128], BF16)
        a2_b = sb("a2_b", [128, 128], BF16)
        g2a_b = sb("g2a_b", [128, 128], BF16)
        g2b_b = sb("g2b_b", [32, 128], BF16)
        vec = sb("vec", [128, 8], F32)
        epsc = sb("epsc", [128, 2], F32)
        tri3 = sb("tri3_s", [128, 192], F32)
        msk = sb("msk_s", [128, 320], F32)
        i2 = sb("i2_s", [128, 64], F32)
        obd_f = sb("obd_f", [128, 128], F32)
        obd_b = sb("obd_b", [128, 128], BF16)
        xb = [sb("xb%d" % i, [128, 8, TT + 1], BF16) for i in range(2)]
        r_f = sb("r_f", [128, TT], F32)
        k_f = sb("k_f", [128, TT], F32)
        v_f = sb("v_f", [128, TT], F32)
        v_b = sb("v_b", [128, TT], BF16)
        twa = sb("twa", [65, TT], BF16)
        a1o = sb("a1o", [128, TT], BF16)
        sg_a = sb("sg_a", [128, TT], BF16)
        sg_b = sb("sg_b", [32, TT], BF16)
        g_f = sb("g_f", [128, TT], F32)
        al_f = sb("al_f", [128, TT], F32)
        kkr = sb("kkr", [128, TT], F32)
        sq_b = sb("sq_b", [128, TT], BF16)
        rn = sb("rn", [128, TT], F32)
        kk_f = sb("kk_f", [128, TT], F32)
        kt_f = sb("kt_f", [128, TT], F32)
        b_f = sb("b_f", [128, TT], F32)
        tmp1 = sb("tmp1", [128, TT], F32)
        rk_b = sb("rk_b", [128, TT], BF16)
        bon_f = sb("bon_f", [128, TT], F32)
        sgw = sb("sgw", [128, NCH, 128], F32)
        e_pos = sb("e_pos", [128, NCH, 64], F32)
        e_neg = sb("e_neg", [128, NCH, 64], F32)
        e_ex = sb("e_ex", [128, NCH, 64], F32)
        e_rem = sb("e_rem", [128, NCH, 64], F32)
        AR = sb("AR", [128, NCH, 2, 64], BF16)
        Rh_f = sb("Rh_f", [128, TT], F32)
        BhT = sb("BhT", [128, TT], BF16)
        KhT = sb("KhT", [128, TT], BF16)
        BbT = sb("BbT", [128, TT], BF16)
        KbT = sb("KbT", [128, TT], BF16)
        TM = sb("TM", [128, NCH, 4, 64], BF16)
        GM = sb("GM", [128, NCH, 320], BF16)
        Xf = sb("Xf", [128, NCH, 128], F32)
        Xb = sb("Xb", [128, NCH, 128], BF16)
        PP = [sb("PP%d" % i, [128, NCH, 128], BF16) for i in range(2)]
        RtT = sb("RtT", [128, TT], BF16)
        Y0 = sb("Y0", [128, NCH, 64], F32)
        DG = sb("DG", [128, NCH, 64], F32)
        PT_f = sb("PT_f", [128, NCH, 64], F32)
        Q_f = sb("Q_f", [128, NCH, 64], F32)
        S_f = [sb("S_f%d" % i, [128, 64], F32) for i in range(2)]
        S_b = [sb("S_b%d" % i, [128, 64], BF16) for i in range(2)]
        y_tm = sb("y_tm", [128, NCH, 64], F32)
        yT_f = sb("yT_f", [128, TT], F32)
        d_f = sb("d_f", [128, TT], F32)
        sq_f = sb("sq_f", [128, TT], F32)
        rstd = sb("rstd", [128, TT], F32)
        yo = [sb("yo%d" % i, [128, TT], BF16) for i in range(2)]
        bank = [es.enter_context(nc.psum_tensor("bank%d" % i, [128, 512], F32)) for i in range(8)]
        bk = lambda i: "bank%d" % i

        p = Prog(nc)
        for idt, nm in ((ident_b, "ident_b"), (ident_f, "ident_f")):
            p.op("pool", lambda e, idt=idt: e.memset(idt[:], 0.0), w=[nm])
            p.op("pool", lambda e, idt=idt: e.affine_select(out=idt[:], in_=idt[:], pattern=[[-1, 128]],
                                                         compare_op=ALU.not_equal, fill=1.0, base=0,
                                                         channel_multiplier=1), r=[nm], w=[nm])
        p.op("pool", lambda e: e.memset(twa[64:65, :], 1.0), w=["twa1"])
        p.op("pool", lambda e: e.memset(epsc[:, 0:1], 1e-24), w=["epsc"])
        p.op("pool", lambda e: e.memset(epsc[:, 1:2], GN_EPS), w=["epsc"])
        p.op("pool", lambda e: e.memset(S_f[0][:], 0.0), w=["S_f0"])
        p.op("pool", lambda e: e.memset(S_b[0][:], 0.0), w=["S_b0"])
        for kc in range(8):
            p.dma("sp", wf[:, kc, :], wbig[kc * 128:(kc + 1) * 128, :], w=["wf"], semkey="wf")
        p.dma("sp", mu_sb[:], mu6.rearrange("(k p) n -> p k n", p=128), w=["mu"])
        p.dma("pool", w2a_b[:], w2a, w=["w2a_b"])
        p.dma("pool", a2_b[:], a2p, w=["a2_b"])
        p.dma("pool", g2a_b[:], g2p[0:128, :], w=["g2a_b"])
        p.dma("pool", g2b_b[:], g2p[128:160, :], w=["g2b_b"])
        p.dma("pool", obd_b[:], obd_d, w=["obd_b"])
        p.dma("sp", obd_f[:], obd_d, w=["obd_f"])
        p.dma("sp", vec[:], vecs, w=["vec"])
        p.dma("sp", tri3[:], tri3_d, w=["tri3"])
        p.dma("sp", msk[:], msk_d, w=["msk"])
        p.dma("sp", i2[:], i2_d, w=["i2"])
        groups = [(0, 128, 0), (128, 256, 2), (256, 384, 3), (384, 448, 1), (448, 512, 4), (512, 672, 5)]
        for kc in range(8):
            for (c0, c1, n) in groups:
                eng = "dve" if (kc % 2 == 0) else "pool"
                p.op(eng, lambda e, kc=kc, c0=c0, c1=c1, n=n: e.tensor_scalar(
                    out=wp[:, kc, c0:c1], in0=wf[:, kc, c0:c1], scalar1=mu_sb[:, kc, n:n + 1], scalar2=None, op0=ALU.mult),
                    r=["wf", "mu"], w=["wp"])
            p.op("dve" if (kc % 2 == 0) else "pool",
                 lambda e, kc=kc: e.tensor_tensor(out=wc[:, kc, :], in0=wf[:, kc, :], in1=wp[:, kc, :], op=ALU.subtract),
                 r=["wf", "wp"], w=["wc"])

        V_KK, V_KA, V_1KA, V_RK, V_GG, V_GB, V_A0 = range(7)
        vcol = lambda i: vec[:, i:i + 1]
        tp = lambda h: (64 * h, 64 * h)
        hs = lambda h: slice(64 * h, 64 * h + 64)
        s_cur = 0

        def load_x(tj):
            for kc in range(8):
                p.dma("pool", xb[tj % 2][:, kc, :], xT[kc * 128:(kc + 1) * 128, tj * TT:tj * TT + TT + 1],
                      w=["xb%d" % (tj % 2)], semkey="xb%d" % (tj % 2))

        load_x(0)
        for ti in range(ntiles):
            t0 = ti * TT
            x_sb = xb[ti % 2]
            xk = "xb%d" % (ti % 2)
            if ti + 1 < ntiles:
                load_x(ti + 1)

            def proj(c0, c1, bi):
                m = c1 - c0
                for kc in range(8):
                    p.op("pe", lambda e, kc=kc, x_sb=x_sb: e.matmul(bank[bi][0:m, :], lhsT=wc[:, kc, c0:c1], rhs=x_sb[:, kc, 1:TT + 1],
                                                        start=(kc == 0), stop=False), r=[xk, "wc"], w=[bk(bi)])
                for kc in range(8):
                    p.op("pe", lambda e, kc=kc, x_sb=x_sb: e.matmul(bank[bi][0:m, :], lhsT=wp[:, kc, c0:c1], rhs=x_sb[:, kc, 0:TT],
                                                        start=False, stop=(kc == 7)), r=[xk, "wp"], w=[bk(bi)])

            proj(0, 128, 0)
            p.op("act", lambda e: e.activation(out=r_f[:], in_=bank[0][:], func=AF.Copy), r=[bk(0)], w=["r_f"])
            proj(128, 256, 1)
            p.op("act", lambda e: e.activation(out=k_f[:], in_=bank[1][:], func=AF.Copy), r=[bk(1)], w=["k_f"])
            proj(256, 384, 2)
            p.op("act", lambda e: e.activation(out=v_f[:], in_=bank[2][:], func=AF.Copy), r=[bk(2)], w=["v_f"])
            p.op("act", lambda e: e.activation(out=v_b[:], in_=bank[2][:], func=AF.Copy), r=[bk(2)], w=["v_b"])
            proj(384, 512, 3)
            p.op("act", lambda e: e.activation(out=twa[0:64, :], in_=bank[3][0:64, :], func=AF.Tanh), r=[bk(3)], w=["twa"])
            p.op("dve", lambda e: e.tensor_copy(out=a1o[64:128, :], in_=bank[3][64:128, :]), r=[bk(3)], w=["a1o"])
            proj(512, 640, 4)
            p.op("act", lambda e: e.activation(out=sg_a[:], in_=bank[4][:], func=AF.Sigmoid), r=[bk(4)], w=["sg_a"])
            proj(640, 672, 5)
            p.op("act", lambda e: e.activation(out=sg_b[:], in_=bank[5][0:32, :], func=AF.Sigmoid), r=[bk(5)], w=["sg_b"])
            p.op("pe", lambda e: e.matmul(bank[6][:], lhsT=g2a_b[:], rhs=sg_a[:], start=True, stop=False), r=["g2a_b", "sg_a"], w=[bk(6)])
            p.op("pe", lambda e: e.matmul(bank[6][:], lhsT=g2b_b[:], rhs=sg_b[:], start=False, stop=True), r=["g2b_b", "sg_b"], w=[bk(6)])
            p.op("act", lambda e: e.activation(out=g_f[:], in_=bank[6][:], func=AF.Copy), r=[bk(6)], w=["g_f"])
            p.op("pe", lambda e: e.matmul(bank[7][:], lhsT=a2_b[64:128, :], rhs=a1o[64:128, :], start=True, stop=True),
                 r=["a2_b", "a1o"], w=[bk(7)])
            p.op("act", lambda e: e.activation(out=al_f[:], in_=bank[7][:], func=AF.Sigmoid, bias=vcol(V_A0)), r=[bk(7), "vec"], w=["al_f"])
            p.op("dve", lambda e: e.tensor_scalar(out=kkr[:], in0=k_f[:], scalar1=vcol(V_KK), scalar2=None, op0=ALU.mult), r=["k_f", "vec"], w=["kkr"])
            p.op("act", lambda e: e.activation(out=sq_b[:], in_=k_f[:], func=AF.Square, scale=vcol(V_KK)), r=["k_f", "vec"], w=["sq_b"])
            p.op("pe", lambda e: e.matmul(bank[0][:], lhsT=obd_b[:], rhs=sq_b[:], start=True, stop=True), r=["obd_b", "sq_b"], w=[bk(0)])
            p.op("act", lambda e: e.activation(out=rn[:], in_=bank[0][:], func=AF.Sqrt, bias=epsc[:, 0:1]), r=[bk(0), "epsc"], w=["rn"])
            p.op("dve", lambda e: e.reciprocal(out=rn[:], in_=rn[:]), r=["rn"], w=["rn"])
            p.op("dve", lambda e: e.tensor_tensor(out=kk_f[:], in0=kkr[:], in1=rn[:], op=ALU.mult), r=["kkr", "rn"], w=["kk_f"])
            p.op("pool", lambda e: e.tensor_scalar(out=tmp1[:], in0=al_f[:], scalar1=-1.0, scalar2=vcol(V_KA), op0=ALU.add, op1=ALU.mult),
                 r=["al_f", "vec"], w=["tmp1"])
            p.op("dve", lambda e: e.scalar_tensor_tensor(out=kt_f[:], in0=tmp1[:], scalar=1.0, in1=k_f[:], op0=ALU.add, op1=ALU.mult),
                 r=["k_f", "tmp1"], w=["kt_f"])
            p.op("dve", lambda e: e.tensor_tensor(out=b_f[:], in0=kk_f[:], in1=al_f[:], op=ALU.mult), r=["kk_f", "al_f"], w=["b_f"])
            p.op("dve", lambda e: e.scalar_tensor_tensor(out=rk_b[:], in0=r_f[:], scalar=vcol(V_RK), in1=kt_f[:], op0=ALU.mult, op1=ALU.mult),
                 r=["r_f", "kt_f", "vec"], w=["rk_b"])
            p.op("pe", lambda e: e.matmul(bank[1][:], lhsT=obd_b[:], rhs=rk_b[:], start=True, stop=True), r=["obd_b", "rk_b"], w=[bk(1)])
            p.op("dve", lambda e: e.tensor_tensor(out=bon_f[:], in0=v_f[:], in1=bank[1][:], op=ALU.mult), r=["v_f", bk(1)], w=["bon_f"])

            for c in range(NCH):
                p.op("pe", lambda e, c=c: e.matmul(bank[2 + c // 4][0:64, (c % 4) * 128:(c % 4 + 1) * 128],
                                                   lhsT=twa[:, c * 64:(c + 1) * 64], rhs=w2a_b[:], start=True, stop=True),
                     r=["twa", "twa1", "w2a_b"], w=[bk(2 + c // 4)])
            for hf in range(2):
                p.op("act", lambda e, hf=hf: e.activation(out=sgw[0:64, hf * 4:(hf + 1) * 4, :],
                                                         in_=bank[2 + hf][0:64, :].rearrange("p (c d) -> p c d", c=4), func=AF.Sigmoid),
                     r=[bk(2 + hf)], w=["sgw"])
            for c in range(NCH):
                for kind in range(3):
                    p.op("pe", lambda e, c=c, kind=kind: e.matmul(bank[4 + kind][:, c * 64:(c + 1) * 64], lhsT=sgw[0:64, c, :],
                                                                 rhs=tri3[0:64, kind * 64:(kind + 1) * 64], start=True, stop=True),
                         r=["sgw", "tri3"], w=[bk(4 + kind)])
            v3 = lambda tl: tl[:].rearrange("p c t -> p (c t)")
            p.op("act", lambda e: e.activation(out=v3(e_pos), in_=bank[4][:], func=AF.Exp), r=[bk(4)], w=["e_pos"])
            p.op("act", lambda e: e.activation(out=v3(e_neg), in_=bank[4][:], func=AF.Exp, scale=-1.0), r=[bk(4)], w=["e_neg"])
            p.op("act", lambda e: e.activation(out=v3(e_ex), in_=bank[5][:], func=AF.Exp), r=[bk(5)], w=["e_ex"])
            p.op("act", lambda e: e.activation(out=v3(e_rem), in_=bank[6][:], func=AF.Exp), r=[bk(6)], w=["e_rem"])
            c3 = lambda ap2: ap2.rearrange("p (c t) -> p c t", c=NCH)
            p.op("dve", lambda e: e.scalar_tensor_tensor(out=AR[:, :, 0, :], in0=c3(kk_f[:]), scalar=-1.0, in1=e_ex[:], op0=ALU.mult, op1=ALU.mult),
                 r=["kk_f", "e_ex"], w=["AR"])
            p.op("dve", lambda e: e.tensor_tensor(out=Rh_f[:], in0=r_f[:], in1=v3(e_pos), op=ALU.mult), r=["r_f", "e_pos"], w=["Rh_f"])
            p.op("act", lambda e: e.activation(out=AR[:, :, 1, :], in_=c3(Rh_f[:]), func=AF.Copy), r=["Rh_f"], w=["AR"])
            p.op("dve", lambda e: e.tensor_tensor(out=BhT[:], in0=b_f[:], in1=v3(e_neg), op=ALU.mult), r=["b_f", "e_neg"], w=["BhT"])
            p.op("pool", lambda e: e.tensor_tensor(out=KhT[:], in0=kt_f[:], in1=v3(e_neg), op=ALU.mult), r=["kt_f", "e_neg"], w=["KhT"])
            p.op("dve", lambda e: e.tensor_tensor(out=BbT[:], in0=b_f[:], in1=v3(e_rem), op=ALU.mult), r=["b_f", "e_rem"], w=["BbT"])
            p.op("pool", lambda e: e.tensor_tensor(out=KbT[:], in0=kt_f[:], in1=v3(e_rem), op=ALU.mult), r=["kt_f", "e_rem"], w=["KbT"])

            tmb = [bank[0][:].bitcast(BF16), bank[1][:].bitcast(BF16)]
            srcs = [(lambda c: AR[:, c, 0, :], "AR"), (lambda c: v_b[:, c * 64:(c + 1) * 64], "v_b"),
                    (lambda c: BbT[:, c * 64:(c + 1) * 64], "BbT"), (lambda c: KbT[:, c * 64:(c + 1) * 64], "KbT")]
            for c in range(NCH):
                for si, (sf, sk) in enumerate(srcs):
                    for h in range(2):
                        col = (c % 4) * 256 + si * 64
                        p.op("pe", lambda e, c=c, sf=sf, h=h, col=col: e.transpose(
                            tmb[c // 4][hs(h), col:col + 64], sf(c)[hs(h), :], ident_b[hs(h), hs(h)], tile_position=tp(h)),
                            r=[sk, "ident_b"], w=[bk(c // 4)])
            for hf in range(2):
                p.op("act" if hf == 0 else "dve",
                     (lambda e, hf=hf: e.activation(out=TM[:, hf * 4:(hf + 1) * 4, :, :].rearrange("p c s t -> p (c s t)"), in_=tmb[hf], func=AF.Copy))
                     if hf == 0 else
                     (lambda e, hf=hf: e.tensor_copy(out=TM[:, hf * 4:(hf + 1) * 4, :, :].rearrange("p c s t -> p (c s t)"), in_=tmb[hf])),
                     r=[bk(hf)], w=["TM"])

            for c in range(NCH):
                bi = 2 + (c % 2)
                cs_ = slice(c * 64, (c + 1) * 64)
                for h in range(2):
                    arh = AR[hs(h), c, :, :].rearrange("p s t -> p (s t)")
                    p.op("pe", lambda e, h=h, arh=arh, cs_=cs_, bi=bi: e.matmul(bank[bi][hs(h), 0:128], lhsT=BhT[hs(h), cs_], rhs=arh,
                                                                           start=True, stop=True, tile_position=tp(h)),
                         r=["BhT", "AR"], w=[bk(bi)])
                    p.op("pe", lambda e, h=h, arh=arh, cs_=cs_, bi=bi: e.matmul(bank[bi][hs(h), 128:256], lhsT=KhT[hs(h), cs_], rhs=arh,
                                                                           start=True, stop=True, tile_position=tp(h)),
                         r=["KhT", "AR"], w=[bk(bi)])
                    p.op("pe", lambda e, h=h, c=c, cs_=cs_, bi=bi: e.matmul(bank[bi][hs(h), 256:320], lhsT=AR[hs(h), c, 0, :], rhs=BhT[hs(h), cs_],
                                                                       start=True, stop=True, tile_position=tp(h)),
                         r=["BhT", "AR"], w=[bk(bi)])
                p.op("dve", lambda e, c=c, bi=bi: e.tensor_tensor(out=GM[:, c, :], in0=bank[bi][:, 0:320], in1=msk[:], op=ALU.mult),
                     r=[bk(bi), "msk"], w=["GM"])

            for c in range(NCH):
                for h in range(2):
                    p.op("pe", lambda e, c=c, h=h: e.matmul(bank[4][hs(h), c * 64:(c + 1) * 64], lhsT=GM[hs(h), c, 128:192], rhs=TM[hs(h), c, 1, :],
                                                           start=True, stop=True, tile_position=tp(h)), r=["GM", "TM"], w=[bk(4)])
            p.op("pool", lambda e: e.tensor_copy(out=Xf[:, :, 0:64], in_=TM[:, :, 0, :]), r=["TM"], w=["Xf"])
            p.op("dve", lambda e: e.tensor_copy(out=Xf[:, :, 64:128], in_=bank[4][:].rearrange("p (c t) -> p c t", c=NCH)), r=[bk(4)], w=["Xf"])
            p.op("dve", lambda e: e.tensor_copy(out=Xb[:], in_=Xf[:]), r=["Xf"], w=["Xb"])

            NLEV = 6
            for lev in range(NLEV):
                if lev == 0:
                    P_of = lambda c: GM[:, c, 256:320]
                    PT_of = lambda c: GM[:, c, 0:64]
                    pk = "GM"
                else:
                    ppt = PP[(lev - 1) % 2]
                    P_of = lambda c, ppt=ppt: ppt[:, c, 0:64]
                    PT_of = lambda c, ppt=ppt: ppt[:, c, 64:128]
                    pk = "PP%d" % ((lev - 1) % 2)
                for c in range(NCH):
                    bi = 0 + c // 4
                    for h in range(2):
                        p.op("pe", lambda e, c=c, h=h, bi=bi, PT_of=PT_of: e.matmul(
                            bank[bi][hs(h), (c % 4) * 128:(c % 4 + 1) * 128], lhsT=PT_of(c)[hs(h), :], rhs=Xb[hs(h), c, :],
                            start=True, stop=True, tile_position=tp(h)), r=[pk, "Xb"], w=[bk(bi)])
                if lev < NLEV - 1:
                    for c in range(NCH):
                        bi = 2 + c // 4
                        for h in range(2):
                            p.op("pe", lambda e, c=c, h=h, bi=bi, P_of=P_of, PT_of=PT_of: e.matmul(
                                bank[bi][hs(h), (c % 4) * 128:(c % 4) * 128 + 64], lhsT=PT_of(c)[hs(h), :], rhs=P_of(c)[hs(h), :],
                                start=True, stop=True, tile_position=tp(h)), r=[pk], w=[bk(bi)])
                            p.op("pe", lambda e, c=c, h=h, bi=bi, P_of=P_of, PT_of=PT_of: e.matmul(
                                bank[bi][hs(h), (c % 4) * 128 + 64:(c % 4 + 1) * 128], lhsT=P_of(c)[hs(h), :], rhs=PT_of(c)[hs(h), :],
                                start=True, stop=True, tile_position=tp(h)), r=[pk], w=[bk(bi)])
                for hf in range(2):
                    xs = Xf[:, hf * 4:(hf + 1) * 4, :].rearrange("p c t -> p (c t)")
                    p.op("dve", lambda e, hf=hf, xs=xs: e.tensor_tensor(out=xs, in0=xs, in1=bank[hf][:], op=ALU.add), r=["Xf", bk(hf)], w=["Xf"])
                p.op("dve", lambda e: e.tensor_copy(out=Xb[:], in_=Xf[:]), r=["Xf"], w=["Xb"])
                if lev < NLEV - 1:
                    ppn = PP[lev % 2]
                    for hf in range(2):
                        p.op("act", lambda e, hf=hf, ppn=ppn: e.activation(out=ppn[:, hf * 4:(hf + 1) * 4, :].rearrange("p c t -> p (c t)"),
                                                                          in_=bank[2 + hf][:], func=AF.Copy),
                             r=[bk(2 + hf)], w=["PP%d" % (lev % 2)])

            for c in range(NCH):
                for h in range(2):
                    p.op("pe", lambda e, c=c, h=h: e.matmul(bank[4][hs(h), c * 64:(c + 1) * 64], lhsT=Xb[hs(h), c, 0:64], rhs=GM[hs(h), c, 64:128],
                                                           start=True, stop=True, tile_position=tp(h)), r=["Xb", "GM"], w=[bk(4)])
                    p.op("pe", lambda e, c=c, h=h: e.matmul(bank[5][hs(h), c * 64:(c + 1) * 64], lhsT=GM[hs(h), c, 64:128], rhs=Xb[hs(h), c, 64:128],
                                                           start=True, stop=False, tile_position=tp(h)), r=["Xb", "GM"], w=[bk(5)])
                    p.op("pe", lambda e, c=c, h=h: e.matmul(bank[5][hs(h), c * 64:(c + 1) * 64], lhsT=GM[hs(h), c, 192:256], rhs=TM[hs(h), c, 1, :],
                                                           start=False, stop=True, tile_position=tp(h)), r=["TM", "GM"], w=[bk(5)])
                    p.op("pe", lambda e, c=c, h=h: e.matmul(bank[6][hs(h), c * 64:(c + 1) * 64], lhsT=Xb[hs(h), c, 0:64], rhs=TM[hs(h), c, 2, :],
                                                           start=True, stop=True, tile_position=tp(h)), r=["Xb", "TM"], w=[bk(6)])
                    p.op("pe", lambda e, c=c, h=h: e.matmul(bank[7][hs(h), c * 64:(c + 1) * 64], lhsT=TM[hs(h), c, 2, :], rhs=Xb[hs(h), c, 64:128],
                                                           start=True, stop=False, tile_position=tp(h)), r=["Xb", "TM"], w=[bk(7)])
                    p.op("pe", lambda e, c=c, h=h: e.matmul(bank[7][hs(h), c * 64:(c + 1) * 64], lhsT=TM[hs(h), c, 3, :], rhs=TM[hs(h), c, 1, :],
                                                           start=False, stop=True, tile_position=tp(h)), r=["TM"], w=[bk(7)])
            p.op("dve", lambda e: e.tensor_tensor(out=RtT[:], in0=bank[4][:], in1=Rh_f[:], op=ALU.add), r=[bk(4), "Rh_f"], w=["RtT"])
            p.op("act", lambda e: e.activation(out=v3(Y0), in_=bank[5][:], func=AF.Copy), r=[bk(5)], w=["Y0"])
            p.op("pool", lambda e: e.tensor_tensor(out=DG[:], in0=i2[:].unsqueeze(1).to_broadcast([128, NCH, 64]),
                                                  in1=e_pos[:, :, 63:64].to_broadcast([128, NCH, 64]), op=ALU.mult), r=["i2", "e_pos"], w=["DG"])
            p.op("dve", lambda e: e.tensor_tensor(out=v3(PT_f), in0=bank[6][:], in1=v3(DG), op=ALU.add), r=[bk(6), "DG"], w=["PT_f"])
            p.op("act", lambda e: e.activation(out=v3(Q_f), in_=bank[7][:], func=AF.Copy), r=[bk(7)], w=["Q_f"])

            for c in range(NCH):
                sn = 1 - s_cur
                for h in range(2):
                    p.op("pe", lambda e, c=c, h=h, s_cur=s_cur: e.matmul(bank[1][hs(h), (c % 2) * 64:(c % 2 + 1) * 64], lhsT=PT_f[hs(h), c, :],
                                                                        rhs=S_f[s_cur][hs(h), :], start=True, stop=True, tile_position=tp(h)),
                         r=["PT_f", "S_f%d" % s_cur, bk(1)], w=[bk(1) + ("a" if c % 2 else "b")])
                p.op("dve", lambda e, c=c, sn=sn: e.tensor_tensor(out=S_f[sn][:], in0=bank[1][:, (c % 2) * 64:(c % 2 + 1) * 64], in1=Q_f[:, c, :], op=ALU.add),
                     r=[bk(1) + ("a" if c % 2 else "b"), bk(1), "Q_f"], w=["S_f%d" % sn])
                for h in range(2):
                    p.op("pe", lambda e, c=c, h=h, s_cur=s_cur: e.matmul(bank[0][hs(h), c * 64:(c + 1) * 64], lhsT=RtT[hs(h), c * 64:(c + 1) * 64],
                                                                        rhs=S_b[s_cur][hs(h), :], start=True, stop=True, tile_position=tp(h)),
                         r=["RtT", "S_b%d" % s_cur], w=[bk(0)])
                p.op("act", lambda e, sn=sn: e.activation(out=S_b[sn][:], in_=S_f[sn][:], func=AF.Copy), r=["S_f%d" % sn], w=["S_b%d" % sn])
                s_cur = sn
            p.op("dve", lambda e: e.tensor_tensor(out=v3(y_tm), in0=bank[0][:], in1=v3(Y0), op=ALU.add), r=[bk(0), "Y0"], w=["y_tm"])

            for c in range(NCH):
                for h in range(2):
                    p.op("pe", lambda e, c=c, h=h: e.matmul(bank[2][hs(h), c * 64:(c + 1) * 64], lhsT=y_tm[hs(h), c, :], rhs=ident_f[hs(h), hs(h)],
                                                           start=True, stop=True, tile_position=tp(h)), r=["y_tm", "ident_f"], w=[bk(2)])
            p.op("act", lambda e: e.activation(out=yT_f[:], in_=bank[2][:], func=AF.Copy), r=[bk(2)], w=["yT_f"])
            p.op("pe", lambda e: e.matmul(bank[3][:], lhsT=obd_f[:], rhs=yT_f[:], start=True, stop=True), r=["obd_f", "yT_f"], w=[bk(3)])
            p.op("dve", lambda e: e.scalar_tensor_tensor(out=d_f[:], in0=bank[3][:], scalar=-1.0 / 64, in1=yT_f[:], op0=ALU.mult, op1=ALU.add),
                 r=[bk(3), "yT_f"], w=["d_f"])
            p.op("act", lambda e: e.activation(out=sq_f[:], in_=d_f[:], func=AF.Square), r=["d_f"], w=["sq_f"])
            p.op("pe", lambda e: e.matmul(bank[4][:], lhsT=obd_f[:], rhs=sq_f[:], start=True, stop=True), r=["obd_f", "sq_f"], w=[bk(4)])
            p.op("act", lambda e: e.activation(out=rstd[:], in_=bank[4][:], func=AF.Sqrt, scale=1.0 / 64, bias=epsc[:, 1:2]),
                 r=[bk(4), "epsc"], w=["rstd"])
            p.op("dve", lambda e: e.reciprocal(out=rstd[:], in_=rstd[:]), r=["rstd"], w=["rstd"])
            p.op("dve", lambda e: e.tensor_tensor(out=d_f[:], in0=d_f[:], in1=rstd[:], op=ALU.mult), r=["d_f", "rstd"], w=["d_f"])
            p.op("act", lambda e: e.activation(out=d_f[:], in_=d_f[:], func=AF.Identity, scale=vcol(V_GG), bias=vcol(V_GB)),
                 r=["d_f", "vec"], w=["d_f"])
            p.op("dve", lambda e: e.tensor_tensor(out=d_f[:], in0=d_f[:], in1=bon_f[:], op=ALU.add), r=["d_f", "bon_f"], w=["d_f"])
            y_o = yo[ti % 2]
            yk = "yo%d" % (ti % 2)
            p.op("dve", lambda e, y_o=y_o: e.tensor_tensor(out=y_o[:], in0=d_f[:], in1=g_f[:], op=ALU.mult), r=["d_f", "g_f"], w=[yk])
            p.dma("sp", ygT[:, t0:t0 + TT], y_o[:], r=[yk], semkey="o_" + yk, store=True)
            if debug and ti == 0:
                f2 = lambda tl: tl[:].rearrange("p c t -> p (c t)")
                dl = [(r_f[:], "r_f"), (k_f[:], "k_f"), (v_f[:], "v_f"), (al_f[:], "al_f"), (g_f[:], "g_f"), (kk_f[:], "kk_f"),
                      (bon_f[:], "bon_f"), (f2(e_pos), "e_pos"), (f2(e_ex), "e_ex"), (f2(e_rem), "e_rem"), (Rh_f[:], "Rh_f"),
                      (yT_f[:], "yT_f"), (rstd[:], "rstd"), (d_f[:], "d_f"), (f2(Y0), "Y0"), (f2(PT_f), "PT_f"), (f2(Q_f), "Q_f"),
                      (f2(y_tm), "y_tm"), (Xf[:, 0:4, :].rearrange("p c t -> p (c t)"), "Xf"), (kt_f[:], "kt_f"), (b_f[:], "b_f"),
                      (f2(e_neg), "e_neg")]
                for i, (ap_, key) in enumerate(dl):
                    p.dma("sp", dbg[i], ap_, r=[key], semkey="dbgo", store=True)
        p.emit()
    return nc


def rwkv_inputs(inp, ntiles=T // TT, cores=range(NCORES)):
    TL = ntiles * TT
    x = inp["x"][0]
    xT = np.zeros((C, TL + 1), np.float32)
    xT[:, 1:] = x[:TL].T
    tri3, msk, i2, onesbd = rwkv_consts()
    mu6 = np.ascontiguousarray(inp["rwkv_mu"][0].T)
    maps = []
    for c in cores:
        cs = slice(c * 128, (c + 1) * 128)
        wrkv = inp["rwkv_w_rkv"][0]
        wbig = np.concatenate([wrkv[0][:, cs], wrkv[1][:, cs], wrkv[2][:, cs], inp["rwkv_w1"][0], inp["rwkv_a1"][0], inp["rwkv_g1"][0]], axis=1)
        w2a = np.concatenate([inp["rwkv_w2"][0][:, cs], inp["rwkv_w0"][0][None, cs]], axis=0)
        a2p = np.zeros((128, 128), np.float32)
        a2p[64:128] = inp["rwkv_a2"][0][:, cs]
        ka = inp["rwkv_k_a"][0][cs]
        one = np.ones_like(ka)
        vecs = np.stack([inp["rwkv_k_k"][0][cs], ka, one, inp["rwkv_r_k"][0].reshape(-1)[cs], inp["rwkv_gn_g"][0][cs],
                         inp["rwkv_gn_b"][0][cs], inp["rwkv_a0"][0][cs], one], axis=1).astype(np.float32)
        maps.append({
            "xT": xT, "wbig": np.ascontiguousarray(wbig), "mu6": mu6, "w2a": np.ascontiguousarray(w2a), "a2p": a2p,
            "g2p": np.ascontiguousarray(inp["rwkv_g2"][0][:, cs]), "vecs": np.ascontiguousarray(vecs),
            "tri3": tri3, "msk": msk, "i2": i2, "onesbd": onesbd,
        })
    return maps


def run_rwkv(inp):
    nc = _get("rwkv", lambda: build_rwkv(T // TT))
    maps = rwkv_inputs(inp)
    res = _run(nc, maps)
    return np.concatenate([r["ygT"] for r in res.results], axis=0)


def kernel(**inputs):
    inp = {k: np.asarray(v) for k, v in inputs.items()}
    x0 = np.ascontiguousarray(inp["x"][0], dtype=np.float32)
    ygT = run_rwkv(inp)
    lnp0 = np.stack([inp["ln_mix_g"][0], inp["ln_mix_b"][0], inp["ln_ffn_g"][0], inp["ln_ffn_b"][0]]).astype(np.float32)
    x1, qkv = run_post(ygT, x0, inp["rwkv_w_o"][0], inp["ffn_w_in"][0], inp["ffn_w_down"][0], lnp0,
                       inp["moba_w_qkv"][0], rope_tables())
    oT = run_attn(qkv)
    lnp1 = np.stack([inp["ln_mix_g"][1], inp["ln_mix_b"][1], inp["ln_ffn_g"][1], inp["ln_ffn_b"][1]]).astype(np.float32)
    out, _ = run_post(oT, x1, inp["moba_w_o"][0], inp["ffn_w_in"][1], inp["ffn_w_down"][1], lnp1)
    return out.reshape(1, T, C).astype(np.float32)
```

```python
import math
from contextlib import ExitStack

import numpy as np
import ml_dtypes

import concourse.bass as bass
import concourse.mybir as mybir
from concourse.bass_utils import run_bass_kernel_spmd

F32 = mybir.dt.float32
BF16 = mybir.dt.bfloat16
ALU = mybir.AluOpType
AF = mybir.ActivationFunctionType
AX = mybir.AxisListType

NCORES = 8
T = 16384
C = 1024
H = 16
DH = 64
DFF = 2816
DEPTH = 2
ALPHA = (2 * DEPTH) ** 0.25
LN_EPS = 1e-5
GN_EPS = 64 * 1e-5
NEG = -30000.0


class _Op:
    __slots__ = ("eng", "fn", "deps", "is_dma", "semkey", "sig", "need_sig", "idx")


class Prog:
    ENGS = ("pe", "act", "dve", "pool", "sp")

    def __init__(self, nc):
        self.nc = nc
        self.ops = []
        self.last_w = {}
        self.readers = {}
        self.store_keys = []

    def _deps(self, r, w):
        deps = set()
        for k in list(r) + list(w):
            lw = self.last_w.get(k)
            if lw is not None:
                deps.add(lw)
        for k in w:
            for rd in self.readers.get(k, ()):
                deps.add(rd)
        return deps

    def _commit(self, idx, r, w):
        for k in r:
            self.readers.setdefault(k, []).append(idx)
        for k in w:
            self.last_w[k] = idx
            self.readers[k] = []

    def op(self, eng, fn, r=(), w=()):
        o = _Op()
        o.eng, o.fn, o.is_dma, o.semkey, o.sig, o.need_sig = eng, fn, False, None, None, False
        o.deps = self._deps(r, w)
        o.idx = len(self.ops)
        self.ops.append(o)
        self._commit(o.idx, r, w)
        return o.idx

    def dma(self, q, out, in_, r=(), w=(), semkey=None, store=False):
        o = _Op()
        o.eng, o.is_dma, o.sig, o.need_sig = q, True, None, True
        o.fn = lambda e, out=out, in_=in_: e.dma_start(out=out, in_=in_)
        o.semkey = semkey if semkey is not None else (list(w)[0] if w else list(r)[0])
        o.deps = self._deps(r, w)
        o.idx = len(self.ops)
        self.ops.append(o)
        self._commit(o.idx, r, w)
        if store:
            self.store_keys.append(o.semkey)
        return o.idx

    def emit(self):
        nc = self.nc
        ops = self.ops
        for o in ops:
            for d in o.deps:
                od = ops[d]
                if od.is_dma:
                    continue
                if od.eng == o.eng and not o.is_dma and o.eng == "pe":
                    continue
                od.need_sig = True
        cnt = {e: 0 for e in self.ENGS}
        dcnt = {}
        for o in ops:
            if o.is_dma:
                dcnt[o.semkey] = dcnt.get(o.semkey, 0) + 16
                o.sig = ("d:" + o.semkey, dcnt[o.semkey])
            elif o.need_sig:
                cnt[o.eng] += 1
                o.sig = ("e:" + o.eng, cnt[o.eng])
        semnames = ["e:" + e for e in self.ENGS] + ["d:" + k for k in dcnt]
        with ExitStack() as es:
            sems = {}
            for i, n in enumerate(semnames):
                sems[n] = es.enter_context(nc.semaphore("s%d" % i))
            block = es.enter_context(nc.Block())
            per_eng = {e: [o for o in ops if o.eng == e] for e in self.ENGS}
            final_waits = [("d:" + k, dcnt[k]) for k in dict.fromkeys(self.store_keys)]

            def run(eng_name, e):
                waited = {}
                for o in per_eng[eng_name]:
                    need = {}
                    for d in o.deps:
                        od = ops[d]
                        if od.sig is None:
                            continue
                        if (not od.is_dma) and od.eng == eng_name and eng_name == "pe" and not o.is_dma:
                            continue
                        s, v = od.sig
                        if need.get(s, 0) < v:
                            need[s] = v
                    for s, v in need.items():
                        if waited.get(s, 0) < v:
                            e.wait_ge(sems[s], v)
                            waited[s] = v
                    ins = o.fn(e)
                    if o.sig is not None:
                        s, v = o.sig
                        ins.then_inc(sems[s], 16 if o.is_dma else 1)
                if eng_name == "sp":
                    for s, v in final_waits:
                        e.wait_ge(sems[s], v)

            @block.tensor
            def _(e):
                run("pe", e)

            @block.scalar
            def _(e):
                run("act", e)

            @block.vector
            def _(e):
                run("dve", e)

            @block.gpsimd
            def _(e):
                run("pool", e)

            @block.sync
            def _(e):
                run("sp", e)


def _bcast_rows(ap_1d, nparts):
    return ap_1d.partition_broadcast(nparts)


TOK = T // NCORES
GRP = 512
NGRP = TOK // GRP
NFB = DFF // 128


def build_post(with_qkv):
    nc = bass.Bass("TRN2", target_bir_lowering=False)
    aT = nc.dram_tensor("aT", [C, TOK], BF16, kind="ExternalInput").ap()
    xres = nc.dram_tensor("xres", [TOK, C], F32, kind="ExternalInput").ap()
    w_o = nc.dram_tensor("w_o", [C, C], F32, kind="ExternalInput").ap()
    w_in = nc.dram_tensor("w_in", [C, 2 * DFF], F32, kind="ExternalInput").ap()
    w_dn = nc.dram_tensor("w_dn", [DFF, C], F32, kind="ExternalInput").ap()
    lnp = nc.dram_tensor("lnp", [4, C], F32, kind="ExternalInput").ap()
    xout = nc.dram_tensor("xout", [TOK, C], F32, kind="ExternalOutput").ap()
    if with_qkv:
        w_qkv = nc.dram_tensor("w_qkv", [C, 3 * C], F32, kind="ExternalInput").ap()
        rope = nc.dram_tensor("rope", [TOK, 4, 32], F32, kind="ExternalInput").ap()
        qkv_out = nc.dram_tensor("qkv", [3, TOK, C], F32, kind="ExternalOutput").ap()

    NUNITS = NFB // 2 + (6 if with_qkv else 0)
    wscr = nc.dram_tensor("wscr", [NUNITS, 128, 8 * 512], BF16).ap()

    es = ExitStack()
    sb = lambda name, shape, dt: es.enter_context(nc.sbuf_tensor(name, shape, dt))
    ps = lambda name, shape, dt: es.enter_context(nc.psum_tensor(name, shape, dt))
    with es:
        ident = sb("ident", [128, 128], BF16)
        lnb = sb("lnb", [128, 4, C], F32)
        wdn_sb = sb("wdn_sb", [128, NFB, C], BF16)
        stg = [sb("stg%d" % i, [128, 4096], F32) for i in range(2)]
        win_sb = [sb("win%d" % i, [128, 8, 512], BF16) for i in range(2)]
        aT_sb = [sb("aT%d" % i, [128, 8, GRP], BF16) for i in range(2)]
        xr_sb = [sb("xr%d" % i, [128, C], F32) for i in range(2)]
        xg = sb("xg", [128, 4, C], F32)
        xb = sb("xb", [128, C], BF16)
        xT = sb("xT", [128, 8, GRP], BF16)
        actT = sb("actT", [128, NFB, GRP], BF16)
        sg = [sb("sg%d" % i, [128, GRP], F32) for i in range(2)]
        junk = sb("junk", [128, C], BF16)
        st = sb("st", [128, 8], F32)
        epsc = sb("epsc", [128, 1], F32)
        if with_qkv:
            rp_sb = sb("rp", [128, 4, 4, 32], F32)
            qo = [sb("qo%d" % i, [128, 512], F32) for i in range(2)]
            tmp = [sb("tmp%d" % i, [128, 8, 32], F32) for i in range(4)]
        acc = ps("acc", [128, C], F32)
        trp = ps("trp", [128, C], BF16)
        gu = [ps("gu%d" % i, [128, 2, GRP], F32) for i in range(2)]

        p = Prog(nc)
        p.op("pool", lambda e: e.memset(ident[:], 0.0), w=["ident"])
        p.op("pool", lambda e: e.affine_select(out=ident[:], in_=ident[:], pattern=[[-1, 128]],
                                               compare_op=ALU.not_equal, fill=1.0, base=0,
                                               channel_multiplier=1), r=["ident"], w=["ident"])
        p.op("pool", lambda e: e.memset(epsc[:], LN_EPS), w=["epsc"])
        p.dma("sp", lnb[:], lnp.partition_broadcast(128), w=["lnb"])

        stg_n = [0]

        def load_cast(parts, dst_ap, dst_keys):
            i = stg_n[0] % 2
            stg_n[0] += 1
            sk = "stg%d" % i
            for view_fn, src in parts:
                p.dma("sp", view_fn(stg[i]), src, w=[sk], semkey=sk)
            return i, sk

        def load_w8(dst, dkey, src_cols_list):
            i = stg_n[0] % 2
            stg_n[0] += 1
            sk = "stg%d" % i
            sv = stg[i][:].rearrange("p (k c) -> p k c", k=8)
            for src, c0 in src_cols_list:
                n = src.shape[1]
                p.dma("sp", sv[:, :, c0:c0 + n], src.rearrange("(k p) c -> p k c", p=128), w=[sk], semkey=sk)
            p.op("act", lambda e, dst=dst, sv=sv: e.activation(out=dst[:, 0:4, :], in_=sv[:, 0:4, :], func=AF.Copy), r=[sk], w=[dkey])
            p.op("dve", lambda e, dst=dst, sv=sv: e.tensor_copy(out=dst[:, 4:8, :], in_=sv[:, 4:8, :]), r=[sk], w=[dkey])

        def load_rows4(dst_ap, dkey, src_rows):
            i = stg_n[0] % 2
            stg_n[0] += 1
            sk = "stg%d" % i
            nblk = src_rows.shape[0] // 128
            sv = stg[i][:, 0:nblk * 1024].rearrange("p (k c) -> p k c", k=nblk)
            p.dma("sp", sv, src_rows.rearrange("(k p) c -> p k c", p=128), w=[sk], semkey=sk)
            h2 = max(1, nblk // 2)
            p.op("act", lambda e, dst_ap=dst_ap, sv=sv, h2=h2: e.activation(out=dst_ap[:, 0:h2, :], in_=sv[:, 0:h2, :], func=AF.Copy), r=[sk], w=[dkey])
            if nblk > h2:
                p.op("dve", lambda e, dst_ap=dst_ap, sv=sv, h2=h2: e.tensor_copy(out=dst_ap[:, h2:, :], in_=sv[:, h2:, :]), r=[sk], w=[dkey])

        for f4 in range(0, NFB, 4):
            n = min(4, NFB - f4)
            load_rows4(wdn_sb[:, f4:f4 + n, :], "wdn", w_dn[f4 * 128:(f4 + n) * 128, :])

        def layer_norm(tile_ap, gi, key):
            p.op("act", lambda e: e.activation(out=junk[:], in_=tile_ap, func=AF.Copy, accum_out=st[:, 0:1]),
                 r=[key], w=["junk", "st"])
            p.op("act", lambda e: e.activation(out=junk[:], in_=tile_ap, func=AF.Square, accum_out=st[:, 1:2]),
                 r=[key], w=["junk", "st"])
            p.op("dve", lambda e: e.tensor_scalar(out=st[:, 2:3], in0=st[:, 0:1], scalar1=1.0 / C, scalar2=None, op0=ALU.mult),
                 r=["st"], w=["st2"])
            p.op("dve", lambda e: e.tensor_tensor(out=st[:, 3:4], in0=st[:, 2:3], in1=st[:, 2:3], op=ALU.mult),
                 r=["st2"], w=["st3"])
            p.op("dve", lambda e: e.scalar_tensor_tensor(out=st[:, 4:5], in0=st[:, 1:2], scalar=1.0 / C, in1=st[:, 3:4],
                                                         op0=ALU.mult, op1=ALU.subtract), r=["st", "st3"], w=["st4"])
            p.op("act", lambda e: e.activation(out=st[:, 5:6], in_=st[:, 4:5], func=AF.Sqrt, bias=epsc[:, 0:1]), r=["st4", "epsc"], w=["st5"])
            p.op("dve", lambda e: e.reciprocal(out=st[:, 6:7], in_=st[:, 5:6]), r=["st5"], w=["st6"])
            p.op("dve", lambda e: e.tensor_scalar(out=tile_ap, in0=tile_ap, scalar1=st[:, 2:3], scalar2=st[:, 6:7],
                                                  op0=ALU.subtract, op1=ALU.mult), r=[key, "st2", "st6"], w=[key])
            p.op("dve", lambda e: e.tensor_tensor(out=tile_ap, in0=tile_ap, in1=lnb[:, gi, :], op=ALU.mult),
                 r=[key, "lnb"], w=[key])
            p.op("dve", lambda e: e.tensor_tensor(out=tile_ap, in0=tile_ap, in1=lnb[:, gi + 1, :], op=ALU.add),
                 r=[key, "lnb"], w=[key])

        def to_channel_major(tile_ap, key_in, ti):
            p.op("act", lambda e: e.activation(out=xb[:], in_=tile_ap, func=AF.Copy), r=[key_in], w=["xb"])
            for kc in range(8):
                p.op("pe", lambda e, kc=kc: e.transpose(trp[:, kc * 128:(kc + 1) * 128], xb[:, kc * 128:(kc + 1) * 128], ident[:]),
                     r=["xb", "ident"], w=["trp"])
            p.op("dve", lambda e: e.tensor_copy(out=xT[:, :, ti * 128:(ti + 1) * 128],
                                                in_=trp[:].rearrange("p (k t) -> p k t", k=8)),
                 r=["trp"], w=["xT"])

        def load_acts(g):
            t0 = g * GRP
            a_sb = aT_sb[g % 2]
            ak = "aT%d" % (g % 2)
            p.dma("sp", a_sb[:], aT[:, t0:t0 + GRP].rearrange("(k p) t -> p k t", p=128), w=[ak], semkey=ak)

        load_acts(0)
        for g in range(NGRP):
            t0 = g * GRP
            a_sb = aT_sb[g % 2]
            ak = "aT%d" % (g % 2)
            if with_qkv:
                p.dma("sp", rp_sb[:], rope[t0:t0 + GRP].rearrange("(i p) f c -> p i f c", p=128), w=["rp"])
            wo_v = [win_sb[h][:].rearrange("p k c -> p (k c)").rearrange("p (k c) -> p k c", k=4) for h in range(2)]
            for h in range(2):
                load_rows4(wo_v[h], "win%d" % h, w_o[h * 512:(h + 1) * 512, :])
            def phaseA_tail(ti):
                layer_norm(xg[:, ti, :], 0, "xg%d" % ti)
                to_channel_major(xg[:, ti, :], "xg%d" % ti, ti)

            for ti in range(4):
                r0 = t0 + ti * 128
                xr = xr_sb[(g * 4 + ti) % 2]
                xk = "xr%d" % ((g * 4 + ti) % 2)
                p.dma("sp", xr[:], xres[r0:r0 + 128, :], w=[xk], semkey=xk)
                for hf in range(2):
                    for kc in range(8):
                        p.op("pe", lambda e, kc=kc, hf=hf, ti=ti, a_sb=a_sb, wv=wo_v[kc // 4]: e.matmul(
                            acc[:, hf * 512:(hf + 1) * 512], lhsT=a_sb[:, kc, ti * 128:(ti + 1) * 128],
                            rhs=wv[:, kc % 4, hf * 512:(hf + 1) * 512], start=(kc == 0), stop=(kc == 7)),
                            r=[ak, "win%d" % (kc // 4)], w=["acc"])
                xk_out = "xg%d" % ti
                p.op("dve", lambda e, xr=xr, ti=ti: e.scalar_tensor_tensor(out=xg[:, ti, :], in0=xr[:], scalar=ALPHA, in1=acc[:],
                                                                           op0=ALU.mult, op1=ALU.add),
                     r=[xk, "acc"], w=[xk_out])
                if ti > 0:
                    phaseA_tail(ti - 1)
            phaseA_tail(3)
            if g + 1 < NGRP:
                load_acts(g + 1)
            NU = NFB // 2
            for u in range(NU):
                wsb = win_sb[u % 2]
                wk = "win%d" % (u % 2)
                if g == 0:
                    load_w8(wsb, wk, [(w_in[:, u * 256:(u + 1) * 256], 0), (w_in[:, DFF + u * 256:DFF + (u + 1) * 256], 256)])
                    p.dma("act", wscr[u], wsb[:].rearrange("p k c -> p (k c)"), r=[wk], w=["wscr%d" % u], semkey="wscr%d" % u)
                else:
                    p.dma("sp", wsb[:].rearrange("p k c -> p (k c)"), wscr[u], r=["wscr%d" % u], w=[wk], semkey=wk)
                for j in range(2):
                    fb = u * 2 + j
                    gps = gu[fb % 2]
                    gk = "gu%d" % (fb % 2)
                    for which in range(2):
                        for kc in range(8):
                            p.op("pe", lambda e, kc=kc, which=which, j=j, wsb=wsb, gps=gps: e.matmul(
                                gps[:, which, :], lhsT=wsb[:, kc, which * 256 + j * 128: which * 256 + (j + 1) * 128],
                                rhs=xT[:, kc, :], start=(kc == 0), stop=(kc == 7)),
                                r=[wk, "xT"], w=[gk])
                    s_sb = sg[fb % 2]
                    sk = "sg%d" % (fb % 2)
                    p.op("act", lambda e, gps=gps, s_sb=s_sb: e.activation(out=s_sb[:], in_=gps[:, 0, :], func=AF.Silu),
                         r=[gk], w=[sk])
                    p.op("dve", lambda e, gps=gps, s_sb=s_sb, fb=fb: e.tensor_tensor(out=actT[:, fb, :], in0=s_sb[:], in1=gps[:, 1, :], op=ALU.mult),
                         r=[gk, sk], w=["actT"])
            def down_tail(ti):
                r0 = t0 + ti * 128
                xk_out = "xg%d" % ti
                layer_norm(xg[:, ti, :], 2, xk_out)
                p.dma("act", xout[r0:r0 + 128, :], xg[:, ti, :], r=[xk_out], semkey="o_" + xk_out, store=True)
                if with_qkv:
                    to_channel_major(xg[:, ti, :], xk_out, ti)

            for ti in range(4):
                for hf in range(2):
                    for fb in range(NFB):
                        p.op("pe", lambda e, fb=fb, hf=hf, ti=ti: e.matmul(
                            acc[:, hf * 512:(hf + 1) * 512], lhsT=actT[:, fb, ti * 128:(ti + 1) * 128],
                            rhs=wdn_sb[:, fb, hf * 512:(hf + 1) * 512], start=(fb == 0), stop=(fb == NFB - 1)),
                            r=["actT", "wdn"], w=["acc"])
                xk_out = "xg%d" % ti
                p.op("dve", lambda e, ti=ti: e.scalar_tensor_tensor(out=xg[:, ti, :], in0=xg[:, ti, :], scalar=ALPHA, in1=acc[:],
                                                                    op0=ALU.mult, op1=ALU.add),
                     r=[xk_out, "acc"], w=[xk_out])
                if ti > 0:
                    down_tail(ti - 1)
            down_tail(3)
            if with_qkv:
                for cb in range(6):
                    wsb = win_sb[cb % 2]
                    wk = "win%d" % (cb % 2)
                    uq = NFB // 2 + cb
                    if g == 0:
                        load_w8(wsb, wk, [(w_qkv[:, cb * 512:(cb + 1) * 512], 0)])
                        p.dma("act", wscr[uq], wsb[:].rearrange("p k c -> p (k c)"), r=[wk], w=["wscr%d" % uq], semkey="wscr%d" % uq)
                    else:
                        p.dma("sp", wsb[:].rearrange("p k c -> p (k c)"), wscr[uq], r=["wscr%d" % uq], w=[wk], semkey=wk)
                    for ti in range(4):
                        r0 = t0 + ti * 128
                        gps = gu[(cb * 4 + ti) % 2]
                        gk = "gu%d" % ((cb * 4 + ti) % 2)
                        for kc in range(8):
                            p.op("pe", lambda e, kc=kc, ti=ti, wsb=wsb, gps=gps: e.matmul(
                                gps[:, 0, :], lhsT=xT[:, kc, ti * 128:(ti + 1) * 128], rhs=wsb[:, kc, :],
                                start=(kc == 0), stop=(kc == 7)), r=[wk, "xT"], w=[gk])
                        o_sb = qo[(cb * 4 + ti) % 2]
                        ok = "qo%d" % ((cb * 4 + ti) % 2)
                        which = cb // 2
                        if which == 2:
                            p.op("act", lambda e, gps=gps, o_sb=o_sb: e.activation(out=o_sb[:], in_=gps[:, 0, :], func=AF.Copy),
                                 r=[gk], w=[ok])
                        else:
                            src = gps[:, 0, :].rearrange("p (h d) -> p h d", h=8)
                            dst = o_sb[:].rearrange("p (h d) -> p h d", h=8)
                            cos = rp_sb[:, ti, 2 * which, :].unsqueeze(1).to_broadcast([128, 8, 32])
                            sin = rp_sb[:, ti, 2 * which + 1, :].unsqueeze(1).to_broadcast([128, 8, 32])
                            tk = ["tmp%d" % i for i in range(4)]
                            p.op("dve", lambda e, src=src, cos=cos: e.tensor_tensor(out=tmp[0][:], in0=src[:, :, 0:32], in1=cos, op=ALU.mult),
                                 r=[gk, "rp"], w=[tk[0]])
                            p.op("dve", lambda e, src=src, sin=sin: e.tensor_tensor(out=tmp[1][:], in0=src[:, :, 32:64], in1=sin, op=ALU.mult),
                                 r=[gk, "rp"], w=[tk[1]])
                            p.op("dve", lambda e, src=src, cos=cos: e.tensor_tensor(out=tmp[2][:], in0=src[:, :, 32:64], in1=cos, op=ALU.mult),
                                 r=[gk, "rp"], w=[tk[2]])
                            p.op("dve", lambda e, src=src, sin=sin: e.tensor_tensor(out=tmp[3][:], in0=src[:, :, 0:32], in1=sin, op=ALU.mult),
                                 r=[gk, "rp"], w=[tk[3]])
                            p.op("pool", lambda e, dst=dst: e.tensor_tensor(out=dst[:, :, 0:32], in0=tmp[0][:], in1=tmp[1][:], op=ALU.subtract),
                                 r=[tk[0], tk[1]], w=[ok])
                            p.op("pool", lambda e, dst=dst: e.tensor_tensor(out=dst[:, :, 32:64], in0=tmp[2][:], in1=tmp[3][:], op=ALU.add),
                                 r=[tk[2], tk[3]], w=[ok])
                        p.dma("act", qkv_out[which, r0:r0 + 128, (cb % 2) * 512:(cb % 2 + 1) * 512], o_sb[:], r=[ok],
                              semkey="o_" + ok, store=True)
        p.emit()
    return nc


_NC_CACHE = {}


def _get(name, builder):
    if name not in _NC_CACHE:
        import time as _t
        t0 = _t.time()
        _NC_CACHE[name] = builder()
        print("[kernel] built", name, "in %.1fs" % (_t.time() - t0), flush=True)
    return _NC_CACHE[name]


def _run(nc, in_maps):
    return run_bass_kernel_spmd(nc, in_maps, core_ids=list(range(NCORES)))


def run_post(aT_full, xres_full, w_o, w_in, w_dn, lnp, w_qkv=None, rope=None):
    with_qkv = w_qkv is not None
    nc = _get("post_qkv" if with_qkv else "post", lambda: build_post(with_qkv))
    in_maps = []
    for c in range(NCORES):
        m = {
            "aT": np.ascontiguousarray(aT_full[:, c * TOK:(c + 1) * TOK]),
            "xres": np.ascontiguousarray(xres_full[c * TOK:(c + 1) * TOK]),
            "w_o": w_o, "w_in": w_in, "w_dn": w_dn, "lnp": lnp,
        }
        if with_qkv:
            m["w_qkv"] = w_qkv
            m["rope"] = np.ascontiguousarray(rope[c * TOK:(c + 1) * TOK])
        in_maps.append(m)
    res = _run(nc, in_maps)
    xo = np.concatenate([r["xout"] for r in res.results], axis=0)
    if with_qkv:
        qkv = np.concatenate([r["qkv"] for r in res.results], axis=1)
        return xo, qkv
    return xo, None


def rope_tables():
    inv = (10000.0 ** (-np.arange(0, DH, 2, dtype=np.float32) / DH)).astype(np.float32)
    ang = (np.arange(T, dtype=np.float32)[:, None] * inv[None, :]).astype(np.float32)
    cos, sin = np.cos(ang).astype(np.float32), np.sin(ang).astype(np.float32)
    s = np.float32(DH ** -0.5)
    return np.ascontiguousarray(np.stack([cos * s, sin * s, cos, sin], axis=1))


NBLK = T // 256
QG = 512
NQG = T // QG


def moba_consts():
    blk1h = np.zeros((64, T), np.float32)
    for b in range(NBLK):
        blk1h[b, b * 256:(b + 1) * 256] = 1.0
    n = np.arange(64)[:, None]
    b = np.arange(64)[None, :]
    p01 = (b < n).astype(np.float32)
    o01 = (b == n).astype(np.float32)
    pbias = np.where(b < n, 0.0, -1e9).astype(np.float32)
    tabs = np.stack([pbias, p01, o01], axis=0)
    dm = np.zeros((4, 128, 4, 128), np.float32)
    key = np.arange(128)[:, None]
    q = np.arange(128)[None, :]
    tri = np.where(key <= q, 0.0, NEG)
    for j in range(4):
        for g in range(4):
            if j > g:
                dm[j, :, g, :] = NEG
            elif j == g:
                dm[j, :, g, :] = tri
    return (blk1h.astype(ml_dtypes.bfloat16), tabs, dm.reshape(4, 128, 512).astype(ml_dtypes.bfloat16))


def build_attn():
    nc = bass.Bass("TRN2", target_bir_lowering=False)
    qT = nc.dram_tensor("qT", [128, T], F32, kind="ExternalInput").ap()
    kT = nc.dram_tensor("kT", [128, T], F32, kind="ExternalInput").ap()
    v = nc.dram_tensor("v", [T, 128], F32, kind="ExternalInput").ap()
    blk1h = nc.dram_tensor("blk1h", [64, T], BF16, kind="ExternalInput").ap()
    tabs = nc.dram_tensor("tabs", [3, 64 * 64], F32, kind="ExternalInput").ap()
    dmask = nc.dram_tensor("dmask", [4, 128, 512], BF16, kind="ExternalInput").ap()
    oT = nc.dram_tensor("oT", [128, T], BF16, kind="ExternalOutput").ap()

    es = ExitStack()
    sb = lambda name, shape, dt: es.enter_context(nc.sbuf_tensor(name, shape, dt))
    ps = lambda name, shape, dt: es.enter_context(nc.psum_tensor(name, shape, dt))
    with es:
        ident = sb("ident", [128, 128], BF16)
        ones_f = sb("ones_f", [128, 64], F32)
        kaug = sb("kaug", [128, T], BF16)
        qaug = sb("qaug", [128, T], BF16)
        vsb = sb("vsb", [128, 128, 2, 65], BF16)
        tb = sb("tb", [128, 3, 64 * 64], F32)
        stg = [sb("stg%d" % i, [128, 2048], F32) for i in range(2)]
        dm = sb("dm", [128, 4, 512], BF16)
        kmean = sb("kmean", [64, 64], F32)
        kmean_b = sb("kmean_b", [64, 64], BF16)
        gm = [sb("gm%d" % i, [128, 8, 64], F32) for i in range(2)]
        top8 = [sb("top8%d" % i, [128, 8, 8], F32) for i in range(2)]
        selt = [sb("selt%d" % i, [128, 8, 64], F32) for i in range(2)]
        negm = [sb("negm%d" % i, [128, 8, 64], BF16) for i in range(2)]
        pT = [sb("pT%d" % i, [128, QG], BF16) for i in range(3)]
        osb = [sb("osb%d" % i, [65, QG], F32) for i in range(2)]
        obf = [sb("obf%d" % i, [64, QG], BF16) for i in range(2)]
        s_ps = [ps("s_ps%d" % i, [128, QG], F32) for i in range(3)]
        o_ps = [ps("o_ps%d" % i, [65, QG], F32) for i in range(2)]
        g_ps = ps("g_ps", [128, 8, 64], F32)
        m_ps = ps("m_ps", [128, 8 * 128], BF16)
        bc_ps = m_ps[:].bitcast(F32)

        p = Prog(nc)
        p.op("pool", lambda e: e.memset(ident[:], 0.0), w=["ident"])
        p.op("pool", lambda e: e.affine_select(out=ident[:], in_=ident[:], pattern=[[-1, 128]],
                                               compare_op=ALU.not_equal, fill=1.0, base=0,
                                               channel_multiplier=1), r=["ident"], w=["ident"])
        p.op("pool", lambda e: e.memset(ones_f[:], 1.0), w=["ones_f"])
        p.op("pool", lambda e: e.memset(vsb[:, :, :, 64:65], 1.0), w=["vsb1"])
        p.dma("sp", tb[:], tabs.partition_broadcast(128), w=["tb"])
        p.dma("sp", dm[:], dmask.rearrange("j k q -> k j q"), w=["dm"])
        p.dma("sp", kaug[64:128, :], blk1h, w=["kaug_hi"])
        stg_n = [0]

        def stage(view_fn, src, cast_fn, wkeys):
            i = stg_n[0] % 2
            stg_n[0] += 1
            sk = "stg%d" % i
            sv = view_fn(stg[i])
            p.dma("sp", sv, src, w=[sk], semkey=sk)
            eng = "act" if (stg_n[0] % 2) else "dve"
            p.op(eng, lambda e, sv=sv, eng=eng: cast_fn(e, sv, eng == "act"), r=[sk], w=wkeys)

        gcount = 0
        for hh in range(2):
            for c8 in range(8):
                sl = slice(c8 * 2048, (c8 + 1) * 2048)
                for dst, src, key in ((kaug, kT, "kaug_lo"), (qaug, qT, "qaug_lo")):
                    stage(lambda t_: t_[0:64, :], src[hh * 64:(hh + 1) * 64, sl],
                          lambda e, sv, is_act, dst=dst, sl=sl: (e.activation(out=dst[0:64, sl], in_=sv, func=AF.Copy) if is_act
                                                        else e.tensor_copy(out=dst[0:64, sl], in_=sv)),
                          [key])
            if hh == 0:
                for c8 in range(8):
                    kb0 = c8 * 16
                    stage(lambda t_: t_[:].rearrange("p (kb c) -> p kb c", c=128),
                          v[kb0 * 128:(kb0 + 16) * 128, :].rearrange("(kb p) c -> p kb c", p=128),
                          lambda e, sv, is_act, kb0=kb0: (e.activation(out=vsb[:, kb0:kb0 + 16, :, 0:64], in_=sv.rearrange("p kb (h d) -> p kb h d", h=2), func=AF.Copy)
                                                  if is_act else
                                                  e.tensor_copy(out=vsb[:, kb0:kb0 + 16, :, 0:64], in_=sv.rearrange("p kb (h d) -> p kb h d", h=2))),
                          ["vsb"])

            p.op("dve", lambda e: e.tensor_reduce(out=kmean[:], in_=kaug[0:64, :].rearrange("p (b s) -> p b s", s=256),
                                                  axis=AX.X, op=ALU.add), r=["kaug_lo"], w=["kmean"])
            p.op("dve", lambda e: e.tensor_scalar(out=kmean_b[:], in0=kmean[:], scalar1=1.0 / 256, scalar2=None, op0=ALU.mult),
                 r=["kmean"], w=["kmean_b"])
            for G8 in range(T // 1024):
                i2 = gcount % 2
                gcount += 1
                n0 = 4 * G8
                for c in range(8):
                    q0 = G8 * 1024 + c * 128
                    p.op("pe", lambda e, c=c, q0=q0: e.matmul(g_ps[:, c, :], lhsT=qaug[0:64, q0:q0 + 128], rhs=kmean_b[:],
                                                            start=True, stop=True), r=["qaug_lo", "kmean_b"], w=["g_ps"])
                gmk, t8k, slk, ngk = "gm%d" % i2, "top8%d" % i2, "selt%d" % i2, "negm%d" % i2
                tbv = tb[:].rearrange("p t (n b) -> p t n b", b=64)
                b0, b1, b2 = [tbv[:, t, n0:n0 + 4, :].unsqueeze(2).to_broadcast([128, 4, 2, 64]) for t in range(3)]
                g4 = lambda tl: tl[:].rearrange("p (a c) b -> p a c b", c=2)
                gm4, sl4, gp4 = g4(gm[i2]), g4(selt[i2]), g_ps[:].rearrange("p (a c) b -> p a c b", c=2)
                p.op("dve", lambda e, gm4=gm4, gp4=gp4, b0=b0: e.tensor_tensor(out=gm4, in0=gp4, in1=b0, op=ALU.add),
                     r=["g_ps", "tb"], w=[gmk])
                for c in range(8):
                    p.op("dve", lambda e, c=c, i2=i2: e.max(out=top8[i2][:, c, :], in_=gm[i2][:, c, :]), r=[gmk], w=[t8k])
                p.op("dve", lambda e, i2=i2: e.tensor_tensor(out=selt[i2][:], in0=gm[i2][:],
                                                            in1=top8[i2][:, :, 2:3].to_broadcast([128, 8, 64]), op=ALU.is_ge),
                     r=[gmk, t8k], w=[slk])
                p.op("dve", lambda e, sl4=sl4, b1=b1: e.tensor_tensor(out=sl4, in0=sl4, in1=b1, op=ALU.mult),
                     r=[slk, "tb"], w=[slk])
                p.op("dve", lambda e, sl4=sl4, b2=b2: e.tensor_tensor(out=sl4, in0=sl4, in1=b2, op=ALU.add),
                     r=[slk, "tb"], w=[slk])
                p.op("dve", lambda e, i2=i2: e.tensor_scalar(out=negm[i2][:], in0=selt[i2][:], scalar1=-1.0, scalar2=-NEG,
                                                             op0=ALU.add, op1=ALU.mult), r=[slk], w=[ngk])
                for c in range(8):
                    p.op("pe", lambda e, c=c, i2=i2: e.transpose(m_ps[64:128, c * 128:(c + 1) * 128], negm[i2][:, c, :], ident[:],
                                                                tile_position=(0, 64)), r=[ngk, "ident"], w=["m_ps"])
                p.op("act", lambda e, G8=G8: e.activation(out=qaug[64:128, G8 * 1024:(G8 + 1) * 1024], in_=m_ps[64:128, :], func=AF.Copy),
                     r=["m_ps"], w=["qaug_hi"])
            for G in range(NQG):
                nkb = 4 * (G + 1)
                op_i = G % 2
                opk = "o_ps%d" % op_i
                qsl = slice(G * QG, (G + 1) * QG)

                def qk(kb, G=G, qsl=qsl):
                    si = kb % 3
                    diag = kb >= 4 * G
                    p.op("pe", lambda e: e.matmul(s_ps[si][:], lhsT=kaug[:, kb * 128:(kb + 1) * 128], rhs=qaug[:, qsl],
                                                  start=True, stop=not diag),
                         r=["kaug_lo", "kaug_hi", "qaug_lo", "qaug_hi"], w=["s_ps%d" % si])
                    if diag:
                        j = kb - 4 * G
                        p.op("pe", lambda e: e.matmul(s_ps[si][:], lhsT=ident[:], rhs=dm[:, j, :], start=False, stop=True),
                             r=["ident", "dm"], w=["s_ps%d" % si])

                def ex_pv(kb, G=G, nkb=nkb, op_i=op_i, opk=opk, hh=hh):
                    si = kb % 3
                    p.op("act", lambda e: e.activation(out=pT[si][:], in_=s_ps[si][:], func=AF.Exp),
                         r=["s_ps%d" % si], w=["pT%d" % si])
                    p.op("pe", lambda e: e.matmul(o_ps[op_i][:], lhsT=vsb[:, kb, hh, :], rhs=pT[si][:],
                                                  start=(kb == 0), stop=(kb == nkb - 1)),
                         r=["vsb", "vsb1", "pT%d" % si], w=[opk])

                LOOK = 2
                for kb in range(min(LOOK, nkb)):
                    qk(kb)
                for kb in range(nkb):
                    if kb + LOOK < nkb:
                        qk(kb + LOOK)
                    ex_pv(kb)
                ob_i = G % 2
                p.op("dve", lambda e, op_i=op_i, ob_i=ob_i: e.tensor_copy(out=osb[ob_i][:], in_=o_ps[op_i][:]), r=[opk], w=["osb%d" % ob_i])
                p.op("dve", lambda e, ob_i=ob_i: e.reciprocal(out=osb[ob_i][64:65, :], in_=osb[ob_i][64:65, :]),
                     r=["osb%d" % ob_i], w=["osb%d" % ob_i])
                p.op("pe", lambda e, ob_i=ob_i: e.matmul(bc_ps[0:64, :], lhsT=ones_f[64:65, :], rhs=osb[ob_i][64:65, :], start=True, stop=True),
                     r=["ones_f", "osb%d" % ob_i], w=["m_ps"])
                p.op("dve", lambda e, ob_i=ob_i: e.tensor_tensor(out=obf[ob_i][:], in0=osb[ob_i][0:64, :], in1=bc_ps[0:64, :], op=ALU.mult),
                     r=["osb%d" % ob_i, "m_ps"], w=["obf%d" % ob_i])
                p.dma("sp", oT[hh * 64:(hh + 1) * 64, qsl], obf[ob_i][:], r=["obf%d" % ob_i], semkey="o_obf%d" % ob_i, store=True)
        p.emit()
    return nc


def run_attn(qkv):
    nc = _get("attn", build_attn)
    blk1h, tabs, dm = moba_consts()
    tabs = np.ascontiguousarray(tabs.reshape(3, 64 * 64))
    in_maps = []
    for c in range(NCORES):
        cs = slice(c * 128, (c + 1) * 128)
        in_maps.append({
            "qT": np.ascontiguousarray(qkv[0][:, cs].T),
            "kT": np.ascontiguousarray(qkv[1][:, cs].T),
            "v": np.ascontiguousarray(qkv[2][:, cs]),
            "blk1h": blk1h, "tabs": tabs, "dmask": dm,
        })
    res = _run(nc, in_maps)
    return np.concatenate([r["oT"] for r in res.results], axis=0)


LCH = 64
TT = 512
NCH = TT // LCH
WCOLS = 672
DECAY_C = -math.exp(-0.5)


def rwkv_consts():
    j = np.arange(64)[:, None]
    t = np.arange(64)[None, :]
    incl = (j <= t).astype(np.float32)
    strict = (j < t).astype(np.float32)
    rev = (j > t).astype(np.float32)
    tri3 = (DECAY_C * np.concatenate([incl, strict, rev], axis=1)).astype(np.float32)
    tri3 = np.concatenate([tri3, tri3], axis=0)
    up_s = (j < t).astype(np.float32)
    up_i = (j <= t).astype(np.float32)
    lo_s = (t < j).astype(np.float32)
    msk = np.concatenate([up_s, up_i, up_s, up_i, lo_s], axis=1)
    msk = np.concatenate([msk, msk], axis=0)
    i2 = np.concatenate([np.eye(64, dtype=np.float32)] * 2, axis=0)
    onesbd = np.zeros((128, 128), np.float32)
    onesbd[:64, :64] = 1.0
    onesbd[64:, 64:] = 1.0
    return tri3, msk, i2, onesbd


def build_rwkv(ntiles, debug=False):
    TL = ntiles * TT
    nc = bass.Bass("TRN2", target_bir_lowering=False)
    xT = nc.dram_tensor("xT", [C, TL + 1], F32, kind="ExternalInput").ap()
    wbig = nc.dram_tensor("wbig", [C, WCOLS], F32, kind="ExternalInput").ap()
    mu6 = nc.dram_tensor("mu6", [C, 6], F32, kind="ExternalInput").ap()
    w2a = nc.dram_tensor("w2a", [65, 128], F32, kind="ExternalInput").ap()
    a2p = nc.dram_tensor("a2p", [128, 128], F32, kind="ExternalInput").ap()
    g2p = nc.dram_tensor("g2p", [160, 128], F32, kind="ExternalInput").ap()
    vecs = nc.dram_tensor("vecs", [128, 8], F32, kind="ExternalInput").ap()
    tri3_d = nc.dram_tensor("tri3", [128, 192], F32, kind="ExternalInput").ap()
    msk_d = nc.dram_tensor("msk", [128, 320], F32, kind="ExternalInput").ap()
    i2_d = nc.dram_tensor("i2", [128, 64], F32, kind="ExternalInput").ap()
    obd_d = nc.dram_tensor("onesbd", [128, 128], F32, kind="ExternalInput").ap()
    ygT = nc.dram_tensor("ygT", [128, TL], BF16, kind="ExternalOutput").ap()
    if debug:
        dbg = nc.dram_tensor("dbg", [24, 128, 512], F32, kind="ExternalOutput").ap()

    es = ExitStack()
    sb = lambda name, shape, dt: es.enter_context(nc.sbuf_tensor(name, shape, dt))
    with es:
        ident_b = sb("ident_b", [128, 128], BF16)
        ident_f = sb("ident_f", [128, 128], F32)
        wf = sb("wf", [128, 8, WCOLS], F32)
        wc = sb("wc", [128, 8, WCOLS], BF16)
        wp = sb("wp", [128, 8, WCOLS], BF16)
        mu_sb = sb("mu_sb", [128, 8, 6], F32)
        w2a_b = sb("w2a_b", [65, 128], BF16)
        a2_b = sb("a2_b", [128, 128], BF16)
        g2a_b = sb("g2a_b", [128, 128], BF16)
        g2b_b = sb("g2b_b", [32, 128], BF16)
        vec = sb("vec", [128, 8], F32)
        epsc = sb("epsc", [128, 2], F32)
        tri3 = sb("tri3_s", [128, 192], F32)
        msk = sb("msk_s", [128, 320], F32)
        i2 = sb("i2_s", [128, 64], F32)
        obd_f = sb("obd_f", [128, 128], F32)
        obd_b = sb("obd_b", [128, 128], BF16)
        xb = [sb("xb%d" % i, [128, 8, TT + 1], BF16) for i in range(2)]
        r_f = sb("r_f", [128, TT], F32)
        k_f = sb("k_f", [128, TT], F32)
        v_f = sb("v_f", [128, TT], F32)
        v_b = sb("v_b", [128, TT], BF16)
        twa = sb("twa", [65, TT], BF16)
        a1o = sb("a1o", [128, TT], BF16)
        sg_a = sb("sg_a", [128, TT], BF16)
        sg_b = sb("sg_b", [32, TT], BF16)
        g_f = sb("g_f", [128, TT], F32)
        al_f = sb("al_f", [128, TT], F32)
        kkr = sb("kkr", [128, TT], F32)
        sq_b = sb("sq_b", [128, TT], BF16)
        rn = sb("rn", [128, TT], F32)
        kk_f = sb("kk_f", [128, TT], F32)
        kt_f = sb("kt_f", [128, TT], F32)
        b_f = sb("b_f", [128, TT], F32)
        tmp1 = sb("tmp1", [128, TT], F32)
        rk_b = sb("rk_b", [128, TT], BF16)
        bon_f = sb("bon_f", [128, TT], F32)
        sgw = sb("sgw", [128, NCH, 128], F32)
        e_pos = sb("e_pos", [128, NCH, 64], F32)
        e_neg = sb("e_neg", [128, NCH, 64], F32)
        e_ex = sb("e_ex", [128, NCH, 64], F32)
        e_rem = sb("e_rem", [128, NCH, 64], F32)
        AR = sb("AR", [128, NCH, 2, 64], BF16)
        Rh_f = sb("Rh_f", [128, TT], F32)
        BhT = sb("BhT", [128, TT], BF16)
        KhT = sb("KhT", [128, TT], BF16)
        BbT = sb("BbT", [128, TT], BF16)
        KbT = sb("KbT", [128, TT], BF16)
        TM = sb("TM", [128, NCH, 4, 64], BF16)
        GM = sb("GM", [128, NCH, 320], BF16)
        Xf = sb("Xf", [128, NCH, 128], F32)
        Xb = sb("Xb", [128, NCH, 128], BF16)
        PP = [sb("PP%d" % i, [128, NCH, 128], BF16) for i in range(2)]
        RtT = sb("RtT", [128, TT], BF16)
        Y0 = sb("Y0", [128, NCH, 64], F32)
        DG = sb("DG", [128, NCH, 64], F32)
        PT_f = sb("PT_f", [128, NCH, 64], F32)
        Q_f = sb("Q_f", [128, NCH, 64], F32)
        S_f = [sb("S_f%d" % i, [128, 64], F32) for i in range(2)]
        S_b = [sb("S_b%d" % i, [128, 64], BF16) for i in range(2)]
        y_tm = sb("y_tm", [128, NCH, 64], F32)
        yT_f = sb("yT_f", [128, TT], F32)
        d_f = sb("d_f", [128, TT], F32)
        sq_f = sb("sq_f", [128, TT], F32)
        rstd = sb("rstd", [128, TT], F32)
        yo = [sb("yo%d" % i, [128, TT], BF16) for i in range(2)]
        bank = [es.enter_context(nc.psum_tensor("bank%d" % i, [128, 512], F32)) for i in range(8)]
        bk = lambda i: "bank%d" % i

        p = Prog(nc)
        for idt, nm in ((ident_b, "ident_b"), (ident_f, "ident_f")):
            p.op("pool", lambda e, idt=idt: e.memset(idt[:], 0.0), w=[nm])
            p.op("pool", lambda e, idt=idt: e.affine_select(out=idt[:], in_=idt[:], pattern=[[-1, 128]],
                                                         compare_op=ALU.not_equal, fill=1.0, base=0,
                                                         channel_multiplier=1), r=[nm], w=[nm])
        p.op("pool", lambda e: e.memset(twa[64:65, :], 1.0), w=["twa1"])
        p.op("pool", lambda e: e.memset(epsc[:, 0:1], 1e-24), w=["epsc"])
        p.op("pool", lambda e: e.memset(epsc[:, 1:2], GN_EPS), w=["epsc"])
        p.op("pool", lambda e: e.memset(S_f[0][:], 0.0), w=["S_f0"])
        p.op("pool", lambda e: e.memset(S_b[0][:], 0.0), w=["S_b0"])
        for kc in range(8):
            p.dma("sp", wf[:, kc, :], wbig[kc * 128:(kc + 1) * 128, :], w=["wf"], semkey="wf")
        p.dma("sp", mu_sb[:], mu6.rearrange("(k p) n -> p k n", p=128), w=["mu"])
        p.dma("pool", w2a_b[:], w2a, w=["w2a_b"])
        p.dma("pool", a2_b[:], a2p, w=["a2_b"])
        p.dma("pool", g2a_b[:], g2p[0:128, :], w=["g2a_b"])
        p.dma("pool", g2b_b[:], g2p[128:160, :], w=["g2b_b"])
        p.dma("pool", obd_b[:], obd_d, w=["obd_b"])
        p.dma("sp", obd_f[:], obd_d, w=["obd_f"])
        p.dma("sp", vec[:], vecs, w=["vec"])
        p.dma("sp", tri3[:], tri3_d, w=["tri3"])
        p.dma("sp", msk[:], msk_d, w=["msk"])
        p.dma("sp", i2[:], i2_d, w=["i2"])
        groups = [(0, 128, 0), (128, 256, 2), (256, 384, 3), (384, 448, 1), (448, 512, 4), (512, 672, 5)]
        for kc in range(8):
            for (c0, c1, n) in groups:
                eng = "dve" if (kc % 2 == 0) else "pool"
                p.op(eng, lambda e, kc=kc, c0=c0, c1=c1, n=n: e.tensor_scalar(
                    out=wp[:, kc, c0:c1], in0=wf[:, kc, c0:c1], scalar1=mu_sb[:, kc, n:n + 1], scalar2=None, op0=ALU.mult),
                    r=["wf", "mu"], w=["wp"])
            p.op("dve" if (kc % 2 == 0) else "pool",
                 lambda e, kc=kc: e.tensor_tensor(out=wc[:, kc, :], in0=wf[:, kc, :], in1=wp[:, kc, :], op=ALU.subtract),
                 r=["wf", "wp"], w=["wc"])

        V_KK, V_KA, V_1KA, V_RK, V_GG, V_GB, V_A0 = range(7)
        vcol = lambda i: vec[:, i:i + 1]
        tp = lambda h: (64 * h, 64 * h)
        hs = lambda h: slice(64 * h, 64 * h + 64)
        s_cur = 0

        def load_x(tj):
            for kc in range(8):
                p.dma("pool", xb[tj % 2][:, kc, :], xT[kc * 128:(kc + 1) * 128, tj * TT:tj * TT + TT + 1],
                      w=["xb%d" % (tj % 2)], semkey="xb%d" % (tj % 2))

        load_x(0)
        for ti in range(ntiles):
            t0 = ti * TT
            x_sb = xb[ti % 2]
            xk = "xb%d" % (ti % 2)
            if ti + 1 < ntiles:
                load_x(ti + 1)

            def proj(c0, c1, bi):
                m = c1 - c0
                for kc in range(8):
                    p.op("pe", lambda e, kc=kc, x_sb=x_sb: e.matmul(bank[bi][0:m, :], lhsT=wc[:, kc, c0:c1], rhs=x_sb[:, kc, 1:TT + 1],
                                                        start=(kc == 0), stop=False), r=[xk, "wc"], w=[bk(bi)])
                for kc in range(8):
                    p.op("pe", lambda e, kc=kc, x_sb=x_sb: e.matmul(bank[bi][0:m, :], lhsT=wp[:, kc, c0:c1], rhs=x_sb[:, kc, 0:TT],
                                                        start=False, stop=(kc == 7)), r=[xk, "wp"], w=[bk(bi)])

            proj(0, 128, 0)
            p.op("act", lambda e: e.activation(out=r_f[:], in_=bank[0][:], func=AF.Copy), r=[bk(0)], w=["r_f"])
            proj(128, 256, 1)
            p.op("act", lambda e: e.activation(out=k_f[:], in_=bank[1][:], func=AF.Copy), r=[bk(1)], w=["k_f"])
            proj(256, 384, 2)
            p.op("act", lambda e: e.activation(out=v_f[:], in_=bank[2][:], func=AF.Copy), r=[bk(2)], w=["v_f"])
            p.op("act", lambda e: e.activation(out=v_b[:], in_=bank[2][:], func=AF.Copy), r=[bk(2)], w=["v_b"])
            proj(384, 512, 3)
            p.op("act", lambda e: e.activation(out=twa[0:64, :], in_=bank[3][0:64, :], func=AF.Tanh), r=[bk(3)], w=["twa"])
            p.op("dve", lambda e: e.tensor_copy(out=a1o[64:128, :], in_=bank[3][64:128, :]), r=[bk(3)], w=["a1o"])
            proj(512, 640, 4)
            p.op("act", lambda e: e.activation(out=sg_a[:], in_=bank[4][:], func=AF.Sigmoid), r=[bk(4)], w=["sg_a"])
            proj(640, 672, 5)
            p.op("act", lambda e: e.activation(out=sg_b[:], in_=bank[5][0:32, :], func=AF.Sigmoid), r=[bk(5)], w=["sg_b"])
            p.op("pe", lambda e: e.matmul(bank[6][:], lhsT=g2a_b[:], rhs=sg_a[:], start=True, stop=False), r=["g2a_b", "sg_a"], w=[bk(6)])
            p.op("pe", lambda e: e.matmul(bank[6][:], lhsT=g2b_b[:], rhs=sg_b[:], start=False, stop=True), r=["g2b_b", "sg_b"], w=[bk(6)])
            p.op("act", lambda e: e.activation(out=g_f[:], in_=bank[6][:], func=AF.Copy), r=[bk(6)], w=["g_f"])
            p.op("pe", lambda e: e.matmul(bank[7][:], lhsT=a2_b[64:128, :], rhs=a1o[64:128, :], start=True, stop=True),
                 r=["a2_b", "a1o"], w=[bk(7)])
            p.op("act", lambda e: e.activation(out=al_f[:], in_=bank[7][:], func=AF.Sigmoid, bias=vcol(V_A0)), r=[bk(7), "vec"], w=["al_f"])
            p.op("dve", lambda e: e.tensor_scalar(out=kkr[:], in0=k_f[:], scalar1=vcol(V_KK), scalar2=None, op0=ALU.mult), r=["k_f", "vec"], w=["kkr"])
            p.op("act", lambda e: e.activation(out=sq_b[:], in_=k_f[:], func=AF.Square, scale=vcol(V_KK)), r=["k_f", "vec"], w=["sq_b"])
            p.op("pe", lambda e: e.matmul(bank[0][:], lhsT=obd_b[:], rhs=sq_b[:], start=True, stop=True), r=["obd_b", "sq_b"], w=[bk(0)])
            p.op("act", lambda e: e.activation(out=rn[:], in_=bank[0][:], func=AF.Sqrt, bias=epsc[:, 0:1]), r=[bk(0), "epsc"], w=["rn"])
            p.op("dve", lambda e: e.reciprocal(out=rn[:], in_=rn[:]), r=["rn"], w=["rn"])
            p.op("dve", lambda e: e.tensor_tensor(out=kk_f[:], in0=kkr[:], in1=rn[:], op=ALU.mult), r=["kkr", "rn"], w=["kk_f"])
            p.op("pool", lambda e: e.tensor_scalar(out=tmp1[:], in0=al_f[:], scalar1=-1.0, scalar2=vcol(V_KA), op0=ALU.add, op1=ALU.mult),
                 r=["al_f", "vec"], w=["tmp1"])
            p.op("dve", lambda e: e.scalar_tensor_tensor(out=kt_f[:], in0=tmp1[:], scalar=1.0, in1=k_f[:], op0=ALU.add, op1=ALU.mult),
                 r=["k_f", "tmp1"], w=["kt_f"])
            p.op("dve", lambda e: e.tensor_tensor(out=b_f[:], in0=kk_f[:], in1=al_f[:], op=ALU.mult), r=["kk_f", "al_f"], w=["b_f"])
            p.op("dve", lambda e: e.scalar_tensor_tensor(out=rk_b[:], in0=r_f[:], scalar=vcol(V_RK), in1=kt_f[:], op0=ALU.mult, op1=ALU.mult),
                 r=["r_f", "kt_f", "vec"], w=["rk_b"])
            p.op("pe", lambda e: e.matmul(bank[1][:], lhsT=obd_b[:], rhs=rk_b[:], start=True, stop=True), r=["obd_b", "rk_b"], w=[bk(1)])
            p.op("dve", lambda e: e.tensor_tensor(out=bon_f[:], in0=v_f[:], in1=bank[1][:], op=ALU.mult), r=["v_f", bk(1)], w=["bon_f"])

            for c in range(NCH):
                p.op("pe", lambda e, c=c: e.matmul(bank[2 + c // 4][0:64, (c % 4) * 128:(c % 4 + 1) * 128],
                                                   lhsT=twa[:, c * 64:(c + 1) * 64], rhs=w2a_b[:], start=True, stop=True),
                     r=["twa", "twa1", "w2a_b"], w=[bk(2 + c // 4)])
            for hf in range(2):
                p.op("act", lambda e, hf=hf: e.activation(out=sgw[0:64, hf * 4:(hf + 1) * 4, :],
                                                         in_=bank[2 + hf][0:64, :].rearrange("p (c d) -> p c d", c=4), func=AF.Sigmoid),
                     r=[bk(2 + hf)], w=["sgw"])
            for c in range(NCH):
                for kind in range(3):
                    p.op("pe", lambda e, c=c, kind=kind: e.matmul(bank[4 + kind][:, c * 64:(c + 1) * 64], lhsT=sgw[0:64, c, :],
                                                                 rhs=tri3[0:64, kind * 64:(kind + 1) * 64], start=True, stop=True),
                         r=["sgw", "tri3"], w=[bk(4 + kind)])
            v3 = lambda tl: tl[:].rearrange("p c t -> p (c t)")
            p.op("act", lambda e: e.activation(out=v3(e_pos), in_=bank[4][:], func=AF.Exp), r=[bk(4)], w=["e_pos"])
            p.op("act", lambda e: e.activation(out=v3(e_neg), in_=bank[4][:], func=AF.Exp, scale=-1.0), r=[bk(4)], w=["e_neg"])
            p.op("act", lambda e: e.activation(out=v3(e_ex), in_=bank[5][:], func=AF.Exp), r=[bk(5)], w=["e_ex"])
            p.op("act", lambda e: e.activation(out=v3(e_rem), in_=bank[6][:], func=AF.Exp), r=[bk(6)], w=["e_rem"])
            c3 = lambda ap2: ap2.rearrange("p (c t) -> p c t", c=NCH)
            p.op("dve", lambda e: e.scalar_tensor_tensor(out=AR[:, :, 0, :], in0=c3(kk_f[:]), scalar=-1.0, in1=e_ex[:], op0=ALU.mult, op1=ALU.mult),
                 r=["kk_f", "e_ex"], w=["AR"])
            p.op("dve", lambda e: e.tensor_tensor(out=Rh_f[:], in0=r_f[:], in1=v3(e_pos), op=ALU.mult), r=["r_f", "e_pos"], w=["Rh_f"])
            p.op("act", lambda e: e.activation(out=AR[:, :, 1, :], in_=c3(Rh_f[:]), func=AF.Copy), r=["Rh_f"], w=["AR"])
            p.op("dve", lambda e: e.tensor_tensor(out=BhT[:], in0=b_f[:], in1=v3(e_neg), op=ALU.mult), r=["b_f", "e_neg"], w=["BhT"])
            p.op("pool", lambda e: e.tensor_tensor(out=KhT[:], in0=kt_f[:], in1=v3(e_neg), op=ALU.mult), r=["kt_f", "e_neg"], w=["KhT"])
            p.op("dve", lambda e: e.tensor_tensor(out=BbT[:], in0=b_f[:], in1=v3(e_rem), op=ALU.mult), r=["b_f", "e_rem"], w=["BbT"])
            p.op("pool", lambda e: e.tensor_tensor(out=KbT[:], in0=kt_f[:], in1=v3(e_rem), op=ALU.mult), r=["kt_f", "e_rem"], w=["KbT"])

            tmb = [bank[0][:].bitcast(BF16), bank[1][:].bitcast(BF16)]
            srcs = [(lambda c: AR[:, c, 0, :], "AR"), (lambda c: v_b[:, c * 64:(c + 1) * 64], "v_b"),
                    (lambda c: BbT[:, c * 64:(c + 1) * 64], "BbT"), (lambda c: KbT[:, c * 64:(c + 1) * 64], "KbT")]
            for c in range(NCH):
                for si, (sf, sk) in enumerate(srcs):
                    for h in range(2):
                        col = (c % 4) * 256 + si * 64
                        p.op("pe", lambda e, c=c, sf=sf, h=h, col=col: e.transpose(
                            tmb[c // 4][hs(h), col:col + 64], sf(c)[hs(h), :], ident_b[hs(h), hs(h)], tile_position=tp(h)),
                            r=[sk, "ident_b"], w=[bk(c // 4)])
            for hf in range(2):
                p.op("act" if hf == 0 else "dve",
                     (lambda e, hf=hf: e.activation(out=TM[:, hf * 4:(hf + 1) * 4, :, :].rearrange("p c s t -> p (c s t)"), in_=tmb[hf], func=AF.Copy))
                     if hf == 0 else
                     (lambda e, hf=hf: e.tensor_copy(out=TM[:, hf * 4:(hf + 1) * 4, :, :].rearrange("p c s t -> p (c s t)"), in_=tmb[hf])),
                     r=[bk(hf)], w=["TM"])

            for c in range(NCH):
                bi = 2 + (c % 2)
                cs_ = slice(c * 64, (c + 1) * 64)
                for h in range(2):
                    arh = AR[hs(h), c, :, :].rearrange("p s t -> p (s t)")
                    p.op("pe", lambda e, h=h, arh=arh, cs_=cs_, bi=bi: e.matmul(bank[bi][hs(h), 0:128], lhsT=BhT[hs(h), cs_], rhs=arh,
                                                                           start=True, stop=True, tile_position=tp(h)),
                         r=["BhT", "AR"], w=[bk(bi)])
                    p.op("pe", lambda e, h=h, arh=arh, cs_=cs_, bi=bi: e.matmul(bank[bi][hs(h), 128:256], lhsT=KhT[hs(h), cs_], rhs=arh,
                                                                           start=True, stop=True, tile_position=tp(h)),
                         r=["KhT", "AR"], w=[bk(bi)])
                    p.op("pe", lambda e, h=h, c=c, cs_=cs_, bi=bi: e.matmul(bank[bi][hs(h), 256:320], lhsT=AR[hs(h), c, 0, :], rhs=BhT[hs(h), cs_],
                                                                       start=True, stop=True, tile_position=tp(h)),
                         r=["BhT", "AR"], w=[bk(bi)])
                p.op("dve", lambda e, c=c, bi=bi: e.tensor_tensor(out=GM[:, c, :], in0=bank[bi][:, 0:320], in1=msk[:], op=ALU.mult),
                     r=[bk(bi), "msk"], w=["GM"])

            for c in range(NCH):
                for h in range(2):
                    p.op("pe", lambda e, c=c, h=h: e.matmul(bank[4][hs(h), c * 64:(c + 1) * 64], lhsT=GM[hs(h), c, 128:192], rhs=TM[hs(h), c, 1, :],
                                                           start=True, stop=True, tile_position=tp(h)), r=["GM", "TM"], w=[bk(4)])
            p.op("pool", lambda e: e.tensor_copy(out=Xf[:, :, 0:64], in_=TM[:, :, 0, :]), r=["TM"], w=["Xf"])
            p.op("dve", lambda e: e.tensor_copy(out=Xf[:, :, 64:128], in_=bank[4][:].rearrange("p (c t) -> p c t", c=NCH)), r=[bk(4)], w=["Xf"])
            p.op("dve", lambda e: e.tensor_copy(out=Xb[:], in_=Xf[:]), r=["Xf"], w=["Xb"])

            NLEV = 6
            for lev in range(NLEV):
                if lev == 0:
                    P_of = lambda c: GM[:, c, 256:320]
                    PT_of = lambda c: GM[:, c, 0:64]
                    pk = "GM"
                else:
                    ppt = PP[(lev - 1) % 2]
                    P_of = lambda c, ppt=ppt: ppt[:, c, 0:64]
                    PT_of = lambda c, ppt=ppt: ppt[:, c, 64:128]
                    pk = "PP%d" % ((lev - 1) % 2)
                for c in range(NCH):
                    bi = 0 + c // 4
                    for h in range(2):
                        p.op("pe", lambda e, c=c, h=h, bi=bi, PT_of=PT_of: e.matmul(
                            bank[bi][hs(h), (c % 4) * 128:(c % 4 + 1) * 128], lhsT=PT_of(c)[hs(h), :], rhs=Xb[hs(h), c, :],
                            start=True, stop=True, tile_position=tp(h)), r=[pk, "Xb"], w=[bk(bi)])
                if lev < NLEV - 1:
                    for c in range(NCH):
                        bi = 2 + c // 4
                        for h in range(2):
                            p.op("pe", lambda e, c=c, h=h, bi=bi, P_of=P_of, PT_of=PT_of: e.matmul(
                                bank[bi][hs(h), (c % 4) * 128:(c % 4) * 128 + 64], lhsT=PT_of(c)[hs(h), :], rhs=P_of(c)[hs(h), :],
                                start=True, stop=True, tile_position=tp(h)), r=[pk], w=[bk(bi)])
                            p.op("pe", lambda e, c=c, h=h, bi=bi, P_of=P_of, PT_of=PT_of: e.matmul(
                                bank[bi][hs(h), (c % 4) * 128 + 64:(c % 4 + 1) * 128], lhsT=P_of(c)[hs(h), :], rhs=PT_of(c)[hs(h), :],
                                start=True, stop=True, tile_position=tp(h)), r=[pk], w=[bk(bi)])
                for hf in range(2):
                    xs = Xf[:, hf * 4:(hf + 1) * 4, :].rearrange("p c t -> p (c t)")
                    p.op("dve", lambda e, hf=hf, xs=xs: e.tensor_tensor(out=xs, in0=xs, in1=bank[hf][:], op=ALU.add), r=["Xf", bk(hf)], w=["Xf"])
                p.op("dve", lambda e: e.tensor_copy(out=Xb[:], in_=Xf[:]), r=["Xf"], w=["Xb"])
                if lev < NLEV - 1:
                    ppn = PP[lev % 2]
                    for hf in range(2):
                        p.op("act", lambda e, hf=hf, ppn=ppn: e.activation(out=ppn[:, hf * 4:(hf + 1) * 4, :].rearrange("p c t -> p (c t)"),
                                                                          in_=bank[2 + hf][:], func=AF.Copy),
                             r=[bk(2 + hf)], w=["PP%d" % (lev % 2)])

            for c in range(NCH):
                for h in range(2):
                    p.op("pe", lambda e, c=c, h=h: e.matmul(bank[4][hs(h), c * 64:(c + 1) * 64], lhsT=Xb[hs(h), c, 0:64], rhs=GM[hs(h), c, 64:128],
                                                           start=True, stop=True, tile_position=tp(h)), r=["Xb", "GM"], w=[bk(4)])
                    p.op("pe", lambda e, c=c, h=h: e.matmul(bank[5][hs(h), c * 64:(c + 1) * 64], lhsT=GM[hs(h), c, 64:128], rhs=Xb[hs(h), c, 64:128],
                                                           start=True, stop=False, tile_position=tp(h)), r=["Xb", "GM"], w=[bk(5)])
                    p.op("pe", lambda e, c=c, h=h: e.matmul(bank[5][hs(h), c * 64:(c + 1) * 64], lhsT=GM[hs(h), c, 192:256], rhs=TM[hs(h), c, 1, :],
                                                           start=False, stop=True, tile_position=tp(h)), r=["TM", "GM"], w=[bk(5)])
                    p.op("pe", lambda e, c=c, h=h: e.matmul(bank[6][hs(h), c * 64:(c + 1) * 64], lhsT=Xb[hs(h), c, 0:64], rhs=TM[hs(h), c, 2, :],
                                                           start=True, stop=True, tile_position=tp(h)), r=["Xb", "TM"], w=[bk(6)])
                    p.op("pe", lambda e, c=c, h=h: e.matmul(bank[7][hs(h), c * 64:(c + 1) * 64], lhsT=TM[hs(h), c, 2, :], rhs=Xb[hs(h), c, 64:128],
                                                           start=True, stop=False, tile_position=tp(h)), r=["Xb", "TM"], w=[bk(7)])
                    p.op("pe", lambda e, c=c, h=h: e.matmul(bank[7][hs(h), c * 64:(c + 1) * 64], lhsT=TM[hs(h), c, 3, :], rhs=TM[hs(h), c, 1, :],
                                                           start=False, stop=True, tile_position=tp(h)), r=["TM"], w=[bk(7)])
            p.op("dve", lambda e: e.tensor_tensor(out=RtT[:], in0=bank[4][:], in1=Rh_f[:], op=ALU.add), r=[bk(4), "Rh_f"], w=["RtT"])
            p.op("act", lambda e: e.activation(out=v3(Y0), in_=bank[5][:], func=AF.Copy), r=[bk(5)], w=["Y0"])
            p.op("pool", lambda e: e.tensor_tensor(out=DG[:], in0=i2[:].unsqueeze(1).to_broadcast([128, NCH, 64]),
                                                  in1=e_pos[:, :, 63:64].to_broadcast([128, NCH, 64]), op=ALU.mult), r=["i2", "e_pos"], w=["DG"])
            p.op("dve", lambda e: e.tensor_tensor(out=v3(PT_f), in0=bank[6][:], in1=v3(DG), op=ALU.add), r=[bk(6), "DG"], w=["PT_f"])
            p.op("act", lambda e: e.activation(out=v3(Q_f), in_=bank[7][:], func=AF.Copy), r=[bk(7)], w=["Q_f"])

            for c in range(NCH):
                sn = 1 - s_cur
                for h in range(2):
                    p.op("pe", lambda e, c=c, h=h, s_cur=s_cur: e.matmul(bank[1][hs(h), (c % 2) * 64:(c % 2 + 1) * 64], lhsT=PT_f[hs(h), c, :],
                                                                        rhs=S_f[s_cur][hs(h), :], start=True, stop=True, tile_position=tp(h)),
                         r=["PT_f", "S_f%d" % s_cur, bk(1)], w=[bk(1) + ("a" if c % 2 else "b")])
                p.op("dve", lambda e, c=c, sn=sn: e.tensor_tensor(out=S_f[sn][:], in0=bank[1][:, (c % 2) * 64:(c % 2 + 1) * 64], in1=Q_f[:, c, :], op=ALU.add),
                     r=[bk(1) + ("a" if c % 2 else "b"), bk(1), "Q_f"], w=["S_f%d" % sn])
                for h in range(2):
                    p.op("pe", lambda e, c=c, h=h, s_cur=s_cur: e.matmul(bank[0][hs(h), c * 64:(c + 1) * 64], lhsT=RtT[hs(h), c * 64:(c + 1) * 64],
                                                                        rhs=S_b[s_cur][hs(h), :], start=True, stop=True, tile_position=tp(h)),
                         r=["RtT", "S_b%d" % s_cur], w=[bk(0)])
                p.op("act", lambda e, sn=sn: e.activation(out=S_b[sn][:], in_=S_f[sn][:], func=AF.Copy), r=["S_f%d" % sn], w=["S_b%d" % sn])
                s_cur = sn
            p.op("dve", lambda e: e.tensor_tensor(out=v3(y_tm), in0=bank[0][:], in1=v3(Y0), op=ALU.add), r=[bk(0), "Y0"], w=["y_tm"])

            for c in range(NCH):
                for h in range(2):
                    p.op("pe", lambda e, c=c, h=h: e.matmul(bank[2][hs(h), c * 64:(c + 1) * 64], lhsT=y_tm[hs(h), c, :], rhs=ident_f[hs(h), hs(h)],
                                                           start=True, stop=True, tile_position=tp(h)), r=["y_tm", "ident_f"], w=[bk(2)])
            p.op("act", lambda e: e.activation(out=yT_f[:], in_=bank[2][:], func=AF.Copy), r=[bk(2)], w=["yT_f"])
            p.op("pe", lambda e: e.matmul(bank[3][:], lhsT=obd_f[:], rhs=yT_f[:], start=True, stop=True), r=["obd_f", "yT_f"], w=[bk(3)])
            p.op("dve", lambda e: e.scalar_tensor_tensor(out=d_f[:], in0=bank[3][:], scalar=-1.0 / 64, in1=yT_f[:], op0=ALU.mult, op1=ALU.add),
                 r=[bk(3), "yT_f"], w=["d_f"])
            p.op("act", lambda e: e.activation(out=sq_f[:], in_=d_f[:], func=AF.Square), r=["d_f"], w=["sq_f"])
            p.op("pe", lambda e: e.matmul(bank[4][:], lhsT=obd_f[:], rhs=sq_f[:], start=True, stop=True), r=["obd_f", "sq_f"], w=[bk(4)])
            p.op("act", lambda e: e.activation(out=rstd[:], in_=bank[4][:], func=AF.Sqrt, scale=1.0 / 64, bias=epsc[:, 1:2]),
                 r=[bk(4), "epsc"], w=["rstd"])
            p.op("dve", lambda e: e.reciprocal(out=rstd[:], in_=rstd[:]), r=["rstd"], w=["rstd"])
            p.op("dve", lambda e: e.tensor_tensor(out=d_f[:], in0=d_f[:], in1=rstd[:], op=ALU.mult), r=["d_f", "rstd"], w=["d_f"])
            p.op("act", lambda e: e.activation(out=d_f[:], in_=d_f[:], func=AF.Identity, scale=vcol(V_GG), bias=vcol(V_GB)),
                 r=["d_f", "vec"], w=["d_f"])
            p.op("dve", lambda e: e.tensor_tensor(out=d_f[:], in0=d_f[:], in1=bon_f[:], op=ALU.add), r=["d_f", "bon_f"], w=["d_f"])
            y_o = yo[ti % 2]
            yk = "yo%d" % (ti % 2)
            p.op("dve", lambda e, y_o=y_o: e.tensor_tensor(out=y_o[:], in0=d_f[:], in1=g_f[:], op=ALU.mult), r=["d_f", "g_f"], w=[yk])
            p.dma("sp", ygT[:, t0:t0 + TT], y_o[:], r=[yk], semkey="o_" + yk, store=True)
            if debug and ti == 0:
                f2 = lambda tl: tl[:].rearrange("p c t -> p (c t)")
                dl = [(r_f[:], "r_f"), (k_f[:], "k_f"), (v_f[:], "v_f"), (al_f[:], "al_f"), (g_f[:], "g_f"), (kk_f[:], "kk_f"),
                      (bon_f[:], "bon_f"), (f2(e_pos), "e_pos"), (f2(e_ex), "e_ex"), (f2(e_rem), "e_rem"), (Rh_f[:], "Rh_f"),
                      (yT_f[:], "yT_f"), (rstd[:], "rstd"), (d_f[:], "d_f"), (f2(Y0), "Y0"), (f2(PT_f), "PT_f"), (f2(Q_f), "Q_f"),
                      (f2(y_tm), "y_tm"), (Xf[:, 0:4, :].rearrange("p c t -> p (c t)"), "Xf"), (kt_f[:], "kt_f"), (b_f[:], "b_f"),
                      (f2(e_neg), "e_neg")]
                for i, (ap_, key) in enumerate(dl):
                    p.dma("sp", dbg[i], ap_, r=[key], semkey="dbgo", store=True)
        p.emit()
    return nc


def rwkv_inputs(inp, ntiles=T // TT, cores=range(NCORES)):
    TL = ntiles * TT
    x = inp["x"][0]
    xT = np.zeros((C, TL + 1), np.float32)
    xT[:, 1:] = x[:TL].T
    tri3, msk, i2, onesbd = rwkv_consts()
    mu6 = np.ascontiguousarray(inp["rwkv_mu"][0].T)
    maps = []
    for c in cores:
        cs = slice(c * 128, (c + 1) * 128)
        wrkv = inp["rwkv_w_rkv"][0]
        wbig = np.concatenate([wrkv[0][:, cs], wrkv[1][:, cs], wrkv[2][:, cs], inp["rwkv_w1"][0], inp["rwkv_a1"][0], inp["rwkv_g1"][0]], axis=1)
        w2a = np.concatenate([inp["rwkv_w2"][0][:, cs], inp["rwkv_w0"][0][None, cs]], axis=0)
        a2p = np.zeros((128, 128), np.float32)
        a2p[64:128] = inp["rwkv_a2"][0][:, cs]
        ka = inp["rwkv_k_a"][0][cs]
        one = np.ones_like(ka)
        vecs = np.stack([inp["rwkv_k_k"][0][cs], ka, one, inp["rwkv_r_k"][0].reshape(-1)[cs], inp["rwkv_gn_g"][0][cs],
                         inp["rwkv_gn_b"][0][cs], inp["rwkv_a0"][0][cs], one], axis=1).astype(np.float32)
        maps.append({
            "xT": xT, "wbig": np.ascontiguousarray(wbig), "mu6": mu6, "w2a": np.ascontiguousarray(w2a), "a2p": a2p,
            "g2p": np.ascontiguousarray(inp["rwkv_g2"][0][:, cs]), "vecs": np.ascontiguousarray(vecs),
            "tri3": tri3, "msk": msk, "i2": i2, "onesbd": onesbd,
        })
    return maps


def run_rwkv(inp):
    nc = _get("rwkv", lambda: build_rwkv(T // TT))
    maps = rwkv_inputs(inp)
    res = _run(nc, maps)
    return np.concatenate([r["ygT"] for r in res.results], axis=0)


def kernel(**inputs):
    inp = {k: np.asarray(v) for k, v in inputs.items()}
    x0 = np.ascontiguousarray(inp["x"][0], dtype=np.float32)
    ygT = run_rwkv(inp)
    lnp0 = np.stack([inp["ln_mix_g"][0], inp["ln_mix_b"][0], inp["ln_ffn_g"][0], inp["ln_ffn_b"][0]]).astype(np.float32)
    x1, qkv = run_post(ygT, x0, inp["rwkv_w_o"][0], inp["ffn_w_in"][0], inp["ffn_w_down"][0], lnp0,
                       inp["moba_w_qkv"][0], rope_tables())
    oT = run_attn(qkv)
    lnp1 = np.stack([inp["ln_mix_g"][1], inp["ln_mix_b"][1], inp["ln_ffn_g"][1], inp["ln_ffn_b"][1]]).astype(np.float32)
    out, _ = run_post(oT, x1, inp["moba_w_o"][0], inp["ffn_w_in"][1], inp["ffn_w_down"][1], lnp1)
    return out.reshape(1, T, C).astype(np.float32)
```
